# Optimizing a Trainium2 kernel written in Bass

```python
import math
import jax, jax.numpy as jnp
from jax import lax
import numpy as np

D_MODEL = 2048
BATCH = 2
SEQ = 8192
DEPTH = 4

CHUNK = 64
N_MEM = 256
N_MIXERS = 2
N_A = (DEPTH + 1) // 2
N_B = DEPTH // 2
HEAD_DIM = 128
MEM_W = D_MODEL // 4
MEM_HEADS = MEM_W // HEAD_DIM
TOK_W = D_MODEL - MEM_W
SB_HEADS = TOK_W // HEAD_DIM
SB_BLOCK = 128
S5_GROUP = 16
S5_GROUPS = TOK_W // S5_GROUP
S5_STATE = 64
FFN_DIM = 256 * math.ceil(8 * D_MODEL / 3 / 256)
EPS = 1e-6

kernel_name = "hybrid_stickbreak_s5_macaron_memory_trunk"


def rms_norm(x, g):
    x32 = x.astype(jnp.float32)
    y = x32 * lax.rsqrt(jnp.mean(x32 * x32, axis=-1, keepdims=True) + EPS) * g.astype(jnp.float32)
    return y.astype(x.dtype)


def swiglu_ffn(h, w_gu, w_down):
    g, u = jnp.split(h @ w_gu, 2, axis=-1)
    return (jax.nn.silu(g) * u) @ w_down


def stick_breaking_attention(q, k, v):
    B, L, H, d = q.shape
    q, k, v = (a.transpose(0, 2, 1, 3) for a in (q, k, v))
    scale = 1.0 / math.sqrt(d)
    outs = []
    for qs in range(0, L, SB_BLOCK):
        qe = qs + SB_BLOCK
        z = jnp.einsum("bhqd,bhkd->bhqk", q[:, :, qs:qe], k[:, :, :qe]).astype(jnp.float32) * scale
        t_idx = qs + jnp.arange(SB_BLOCK)[:, None]
        s_idx = jnp.arange(qe)[None, :]
        before = s_idx < t_idx
        log_not = jnp.where(before, jax.nn.log_sigmoid(-z), 0.0)
        between = lax.cumsum(log_not, axis=3, reverse=True) - log_not
        w = jnp.where(before, jnp.exp(jax.nn.log_sigmoid(z) + between), 0.0)
        outs.append(jnp.einsum("bhqk,bhkd->bhqd", w.astype(v.dtype), v[:, :, :qe]))
    o = jnp.concatenate(outs, axis=2)
    return o.transpose(0, 2, 1, 3).reshape(B, L, H * d)


def _ssm_combine(left, right):
    a_l, b_l = left
    a_r, b_r = right
    return (a_r * a_l, a_r * b_l + b_r)


def s5_glu(u, log_dt, a_re, a_im, b_re, b_im, c_re, c_im, d_skip, w_glu):
    B, L, _ = u.shape
    f32 = jnp.float32
    dt = jnp.exp(log_dt.astype(f32))[:, None]
    lam = lax.complex(a_re.astype(f32), a_im.astype(f32))
    a_bar = jnp.exp(lam * dt)
    b = lax.complex(b_re.astype(f32), b_im.astype(f32))
    b_bar = ((a_bar - 1.0) / lam)[..., None] * b
    c = lax.complex(c_re.astype(f32), c_im.astype(f32))
    uc = u.astype(f32).reshape(B, L, S5_GROUPS, S5_GROUP)
    bu = jnp.einsum("gnc,blgc->blgn", b_bar, uc.astype(jnp.complex64))
    a_elems = jnp.broadcast_to(a_bar[None, None], bu.shape)
    _, states = lax.associative_scan(_ssm_combine, (a_elems, bu), axis=1)
    y = jnp.einsum("gcn,blgn->blgc", c, states).real + d_skip.astype(f32).reshape(S5_GROUPS, S5_GROUP) * uc
    y = jax.nn.gelu(y.reshape(B, L, TOK_W))
    y = y * jax.nn.sigmoid(y @ w_glu.astype(f32))
    return y.astype(u.dtype)


def memory_cross_attention(q, mem_h, w_mem_kv, q_gain, k_gain):
    B, L, _ = q.shape
    M = mem_h.shape[1]
    k, v = jnp.split(mem_h @ w_mem_kv, 2, axis=-1)
    q = rms_norm(q.reshape(B, L, MEM_HEADS, HEAD_DIM), q_gain)
    k = rms_norm(k.reshape(B, M, MEM_HEADS, HEAD_DIM), k_gain)
    v = v.reshape(B, M, MEM_HEADS, HEAD_DIM)
    s = jnp.einsum("blhd,bmhd->bhlm", q, k).astype(jnp.float32) / math.sqrt(HEAD_DIM)
    p = jax.nn.softmax(s, axis=-1).astype(v.dtype)
    o = jnp.einsum("bhlm,bmhd->blhd", p, v)
    return o.reshape(B, L, MEM_W)


def setup_inputs(seed: int = 0) -> dict:
    key = jax.random.key(seed)
    ks = jax.random.split(key, 32)
    f32 = jnp.float32

    def nrm(k, shape, fan_in):
        return jax.random.normal(k, shape, f32) * fan_in ** -0.5

    def gain(k, shape):
        return 1.0 + 0.02 * jax.random.normal(k, shape, f32)

    G, N, C = S5_GROUPS, S5_STATE, S5_GROUP
    return dict(
        x=jax.random.normal(ks[0], (BATCH, SEQ, D_MODEL), f32),
        mem=jax.random.normal(ks[1], (BATCH, N_MEM, D_MODEL), f32),
        ffn1_norm=gain(ks[2], (DEPTH, D_MODEL)),
        ffn1_w_gu=nrm(ks[3], (DEPTH, D_MODEL, 2 * FFN_DIM), D_MODEL),
        ffn1_w_down=nrm(ks[4], (DEPTH, FFN_DIM, D_MODEL), FFN_DIM),
        mix_norm=gain(ks[5], (DEPTH, D_MODEL)),
        mem_norm=gain(ks[6], (DEPTH, D_MODEL)),
        w_mem_kv=nrm(ks[7], (DEPTH, D_MODEL, 2 * MEM_W), D_MODEL),
        xq_norm=gain(ks[8], (DEPTH, HEAD_DIM)),
        xk_norm=gain(ks[9], (DEPTH, HEAD_DIM)),
        w_out=nrm(ks[10], (DEPTH, TOK_W + MEM_W, D_MODEL), TOK_W + MEM_W),
        ffn2_norm=gain(ks[11], (DEPTH, D_MODEL)),
        ffn2_w_gu=nrm(ks[12], (DEPTH, D_MODEL, 2 * FFN_DIM), D_MODEL),
        ffn2_w_down=nrm(ks[13], (DEPTH, FFN_DIM, D_MODEL), FFN_DIM),
        sb_w_in=nrm(ks[14], (N_A, D_MODEL, 3 * TOK_W + MEM_W), D_MODEL),
        s5_w_in=nrm(ks[15], (N_B, D_MODEL, TOK_W + MEM_W), D_MODEL),
        s5_log_dt=jax.random.uniform(ks[16], (N_B, G), f32, math.log(1e-3), math.log(1e-1)),
        s5_a_re=-0.5 + 0.01 * jax.random.normal(ks[17], (N_B, G, N), f32),
        s5_a_im=jnp.pi * jnp.arange(N, dtype=f32) + 0.01 * jax.random.normal(ks[18], (N_B, G, N), f32),
        s5_b_re=nrm(ks[19], (N_B, G, N, C), 2 * C),
        s5_b_im=nrm(ks[20], (N_B, G, N, C), 2 * C),
        s5_c_re=nrm(ks[21], (N_B, G, C, N), 2 * N),
        s5_c_im=nrm(ks[22], (N_B, G, C, N), 2 * N),
        s5_d=jax.random.normal(ks[23], (N_B, TOK_W), f32),
        s5_w_glu=nrm(ks[24], (N_B, TOK_W, TOK_W), TOK_W),
    )


def reference(x, mem, ffn1_norm, ffn1_w_gu, ffn1_w_down, mix_norm, mem_norm, w_mem_kv,
              xq_norm, xk_norm, w_out, ffn2_norm, ffn2_w_gu, ffn2_w_down, sb_w_in,
              s5_w_in, s5_log_dt, s5_a_re, s5_a_im, s5_b_re, s5_b_im, s5_c_re, s5_c_im,
              s5_d, s5_w_glu):
    B, L, _ = x.shape
    for i in range(DEPTH):
        x = x + 0.5 * swiglu_ffn(rms_norm(x, ffn1_norm[i]), ffn1_w_gu[i], ffn1_w_down[i])
        h = rms_norm(x, mix_norm[i])
        j = i // N_MIXERS
        if i % N_MIXERS == 0:
            q, k, v, q_mem = jnp.split(h @ sb_w_in[j], [TOK_W, 2 * TOK_W, 3 * TOK_W], axis=-1)
            shp = (B, L, SB_HEADS, HEAD_DIM)
            tok = stick_breaking_attention(q.reshape(shp), k.reshape(shp), v.reshape(shp))
        else:
            u, q_mem = jnp.split(h @ s5_w_in[j], [TOK_W], axis=-1)
            tok = s5_glu(u, s5_log_dt[j], s5_a_re[j], s5_a_im[j], s5_b_re[j], s5_b_im[j],
                         s5_c_re[j], s5_c_im[j], s5_d[j], s5_w_glu[j])
        mem_h = rms_norm(mem, mem_norm[i])
        cross = memory_cross_attention(q_mem, mem_h, w_mem_kv[i], xq_norm[i], xk_norm[i])
        x = x + jnp.concatenate([tok, cross], axis=-1) @ w_out[i]
        x = x + 0.5 * swiglu_ffn(rms_norm(x, ffn2_norm[i]), ffn2_w_gu[i], ffn2_w_down[i])
    return x
```

```python
import contextlib
import math
import numpy as np
import concourse.bass as bass
import concourse.mybir as mybir
from concourse.bass_utils import run_bass_kernel_spmd

F32 = mybir.dt.float32
BF16 = mybir.dt.bfloat16
I32 = mybir.dt.int32
AF = mybir.ActivationFunctionType
ALU = mybir.AluOpType

D = 2048
L = 8192
NB = 2
DEPTH = 4
FF = 5632
TOKW = 1536
MEMW = 512
NMEM = 256
NCORES = 8
TPC = 2048
TB = 512
NTB = TPC // TB
KT = D // 128
FT = FF // 128
EPS = 1e-6


class Buf:
    __slots__ = ("w", "r")

    def __init__(self):
        self.w = None
        self.r = {}


class Eng:
    def __init__(self, name, h, sem, sid):
        self.name = name
        self.h = h
        self.sem = sem
        self.sid = sid
        self.cnt = 0
        self.waited = {}


class KB:
    NDMA = 40

    def __init__(self, nc):
        self.nc = nc
        self.es = contextlib.ExitStack()
        self.sems = {}
        self.engs = {}
        for name, h in (("pe", nc.tensor), ("act", nc.scalar), ("dve", nc.vector),
                        ("pool", nc.gpsimd), ("sp", nc.sync)):
            sem = self.es.enter_context(nc.semaphore("s_" + name))
            self.sems[name] = sem
            self.engs[name] = Eng(name, h, sem, name)
        self.dsem = []
        self.dcnt = []
        for i in range(self.NDMA):
            sem = self.es.enter_context(nc.semaphore("d%d" % i))
            self.sems[("d", i)] = sem
            self.dsem.append(sem)
            self.dcnt.append(0)
        self.dnext = 0
        self.nalloc = 0
        self.scopes = [self.es]
        self.csem = []
        self.ccnt = []
        self.uid = 0

    def push(self):
        sc = contextlib.ExitStack()
        self.scopes.append(sc)

    def pop(self):
        self.barrier()
        self.scopes.pop().close()

    def barrier(self):
        for E in self.engs.values():
            for name, O in self.engs.items():
                if O is not E:
                    self._wait(E, name, O.cnt)
            for i in range(self.NDMA):
                self._wait(E, ("d", i), self.dcnt[i])
            for ci in range(len(self.csem)):
                self._wait(E, ("c", ci), self.ccnt[ci])

    def coll_chunks(self, ex, groups):
        E = self.engs["pool"]
        ci = len(self.csem)
        sem = self.es.enter_context(self.nc.semaphore("cc%d" % ci))
        self.csem.append(sem)
        self.ccnt.append(0)
        self.sems[("c", ci)] = sem
        R4 = 4 * ex.rows
        for c in range(ex.n):
            self._deps(E, [ex.snd_b[c]], [ex.rcv_b[c]], is_dma=True)
            ins = E.h.collective_compute("AllGather", ALU.bypass, replica_groups=groups,
                                         ins=[ex.snd.ap()[c * R4:(c + 1) * R4, :]],
                                         outs=[ex.rcv.ap()[c * 4 * R4:(c + 1) * 4 * R4, :]])
            ins.then_inc(sem)
            self.ccnt[ci] += 1
            self._commit((("c", ci), self.ccnt[ci]), [ex.snd_b[c]], [ex.rcv_b[c]])

    def coll_allgather(self, snd_t, rcv_t, groups, reads=(), writes=()):
        E = self.engs["pool"]
        ci = len(self.csem)
        sem = self.es.enter_context(self.nc.semaphore("cc%d" % ci))
        self.csem.append(sem)
        self.sems[("c", ci)] = sem
        self._deps(E, reads, writes, is_dma=True)
        ins = E.h.collective_compute("AllGather", ALU.bypass, replica_groups=groups,
                                     ins=[snd_t.ap().opt()], outs=[rcv_t.ap().opt()])
        ins.then_inc(sem)
        self.ccnt.append(1)
        self._commit((("c", ci), 1), reads, writes)

    def sb(self, name, shape, dt):
        self.uid += 1
        t = self.scopes[-1].enter_context(self.nc.sbuf_tensor("sb%d_%s" % (self.uid, name), list(shape), dt))
        return t

    def ps(self, name, shape=(128, 512), dt=F32):
        self.uid += 1
        return self.scopes[-1].enter_context(self.nc.psum_tensor("ps%d_%s" % (self.uid, name), list(shape), dt))

    def _wait(self, E, sid, val):
        if E.waited.get(sid, 0) >= val:
            return
        E.h.wait_ge(self.sems[sid], val)
        E.waited[sid] = val

    def _deps(self, E, reads, writes, is_dma=False):
        deps = {}

        def add(tok, kind):
            if tok is None:
                return
            sid, val = tok
            if sid == E.sid and not is_dma:
                if E.name == "pe":
                    return
                if kind != "raw":
                    return
            if deps.get(sid, 0) < val:
                deps[sid] = val

        for b in reads:
            add(b.w, "raw")
        for b in writes:
            add(b.w, "waw")
            for sid, val in b.r.items():
                add((sid, val), "war")
        for sid, val in deps.items():
            self._wait(E, sid, val)

    def _commit(self, tok, reads, writes):
        sid, val = tok
        for b in reads:
            if b.r.get(sid, 0) < val:
                b.r[sid] = val
        for b in writes:
            b.w = tok
            b.r = {}

    def op(self, eng, fn, reads=(), writes=()):
        E = self.engs[eng]
        self._deps(E, reads, writes)
        ins = fn(E.h)
        E.cnt += 1
        ins.then_inc(E.sem, 1)
        self._commit((E.sid, E.cnt), reads, writes)

    def dma(self, eng, out, in_, reads=(), writes=(), **kw):
        E = self.engs[eng]
        i = self.dnext
        self.dnext = (self.dnext + 1) % self.NDMA
        self._deps(E, reads, writes, is_dma=True)
        self._wait(E, ("d", i), self.dcnt[i])
        ins = E.h.dma_start(out=out, in_=in_, **kw)
        self.dcnt[i] += 16
        ins.then_inc(self.dsem[i], 16)
        self._commit((("d", i), self.dcnt[i]), reads, writes)

    def finish(self):
        E = self.engs["sp"]
        for i in range(self.NDMA):
            self._wait(E, ("d", i), self.dcnt[i])
        for name in ("pe", "act", "dve", "pool"):
            self._wait(E, name, self.engs[name].cnt)
        for ci in range(len(self.csem)):
            self._wait(E, ("c", ci), self.ccnt[ci])
        while len(self.scopes) > 1:
            self.scopes.pop().close()
        self.es.close()


class RowRes:
    def __init__(self, kb):
        self.kb = kb
        nc = kb.nc
        self.xblk = kb.sb("xblk", (128, KT, TB), F32)
        self.xblk_b = [Buf() for _ in range(KT)]
        self.hT = kb.sb("hT", (128, KT, TB), BF16)
        self.hT_b = [Buf() for _ in range(KT)]
        self.actT = kb.sb("actT", (128, FT, TB), BF16)
        self.actT_b = [Buf() for _ in range(FT)]
        self.wA = [kb.sb("wA%d" % i, (128, KT, 128), BF16) for i in range(2)]
        self.wA_b = [Buf() for _ in range(2)]
        self.wB = [kb.sb("wB%d" % i, (128, KT, 128), BF16) for i in range(2)]
        self.wB_b = [Buf() for _ in range(2)]
        self.wD = [kb.sb("wD%d" % i, (128, FT, 128), BF16) for i in range(2)]
        self.wD_b = [Buf() for _ in range(2)]
        self.sq = [kb.sb("sq%d" % i, (128, TB), F32) for i in range(2)]
        self.sq_b = [Buf() for _ in range(2)]
        self.rstd = kb.sb("rstd", (128, TB), F32)
        self.rstd_b = Buf()
        self.sg = [kb.sb("sg%d" % i, (128, TB), F32) for i in range(2)]
        self.sg_b = [Buf() for _ in range(2)]
        self.gcol = kb.sb("gcol", (128, 8, KT), F32)
        self.gcol_b = Buf()
        self.ones = kb.sb("ones", (128, 128), F32)
        self.ones_b = Buf()
        self.psA = [kb.ps("psA%d" % i) for i in range(2)]
        self.psA_b = [Buf() for _ in range(2)]
        self.psB = [kb.ps("psB%d" % i) for i in range(2)]
        self.psB_b = [Buf() for _ in range(2)]
        self.psO = [kb.ps("psO%d" % i) for i in range(2)]
        self.psO_b = [Buf() for _ in range(2)]
        self.psS = kb.ps("psS")
        self.psS_b = Buf()
        self.epsc = kb.sb("epsc", (128, 1), F32)
        self.ones_bf = kb.sb("ones_bf", (128, 128), BF16)
        kb.op("dve", lambda h: h.memset(self.ones[:], 1.0), writes=[self.ones_b])
        kb.op("dve", lambda h: h.memset(self.epsc[:], EPS), writes=[self.ones_b])
        kb.op("dve", lambda h: h.memset(self.ones_bf[:], 1.0), writes=[self.ones_b])
        self.memh = kb.sb("memh", (128, KT, NMEM), BF16)
        self.memh_b = [Buf() for _ in range(KT)]
        self.KmT = kb.sb("KmT", (128, 4, NMEM), BF16)
        self.Vm = kb.sb("Vm", (128, 2, MEMW), BF16)
        self.kv_b = Buf()
        self.qn = kb.sb("qn", (128, TB), BF16)
        self.qn_b = Buf()
        self.pT = kb.sb("pT", (128, 2, TB), BF16)
        self.pT_b = [Buf() for _ in range(2)]
        self.stg = [kb.sb("stg%d" % i, (128, TB), BF16) for i in range(2)]
        self.stg_b = [Buf() for _ in range(2)]
        self.gqk = kb.sb("gqk", (128, 2), F32)
        self.gqk_b = Buf()
        self.ctr = 0
        self.sctr = 0

    def load_gains(self, slot, g_dram):
        kb = self.kb
        with kb.nc.allow_non_contiguous_dma(reason="tiny gain vector"):
            kb.dma("sp", self.gcol[:, slot, :], g_dram.rearrange("(kt p) -> p kt", p=128),
                   writes=[self.gcol_b])


def rmsnorm_block(R, slot, src, src_b, nkt=KT, dim=D, dst=None, dst_b=None, width=TB):
    kb = R.kb
    if dst is None:
        dst, dst_b = R.hT, R.hT_b
    W = width
    for kt in range(nkt):
        j = kt % 2
        kb.op("act", lambda h, kt=kt, j=j: h.activation(out=R.sq[j][:, 0:W], in_=src[:, kt, 0:W], func=AF.Square),
              reads=[src_b[kt]], writes=[R.sq_b[j]])
        kb.op("pe", lambda h, kt=kt, j=j: h.matmul(R.psS[:, 0:W], R.ones[:], R.sq[j][:, 0:W], start=(kt == 0),
                                                    stop=(kt == nkt - 1)),
              reads=[R.sq_b[j], R.ones_b], writes=[R.psS_b])
    kb.op("act", lambda h: h.activation(out=R.rstd[:, 0:W], in_=R.psS[:, 0:W], func=AF.Ln, scale=1.0 / dim,
                                        bias=EPS),
          reads=[R.psS_b], writes=[R.rstd_b])
    kb.op("act", lambda h: h.activation(out=R.rstd[:, 0:W], in_=R.rstd[:, 0:W], func=AF.Exp, scale=-0.5),
          reads=[R.rstd_b], writes=[R.rstd_b])
    for kt in range(nkt):
        kb.op("dve", lambda h, kt=kt: h.scalar_tensor_tensor(out=dst[:, kt, 0:W], in0=src[:, kt, 0:W],
                                                             scalar=R.gcol[:, slot, kt:kt + 1], in1=R.rstd[:, 0:W],
                                                             op0=ALU.mult, op1=ALU.mult),
              reads=[src_b[kt], R.gcol_b, R.rstd_b], writes=[dst_b[kt]])


def load_slab(kb, dst, dst_b, w_dram, c0, ncols, nkt):
    src = w_dram[:, c0:c0 + ncols].rearrange("(kt p) n -> p kt n", p=128)
    kb.dma("pool", dst[:, 0:nkt, 0:ncols], src, writes=[dst_b])


def ffn_block(R, slot, w_gu, w_down):
    kb = R.kb
    rmsnorm_block(R, slot, R.xblk, R.xblk_b)
    for ft in range(FT):
        j = R.ctr % 2
        R.ctr += 1
        load_slab(kb, R.wA[j], R.wA_b[j], w_gu, ft * 128, 128, KT)
        load_slab(kb, R.wB[j], R.wB_b[j], w_gu, FF + ft * 128, 128, KT)
        for kt in range(KT):
            kb.op("pe", lambda h, kt=kt, j=j: h.matmul(R.psA[j][:], R.wA[j][:, kt, :], R.hT[:, kt, :],
                                                        start=(kt == 0), stop=(kt == KT - 1)),
                  reads=[R.wA_b[j], R.hT_b[kt]], writes=[R.psA_b[j]])
        for kt in range(KT):
            kb.op("pe", lambda h, kt=kt, j=j: h.matmul(R.psB[j][:], R.wB[j][:, kt, :], R.hT[:, kt, :],
                                                        start=(kt == 0), stop=(kt == KT - 1)),
                  reads=[R.wB_b[j], R.hT_b[kt]], writes=[R.psB_b[j]])
        kb.op("act", lambda h, j=j: h.activation(out=R.sg[j][:], in_=R.psA[j][:], func=AF.Silu),
              reads=[R.psA_b[j]], writes=[R.sg_b[j]])
        kb.op("dve", lambda h, j=j, ft=ft: h.tensor_tensor(out=R.actT[:, ft, :], in0=R.psB[j][:], in1=R.sg[j][:],
                                                           op=ALU.mult),
              reads=[R.psB_b[j], R.sg_b[j]], writes=[R.actT_b[ft]])
    for dt in range(KT):
        j = R.ctr % 2
        R.ctr += 1
        load_slab(kb, R.wD[j], R.wD_b[j], w_down, dt * 128, 128, FT)
        for ft in range(FT):
            kb.op("pe", lambda h, ft=ft, j=j: h.matmul(R.psO[j][:], R.wD[j][:, ft, :], R.actT[:, ft, :],
                                                        start=(ft == 0), stop=(ft == FT - 1)),
                  reads=[R.wD_b[j], R.actT_b[ft]], writes=[R.psO_b[j]])
        kb.op("dve", lambda h, dt=dt, j=j: h.scalar_tensor_tensor(out=R.xblk[:, dt, :], in0=R.psO[j][:], scalar=0.5,
                                                                  in1=R.xblk[:, dt, :], op0=ALU.mult, op1=ALU.add),
              reads=[R.psO_b[j], R.xblk_b[dt]], writes=[R.xblk_b[dt]])


def load_xblk(R, xT, tb, dbuf=None):
    kb = R.kb
    rd = [dbuf] if dbuf is not None else []
    for kt in range(KT):
        kb.dma("sp", R.xblk[:, kt, :], xT[kt * 128:(kt + 1) * 128, tb * TB:(tb + 1) * TB], reads=rd,
               writes=[R.xblk_b[kt]])


def store_xblk(R, xT, tb, dbufs=None):
    kb = R.kb
    for kt in range(KT):
        wr = [dbufs[kt]] if dbufs is not None else []
        kb.dma("sp", xT[kt * 128:(kt + 1) * 128, tb * TB:(tb + 1) * TB], R.xblk[:, kt, :], reads=[R.xblk_b[kt]],
               writes=wr)


def build_ffn_only():
    nc = bass.Bass("TRN2", target_bir_lowering=False)
    xT = nc.dram_tensor("xT", [D, TPC], F32, kind="ExternalInput").ap()
    g = nc.dram_tensor("g", [D], F32, kind="ExternalInput").ap()
    w_gu = nc.dram_tensor("w_gu", [D, 2 * FF], F32, kind="ExternalInput").ap()
    w_down = nc.dram_tensor("w_down", [FF, D], F32, kind="ExternalInput").ap()
    yT = nc.dram_tensor("yT", [D, TPC], F32, kind="ExternalOutput").ap()
    kb = KB(nc)
    R = RowRes(kb)
    R.load_gains(0, g)
    for tb in range(NTB):
        load_xblk(R, xT, tb)
        ffn_block(R, 0, w_gu, w_down)
        store_xblk(R, yT, tb)
    kb.finish()
    return nc


NPAIR = 3
QB = 512
NQB = L // QB
NKB = L // 128


def build_sb_attn(npair=NPAIR, nqb=NQB):
    nc = bass.Bass("TRN2", target_bir_lowering=False)
    qT = nc.dram_tensor("qT", [npair, 128, L], BF16, kind="ExternalInput").ap()
    kT = nc.dram_tensor("kT", [npair, 128, L], BF16, kind="ExternalInput").ap()
    v = nc.dram_tensor("v", [npair, L, 128], BF16, kind="ExternalInput").ap()
    oT = nc.dram_tensor("oT", [npair, 128, L], BF16, kind="ExternalOutput").ap()
    kb = KB(nc)
    sb_attn_phase(kb, qT, kT, v, oT, npair, nqb)
    kb.finish()
    return nc


def sb_attn_phase(kb, qT, kT, v, oT, npair, nqb, load_fn=None, store_fn=None):
    nc = kb.nc
    q_s = [kb.sb("q_s%d" % i, (128, L), BF16) for i in range(2)]
    k_s = [kb.sb("k_s%d" % i, (128, L), BF16) for i in range(2)]
    v_s = [kb.sb("v_s%d" % i, (128, NKB, 128), BF16) for i in range(2)]
    qkv_b = [Buf() for _ in range(2)]
    NE = 6
    NZ = 3
    e_s = [kb.sb("e_s%d" % i, (128, QB), F32) for i in range(NE)]
    e_b = [Buf() for _ in range(NE)]
    sp_s = [kb.sb("sp_s%d" % i, (128, QB), BF16) for i in range(NE)]
    sp_b = [Buf() for _ in range(NE)]
    NX = 3
    x_s = [kb.sb("x_s%d" % i, (128, QB), F32) for i in range(NX)]
    x_b = [Buf() for _ in range(NX)]
    w_s = [kb.sb("w_s%d" % i, (128, QB), BF16) for i in range(NX)]
    w_b = [Buf() for _ in range(NX)]
    o_s = [kb.sb("o_s%d" % i, (128, QB), BF16) for i in range(2)]
    o_b = [Buf() for _ in range(2)]
    ones_f = kb.sb("ones_f", (128, 128), F32)
    uinc = kb.sb("uinc", (128, 128), BF16)
    lstr = kb.sb("lstr", (128, 128), BF16)
    c_b = Buf()
    psZ = [kb.ps("psZ%d" % i) for i in range(3)]
    psZ_b = [Buf() for _ in range(3)]
    psP = [kb.ps("psP%d" % i) for i in range(2)]
    psP_b = [Buf() for _ in range(2)]
    psO = [kb.ps("psOa%d" % i) for i in range(2)]
    psO_b = [Buf() for _ in range(2)]

    kb.op("pool", lambda h: h.memset(ones_f[:], 1.0), writes=[c_b])
    kb.op("pool", lambda h: h.affine_select(out=uinc[:], in_=ones_f[:], pattern=[[-1, 128]], compare_op=ALU.is_ge,
                                            fill=0.0, base=0, channel_multiplier=1), reads=[c_b], writes=[c_b])
    kb.op("pool", lambda h: h.affine_select(out=lstr[:], in_=ones_f[:], pattern=[[1, 128]], compare_op=ALU.is_gt,
                                            fill=0.0, base=0, channel_multiplier=-1), reads=[c_b], writes=[c_b])

    def load_pair(pi):
        j = pi % 2
        if load_fn is not None:
            load_fn(pi, q_s[j], k_s[j], v_s[j], qkv_b[j])
            return
        kb.dma("sp", q_s[j][:], qT[pi], writes=[qkv_b[j]])
        kb.dma("sp", k_s[j][:], kT[pi], writes=[qkv_b[j]])
        kb.dma("sp", v_s[j][:], v[pi].rearrange("(kb p) d -> p kb d", p=128), writes=[qkv_b[j]])

    S0 = [0, 3, 4, 7, 8, 11, 12, 15]
    S1 = [1, 2, 5, 6, 9, 10, 13, 14]
    tiles = []
    for pi in range(npair):
        streams = []
        for si, qbs in enumerate((S0, S1)):
            lst = []
            for qb in qbs:
                if qb >= nqb:
                    continue
                nk = 4 * (qb + 1)
                for idx in range(nk):
                    lst.append((pi, qb, nk - 1 - idx, idx, nk, si))
            streams.append(lst)
        n_ = max(len(streams[0]), len(streams[1]))
        for k in range(n_):
            for si in range(2):
                tiles.append(streams[si][k] if k < len(streams[si]) else None)

    tctr = [0]

    def stage1(t):
        pi, qb, kbi, idx, nk, g = t
        j = pi % 2
        n = tctr[0] % NE
        z = tctr[0] % NZ
        tctr[0] += 1
        kb.op("pe", lambda h: h.matmul(psZ[z][:], k_s[j][:, kbi * 128:(kbi + 1) * 128],
                                       q_s[j][:, qb * QB:(qb + 1) * QB], start=True, stop=True),
              reads=[qkv_b[j]], writes=[psZ_b[z]])
        kb.op("act", lambda h: h.activation(out=e_s[n][:], in_=psZ[z][:], func=AF.Exp),
              reads=[psZ_b[z]], writes=[e_b[n]])
        kb.op("act", lambda h: h.activation(out=sp_s[n][:], in_=e_s[n][:], func=AF.Ln, bias=1.0),
              reads=[e_b[n]], writes=[sp_b[n]])
        dj = kbi - 4 * qb
        if dj >= 0:
            kb.op("pool", lambda h: h.affine_select(out=sp_s[n][:], in_=sp_s[n][:], pattern=[[1, QB]],
                                                    compare_op=ALU.is_gt, fill=0.0, base=-128 * dj,
                                                    channel_multiplier=-1),
                  reads=[sp_b[n]], writes=[sp_b[n]])
        return n

    xctr = [0]

    def stageB1(t, n):
        pi, qb, kbi, idx, nk, g = t
        xi = xctr[0] % NX
        xctr[0] += 1
        kb.op("pe", lambda h: h.matmul(psP[g][:], uinc[:], sp_s[n][:], start=(idx == 0), stop=(idx == nk - 1)),
              reads=[sp_b[n], c_b], writes=[psP_b[g]])
        kb.op("act", lambda h: h.activation(out=x_s[xi][:], in_=psP[g][:], func=AF.Exp, scale=-1.0),
              reads=[psP_b[g]], writes=[x_b[xi]])
        return xi

    def stageB2(t, n):
        pi, qb, kbi, idx, nk, g = t
        if idx != nk - 1:
            kb.op("pe", lambda h: h.matmul(psP[g][:], lstr[:], sp_s[n][:], start=False, stop=False),
                  reads=[sp_b[n], c_b], writes=[psP_b[g]])

    def stageC(t, n, xi):
        pi, qb, kbi, idx, nk, g = t
        j = pi % 2
        kb.op("dve", lambda h: h.tensor_tensor(out=w_s[xi][:], in0=e_s[n][:], in1=x_s[xi][:], op=ALU.mult),
              reads=[e_b[n], x_b[xi]], writes=[w_b[xi]])
        dj = kbi - 4 * qb
        if dj >= 0:
            kb.op("pool", lambda h: h.affine_select(out=w_s[xi][:], in_=w_s[xi][:], pattern=[[1, QB]],
                                                    compare_op=ALU.is_gt, fill=0.0, base=-128 * dj,
                                                    channel_multiplier=-1),
                  reads=[w_b[xi]], writes=[w_b[xi]])
        kb.op("pe", lambda h: h.matmul(psO[g][:], v_s[j][:, kbi, :], w_s[xi][:], start=(idx == 0),
                                       stop=(idx == nk - 1)),
              reads=[w_b[xi], qkv_b[j]], writes=[psO_b[g]])
        if idx == nk - 1:
            kb.op("act", lambda h: h.activation(out=o_s[g][:], in_=psO[g][:], func=AF.Copy),
                  reads=[psO_b[g]], writes=[o_b[g]])
            if store_fn is not None:
                store_fn(pi, qb, o_s[g], o_b[g])
            else:
                kb.dma("sp", oT[pi, :, qb * QB:(qb + 1) * QB], o_s[g][:], reads=[o_b[g]])

    load_pair(0)
    loaded = 1
    pipe = [None, None, None, None]
    seen_pairs = set()

    def step(new):
        a1, a2, b1, b2 = pipe
        nb1 = None
        if a2 is not None:
            t_, n_ = a2
            xi = stageB1(t_, n_)
            nb1 = (t_, n_, xi)
        nb2 = None
        if b1 is not None:
            stageB2(b1[0], b1[1])
            nb2 = b1
        if b2 is not None:
            stageC(*b2)
        pipe[0], pipe[1], pipe[2], pipe[3] = new, a1, nb1, nb2

    cur_pair = 0
    for t in tiles + [("end",)]:
        if t is None:
            step(None)
            continue
        if t[0] != cur_pair or t[0] == "end":
            for _ in range(4):
                step(None)
            if t[0] == "end":
                break
            cur_pair = t[0]
        if t[0] not in seen_pairs:
            seen_pairs.add(t[0])
            if loaded < npair and loaded <= t[0] + 1:
                load_pair(loaded)
                loaded += 1
        n = stage1(t)
        step((t, n))


NST = 12
SCH = 512
TWO_PI = 2.0 * math.pi
GELU_C = 2.0 * math.sqrt(2.0 / math.pi)


def build_s5(nchunks=L // SCH):
    nc = bass.Bass("TRN2", target_bir_lowering=False)
    dr = {}
    for name, shape in (("uT", [384, L]), ("ldt", [128, NST]), ("are", [128, NST]), ("aim", [128, NST]),
                        ("Bre", [128, NST, 128]), ("Bim", [128, NST, 128]), ("Cre", [128, NST, 128]),
                        ("Cim", [128, NST, 128]), ("dcol", [128, 3])):
        dr[name] = nc.dram_tensor(name, shape, F32, kind="ExternalInput").ap()
    ygT = nc.dram_tensor("ygT", [384, L], BF16, kind="ExternalOutput").ap()
    kb = KB(nc)
    s5_phase(kb, dr, ygT, nchunks)
    kb.finish()
    return nc


def s5_phase(kb, dr, ygT, nchunks, uload_fn=None, ystore_fn=None):
    def small(name, w=NST):
        return kb.sb(name, (128, w), F32), Buf()

    def tt(eng, out, a, b, op, reads, writes):
        kb.op(eng, lambda h: h.tensor_tensor(out=out, in0=a, in1=b, op=op), reads=reads, writes=writes)

    def ts(eng, out, a, s1, op0, reads, writes, s2=None, op1=None):
        if op1 is None:
            kb.op(eng, lambda h: h.tensor_scalar(out=out, in0=a, scalar1=s1, scalar2=None, op0=op0),
                  reads=reads, writes=writes)
        else:
            kb.op(eng, lambda h: h.tensor_scalar(out=out, in0=a, scalar1=s1, scalar2=s2, op0=op0, op1=op1),
                  reads=reads, writes=writes)

    def stt(eng, out, a, s, b, op0, op1, reads, writes):
        kb.op(eng, lambda h: h.scalar_tensor_tensor(out=out, in0=a, scalar=s, in1=b, op0=op0, op1=op1),
              reads=reads, writes=writes)

    def act(out, a, func, reads, writes, **kw):
        kb.op("act", lambda h: h.activation(out=out, in_=a, func=func, **kw), reads=reads, writes=writes)

    P = {}
    for name in ("ldt", "are", "aim"):
        P[name] = small("p_" + name)
        kb.dma("sp", P[name][0][:], dr[name], writes=[P[name][1]])
    Bre = kb.sb("Bre_s", (128, NST, 128), F32)
    Bim = kb.sb("Bim_s", (128, NST, 128), F32)
    Cre = kb.sb("Cre_s", (128, NST, 128), F32)
    Cim = kb.sb("Cim_s", (128, NST, 128), F32)
    C2re = kb.sb("C2re", (128, NST, 128), F32)
    C2im = kb.sb("C2im", (128, NST, 128), F32)
    C2ren = kb.sb("C2ren", (128, NST, 128), F32)
    dcol = kb.sb("dcol_s", (128, 3), F32)
    par_b = Buf()
    c2_b = Buf()
    for t, name in ((Bre, "Bre"), (Bim, "Bim"), (Cre, "Cre"), (Cim, "Cim"), (dcol, "dcol")):
        kb.dma("sp", t[:], dr[name], writes=[par_b])

    dt_, dt_b = small("dt_")
    act(dt_[:], P["ldt"][0][:], AF.Exp, [P["ldt"][1]], [dt_b])
    rl, rl_b = small("rl")
    th, th_b = small("th")
    tt("dve", rl[:], P["are"][0][:], dt_[:], ALU.mult, [P["are"][1], dt_b], [rl_b])
    tt("dve", th[:], P["aim"][0][:], dt_[:], ALU.mult, [P["aim"][1], dt_b], [th_b])
    r_, r_b = small("r_")
    act(r_[:], rl[:], AF.Exp, [rl_b], [r_b])
    phi, phi_b = small("phi")
    ts("dve", phi[:], th[:], 1.0 / TWO_PI, ALU.mult, [th_b], [phi_b])
    ki = kb.sb("ki", (128, NST), I32)
    ki_b = Buf()
    kb.op("dve", lambda h: h.tensor_copy(out=ki[:], in_=phi[:]), reads=[phi_b], writes=[ki_b])
    kf, kf_b = small("kf")
    kb.op("dve", lambda h: h.tensor_copy(out=kf[:], in_=ki[:]), reads=[ki_b], writes=[kf_b])
    f_, f_b = small("f_")
    tt("dve", f_[:], phi[:], kf[:], ALU.subtract, [phi_b, kf_b], [f_b])
    cos1, cos1_b = small("cos1")
    sin1, sin1_b = small("sin1")
    tmpa, tmpa_b = small("tmpa")
    tmpb, tmpb_b = small("tmpb")

    def sin_of_frac(out, out_b, frac, frac_b):
        ts("dve", tmpa[:], frac, 0.5, ALU.is_gt, [frac_b], [tmpa_b])
        tt("dve", tmpb[:], frac, tmpa[:], ALU.subtract, [frac_b, tmpa_b], [tmpb_b])
        ts("dve", tmpa[:], tmpb[:], -0.5, ALU.is_lt, [tmpb_b], [tmpa_b])
        tt("dve", tmpb[:], tmpb[:], tmpa[:], ALU.add, [tmpb_b, tmpa_b], [tmpb_b])
        act(out, tmpb[:], AF.Sin, [tmpb_b], [out_b], scale=TWO_PI)

    sin_of_frac(sin1[:], sin1_b, f_[:], f_b)
    fc, fc_b = small("fc")
    ts("dve", fc[:], f_[:], 0.25, ALU.add, [f_b], [fc_b])
    sin_of_frac(cos1[:], cos1_b, fc[:], fc_b)

    p_, p_b = small("p_")
    q_, q_b = small("q_")
    tt("dve", p_[:], r_[:], cos1[:], ALU.mult, [r_b, cos1_b], [p_b])
    ts("dve", p_[:], p_[:], -1.0, ALU.add, [p_b], [p_b])
    tt("dve", q_[:], r_[:], sin1[:], ALU.mult, [r_b, sin1_b], [q_b])
    den, den_b = small("den")
    t1, t1_b = small("t1")
    t2, t2_b = small("t2")
    are, are_b = P["are"]
    aim, aim_b = P["aim"]
    tt("dve", den[:], are[:], are[:], ALU.mult, [are_b], [den_b])
    tt("dve", t1[:], aim[:], aim[:], ALU.mult, [aim_b], [t1_b])
    tt("dve", den[:], den[:], t1[:], ALU.add, [den_b, t1_b], [den_b])
    kb.op("dve", lambda h: h.reciprocal(out=den[:], in_=den[:]), reads=[den_b], writes=[den_b])
    gre, gre_b = small("gre")
    gim, gim_b = small("gim")
    tt("dve", t1[:], p_[:], are[:], ALU.mult, [p_b, are_b], [t1_b])
    tt("dve", t2[:], q_[:], aim[:], ALU.mult, [q_b, aim_b], [t2_b])
    tt("dve", gre[:], t1[:], t2[:], ALU.add, [t1_b, t2_b], [gre_b])
    tt("dve", gre[:], gre[:], den[:], ALU.mult, [gre_b, den_b], [gre_b])
    tt("dve", t1[:], q_[:], are[:], ALU.mult, [q_b, are_b], [t1_b])
    tt("dve", t2[:], p_[:], aim[:], ALU.mult, [p_b, aim_b], [t2_b])
    tt("dve", gim[:], t1[:], t2[:], ALU.subtract, [t1_b, t2_b], [gim_b])
    tt("dve", gim[:], gim[:], den[:], ALU.mult, [gim_b, den_b], [gim_b])
    zre, zre_b = small("zre")
    zim, zim_b = small("zim")
    nzim, nzim_b = small("nzim")
    tt("dve", t1[:], gre[:], cos1[:], ALU.mult, [gre_b, cos1_b], [t1_b])
    tt("dve", t2[:], gim[:], sin1[:], ALU.mult, [gim_b, sin1_b], [t2_b])
    tt("dve", zre[:], t1[:], t2[:], ALU.add, [t1_b, t2_b], [zre_b])
    tt("dve", t1[:], gim[:], cos1[:], ALU.mult, [gim_b, cos1_b], [t1_b])
    tt("dve", t2[:], gre[:], sin1[:], ALU.mult, [gre_b, sin1_b], [t2_b])
    tt("dve", zim[:], t1[:], t2[:], ALU.subtract, [t1_b, t2_b], [zim_b])
    ts("dve", nzim[:], zim[:], -1.0, ALU.mult, [zim_b], [nzim_b])

    ctmp = kb.sb("ctmp", (128, 128), F32)
    ctmp_b = Buf()
    for st in range(NST):
        ts("dve", ctmp[:], Cim[:, st, :], zim[:, st:st + 1], ALU.mult, [par_b, zim_b], [ctmp_b])
        stt("dve", C2re[:, st, :], Cre[:, st, :], zre[:, st:st + 1], ctmp[:], ALU.mult, ALU.subtract,
            [par_b, zre_b, ctmp_b], [c2_b])
        ts("dve", ctmp[:], Cim[:, st, :], zre[:, st:st + 1], ALU.mult, [par_b, zre_b], [ctmp_b])
        stt("dve", C2im[:, st, :], Cre[:, st, :], nzim[:, st:st + 1], ctmp[:], ALU.mult, ALU.subtract,
            [par_b, nzim_b, ctmp_b], [c2_b])
        ts("dve", C2ren[:, st, :], C2re[:, st, :], -1.0, ALU.mult, [c2_b], [c2_b])

    TW = SCH + 8
    tabC = kb.sb("tabC", (128, NST, TW), F32)
    tabS = kb.sb("tabS", (128, NST, TW), F32)
    tab_b = [Buf() for _ in range(NST)]
    ttmp = kb.sb("ttmp", (128, 256), F32)
    ttmp_b = Buf()
    rt = kb.sb("rt", (128, NST, SCH), F32)
    rt_b = Buf()
    onesw = kb.sb("onesw", (128, SCH), F32)
    onesw_b = Buf()
    kb.op("pool", lambda h: h.memset(onesw[:], 1.0), writes=[onesw_b])
    for st in range(NST):
        b = tab_b[st]
        kb.op("pool", lambda h, st=st: h.memset(tabC[:, st, 0:1], 1.0), writes=[b])
        kb.op("pool", lambda h, st=st: h.memset(tabS[:, st, 0:1], 0.0), writes=[b])
        kb.op("act", lambda h, st=st: h.activation(out=tabC[:, st, 1:2], in_=cos1[:, st:st + 1], func=AF.Copy),
              reads=[cos1_b], writes=[b])
        kb.op("act", lambda h, st=st: h.activation(out=tabS[:, st, 1:2], in_=sin1[:, st:st + 1], func=AF.Copy),
              reads=[sin1_b], writes=[b])
        m = 1
        while m < SCH:
            cm = tabC[:, st, m:m + 1]
            sm = tabS[:, st, m:m + 1]
            ts("dve", ttmp[:, 0:m], tabS[:, st, 1:m + 1], sm, ALU.mult, [b], [ttmp_b])
            stt("dve", tabC[:, st, m + 1:2 * m + 1], tabC[:, st, 1:m + 1], cm, ttmp[:, 0:m], ALU.mult, ALU.subtract,
                [b, ttmp_b], [b])
            ts("dve", ttmp[:, 0:m], tabC[:, st, 1:m + 1], sm, ALU.mult, [b], [ttmp_b])
            stt("dve", tabS[:, st, m + 1:2 * m + 1], tabS[:, st, 1:m + 1], cm, ttmp[:, 0:m], ALU.mult, ALU.add,
                [b, ttmp_b], [b])
            m *= 2
        ts("pool", rt[:, st, :], onesw[:], r_[:, st:st + 1], ALU.mult, [onesw_b, r_b], [rt_b])

    u_s = [kb.sb("u_s%d" % i, (128, SCH), F32) for i in range(2)]
    u_b = [Buf() for _ in range(2)]
    NR = 2
    m_s = [[kb.sb("m%d_%d" % (k, i), (128, SCH), F32) for k in range(4)] for i in range(NR)]
    m_b = [[Buf() for k in range(4)] for i in range(NR)]
    wv_s = [[kb.sb("wv%d_%d" % (k, i), (128, SCH), F32) for k in range(2)] for i in range(NR)]
    wv_b = [[Buf() for k in range(2)] for i in range(NR)]
    n_s = [[kb.sb("n%d_%d" % (k, i), (128, SCH), F32) for k in range(4)] for i in range(NR)]
    n_b = [[Buf() for k in range(4)] for i in range(NR)]
    car_r = kb.sb("car_r", (128, NST), F32)
    car_i = kb.sb("car_i", (128, NST), F32)
    car_b = [Buf() for _ in range(NST)]
    yv = kb.sb("yv", (128, SCH), F32)
    yv_b = Buf()
    e1 = kb.sb("e1", (128, SCH), F32)
    e1_b = Buf()
    e2 = kb.sb("e2", (128, SCH), F32)
    e2_b = Buf()
    yo = [kb.sb("yo%d" % i, (128, SCH), BF16) for i in range(2)]
    yo_b = [Buf() for _ in range(2)]
    psR = [kb.ps("psR%d" % i) for i in range(2)]
    psR_b = [Buf() for _ in range(2)]
    psI = [kb.ps("psI%d" % i) for i in range(2)]
    psI_b = [Buf() for _ in range(2)]
    psY = [kb.ps("psY%d" % i) for i in range(2)]
    psY_b = [Buf() for _ in range(2)]
    psW = [kb.ps("psW%d" % i) for i in range(2)]
    psW_b = [Buf() for _ in range(2)]

    Bre16 = kb.sb("Bre16", (128, NST, 128), BF16)
    Bim16 = kb.sb("Bim16", (128, NST, 128), BF16)
    C2re16 = kb.sb("C2re16", (128, NST, 128), BF16)
    C2ren16 = kb.sb("C2ren16", (128, NST, 128), BF16)
    C2im16 = kb.sb("C2im16", (128, NST, 128), BF16)
    t16_b = Buf()
    for dst, src, sb_ in ((Bre16, Bre, par_b), (Bim16, Bim, par_b), (C2re16, C2re, c2_b), (C2ren16, C2ren, c2_b),
                          (C2im16, C2im, c2_b)):
        kb.op("pool", lambda h, dst=dst, src=src: h.tensor_copy(out=dst[:], in_=src[:]), reads=[sb_], writes=[t16_b])
    u16 = [kb.sb("u16_%d" % i, (128, SCH), BF16) for i in range(2)]
    u16_b = [Buf() for _ in range(2)]
    n16 = [[kb.sb("n16_%d_%d" % (k, i), (128, SCH), BF16) for k in range(4)] for i in range(NR)]
    n16_b = [[Buf() for k in range(4)] for i in range(NR)]
    ctmp4 = kb.sb("ctmp4", (128, 4), F32)
    ctmp4_b = Buf()

    iters = [(ch, ct, sl) for ch in range(nchunks) for ct in range(3) for sl in range(4)]

    def P(k):
        ch, ct, sl = iters[k]
        ui = (ch * 3 + ct) % 2
        if sl == 0:
            if uload_fn is not None:
                uload_fn(ch, ct, u_s[ui], u_b[ui])
            else:
                kb.dma("sp", u_s[ui][:], dr["uT"][ct * 128:(ct + 1) * 128, ch * SCH:(ch + 1) * SCH],
                       writes=[u_b[ui]])
            kb.op("act", lambda h: h.activation(out=u16[ui][:], in_=u_s[ui][:], func=AF.Copy),
                  reads=[u_b[ui]], writes=[u16_b[ui]])
        st = ct * 4 + sl
        i = k % 2
        kb.op("pe", lambda h: h.matmul(psR[i][:], Bre16[:, st, :], u16[ui][:], start=True, stop=True),
              reads=[t16_b, u16_b[ui]], writes=[psR_b[i]])
        kb.op("pe", lambda h: h.matmul(psI[i][:], Bim16[:, st, :], u16[ui][:], start=True, stop=True),
              reads=[t16_b, u16_b[ui]], writes=[psI_b[i]])

    def Dk(k):
        ch, ct, sl = iters[k]
        ui = (ch * 3 + ct) % 2
        yi = (ch * 3 + ct) % 2
        st = ct * 4 + sl
        i = k % 2
        pR, pI, pRb, pIb = psR[i], psI[i], psR_b[i], psI_b[i]
        m, mb = m_s[i], m_b[i]
        Cc = tabC[:, st, 0:SCH]
        Ss = tabS[:, st, 0:SCH]
        tb_ = tab_b[st]
        tt("dve", m[0][:], pR[:], Cc, ALU.mult, [pRb, tb_], [mb[0]])
        tt("dve", m[1][:], pI[:], Ss, ALU.mult, [pIb, tb_], [mb[1]])
        tt("dve", m[2][:], pI[:], Cc, ALU.mult, [pIb, tb_], [mb[2]])
        tt("dve", m[3][:], pR[:], Ss, ALU.mult, [pRb, tb_], [mb[3]])
        tt("pool", m[0][:], m[0][:], m[1][:], ALU.add, [mb[0], mb[1]], [mb[0]])
        tt("pool", m[2][:], m[2][:], m[3][:], ALU.subtract, [mb[2], mb[3]], [mb[2]])
        if ch == 0:
            ini_r = 0.0
            ini_i = 0.0
        else:
            ini_r = car_r[:, st:st + 1]
            ini_i = car_i[:, st:st + 1]
        kb.op("dve", lambda h: h.tensor_tensor_scan(out=psW[0][:], data0=rt[:, st, :], data1=m[0][:],
                                                    initial=ini_r, op0=ALU.mult, op1=ALU.add),
              reads=[rt_b, mb[0], car_b[st]], writes=[psW_b[0]])
        kb.op("dve", lambda h: h.tensor_tensor_scan(out=psW[1][:], data0=rt[:, st, :], data1=m[2][:],
                                                    initial=ini_i, op0=ALU.mult, op1=ALU.add),
              reads=[rt_b, mb[2], car_b[st]], writes=[psW_b[1]])
        n, nb = n16[i], n16_b[i]
        C1 = tabC[:, st, 1:SCH + 1]
        S1 = tabS[:, st, 1:SCH + 1]
        tt("dve", n[0][:], psW[0][:], C1, ALU.mult, [psW_b[0], tb_], [nb[0]])
        tt("dve", n[1][:], psW[1][:], S1, ALU.mult, [psW_b[1], tb_], [nb[1]])
        tt("dve", n[2][:], psW[0][:], S1, ALU.mult, [psW_b[0], tb_], [nb[2]])
        tt("dve", n[3][:], psW[1][:], C1, ALU.mult, [psW_b[1], tb_], [nb[3]])
        L1 = slice(SCH - 1, SCH)
        cl = tabC[:, st, SCH:SCH + 1]
        sl_ = tabS[:, st, SCH:SCH + 1]
        tt("dve", ctmp4[:, 0:1], psW[0][:, L1], cl, ALU.mult, [psW_b[0], tb_], [ctmp4_b])
        tt("dve", ctmp4[:, 1:2], psW[1][:, L1], sl_, ALU.mult, [psW_b[1], tb_], [ctmp4_b])
        tt("dve", ctmp4[:, 2:3], psW[0][:, L1], sl_, ALU.mult, [psW_b[0], tb_], [ctmp4_b])
        tt("dve", ctmp4[:, 3:4], psW[1][:, L1], cl, ALU.mult, [psW_b[1], tb_], [ctmp4_b])
        tt("dve", car_r[:, st:st + 1], ctmp4[:, 0:1], ctmp4[:, 1:2], ALU.subtract, [ctmp4_b], [car_b[st]])
        tt("dve", car_i[:, st:st + 1], ctmp4[:, 2:3], ctmp4[:, 3:4], ALU.add, [ctmp4_b], [car_b[st]])
        kb.op("pe", lambda h: h.matmul(psY[yi][:], C2re16[:, st, :], n[0][:], start=(sl == 0), stop=False),
              reads=[t16_b, nb[0]], writes=[psY_b[yi]])
        kb.op("pe", lambda h: h.matmul(psY[yi][:], C2ren16[:, st, :], n[1][:], start=False, stop=False),
              reads=[t16_b, nb[1]], writes=[psY_b[yi]])
        kb.op("pe", lambda h: h.matmul(psY[yi][:], C2im16[:, st, :], n[2][:], start=False, stop=False),
              reads=[t16_b, nb[2]], writes=[psY_b[yi]])
        kb.op("pe", lambda h: h.matmul(psY[yi][:], C2im16[:, st, :], n[3][:], start=False, stop=(sl == 3)),
              reads=[t16_b, nb[3]], writes=[psY_b[yi]])
        if sl == 3:
            stt("dve", yv[:], u_s[ui][:], dcol[:, ct:ct + 1], psY[yi][:], ALU.mult, ALU.add,
                [u_b[ui], par_b, psY_b[yi]], [yv_b])
            act(e1[:], yv[:], AF.Square, [yv_b], [e1_b])
            ts("dve", e1[:], e1[:], 0.044715, ALU.mult, [e1_b], [e1_b], s2=1.0, op1=ALU.add)
            tt("pool", e2[:], e1[:], yv[:], ALU.mult, [e1_b, yv_b], [e2_b])
            act(e2[:], e2[:], AF.Sigmoid, [e2_b], [e2_b], scale=GELU_C)
            tt("pool", yo[yi][:], e2[:], yv[:], ALU.mult, [e2_b, yv_b], [yo_b[yi]])
            if ystore_fn is not None:
                ystore_fn(ch, ct, yo[yi], yo_b[yi])
            else:
                kb.dma("sp", ygT[ct * 128:(ct + 1) * 128, ch * SCH:(ch + 1) * SCH], yo[yi][:], reads=[yo_b[yi]])

    P(0)
    for k in range(len(iters)):
        if k + 1 < len(iters):
            P(k + 1)
        Dk(k)


def s5_host_params(log_dt, a_re, a_im, b_re, b_im, c_re, c_im, d, gs):
    g0 = gs * 24
    out = {}

    def per_state(a):
        return np.ascontiguousarray(a.reshape(NST, 2, 64).transpose(1, 2, 0).reshape(128, NST))

    out["ldt"] = per_state(np.repeat(log_dt[g0:g0 + 24, None], 64, axis=1))
    out["are"] = per_state(a_re[g0:g0 + 24])
    out["aim"] = per_state(a_im[g0:g0 + 24])
    Bre = np.zeros((128, NST, 128), np.float32)
    Bim = np.zeros((128, NST, 128), np.float32)
    Cre = np.zeros((128, NST, 128), np.float32)
    Cim = np.zeros((128, NST, 128), np.float32)
    for st in range(NST):
        for gl in range(2):
            g = g0 + 2 * st + gl
            r0 = (2 * (st % 4) + gl) * 16
            Bre[r0:r0 + 16, st, gl * 64:(gl + 1) * 64] = b_re[g].T
            Bim[r0:r0 + 16, st, gl * 64:(gl + 1) * 64] = b_im[g].T
            Cre[gl * 64:(gl + 1) * 64, st, r0:r0 + 16] = c_re[g].T
            Cim[gl * 64:(gl + 1) * 64, st, r0:r0 + 16] = c_im[g].T
    out["Bre"], out["Bim"], out["Cre"], out["Cim"] = Bre, Bim, Cre, Cim
    out["dcol"] = np.ascontiguousarray(d[g0 * 16:(g0 + 24) * 16].reshape(3, 128).T)
    return out


ISQ = 1.0 / math.sqrt(128.0)


def gemm16(R, ps, ps_b, slab, slab_b, rhs, rhs_b, nkt=KT, width=TB, col0=0):
    kb = R.kb
    for kt in range(nkt):
        kb.op("pe", lambda h, kt=kt: h.matmul(ps[:, col0:col0 + width], slab[:, kt, :], rhs[:, kt, 0:width],
                                              start=(kt == 0), stop=(kt == nkt - 1)),
              reads=[slab_b, rhs_b[kt]], writes=[ps_b])


def next_slab(R):
    j = R.ctr % 2
    R.ctr += 1
    return j


def head_rstd(R, ps, ps_b, width):
    kb = R.kb
    kb.op("act", lambda h: h.activation(out=R.sq[0][:, 0:width], in_=ps[:, 0:width], func=AF.Square),
          reads=[ps_b], writes=[R.sq_b[0]])
    kb.op("pe", lambda h: h.matmul(R.psS[:, 0:width], R.ones[:], R.sq[0][:, 0:width], start=True, stop=True),
          reads=[R.sq_b[0], R.ones_b], writes=[R.psS_b])
    kb.op("act", lambda h: h.activation(out=R.rstd[:, 0:width], in_=R.psS[:, 0:width], func=AF.Ln, scale=1.0 / 128,
                                        bias=EPS), reads=[R.psS_b], writes=[R.rstd_b])
    kb.op("act", lambda h: h.activation(out=R.rstd[:, 0:width], in_=R.rstd[:, 0:width], func=AF.Exp, scale=-0.5),
          reads=[R.rstd_b], writes=[R.rstd_b])


def mem_prep(R, memT, w_mem_kv, slot_mem):
    kb = R.kb
    for kt in range(KT):
        kb.dma("sp", R.xblk[:, kt, 0:NMEM], memT[kt * 128:(kt + 1) * 128, :], writes=[R.xblk_b[kt]])
    rmsnorm_block(R, slot_mem, R.xblk, R.xblk_b, dst=R.memh, dst_b=R.memh_b, width=NMEM)
    for hd in range(4):
        j = next_slab(R)
        load_slab(kb, R.wA[j], R.wA_b[j], w_mem_kv, hd * 128, 128, KT)
        gemm16(R, R.psA[j], R.psA_b[j], R.wA[j], R.wA_b[j], R.memh, R.memh_b, width=NMEM)
        head_rstd(R, R.psA[j], R.psA_b[j], NMEM)
        kb.op("dve", lambda h, j=j, hd=hd: h.scalar_tensor_tensor(out=R.KmT[:, hd, :], in0=R.psA[j][:, 0:NMEM],
                                                                  scalar=R.gqk[:, 1:2], in1=R.rstd[:, 0:NMEM],
                                                                  op0=ALU.mult, op1=ALU.mult),
              reads=[R.psA_b[j], R.gqk_b, R.rstd_b], writes=[R.kv_b])
    for hd in range(4):
        j = next_slab(R)
        load_slab(kb, R.wA[j], R.wA_b[j], w_mem_kv, MEMW + hd * 128, 128, KT)
        for mt in range(2):
            for kt in range(KT):
                kb.op("pe", lambda h, kt=kt, mt=mt, j=j: h.matmul(R.psB[j][:, mt * 128:(mt + 1) * 128],
                                                                  R.memh[:, kt, mt * 128:(mt + 1) * 128],
                                                                  R.wA[j][:, kt, :], start=(kt == 0),
                                                                  stop=(kt == KT - 1)),
                      reads=[R.wA_b[j], R.memh_b[kt]], writes=[R.psB_b[j]])
        for mt in range(2):
            kb.op("act", lambda h, mt=mt, j=j, hd=hd: h.activation(out=R.Vm[:, mt, hd * 128:(hd + 1) * 128],
                                                                   in_=R.psB[j][:, mt * 128:(mt + 1) * 128],
                                                                   func=AF.Copy),
                  reads=[R.psB_b[j]], writes=[R.kv_b])


def _wb(outs):
    lst = outs.get("wb_list")
    if not lst:
        return []
    k = outs["wb_ctr"][0]
    outs["wb_ctr"][0] = k + 1
    return [lst[k % len(lst)]]


def mixin_block(R, kind, slot_mix, w_in, outs, tb):
    kb = R.kb
    tsl = slice(tb * TB, (tb + 1) * TB)
    rmsnorm_block(R, slot_mix, R.xblk, R.xblk_b)
    if kind == "sb":
        for nt in range(24):
            j = next_slab(R)
            load_slab(kb, R.wA[j], R.wA_b[j], w_in, nt * 128, 128, KT)
            gemm16(R, R.psA[j], R.psA_b[j], R.wA[j], R.wA_b[j], R.hT, R.hT_b)
            s = R.sctr % 2
            R.sctr += 1
            kb.op("act", lambda h, j=j, s=s, nt=nt: h.activation(out=R.stg[s][:], in_=R.psA[j][:], func=AF.Copy,
                                                                 scale=(ISQ if nt < 12 else 1.0)),
                  reads=[R.psA_b[j]], writes=[R.stg_b[s]])
            if "qk_dst" in outs:
                dap, dwb = outs["qk_dst"](nt, tb)
                kb.dma("sp", dap, R.stg[s][:], reads=[R.stg_b[s]], writes=dwb)
            else:
                kb.dma("sp", outs["qkT"][nt * 128:(nt + 1) * 128, tsl], R.stg[s][:], reads=[R.stg_b[s]])
        for nt in range(12):
            j = next_slab(R)
            load_slab(kb, R.wA[j], R.wA_b[j], w_in, 3072 + nt * 128, 128, KT)
            for t4 in range(4):
                for kt in range(KT):
                    kb.op("pe", lambda h, kt=kt, t4=t4, j=j: h.matmul(R.psB[j][:, t4 * 128:(t4 + 1) * 128],
                                                                      R.hT[:, kt, t4 * 128:(t4 + 1) * 128],
                                                                      R.wA[j][:, kt, :], start=(kt == 0),
                                                                      stop=(kt == KT - 1)),
                          reads=[R.wA_b[j], R.hT_b[kt]], writes=[R.psB_b[j]])
            s = R.sctr % 2
            R.sctr += 1
            kb.op("act", lambda h, j=j, s=s: h.activation(out=R.stg[s][:], in_=R.psB[j][:], func=AF.Copy),
                  reads=[R.psB_b[j]], writes=[R.stg_b[s]])
            if "v_dst" in outs:
                dap, dwb = outs["v_dst"](nt, tb)
                kb.dma("sp", dap, R.stg[s][:].rearrange("p (t c) -> p t c", t=4), reads=[R.stg_b[s]], writes=dwb)
            else:
                kb.dma("sp", outs["v"][tsl, nt * 128:(nt + 1) * 128].rearrange("(t p) c -> p t c", p=128),
                       R.stg[s][:].rearrange("p (t c) -> p t c", t=4), reads=[R.stg_b[s]])
        qm0 = 4608
    else:
        for nt in range(12):
            j = next_slab(R)
            load_slab(kb, R.wA[j], R.wA_b[j], w_in, nt * 128, 128, KT)
            gemm16(R, R.psA[j], R.psA_b[j], R.wA[j], R.wA_b[j], R.hT, R.hT_b)
            kb.op("act", lambda h, j=j: h.activation(out=R.sg[j][:], in_=R.psA[j][:], func=AF.Copy),
                  reads=[R.psA_b[j]], writes=[R.sg_b[j]])
            if "u_dst" in outs:
                dap, dwb = outs["u_dst"](nt, tb)
                kb.dma("sp", dap, R.sg[j][:], reads=[R.sg_b[j]], writes=dwb)
            else:
                kb.dma("sp", outs["uT"][nt * 128:(nt + 1) * 128, tsl], R.sg[j][:], reads=[R.sg_b[j]])
        qm0 = 1536
    for hd in range(4):
        j = next_slab(R)
        load_slab(kb, R.wA[j], R.wA_b[j], w_in, qm0 + hd * 128, 128, KT)
        gemm16(R, R.psA[j], R.psA_b[j], R.wA[j], R.wA_b[j], R.hT, R.hT_b)
        head_rstd(R, R.psA[j], R.psA_b[j], TB)
        kb.op("dve", lambda h, j=j: h.scalar_tensor_tensor(out=R.qn[:], in0=R.psA[j][:], scalar=R.gqk[:, 0:1],
                                                           in1=R.rstd[:], op0=ALU.mult, op1=ALU.mult),
              reads=[R.psA_b[j], R.gqk_b, R.rstd_b], writes=[R.qn_b])
        for mt in range(2):
            kb.op("pe", lambda h, mt=mt, hd=hd: h.matmul(R.psB[mt][:], R.KmT[:, hd, mt * 128:(mt + 1) * 128], R.qn[:],
                                                         start=True, stop=True),
                  reads=[R.kv_b, R.qn_b], writes=[R.psB_b[mt]])
            kb.op("act", lambda h, mt=mt: h.activation(out=R.pT[:, mt, :], in_=R.psB[mt][:], func=AF.Exp, scale=ISQ),
                  reads=[R.psB_b[mt]], writes=[R.pT_b[mt]])
        for mt in range(2):
            kb.op("pe", lambda h, mt=mt: h.matmul(R.psS[:], R.ones_bf[:], R.pT[:, mt, :], start=(mt == 0),
                                                  stop=(mt == 1)),
                  reads=[R.ones_b, R.pT_b[mt]], writes=[R.psS_b])
        o = hd % 2
        for mt in range(2):
            kb.op("pe", lambda h, mt=mt, hd=hd, o=o: h.matmul(R.psO[o][:], R.Vm[:, mt, hd * 128:(hd + 1) * 128],
                                                              R.pT[:, mt, :], start=(mt == 0), stop=(mt == 1)),
                  reads=[R.kv_b, R.pT_b[mt]], writes=[R.psO_b[o]])
        kb.op("dve", lambda h: h.reciprocal(out=R.rstd[:], in_=R.psS[:]), reads=[R.psS_b], writes=[R.rstd_b])
        s = R.sctr % 2
        R.sctr += 1
        kb.op("dve", lambda h, o=o, s=s: h.tensor_tensor(out=R.stg[s][:], in0=R.psO[o][:], in1=R.rstd[:],
                                                         op=ALU.mult),
              reads=[R.psO_b[o], R.rstd_b], writes=[R.stg_b[s]])
        kb.dma("sp", outs["crossT_out"][hd * 128:(hd + 1) * 128, tsl], R.stg[s][:], reads=[R.stg_b[s]],
               writes=(outs.get("cross_wb") or []))


def post_block(R, kind, tokT, crossT, w_out, w_glu, tb, tok_src=None, rd=()):
    kb = R.kb
    tsl = slice(tb * TB, (tb + 1) * TB)
    if tok_src is None:
        tok_src = lambda kt: tokT[kt * 128:(kt + 1) * 128, tsl]
    rd = list(rd)
    if kind == "sb":
        for kt in range(12):
            kb.dma("sp", R.hT[:, kt, :], tok_src(kt), reads=rd, writes=[R.hT_b[kt]])
    else:
        for kt in range(12):
            kb.dma("sp", R.actT[:, kt, :], tok_src(kt), reads=rd, writes=[R.actT_b[kt]])
        for nt in range(12):
            j = next_slab(R)
            load_slab(kb, R.wA[j], R.wA_b[j], w_glu, nt * 128, 128, 12)
            gemm16(R, R.psA[j], R.psA_b[j], R.wA[j], R.wA_b[j], R.actT, R.actT_b, nkt=12)
            kb.op("act", lambda h, j=j: h.activation(out=R.sg[j][:], in_=R.psA[j][:], func=AF.Sigmoid),
                  reads=[R.psA_b[j]], writes=[R.sg_b[j]])
            kb.op("dve", lambda h, j=j, nt=nt: h.tensor_tensor(out=R.hT[:, nt, :], in0=R.actT[:, nt, :],
                                                               in1=R.sg[j][:], op=ALU.mult),
                  reads=[R.actT_b[nt], R.sg_b[j]], writes=[R.hT_b[nt]])
    for kt in range(12, 16):
        kb.dma("sp", R.hT[:, kt, :], crossT[(kt - 12) * 128:(kt - 11) * 128, tsl], reads=rd,
               writes=[R.hT_b[kt]])
    for dt in range(KT):
        j = next_slab(R)
        load_slab(kb, R.wB[j], R.wB_b[j], w_out, dt * 128, 128, KT)
        gemm16(R, R.psO[j], R.psO_b[j], R.wB[j], R.wB_b[j], R.hT, R.hT_b)
        kb.op("dve", lambda h, dt=dt, j=j: h.tensor_tensor(out=R.xblk[:, dt, :], in0=R.psO[j][:],
                                                           in1=R.xblk[:, dt, :], op=ALU.add),
              reads=[R.psO_b[j], R.xblk_b[dt]], writes=[R.xblk_b[dt]])


def build_row(post, ffn2, ffn1, mix):
    nc = bass.Bass("TRN2", target_bir_lowering=False)

    def din(name, shape, dt=F32):
        return nc.dram_tensor(name, shape, dt, kind="ExternalInput").ap()

    def dout(name, shape, dt=F32):
        return nc.dram_tensor(name, shape, dt, kind="ExternalOutput").ap()

    xT = din("xT", [D, TPC])
    xT_out = dout("xT_out", [D, TPC])
    a = {}
    if post:
        a["tokT"] = din("tokT", [TOKW, TPC], BF16)
        a["crossT_in"] = din("crossT_in", [MEMW, TPC], BF16)
        a["w_out"] = din("w_out", [D, D])
        if post == "s5":
            a["w_glu"] = din("w_glu", [TOKW, TOKW])
    if ffn2:
        a["g_ffn2"] = din("g_ffn2", [D])
        a["w_gu2"] = din("w_gu2", [D, 2 * FF])
        a["w_down2"] = din("w_down2", [FF, D])
    if ffn1:
        a["g_ffn1"] = din("g_ffn1", [D])
        a["w_gu1"] = din("w_gu1", [D, 2 * FF])
        a["w_down1"] = din("w_down1", [FF, D])
    outs = {}
    if mix:
        a["g_mix"] = din("g_mix", [D])
        a["w_in"] = din("w_in", [D, 5120 if mix == "sb" else 2048])
        a["memT"] = din("memT", [D, NMEM])
        a["g_mem"] = din("g_mem", [D])
        a["w_mem_kv"] = din("w_mem_kv", [D, 2 * MEMW])
        a["gqk"] = din("gqk", [128, 2])
        outs["crossT_out"] = dout("crossT_out", [MEMW, TPC], BF16)
        if mix == "sb":
            outs["qkT"] = dout("qkT", [3072, TPC], BF16)
            outs["v"] = dout("v", [TPC, TOKW], BF16)
        else:
            outs["uT"] = dout("uT", [TOKW, TPC], F32)
    kb = KB(nc)
    R = RowRes(kb)
    if ffn2:
        R.load_gains(0, a["g_ffn2"])
    if ffn1:
        R.load_gains(1, a["g_ffn1"])
    if mix:
        R.load_gains(2, a["g_mix"])
        R.load_gains(3, a["g_mem"])
        kb.dma("sp", R.gqk[:], a["gqk"], writes=[R.gqk_b])
        mem_prep(R, a["memT"], a["w_mem_kv"], 3)
    for tb in range(NTB):
        load_xblk(R, xT, tb)
        if post:
            post_block(R, post, a["tokT"], a["crossT_in"], a["w_out"], a.get("w_glu"), tb)
        if ffn2:
            ffn_block(R, 0, a["w_gu2"], a["w_down2"])
        if ffn1:
            ffn_block(R, 1, a["w_gu1"], a["w_down1"])
        if mix:
            mixin_block(R, mix, 2, a["w_in"], outs, tb)
        store_xblk(R, xT_out, tb)
    kb.finish()
    return nc


_PROGS = {}


def _prog(key, fn):
    if key not in _PROGS:
        _PROGS[key] = fn()
    return _PROGS[key]


def _run(nc, in_maps):
    res = run_bass_kernel_spmd(nc, in_maps, core_ids=list(range(NCORES)))
    return res.results


def kernel_unfused(x, mem, ffn1_norm, ffn1_w_gu, ffn1_w_down, mix_norm, mem_norm, w_mem_kv, xq_norm, xk_norm, w_out,
           ffn2_norm, ffn2_w_gu, ffn2_w_down, sb_w_in, s5_w_in, s5_log_dt, s5_a_re, s5_a_im, s5_b_re, s5_b_im,
           s5_c_re, s5_c_im, s5_d, s5_w_glu, _debug=None):
    f32 = np.float32
    A = lambda t: np.ascontiguousarray(np.asarray(t, dtype=f32))
    x = A(x)
    mem = A(mem)
    xT = [np.ascontiguousarray(x[c // 4, (c % 4) * TPC:(c % 4 + 1) * TPC, :].T) for c in range(NCORES)]
    memT = [np.ascontiguousarray(mem[b].T) for b in range(NB)]
    tokT = None
    crossT = None
    for stage in range(DEPTH + 1):
        li = stage
        lp = stage - 1
        post = None if lp < 0 else ("sb" if lp % 2 == 0 else "s5")
        mix = None if li >= DEPTH else ("sb" if li % 2 == 0 else "s5")
        ffn2 = lp >= 0
        ffn1 = li < DEPTH
        nc = _prog(("row", post, ffn2, ffn1, mix), lambda: build_row(post, ffn2, ffn1, mix))
        maps = []
        for c in range(NCORES):
            m = {"xT": xT[c]}
            if post:
                m["tokT"] = tokT[c]
                m["crossT_in"] = crossT[c]
                m["w_out"] = A(w_out[lp])
                if post == "s5":
                    m["w_glu"] = A(s5_w_glu[lp // 2])
            if ffn2:
                m["g_ffn2"] = A(ffn2_norm[lp])
                m["w_gu2"] = A(ffn2_w_gu[lp])
                m["w_down2"] = A(ffn2_w_down[lp])
            if ffn1:
                m["g_ffn1"] = A(ffn1_norm[li])
                m["w_gu1"] = A(ffn1_w_gu[li])
                m["w_down1"] = A(ffn1_w_down[li])
            if mix:
                m["g_mix"] = A(mix_norm[li])
                m["w_in"] = A(sb_w_in[li // 2]) if mix == "sb" else A(s5_w_in[li // 2])
                m["memT"] = memT[c // 4]
                m["g_mem"] = A(mem_norm[li])
                m["w_mem_kv"] = A(w_mem_kv[li])
                m["gqk"] = np.ascontiguousarray(np.stack([A(xq_norm[li]), A(xk_norm[li])], axis=1))
            maps.append(m)
        res = _run(nc, maps)
        xT = [res[c]["xT_out"] for c in range(NCORES)]
        if _debug is not None:
            _debug["x_stage%d" % stage] = [a.copy() for a in xT]
        if not mix:
            break
        crossT = [res[c]["crossT_out"] for c in range(NCORES)]
        if mix == "sb":
            maps = []
            for c in range(NCORES):
                qs, ks, vs = [], [], []
                for i in range(NPAIR):
                    p = c * NPAIR + i
                    b, hd = p // 12, p % 12
                    qs.append(np.concatenate([res[b * 4 + qd]["qkT"][hd * 128:(hd + 1) * 128] for qd in range(4)], axis=1))
                    ks.append(np.concatenate([res[b * 4 + qd]["qkT"][(12 + hd) * 128:(13 + hd) * 128] for qd in range(4)], axis=1))
                    vs.append(np.concatenate([res[b * 4 + qd]["v"][:, hd * 128:(hd + 1) * 128] for qd in range(4)], axis=0))
                maps.append({"qT": np.ascontiguousarray(np.stack(qs)), "kT": np.ascontiguousarray(np.stack(ks)),
                             "v": np.ascontiguousarray(np.stack(vs))})
            nc2 = _prog(("sb",), lambda: build_sb_attn())
            r2 = _run(nc2, maps)
            tokT = []
            for c in range(NCORES):
                b, qd = c // 4, c % 4
                rows = []
                for hd in range(12):
                    p = b * 12 + hd
                    rows.append(r2[p // NPAIR]["oT"][p % NPAIR][:, qd * TPC:(qd + 1) * TPC])
                tokT.append(np.ascontiguousarray(np.concatenate(rows, axis=0)))
        else:
            jj = li // 2
            maps = []
            for c in range(NCORES):
                b, gs = c // 4, c % 4
                m = s5_host_params(A(s5_log_dt[jj]), A(s5_a_re[jj]), A(s5_a_im[jj]), A(s5_b_re[jj]), A(s5_b_im[jj]),
                                   A(s5_c_re[jj]), A(s5_c_im[jj]), A(s5_d[jj]), gs)
                m["uT"] = np.ascontiguousarray(
                    np.concatenate([res[b * 4 + qd]["uT"][gs * 384:(gs + 1) * 384] for qd in range(4)], axis=1))
                maps.append(m)
            nc2 = _prog(("s5",), lambda: build_s5())
            r2 = _run(nc2, maps)
            tokT = []
            for c in range(NCORES):
                b, qd = c // 4, c % 4
                tokT.append(np.ascontiguousarray(
                    np.concatenate([r2[b * 4 + gs]["ygT"][:, qd * TPC:(qd + 1) * TPC] for gs in range(4)], axis=0)))
        if _debug is not None:
            _debug["tok_stage%d" % stage] = [a.copy() for a in tokT]
            _debug["cross_stage%d" % stage] = [a.copy() for a in crossT]
    out = np.empty((NB, L, D), f32)
    for c in range(NCORES):
        out[c // 4, (c % 4) * TPC:(c % 4 + 1) * TPC, :] = xT[c].T
    return out


GROUPS = [[0, 1, 2, 3], [4, 5, 6, 7]]
NWB = 8


class Exchange:
    def __init__(self, kb, name, nchunks, rows, cols, dt):
        nc = kb.nc
        self.kb, self.n, self.rows, self.cols = kb, nchunks, rows, cols
        self.snd = nc.dram_tensor(name + "_snd", [nchunks * 4 * rows, cols], dt)
        self.rcv = nc.dram_tensor(name + "_rcv", [nchunks * 16 * rows, cols], dt)
        self.loc = nc.dram_tensor(name + "_loc", [nchunks * 4 * rows, cols], dt)
        self.snd_b = [Buf() for _ in range(nchunks)]
        self.rcv_b = [Buf() for _ in range(nchunks)]
        self.loc_b = Buf()

    def snd_rows(self, c, j, r0=0, n=None):
        n = self.rows if n is None else n
        base = (c * 4 + j) * self.rows + r0
        return self.snd.ap()[base:base + n, :]

    def loc_rows(self, c, r, r0=0, n=None):
        n = self.rows if n is None else n
        base = (c * 4 + r) * self.rows + r0
        return self.loc.ap()[base:base + n, :]

    def run(self, jv):
        kb = self.kb
        kb.coll_chunks(self, GROUPS)
        kb.dma("sp", self.loc.ap().rearrange("(cr x) t -> cr x t", x=self.rows),
               self.rcv.ap().rearrange("(cr j x) t -> cr j x t", j=4, x=self.rows)[:, jv],
               reads=self.rcv_b, writes=[self.loc_b])


def build_fused():
    nc = bass.Bass("TRN2", target_bir_lowering=False)

    def din(name, shape, dt=F32):
        return nc.dram_tensor(name, shape, dt, kind="ExternalInput").ap()

    xT = din("xT", [D, TPC])
    memT = din("memT", [D, NMEM])
    W = {}
    for name, shape in (("ffn1_norm", [DEPTH, D]), ("ffn1_w_gu", [DEPTH, D, 2 * FF]), ("ffn1_w_down", [DEPTH, FF, D]),
                        ("mix_norm", [DEPTH, D]), ("mem_norm", [DEPTH, D]), ("w_mem_kv", [DEPTH, D, 2 * MEMW]),
                        ("gqk", [DEPTH, 128, 2]), ("w_out", [DEPTH, D, D]), ("ffn2_norm", [DEPTH, D]),
                        ("ffn2_w_gu", [DEPTH, D, 2 * FF]), ("ffn2_w_down", [DEPTH, FF, D]),
                        ("sb_w_in", [2, D, 5120]), ("s5_w_in", [2, D, 2048]), ("s5_w_glu", [2, TOKW, TOKW]),
                        ("ldt", [2, 128, NST]), ("are", [2, 128, NST]), ("aim", [2, 128, NST]),
                        ("Bre", [2, 128, NST, 128]), ("Bim", [2, 128, NST, 128]), ("Cre", [2, 128, NST, 128]),
                        ("Cim", [2, 128, NST, 128]), ("dcol", [2, 128, 3])):
        W[name] = din(name, shape)
    yT = nc.dram_tensor("yT", [D, TPC], F32, kind="ExternalOutput").ap()

    xs = nc.dram_tensor("xs", [D, TPC], F32).ap()
    xs_b = [[Buf() for _ in range(KT)] for _ in range(NTB)]
    cross = [nc.dram_tensor("cross%d" % i, [MEMW, TPC], BF16).ap() for i in range(DEPTH)]
    cross_b = [[Buf() for _ in range(NTB)] for _ in range(DEPTH)]

    kb = KB(nc)
    pid = nc.sync.partition_id()
    jv = pid % 4
    ex_tok = None

    for li in range(DEPTH + 1):
        lp = li - 1
        post = None if lp < 0 else ("sb" if lp % 2 == 0 else "s5")
        mix = None if li >= DEPTH else ("sb" if li % 2 == 0 else "s5")
        kb.push()
        R = RowRes(kb)
        outs = {}
        if lp >= 0:
            R.load_gains(0, W["ffn2_norm"][lp])
        if li < DEPTH:
            R.load_gains(1, W["ffn1_norm"][li])
        if mix:
            R.load_gains(2, W["mix_norm"][li])
            R.load_gains(3, W["mem_norm"][li])
            kb.dma("sp", R.gqk[:], W["gqk"][li], writes=[R.gqk_b])
            mem_prep(R, memT, W["w_mem_kv"][li], 3)
            outs["crossT_out"] = cross[li]
            if mix == "sb":
                ex_qk = Exchange(kb, "qk%d" % li, 12, 128, 1024, BF16)
                ex_v = Exchange(kb, "v%d" % li, 6, 1024, 128, BF16)

                def qk_dst(nt, tb, ex=ex_qk):
                    a_, h = nt // 12, nt % 12
                    j, i = h // 3, h % 3
                    c = (a_ * 3 + i) * 2 + tb // 2
                    return ex.snd_rows(c, j)[:, (tb % 2) * TB:(tb % 2 + 1) * TB], [ex.snd_b[c]]

                def v_dst(nt, tb, ex=ex_v):
                    j, i = nt // 3, nt % 3
                    c = i * 2 + tb // 2
                    ap = ex.snd_rows(c, j, (tb % 2) * TB, TB).rearrange("(t p) c -> p t c", p=128)
                    return ap, [ex.snd_b[c]]

                outs["qk_dst"] = qk_dst
                outs["v_dst"] = v_dst
            else:
                ex_u = Exchange(kb, "u%d" % li, 12, 128, 512, F32)

                def u_dst(nt, tb, ex=ex_u):
                    j, ct = nt // 3, nt % 3
                    c = ct * 4 + tb
                    return ex.snd_rows(c, j), [ex.snd_b[c]]

                outs["u_dst"] = u_dst
        for tb in range(NTB):
            if li == 0:
                load_xblk(R, xT, tb)
            else:
                for kt in range(KT):
                    kb.dma("sp", R.xblk[:, kt, :], xs[kt * 128:(kt + 1) * 128, tb * TB:(tb + 1) * TB],
                           reads=[xs_b[tb][kt]], writes=[R.xblk_b[kt]])
            if post:
                def tok_src(kt, tb=tb, ex=ex_tok):
                    r, i = kt // 3, kt % 3
                    c = i * 2 + tb // 2
                    return ex.loc_rows(c, r)[:, (tb % 2) * TB:(tb % 2 + 1) * TB]

                post_block(R, post, None, cross[lp], W["w_out"][lp],
                           W["s5_w_glu"][lp // 2] if post == "s5" else None, tb, tok_src=tok_src,
                           rd=[ex_tok.loc_b, cross_b[lp][tb]])
                ffn_block(R, 0, W["ffn2_w_gu"][lp], W["ffn2_w_down"][lp])
            if li < DEPTH:
                ffn_block(R, 1, W["ffn1_w_gu"][li], W["ffn1_w_down"][li])
            if mix:
                outs["cross_wb"] = [cross_b[li][tb]]
                mixin_block(R, mix, 2, W["sb_w_in"][li // 2] if mix == "sb" else W["s5_w_in"][li // 2], outs, tb)
            if li == DEPTH:
                store_xblk(R, yT, tb)
            else:
                store_xblk(R, xs, tb, dbufs=xs_b[tb])
        kb.pop()
        if not mix:
            break
        if mix == "sb":
            ex_qk.run(jv)
            ex_v.run(jv)
            ex_o = Exchange(kb, "o%d" % li, 6, 128, 1024, BF16)

            def load_fn(pi, q_s, k_s, v_s, b, ex_qk=ex_qk, ex_v=ex_v):
                for r in range(4):
                    for th in range(2):
                        c0 = r * TPC + th * 1024
                        kb.dma("sp", q_s[:, c0:c0 + 1024], ex_qk.loc_rows(pi * 2 + th, r), reads=[ex_qk.loc_b],
                               writes=[b])
                        kb.dma("sp", k_s[:, c0:c0 + 1024], ex_qk.loc_rows((3 + pi) * 2 + th, r), reads=[ex_qk.loc_b],
                               writes=[b])
                        kb.dma("sp", v_s[:, r * 16 + th * 8:r * 16 + th * 8 + 8, :],
                               ex_v.loc_rows(pi * 2 + th, r).rearrange("(kb p) d -> p kb d", p=128),
                               reads=[ex_v.loc_b], writes=[b])

            def store_fn(pi, qb, o_s, o_b, ex=ex_o):
                qd, tq = qb // 4, qb % 4
                c = pi * 2 + tq // 2
                kb.dma("sp", ex.snd_rows(c, qd)[:, (tq % 2) * QB:(tq % 2 + 1) * QB], o_s[:], reads=[o_b],
                       writes=[ex.snd_b[c]])

            kb.push()
            sb_attn_phase(kb, None, None, None, None, NPAIR, NQB, load_fn=load_fn, store_fn=store_fn)
            kb.pop()
            ex_o.run(jv)
            ex_tok = ex_o
        else:
            ex_u.run(jv)
            ex_y = Exchange(kb, "y%d" % li, 6, 128, 1024, BF16)

            def uload_fn(ch, ct, u_s, u_b, ex=ex_u):
                qd, tq = ch // 4, ch % 4
                kb.dma("sp", u_s[:], ex.loc_rows(ct * 4 + tq, qd), reads=[ex.loc_b], writes=[u_b])

            def ystore_fn(ch, ct, yo, yo_b, ex=ex_y):
                qd, tq = ch // 4, ch % 4
                c = ct * 2 + tq // 2
                kb.dma("sp", ex.snd_rows(c, qd)[:, (tq % 2) * SCH:(tq % 2 + 1) * SCH], yo[:], reads=[yo_b],
                       writes=[ex.snd_b[c]])

            jj = li // 2
            dr = {k: W[k][jj] for k in ("ldt", "are", "aim", "Bre", "Bim", "Cre", "Cim", "dcol")}
            kb.push()
            s5_phase(kb, dr, None, L // SCH, uload_fn=uload_fn, ystore_fn=ystore_fn)
            kb.pop()
            ex_y.run(jv)
            ex_tok = ex_y
    kb.finish()
    return nc


def kernel(x, mem, ffn1_norm, ffn1_w_gu, ffn1_w_down, mix_norm, mem_norm, w_mem_kv, xq_norm, xk_norm, w_out,
           ffn2_norm, ffn2_w_gu, ffn2_w_down, sb_w_in, s5_w_in, s5_log_dt, s5_a_re, s5_a_im, s5_b_re, s5_b_im,
           s5_c_re, s5_c_im, s5_d, s5_w_glu):
    f32 = np.float32
    A = lambda t: np.ascontiguousarray(np.asarray(t, dtype=f32))
    x = A(x)
    mem = A(mem)
    nc = _prog(("fused",), build_fused)
    shared = {
        "ffn1_norm": A(ffn1_norm), "ffn1_w_gu": A(ffn1_w_gu), "ffn1_w_down": A(ffn1_w_down),
        "mix_norm": A(mix_norm), "mem_norm": A(mem_norm), "w_mem_kv": A(w_mem_kv),
        "gqk": np.ascontiguousarray(np.stack([A(xq_norm), A(xk_norm)], axis=2)),
        "w_out": A(w_out), "ffn2_norm": A(ffn2_norm), "ffn2_w_gu": A(ffn2_w_gu), "ffn2_w_down": A(ffn2_w_down),
        "sb_w_in": A(sb_w_in), "s5_w_in": A(s5_w_in), "s5_w_glu": A(s5_w_glu),
    }
    s5p = []
    for gs in range(4):
        per = [s5_host_params(A(s5_log_dt[jj]), A(s5_a_re[jj]), A(s5_a_im[jj]), A(s5_b_re[jj]), A(s5_b_im[jj]),
                              A(s5_c_re[jj]), A(s5_c_im[jj]), A(s5_d[jj]), gs) for jj in range(2)]
        s5p.append({k: np.ascontiguousarray(np.stack([per[0][k], per[1][k]])) for k in per[0]})
    maps = []
    for c in range(NCORES):
        m = dict(shared)
        m.update(s5p[c % 4])
        m["xT"] = np.ascontiguousarray(x[c // 4, (c % 4) * TPC:(c % 4 + 1) * TPC, :].T)
        m["memT"] = np.ascontiguousarray(mem[c // 4].T)
        maps.append(m)
    res = _run(nc, maps)
    out = np.empty((NB, L, D), f32)
    for c in range(NCORES):
        out[c // 4, (c % 4) * TPC:(c % 4 + 1) * TPC, :] = res[c]["yT"].T
    return out
```

```python
import contextlib
import math
import numpy as np
import concourse.bass as bass
import concourse.mybir as mybir
from concourse.bass_utils import run_bass_kernel_spmd

F32 = mybir.dt.float32
BF16 = mybir.dt.bfloat16
I32 = mybir.dt.int32
AF = mybir.ActivationFunctionType
ALU = mybir.AluOpType

D = 2048
L = 8192
NB = 2
DEPTH = 4
FF = 5632
TOKW = 1536
MEMW = 512
NMEM = 256
NCORES = 8
TPC = 2048
TB = 512
NTB = TPC // TB
KT = D // 128
FT = FF // 128
EPS = 1e-6


class Buf:
    __slots__ = ("w", "r")

    def __init__(self):
        self.w = None
        self.r = {}


class Eng:
    def __init__(self, name, h, sem, sid):
        self.name = name
        self.h = h
        self.sem = sem
        self.sid = sid
        self.cnt = 0
        self.waited = {}


class KB:
    NDMA = 40

    def __init__(self, nc):
        self.nc = nc
        self.es = contextlib.ExitStack()
        self.sems = {}
        self.engs = {}
        for name, h in (("pe", nc.tensor), ("act", nc.scalar), ("dve", nc.vector),
                        ("pool", nc.gpsimd), ("sp", nc.sync)):
            sem = self.es.enter_context(nc.semaphore("s_" + name))
            self.sems[name] = sem
            self.engs[name] = Eng(name, h, sem, name)
        self.dsem = []
        self.dcnt = []
        for i in range(self.NDMA):
            sem = self.es.enter_context(nc.semaphore("d%d" % i))
            self.sems[("d", i)] = sem
            self.dsem.append(sem)
            self.dcnt.append(0)
        self.dnext = 0
        self.nalloc = 0
        self.scopes = [self.es]
        self.csem = []
        self.ccnt = []
        self.uid = 0

    def push(self):
        sc = contextlib.ExitStack()
        self.scopes.append(sc)

    def pop(self):
        self.barrier()
        self.scopes.pop().close()

    def barrier(self):
        for E in self.engs.values():
            for name, O in self.engs.items():
                if O is not E:
                    self._wait(E, name, O.cnt)
            for i in range(self.NDMA):
                self._wait(E, ("d", i), self.dcnt[i])
            for ci in range(len(self.csem)):
                self._wait(E, ("c", ci), self.ccnt[ci])

    def coll_chunks(self, ex, groups):
        E = self.engs["pool"]
        ci = len(self.csem)
        sem = self.es.enter_context(self.nc.semaphore("cc%d" % ci))
        self.csem.append(sem)
        self.ccnt.append(0)
        self.sems[("c", ci)] = sem
        R4 = 4 * ex.rows
        for c in range(ex.n):
            self._deps(E, [ex.snd_b[c]], [ex.rcv_b[c]], is_dma=True)
            ins = E.h.collective_compute("AllGather", ALU.bypass, replica_groups=groups,
                                         ins=[ex.snd.ap()[c * R4:(c + 1) * R4, :]],
                                         outs=[ex.rcv.ap()[c * 4 * R4:(c + 1) * 4 * R4, :]])
            ins.then_inc(sem)
            self.ccnt[ci] += 1
            self._commit((("c", ci), self.ccnt[ci]), [ex.snd_b[c]], [ex.rcv_b[c]])

    def coll_allgather(self, snd_t, rcv_t, groups, reads=(), writes=()):
        E = self.engs["pool"]
        ci = len(self.csem)
        sem = self.es.enter_context(self.nc.semaphore("cc%d" % ci))
        self.csem.append(sem)
        self.sems[("c", ci)] = sem
        self._deps(E, reads, writes, is_dma=True)
        ins = E.h.collective_compute("AllGather", ALU.bypass, replica_groups=groups,
                                     ins=[snd_t.ap().opt()], outs=[rcv_t.ap().opt()])
        ins.then_inc(sem)
        self.ccnt.append(1)
        self._commit((("c", ci), 1), reads, writes)

    def sb(self, name, shape, dt):
        self.uid += 1
        t = self.scopes[-1].enter_context(self.nc.sbuf_tensor("sb%d_%s" % (self.uid, name), list(shape), dt))
        return t

    def ps(self, name, shape=(128, 512), dt=F32):
        self.uid += 1
        return self.scopes[-1].enter_context(self.nc.psum_tensor("ps%d_%s" % (self.uid, name), list(shape), dt))

    def _wait(self, E, sid, val):
        if E.waited.get(sid, 0) >= val:
            return
        E.h.wait_ge(self.sems[sid], val)
        E.waited[sid] = val

    def _deps(self, E, reads, writes, is_dma=False):
        deps = {}

        def add(tok, kind):
            if tok is None:
                return
            sid, val = tok
            if sid == E.sid and not is_dma:
                if E.name == "pe":
                    return
                if kind != "raw":
                    return
            if deps.get(sid, 0) < val:
                deps[sid] = val

        for b in reads:
            add(b.w, "raw")
        for b in writes:
            add(b.w, "waw")
            for sid, val in b.r.items():
                add((sid, val), "war")
        for sid, val in deps.items():
            self._wait(E, sid, val)

    def _commit(self, tok, reads, writes):
        sid, val = tok
        for b in reads:
            if b.r.get(sid, 0) < val:
                b.r[sid] = val
        for b in writes:
            b.w = tok
            b.r = {}

    def op(self, eng, fn, reads=(), writes=()):
        E = self.engs[eng]
        self._deps(E, reads, writes)
        ins = fn(E.h)
        E.cnt += 1
        ins.then_inc(E.sem, 1)
        self._commit((E.sid, E.cnt), reads, writes)

    def dma(self, eng, out, in_, reads=(), writes=(), **kw):
        E = self.engs[eng]
        i = self.dnext
        self.dnext = (self.dnext + 1) % self.NDMA
        self._deps(E, reads, writes, is_dma=True)
        self._wait(E, ("d", i), self.dcnt[i])
        ins = E.h.dma_start(out=out, in_=in_, **kw)
        self.dcnt[i] += 16
        ins.then_inc(self.dsem[i], 16)
        self._commit((("d", i), self.dcnt[i]), reads, writes)

    def finish(self):
        E = self.engs["sp"]
        for i in range(self.NDMA):
            self._wait(E, ("d", i), self.dcnt[i])
        for name in ("pe", "act", "dve", "pool"):
            self._wait(E, name, self.engs[name].cnt)
        for ci in range(len(self.csem)):
            self._wait(E, ("c", ci), self.ccnt[ci])
        while len(self.scopes) > 1:
            self.scopes.pop().close()
        self.es.close()


class RowRes:
    def __init__(self, kb):
        self.kb = kb
        nc = kb.nc
        self.xblk = kb.sb("xblk", (128, KT, TB), F32)
        self.xblk_b = [Buf() for _ in range(KT)]
        self.hT = kb.sb("hT", (128, KT, TB), BF16)
        self.hT_b = [Buf() for _ in range(KT)]
        self.actT = kb.sb("actT", (128, FT, TB), BF16)
        self.actT_b = [Buf() for _ in range(FT)]
        self.wA = [kb.sb("wA%d" % i, (128, KT, 128), BF16) for i in range(2)]
        self.wA_b = [Buf() for _ in range(2)]
        self.wB = [kb.sb("wB%d" % i, (128, KT, 128), BF16) for i in range(2)]
        self.wB_b = [Buf() for _ in range(2)]
        self.wD = [kb.sb("wD%d" % i, (128, FT, 128), BF16) for i in range(2)]
        self.wD_b = [Buf() for _ in range(2)]
        self.sq = [kb.sb("sq%d" % i, (128, TB), F32) for i in range(2)]
        self.sq_b = [Buf() for _ in range(2)]
        self.rstd = kb.sb("rstd", (128, TB), F32)
        self.rstd_b = Buf()
        self.sg = [kb.sb("sg%d" % i, (128, TB), F32) for i in range(2)]
        self.sg_b = [Buf() for _ in range(2)]
        self.gcol = kb.sb("gcol", (128, 8, KT), F32)
        self.gcol_b = Buf()
        self.ones = kb.sb("ones", (128, 128), F32)
        self.ones_b = Buf()
        self.psA = [kb.ps("psA%d" % i) for i in range(2)]
        self.psA_b = [Buf() for _ in range(2)]
        self.psB = [kb.ps("psB%d" % i) for i in range(2)]
        self.psB_b = [Buf() for _ in range(2)]
        self.psO = [kb.ps("psO%d" % i) for i in range(2)]
        self.psO_b = [Buf() for _ in range(2)]
        self.psS = kb.ps("psS")
        self.psS_b = Buf()
        self.epsc = kb.sb("epsc", (128, 1), F32)
        self.ones_bf = kb.sb("ones_bf", (128, 128), BF16)
        kb.op("dve", lambda h: h.memset(self.ones[:], 1.0), writes=[self.ones_b])
        kb.op("dve", lambda h: h.memset(self.epsc[:], EPS), writes=[self.ones_b])
        kb.op("dve", lambda h: h.memset(self.ones_bf[:], 1.0), writes=[self.ones_b])
        self.memh = kb.sb("memh", (128, KT, NMEM), BF16)
        self.memh_b = [Buf() for _ in range(KT)]
        self.KmT = kb.sb("KmT", (128, 4, NMEM), BF16)
        self.Vm = kb.sb("Vm", (128, 2, MEMW), BF16)
        self.kv_b = Buf()
        self.qn = kb.sb("qn", (128, TB), BF16)
        self.qn_b = Buf()
        self.pT = kb.sb("pT", (128, 2, TB), BF16)
        self.pT_b = [Buf() for _ in range(2)]
        self.stg = [kb.sb("stg%d" % i, (128, TB), BF16) for i in range(2)]
        self.stg_b = [Buf() for _ in range(2)]
        self.gqk = kb.sb("gqk", (128, 2), F32)
        self.gqk_b = Buf()
        self.ctr = 0
        self.sctr = 0

    def load_gains(self, slot, g_dram):
        kb = self.kb
        with kb.nc.allow_non_contiguous_dma(reason="tiny gain vector"):
            kb.dma("sp", self.gcol[:, slot, :], g_dram.rearrange("(kt p) -> p kt", p=128),
                   writes=[self.gcol_b])


def rmsnorm_block(R, slot, src, src_b, nkt=KT, dim=D, dst=None, dst_b=None, width=TB):
    kb = R.kb
    if dst is None:
        dst, dst_b = R.hT, R.hT_b
    W = width
    for kt in range(nkt):
        j = kt % 2
        kb.op("act", lambda h, kt=kt, j=j: h.activation(out=R.sq[j][:, 0:W], in_=src[:, kt, 0:W], func=AF.Square),
              reads=[src_b[kt]], writes=[R.sq_b[j]])
        kb.op("pe", lambda h, kt=kt, j=j: h.matmul(R.psS[:, 0:W], R.ones[:], R.sq[j][:, 0:W], start=(kt == 0),
                                                    stop=(kt == nkt - 1)),
              reads=[R.sq_b[j], R.ones_b], writes=[R.psS_b])
    kb.op("act", lambda h: h.activation(out=R.rstd[:, 0:W], in_=R.psS[:, 0:W], func=AF.Ln, scale=1.0 / dim,
                                        bias=EPS),
          reads=[R.psS_b], writes=[R.rstd_b])
    kb.op("act", lambda h: h.activation(out=R.rstd[:, 0:W], in_=R.rstd[:, 0:W], func=AF.Exp, scale=-0.5),
          reads=[R.rstd_b], writes=[R.rstd_b])
    for kt in range(nkt):
        kb.op("dve", lambda h, kt=kt: h.scalar_tensor_tensor(out=dst[:, kt, 0:W], in0=src[:, kt, 0:W],
                                                             scalar=R.gcol[:, slot, kt:kt + 1], in1=R.rstd[:, 0:W],
                                                             op0=ALU.mult, op1=ALU.mult),
              reads=[src_b[kt], R.gcol_b, R.rstd_b], writes=[dst_b[kt]])


def load_slab(kb, dst, dst_b, w_dram, c0, ncols, nkt):
    wc = getattr(kb, "wc", None)
    if wc is None or not wc["on"]:
        src = w_dram[:, c0:c0 + ncols].rearrange("(kt p) n -> p kt n", p=128)
        kb.dma("pool", dst[:, 0:nkt, 0:ncols], src, writes=[dst_b])
        return
    grp = "D" if nkt > KT else "A"
    idx = wc["idx" + grp]
    wc["idx" + grp] += 1
    scr = wc["t" + grp].ap()[idx, :, 0:nkt * 128].rearrange("p (kt n) -> p kt n", n=128)
    bl = wc["bufs" + grp]
    while len(bl) <= idx:
        bl.append(Buf())
    sb_ = bl[idx]
    if wc["first"]:
        src = w_dram[:, c0:c0 + ncols].rearrange("(kt p) n -> p kt n", p=128)
        kb.dma("pool", dst[:, 0:nkt, 0:ncols], src, writes=[dst_b])
        kb.dma("sp", scr, dst[:, 0:nkt, 0:ncols], reads=[dst_b], writes=[sb_])
    else:
        kb.dma("sp", dst[:, 0:nkt, 0:ncols], scr, reads=[sb_], writes=[dst_b])


def ffn_block(R, slot, w_gu, w_down):
    kb = R.kb
    rmsnorm_block(R, slot, R.xblk, R.xblk_b)
    for ft in range(FT):
        j = R.ctr % 2
        R.ctr += 1
        load_slab(kb, R.wA[j], R.wA_b[j], w_gu, ft * 128, 128, KT)
        load_slab(kb, R.wB[j], R.wB_b[j], w_gu, FF + ft * 128, 128, KT)
        for kt in range(KT):
            kb.op("pe", lambda h, kt=kt, j=j: h.matmul(R.psA[j][:], R.wA[j][:, kt, :], R.hT[:, kt, :],
                                                        start=(kt == 0), stop=(kt == KT - 1)),
                  reads=[R.wA_b[j], R.hT_b[kt]], writes=[R.psA_b[j]])
        for kt in range(KT):
            kb.op("pe", lambda h, kt=kt, j=j: h.matmul(R.psB[j][:], R.wB[j][:, kt, :], R.hT[:, kt, :],
                                                        start=(kt == 0), stop=(kt == KT - 1)),
                  reads=[R.wB_b[j], R.hT_b[kt]], writes=[R.psB_b[j]])
        kb.op("act", lambda h, j=j: h.activation(out=R.sg[j][:], in_=R.psA[j][:], func=AF.Silu),
              reads=[R.psA_b[j]], writes=[R.sg_b[j]])
        kb.op("dve", lambda h, j=j, ft=ft: h.tensor_tensor(out=R.actT[:, ft, :], in0=R.psB[j][:], in1=R.sg[j][:],
                                                           op=ALU.mult),
              reads=[R.psB_b[j], R.sg_b[j]], writes=[R.actT_b[ft]])
    for dt in range(KT):
        j = R.ctr % 2
        R.ctr += 1
        load_slab(kb, R.wD[j], R.wD_b[j], w_down, dt * 128, 128, FT)
        for ft in range(FT):
            kb.op("pe", lambda h, ft=ft, j=j: h.matmul(R.psO[j][:], R.wD[j][:, ft, :], R.actT[:, ft, :],
                                                        start=(ft == 0), stop=(ft == FT - 1)),
                  reads=[R.wD_b[j], R.actT_b[ft]], writes=[R.psO_b[j]])
        kb.op("dve", lambda h, dt=dt, j=j: h.scalar_tensor_tensor(out=R.xblk[:, dt, :], in0=R.psO[j][:], scalar=0.5,
                                                                  in1=R.xblk[:, dt, :], op0=ALU.mult, op1=ALU.add),
              reads=[R.psO_b[j], R.xblk_b[dt]], writes=[R.xblk_b[dt]])


def load_xblk(R, xT, tb, dbuf=None):
    kb = R.kb
    rd = [dbuf] if dbuf is not None else []
    for kt in range(KT):
        kb.dma("sp", R.xblk[:, kt, :], xT[kt * 128:(kt + 1) * 128, tb * TB:(tb + 1) * TB], reads=rd,
               writes=[R.xblk_b[kt]])


def store_xblk(R, xT, tb, dbufs=None):
    kb = R.kb
    for kt in range(KT):
        wr = [dbufs[kt]] if dbufs is not None else []
        kb.dma("sp", xT[kt * 128:(kt + 1) * 128, tb * TB:(tb + 1) * TB], R.xblk[:, kt, :], reads=[R.xblk_b[kt]],
               writes=wr)


def build_ffn_only():
    nc = bass.Bass("TRN2", target_bir_lowering=False)
    xT = nc.dram_tensor("xT", [D, TPC], F32, kind="ExternalInput").ap()
    g = nc.dram_tensor("g", [D], F32, kind="ExternalInput").ap()
    w_gu = nc.dram_tensor("w_gu", [D, 2 * FF], F32, kind="ExternalInput").ap()
    w_down = nc.dram_tensor("w_down", [FF, D], F32, kind="ExternalInput").ap()
    yT = nc.dram_tensor("yT", [D, TPC], F32, kind="ExternalOutput").ap()
    kb = KB(nc)
    R = RowRes(kb)
    R.load_gains(0, g)
    for tb in range(NTB):
        load_xblk(R, xT, tb)
        ffn_block(R, 0, w_gu, w_down)
        store_xblk(R, yT, tb)
    kb.finish()
    return nc


NPAIR = 3
QB = 512
NQB = L // QB
NKB = L // 128


def build_sb_attn(npair=NPAIR, nqb=NQB):
    nc = bass.Bass("TRN2", target_bir_lowering=False)
    qT = nc.dram_tensor("qT", [npair, 128, L], BF16, kind="ExternalInput").ap()
    kT = nc.dram_tensor("kT", [npair, 128, L], BF16, kind="ExternalInput").ap()
    v = nc.dram_tensor("v", [npair, L, 128], BF16, kind="ExternalInput").ap()
    oT = nc.dram_tensor("oT", [npair, 128, L], BF16, kind="ExternalOutput").ap()
    kb = KB(nc)
    sb_attn_phase(kb, qT, kT, v, oT, npair, nqb)
    kb.finish()
    return nc


def sb_attn_phase(kb, qT, kT, v, oT, npair, nqb, load_fn=None, store_fn=None):
    nc = kb.nc
    q_s = [kb.sb("q_s%d" % i, (128, L), BF16) for i in range(2)]
    k_s = [kb.sb("k_s%d" % i, (128, L), BF16) for i in range(2)]
    v_s = [kb.sb("v_s%d" % i, (128, NKB, 128), BF16) for i in range(2)]
    qkv_b = [Buf() for _ in range(2)]
    NE = 6
    NZ = 3
    e_s = [kb.sb("e_s%d" % i, (128, QB), F32) for i in range(NE)]
    e_b = [Buf() for _ in range(NE)]
    sp_s = [kb.sb("sp_s%d" % i, (128, QB), BF16) for i in range(NE)]
    sp_b = [Buf() for _ in range(NE)]
    NX = 3
    x_s = [kb.sb("x_s%d" % i, (128, QB), F32) for i in range(NX)]
    x_b = [Buf() for _ in range(NX)]
    w_s = [kb.sb("w_s%d" % i, (128, QB), BF16) for i in range(NX)]
    w_b = [Buf() for _ in range(NX)]
    o_s = [kb.sb("o_s%d" % i, (128, QB), BF16) for i in range(2)]
    o_b = [Buf() for _ in range(2)]
    ones_f = kb.sb("ones_f", (128, 128), F32)
    uinc = kb.sb("uinc", (128, 128), BF16)
    lstr = kb.sb("lstr", (128, 128), BF16)
    c_b = Buf()
    psZ = [kb.ps("psZ%d" % i) for i in range(3)]
    psZ_b = [Buf() for _ in range(3)]
    psP = [kb.ps("psP%d" % i) for i in range(2)]
    psP_b = [Buf() for _ in range(2)]
    psO = [kb.ps("psOa%d" % i) for i in range(2)]
    psO_b = [Buf() for _ in range(2)]

    kb.op("pool", lambda h: h.memset(ones_f[:], 1.0), writes=[c_b])
    kb.op("pool", lambda h: h.affine_select(out=uinc[:], in_=ones_f[:], pattern=[[-1, 128]], compare_op=ALU.is_ge,
                                            fill=0.0, base=0, channel_multiplier=1), reads=[c_b], writes=[c_b])
    kb.op("pool", lambda h: h.affine_select(out=lstr[:], in_=ones_f[:], pattern=[[1, 128]], compare_op=ALU.is_gt,
                                            fill=0.0, base=0, channel_multiplier=-1), reads=[c_b], writes=[c_b])

    def load_pair(pi):
        j = pi % 2
        if load_fn is not None:
            load_fn(pi, q_s[j], k_s[j], v_s[j], qkv_b[j])
            return
        kb.dma("sp", q_s[j][:], qT[pi], writes=[qkv_b[j]])
        kb.dma("sp", k_s[j][:], kT[pi], writes=[qkv_b[j]])
        kb.dma("sp", v_s[j][:], v[pi].rearrange("(kb p) d -> p kb d", p=128), writes=[qkv_b[j]])

    S0 = [0, 3, 4, 7, 8, 11, 12, 15]
    S1 = [1, 2, 5, 6, 9, 10, 13, 14]
    tiles = []
    for pi in range(npair):
        streams = []
        for si, qbs in enumerate((S0, S1)):
            lst = []
            for qb in qbs:
                if qb >= nqb:
                    continue
                nk = 4 * (qb + 1)
                for idx in range(nk):
                    lst.append((pi, qb, nk - 1 - idx, idx, nk, si))
            streams.append(lst)
        n_ = max(len(streams[0]), len(streams[1]))
        for k in range(n_):
            for si in range(2):
                tiles.append(streams[si][k] if k < len(streams[si]) else None)

    tctr = [0]

    def stage1(t):
        pi, qb, kbi, idx, nk, g = t
        j = pi % 2
        n = tctr[0] % NE
        z = tctr[0] % NZ
        tctr[0] += 1
        kb.op("pe", lambda h: h.matmul(psZ[z][:], k_s[j][:, kbi * 128:(kbi + 1) * 128],
                                       q_s[j][:, qb * QB:(qb + 1) * QB], start=True, stop=True),
              reads=[qkv_b[j]], writes=[psZ_b[z]])
        kb.op("act", lambda h: h.activation(out=e_s[n][:], in_=psZ[z][:], func=AF.Exp),
              reads=[psZ_b[z]], writes=[e_b[n]])
        kb.op("act", lambda h: h.activation(out=sp_s[n][:], in_=e_s[n][:], func=AF.Ln, bias=1.0),
              reads=[e_b[n]], writes=[sp_b[n]])
        dj = kbi - 4 * qb
        if dj >= 0:
            kb.op("pool", lambda h: h.affine_select(out=sp_s[n][:], in_=sp_s[n][:], pattern=[[1, QB]],
                                                    compare_op=ALU.is_gt, fill=0.0, base=-128 * dj,
                                                    channel_multiplier=-1),
                  reads=[sp_b[n]], writes=[sp_b[n]])
        return n

    xctr = [0]

    def stageB1(t, n):
        pi, qb, kbi, idx, nk, g = t
        xi = xctr[0] % NX
        xctr[0] += 1
        kb.op("pe", lambda h: h.matmul(psP[g][:], uinc[:], sp_s[n][:], start=(idx == 0), stop=(idx == nk - 1)),
              reads=[sp_b[n], c_b], writes=[psP_b[g]])
        kb.op("act", lambda h: h.activation(out=x_s[xi][:], in_=psP[g][:], func=AF.Exp, scale=-1.0),
              reads=[psP_b[g]], writes=[x_b[xi]])
        return xi

    def stageB2(t, n):
        pi, qb, kbi, idx, nk, g = t
        if idx != nk - 1:
            kb.op("pe", lambda h: h.matmul(psP[g][:], lstr[:], sp_s[n][:], start=False, stop=False),
                  reads=[sp_b[n], c_b], writes=[psP_b[g]])

    def stageC(t, n, xi):
        pi, qb, kbi, idx, nk, g = t
        j = pi % 2
        kb.op("dve", lambda h: h.tensor_tensor(out=w_s[xi][:], in0=e_s[n][:], in1=x_s[xi][:], op=ALU.mult),
              reads=[e_b[n], x_b[xi]], writes=[w_b[xi]])
        dj = kbi - 4 * qb
        if dj >= 0:
            kb.op("pool", lambda h: h.affine_select(out=w_s[xi][:], in_=w_s[xi][:], pattern=[[1, QB]],
                                                    compare_op=ALU.is_gt, fill=0.0, base=-128 * dj,
                                                    channel_multiplier=-1),
                  reads=[w_b[xi]], writes=[w_b[xi]])
        kb.op("pe", lambda h: h.matmul(psO[g][:], v_s[j][:, kbi, :], w_s[xi][:], start=(idx == 0),
                                       stop=(idx == nk - 1)),
              reads=[w_b[xi], qkv_b[j]], writes=[psO_b[g]])
        if idx == nk - 1:
            kb.op("act", lambda h: h.activation(out=o_s[g][:], in_=psO[g][:], func=AF.Copy),
                  reads=[psO_b[g]], writes=[o_b[g]])
            if store_fn is not None:
                store_fn(pi, qb, o_s[g], o_b[g])
            else:
                kb.dma("sp", oT[pi, :, qb * QB:(qb + 1) * QB], o_s[g][:], reads=[o_b[g]])

    load_pair(0)
    loaded = 1
    pipe = [None, None, None, None]
    seen_pairs = set()

    def step(new):
        a1, a2, b1, b2 = pipe
        nb1 = None
        if a2 is not None:
            t_, n_ = a2
            xi = stageB1(t_, n_)
            nb1 = (t_, n_, xi)
        nb2 = None
        if b1 is not None:
            stageB2(b1[0], b1[1])
            nb2 = b1
        if b2 is not None:
            stageC(*b2)
        pipe[0], pipe[1], pipe[2], pipe[3] = new, a1, nb1, nb2

    cur_pair = 0
    for t in tiles + [("end",)]:
        if t is None:
            step(None)
            continue
        if t[0] != cur_pair or t[0] == "end":
            for _ in range(4):
                step(None)
            if t[0] == "end":
                break
            cur_pair = t[0]
        if t[0] not in seen_pairs:
            seen_pairs.add(t[0])
            if loaded < npair and loaded <= t[0] + 1:
                load_pair(loaded)
                loaded += 1
        n = stage1(t)
        step((t, n))


NST = 12
SCH = 512
TWO_PI = 2.0 * math.pi
GELU_C = 2.0 * math.sqrt(2.0 / math.pi)


def build_s5(nchunks=L // SCH):
    nc = bass.Bass("TRN2", target_bir_lowering=False)
    dr = {}
    for name, shape in (("uT", [384, L]), ("ldt", [128, NST]), ("are", [128, NST]), ("aim", [128, NST]),
                        ("Bre", [128, NST, 128]), ("Bim", [128, NST, 128]), ("Cre", [128, NST, 128]),
                        ("Cim", [128, NST, 128]), ("dcol", [128, 3])):
        dr[name] = nc.dram_tensor(name, shape, F32, kind="ExternalInput").ap()
    ygT = nc.dram_tensor("ygT", [384, L], BF16, kind="ExternalOutput").ap()
    kb = KB(nc)
    s5_phase(kb, dr, ygT, nchunks)
    kb.finish()
    return nc


def s5_phase(kb, dr, ygT, nchunks, uload_fn=None, ystore_fn=None):
    def small(name, w=NST):
        return kb.sb(name, (128, w), F32), Buf()

    def tt(eng, out, a, b, op, reads, writes):
        kb.op(eng, lambda h: h.tensor_tensor(out=out, in0=a, in1=b, op=op), reads=reads, writes=writes)

    def ts(eng, out, a, s1, op0, reads, writes, s2=None, op1=None):
        if op1 is None:
            kb.op(eng, lambda h: h.tensor_scalar(out=out, in0=a, scalar1=s1, scalar2=None, op0=op0),
                  reads=reads, writes=writes)
        else:
            kb.op(eng, lambda h: h.tensor_scalar(out=out, in0=a, scalar1=s1, scalar2=s2, op0=op0, op1=op1),
                  reads=reads, writes=writes)

    def stt(eng, out, a, s, b, op0, op1, reads, writes):
        kb.op(eng, lambda h: h.scalar_tensor_tensor(out=out, in0=a, scalar=s, in1=b, op0=op0, op1=op1),
              reads=reads, writes=writes)

    def act(out, a, func, reads, writes, **kw):
        kb.op("act", lambda h: h.activation(out=out, in_=a, func=func, **kw), reads=reads, writes=writes)

    P = {}
    for name in ("ldt", "are", "aim"):
        P[name] = small("p_" + name)
        kb.dma("sp", P[name][0][:], dr[name], writes=[P[name][1]])
    Bre = kb.sb("Bre_s", (128, NST, 128), F32)
    Bim = kb.sb("Bim_s", (128, NST, 128), F32)
    Cre = kb.sb("Cre_s", (128, NST, 128), F32)
    Cim = kb.sb("Cim_s", (128, NST, 128), F32)
    C2re = kb.sb("C2re", (128, NST, 128), F32)
    C2im = kb.sb("C2im", (128, NST, 128), F32)
    C2ren = kb.sb("C2ren", (128, NST, 128), F32)
    dcol = kb.sb("dcol_s", (128, 3), F32)
    par_b = Buf()
    c2_b = Buf()
    for t, name in ((Bre, "Bre"), (Bim, "Bim"), (Cre, "Cre"), (Cim, "Cim"), (dcol, "dcol")):
        kb.dma("sp", t[:], dr[name], writes=[par_b])

    dt_, dt_b = small("dt_")
    act(dt_[:], P["ldt"][0][:], AF.Exp, [P["ldt"][1]], [dt_b])
    rl, rl_b = small("rl")
    th, th_b = small("th")
    tt("dve", rl[:], P["are"][0][:], dt_[:], ALU.mult, [P["are"][1], dt_b], [rl_b])
    tt("dve", th[:], P["aim"][0][:], dt_[:], ALU.mult, [P["aim"][1], dt_b], [th_b])
    r_, r_b = small("r_")
    act(r_[:], rl[:], AF.Exp, [rl_b], [r_b])
    phi, phi_b = small("phi")
    ts("dve", phi[:], th[:], 1.0 / TWO_PI, ALU.mult, [th_b], [phi_b])
    ki = kb.sb("ki", (128, NST), I32)
    ki_b = Buf()
    kb.op("dve", lambda h: h.tensor_copy(out=ki[:], in_=phi[:]), reads=[phi_b], writes=[ki_b])
    kf, kf_b = small("kf")
    kb.op("dve", lambda h: h.tensor_copy(out=kf[:], in_=ki[:]), reads=[ki_b], writes=[kf_b])
    f_, f_b = small("f_")
    tt("dve", f_[:], phi[:], kf[:], ALU.subtract, [phi_b, kf_b], [f_b])
    cos1, cos1_b = small("cos1")
    sin1, sin1_b = small("sin1")
    tmpa, tmpa_b = small("tmpa")
    tmpb, tmpb_b = small("tmpb")

    def sin_of_frac(out, out_b, frac, frac_b):
        ts("dve", tmpa[:], frac, 0.5, ALU.is_gt, [frac_b], [tmpa_b])
        tt("dve", tmpb[:], frac, tmpa[:], ALU.subtract, [frac_b, tmpa_b], [tmpb_b])
        ts("dve", tmpa[:], tmpb[:], -0.5, ALU.is_lt, [tmpb_b], [tmpa_b])
        tt("dve", tmpb[:], tmpb[:], tmpa[:], ALU.add, [tmpb_b, tmpa_b], [tmpb_b])
        act(out, tmpb[:], AF.Sin, [tmpb_b], [out_b], scale=TWO_PI)

    sin_of_frac(sin1[:], sin1_b, f_[:], f_b)
    fc, fc_b = small("fc")
    ts("dve", fc[:], f_[:], 0.25, ALU.add, [f_b], [fc_b])
    sin_of_frac(cos1[:], cos1_b, fc[:], fc_b)

    p_, p_b = small("p_")
    q_, q_b = small("q_")
    tt("dve", p_[:], r_[:], cos1[:], ALU.mult, [r_b, cos1_b], [p_b])
    ts("dve", p_[:], p_[:], -1.0, ALU.add, [p_b], [p_b])
    tt("dve", q_[:], r_[:], sin1[:], ALU.mult, [r_b, sin1_b], [q_b])
    den, den_b = small("den")
    t1, t1_b = small("t1")
    t2, t2_b = small("t2")
    are, are_b = P["are"]
    aim, aim_b = P["aim"]
    tt("dve", den[:], are[:], are[:], ALU.mult, [are_b], [den_b])
    tt("dve", t1[:], aim[:], aim[:], ALU.mult, [aim_b], [t1_b])
    tt("dve", den[:], den[:], t1[:], ALU.add, [den_b, t1_b], [den_b])
    kb.op("dve", lambda h: h.reciprocal(out=den[:], in_=den[:]), reads=[den_b], writes=[den_b])
    gre, gre_b = small("gre")
    gim, gim_b = small("gim")
    tt("dve", t1[:], p_[:], are[:], ALU.mult, [p_b, are_b], [t1_b])
    tt("dve", t2[:], q_[:], aim[:], ALU.mult, [q_b, aim_b], [t2_b])
    tt("dve", gre[:], t1[:], t2[:], ALU.add, [t1_b, t2_b], [gre_b])
    tt("dve", gre[:], gre[:], den[:], ALU.mult, [gre_b, den_b], [gre_b])
    tt("dve", t1[:], q_[:], are[:], ALU.mult, [q_b, are_b], [t1_b])
    tt("dve", t2[:], p_[:], aim[:], ALU.mult, [p_b, aim_b], [t2_b])
    tt("dve", gim[:], t1[:], t2[:], ALU.subtract, [t1_b, t2_b], [gim_b])
    tt("dve", gim[:], gim[:], den[:], ALU.mult, [gim_b, den_b], [gim_b])
    zre, zre_b = small("zre")
    zim, zim_b = small("zim")
    nzim, nzim_b = small("nzim")
    tt("dve", t1[:], gre[:], cos1[:], ALU.mult, [gre_b, cos1_b], [t1_b])
    tt("dve", t2[:], gim[:], sin1[:], ALU.mult, [gim_b, sin1_b], [t2_b])
    tt("dve", zre[:], t1[:], t2[:], ALU.add, [t1_b, t2_b], [zre_b])
    tt("dve", t1[:], gim[:], cos1[:], ALU.mult, [gim_b, cos1_b], [t1_b])
    tt("dve", t2[:], gre[:], sin1[:], ALU.mult, [gre_b, sin1_b], [t2_b])
    tt("dve", zim[:], t1[:], t2[:], ALU.subtract, [t1_b, t2_b], [zim_b])
    ts("dve", nzim[:], zim[:], -1.0, ALU.mult, [zim_b], [nzim_b])

    ctmp = kb.sb("ctmp", (128, 128), F32)
    ctmp_b = Buf()
    for st in range(NST):
        ts("dve", ctmp[:], Cim[:, st, :], zim[:, st:st + 1], ALU.mult, [par_b, zim_b], [ctmp_b])
        stt("dve", C2re[:, st, :], Cre[:, st, :], zre[:, st:st + 1], ctmp[:], ALU.mult, ALU.subtract,
            [par_b, zre_b, ctmp_b], [c2_b])
        ts("dve", ctmp[:], Cim[:, st, :], zre[:, st:st + 1], ALU.mult, [par_b, zre_b], [ctmp_b])
        stt("dve", C2im[:, st, :], Cre[:, st, :], nzim[:, st:st + 1], ctmp[:], ALU.mult, ALU.subtract,
            [par_b, nzim_b, ctmp_b], [c2_b])
        ts("dve", C2ren[:, st, :], C2re[:, st, :], -1.0, ALU.mult, [c2_b], [c2_b])

    TW = SCH + 8
    tabC = kb.sb("tabC", (128, NST, TW), F32)
    tabS = kb.sb("tabS", (128, NST, TW), F32)
    tab_b = [Buf() for _ in range(NST)]
    ttmp = kb.sb("ttmp", (128, 256), F32)
    ttmp_b = Buf()
    rt = kb.sb("rt", (128, NST, SCH), F32)
    rt_b = Buf()
    onesw = kb.sb("onesw", (128, SCH), F32)
    onesw_b = Buf()
    kb.op("pool", lambda h: h.memset(onesw[:], 1.0), writes=[onesw_b])
    for st in range(NST):
        b = tab_b[st]
        kb.op("pool", lambda h, st=st: h.memset(tabC[:, st, 0:1], 1.0), writes=[b])
        kb.op("pool", lambda h, st=st: h.memset(tabS[:, st, 0:1], 0.0), writes=[b])
        kb.op("act", lambda h, st=st: h.activation(out=tabC[:, st, 1:2], in_=cos1[:, st:st + 1], func=AF.Copy),
              reads=[cos1_b], writes=[b])
        kb.op("act", lambda h, st=st: h.activation(out=tabS[:, st, 1:2], in_=sin1[:, st:st + 1], func=AF.Copy),
              reads=[sin1_b], writes=[b])
        m = 1
        while m < SCH:
            cm = tabC[:, st, m:m + 1]
            sm = tabS[:, st, m:m + 1]
            ts("dve", ttmp[:, 0:m], tabS[:, st, 1:m + 1], sm, ALU.mult, [b], [ttmp_b])
            stt("dve", tabC[:, st, m + 1:2 * m + 1], tabC[:, st, 1:m + 1], cm, ttmp[:, 0:m], ALU.mult, ALU.subtract,
                [b, ttmp_b], [b])
            ts("dve", ttmp[:, 0:m], tabC[:, st, 1:m + 1], sm, ALU.mult, [b], [ttmp_b])
            stt("dve", tabS[:, st, m + 1:2 * m + 1], tabS[:, st, 1:m + 1], cm, ttmp[:, 0:m], ALU.mult, ALU.add,
                [b, ttmp_b], [b])
            m *= 2
        ts("pool", rt[:, st, :], onesw[:], r_[:, st:st + 1], ALU.mult, [onesw_b, r_b], [rt_b])

    u_s = [kb.sb("u_s%d" % i, (128, SCH), F32) for i in range(2)]
    u_b = [Buf() for _ in range(2)]
    NR = 2
    m_s = [[kb.sb("m%d_%d" % (k, i), (128, SCH), F32) for k in range(4)] for i in range(NR)]
    m_b = [[Buf() for k in range(4)] for i in range(NR)]
    wv_s = [[kb.sb("wv%d_%d" % (k, i), (128, SCH), F32) for k in range(2)] for i in range(NR)]
    wv_b = [[Buf() for k in range(2)] for i in range(NR)]
    n_s = [[kb.sb("n%d_%d" % (k, i), (128, SCH), F32) for k in range(4)] for i in range(NR)]
    n_b = [[Buf() for k in range(4)] for i in range(NR)]
    car_r = kb.sb("car_r", (128, NST), F32)
    car_i = kb.sb("car_i", (128, NST), F32)
    car_b = [Buf() for _ in range(NST)]
    yv = kb.sb("yv", (128, SCH), F32)
    yv_b = Buf()
    e1 = kb.sb("e1", (128, SCH), F32)
    e1_b = Buf()
    e2 = kb.sb("e2", (128, SCH), F32)
    e2_b = Buf()
    yo = [kb.sb("yo%d" % i, (128, SCH), BF16) for i in range(2)]
    yo_b = [Buf() for _ in range(2)]
    psR = [kb.ps("psR%d" % i) for i in range(2)]
    psR_b = [Buf() for _ in range(2)]
    psI = [kb.ps("psI%d" % i) for i in range(2)]
    psI_b = [Buf() for _ in range(2)]
    psY = [kb.ps("psY%d" % i) for i in range(2)]
    psY_b = [Buf() for _ in range(2)]
    psW = [kb.ps("psW%d" % i) for i in range(2)]
    psW_b = [Buf() for _ in range(2)]

    Bre16 = kb.sb("Bre16", (128, NST, 128), BF16)
    Bim16 = kb.sb("Bim16", (128, NST, 128), BF16)
    C2re16 = kb.sb("C2re16", (128, NST, 128), BF16)
    C2ren16 = kb.sb("C2ren16", (128, NST, 128), BF16)
    C2im16 = kb.sb("C2im16", (128, NST, 128), BF16)
    t16_b = Buf()
    for dst, src, sb_ in ((Bre16, Bre, par_b), (Bim16, Bim, par_b), (C2re16, C2re, c2_b), (C2ren16, C2ren, c2_b),
                          (C2im16, C2im, c2_b)):
        kb.op("pool", lambda h, dst=dst, src=src: h.tensor_copy(out=dst[:], in_=src[:]), reads=[sb_], writes=[t16_b])
    u16 = [kb.sb("u16_%d" % i, (128, SCH), BF16) for i in range(2)]
    u16_b = [Buf() for _ in range(2)]
    n16 = [[kb.sb("n16_%d_%d" % (k, i), (128, SCH), BF16) for k in range(4)] for i in range(NR)]
    n16_b = [[Buf() for k in range(4)] for i in range(NR)]
    ctmp4 = kb.sb("ctmp4", (128, 4), F32)
    ctmp4_b = Buf()

    iters = [(ch, ct, sl) for ch in range(nchunks) for ct in range(3) for sl in range(4)]

    def P(k):
        ch, ct, sl = iters[k]
        ui = (ch * 3 + ct) % 2
        if sl == 0:
            if uload_fn is not None:
                uload_fn(ch, ct, u_s[ui], u_b[ui])
            else:
                kb.dma("sp", u_s[ui][:], dr["uT"][ct * 128:(ct + 1) * 128, ch * SCH:(ch + 1) * SCH],
                       writes=[u_b[ui]])
            kb.op("act", lambda h: h.activation(out=u16[ui][:], in_=u_s[ui][:], func=AF.Copy),
                  reads=[u_b[ui]], writes=[u16_b[ui]])
        st = ct * 4 + sl
        i = k % 2
        kb.op("pe", lambda h: h.matmul(psR[i][:], Bre16[:, st, :], u16[ui][:], start=True, stop=True),
              reads=[t16_b, u16_b[ui]], writes=[psR_b[i]])
        kb.op("pe", lambda h: h.matmul(psI[i][:], Bim16[:, st, :], u16[ui][:], start=True, stop=True),
              reads=[t16_b, u16_b[ui]], writes=[psI_b[i]])

    def Dk(k):
        ch, ct, sl = iters[k]
        ui = (ch * 3 + ct) % 2
        yi = (ch * 3 + ct) % 2
        st = ct * 4 + sl
        i = k % 2
        pR, pI, pRb, pIb = psR[i], psI[i], psR_b[i], psI_b[i]
        m, mb = m_s[i], m_b[i]
        Cc = tabC[:, st, 0:SCH]
        Ss = tabS[:, st, 0:SCH]
        tb_ = tab_b[st]
        tt("dve", m[0][:], pR[:], Cc, ALU.mult, [pRb, tb_], [mb[0]])
        tt("dve", m[1][:], pI[:], Ss, ALU.mult, [pIb, tb_], [mb[1]])
        tt("dve", m[2][:], pI[:], Cc, ALU.mult, [pIb, tb_], [mb[2]])
        tt("dve", m[3][:], pR[:], Ss, ALU.mult, [pRb, tb_], [mb[3]])
        tt("pool", m[0][:], m[0][:], m[1][:], ALU.add, [mb[0], mb[1]], [mb[0]])
        tt("pool", m[2][:], m[2][:], m[3][:], ALU.subtract, [mb[2], mb[3]], [mb[2]])
        if ch == 0:
            ini_r = 0.0
            ini_i = 0.0
        else:
            ini_r = car_r[:, st:st + 1]
            ini_i = car_i[:, st:st + 1]
        kb.op("dve", lambda h: h.tensor_tensor_scan(out=psW[0][:], data0=rt[:, st, :], data1=m[0][:],
                                                    initial=ini_r, op0=ALU.mult, op1=ALU.add),
              reads=[rt_b, mb[0], car_b[st]], writes=[psW_b[0]])
        kb.op("dve", lambda h: h.tensor_tensor_scan(out=psW[1][:], data0=rt[:, st, :], data1=m[2][:],
                                                    initial=ini_i, op0=ALU.mult, op1=ALU.add),
              reads=[rt_b, mb[2], car_b[st]], writes=[psW_b[1]])
        n, nb = n16[i], n16_b[i]
        C1 = tabC[:, st, 1:SCH + 1]
        S1 = tabS[:, st, 1:SCH + 1]
        tt("dve", n[0][:], psW[0][:], C1, ALU.mult, [psW_b[0], tb_], [nb[0]])
        tt("dve", n[1][:], psW[1][:], S1, ALU.mult, [psW_b[1], tb_], [nb[1]])
        tt("dve", n[2][:], psW[0][:], S1, ALU.mult, [psW_b[0], tb_], [nb[2]])
        tt("dve", n[3][:], psW[1][:], C1, ALU.mult, [psW_b[1], tb_], [nb[3]])
        L1 = slice(SCH - 1, SCH)
        cl = tabC[:, st, SCH:SCH + 1]
        sl_ = tabS[:, st, SCH:SCH + 1]
        tt("dve", ctmp4[:, 0:1], psW[0][:, L1], cl, ALU.mult, [psW_b[0], tb_], [ctmp4_b])
        tt("dve", ctmp4[:, 1:2], psW[1][:, L1], sl_, ALU.mult, [psW_b[1], tb_], [ctmp4_b])
        tt("dve", ctmp4[:, 2:3], psW[0][:, L1], sl_, ALU.mult, [psW_b[0], tb_], [ctmp4_b])
        tt("dve", ctmp4[:, 3:4], psW[1][:, L1], cl, ALU.mult, [psW_b[1], tb_], [ctmp4_b])
        tt("dve", car_r[:, st:st + 1], ctmp4[:, 0:1], ctmp4[:, 1:2], ALU.subtract, [ctmp4_b], [car_b[st]])
        tt("dve", car_i[:, st:st + 1], ctmp4[:, 2:3], ctmp4[:, 3:4], ALU.add, [ctmp4_b], [car_b[st]])
        kb.op("pe", lambda h: h.matmul(psY[yi][:], C2re16[:, st, :], n[0][:], start=(sl == 0), stop=False),
              reads=[t16_b, nb[0]], writes=[psY_b[yi]])
        kb.op("pe", lambda h: h.matmul(psY[yi][:], C2ren16[:, st, :], n[1][:], start=False, stop=False),
              reads=[t16_b, nb[1]], writes=[psY_b[yi]])
        kb.op("pe", lambda h: h.matmul(psY[yi][:], C2im16[:, st, :], n[2][:], start=False, stop=False),
              reads=[t16_b, nb[2]], writes=[psY_b[yi]])
        kb.op("pe", lambda h: h.matmul(psY[yi][:], C2im16[:, st, :], n[3][:], start=False, stop=(sl == 3)),
              reads=[t16_b, nb[3]], writes=[psY_b[yi]])
        if sl == 3:
            stt("dve", yv[:], u_s[ui][:], dcol[:, ct:ct + 1], psY[yi][:], ALU.mult, ALU.add,
                [u_b[ui], par_b, psY_b[yi]], [yv_b])
            act(e1[:], yv[:], AF.Square, [yv_b], [e1_b])
            ts("dve", e1[:], e1[:], 0.044715, ALU.mult, [e1_b], [e1_b], s2=1.0, op1=ALU.add)
            tt("pool", e2[:], e1[:], yv[:], ALU.mult, [e1_b, yv_b], [e2_b])
            act(e2[:], e2[:], AF.Sigmoid, [e2_b], [e2_b], scale=GELU_C)
            tt("pool", yo[yi][:], e2[:], yv[:], ALU.mult, [e2_b, yv_b], [yo_b[yi]])
            if ystore_fn is not None:
                ystore_fn(ch, ct, yo[yi], yo_b[yi])
            else:
                kb.dma("sp", ygT[ct * 128:(ct + 1) * 128, ch * SCH:(ch + 1) * SCH], yo[yi][:], reads=[yo_b[yi]])

    P(0)
    for k in range(len(iters)):
        if k + 1 < len(iters):
            P(k + 1)
        Dk(k)


def s5_host_params(log_dt, a_re, a_im, b_re, b_im, c_re, c_im, d, gs):
    g0 = gs * 24
    out = {}

    def per_state(a):
        return np.ascontiguousarray(a.reshape(NST, 2, 64).transpose(1, 2, 0).reshape(128, NST))

    out["ldt"] = per_state(np.repeat(log_dt[g0:g0 + 24, None], 64, axis=1))
    out["are"] = per_state(a_re[g0:g0 + 24])
    out["aim"] = per_state(a_im[g0:g0 + 24])
    Bre = np.zeros((128, NST, 128), np.float32)
    Bim = np.zeros((128, NST, 128), np.float32)
    Cre = np.zeros((128, NST, 128), np.float32)
    Cim = np.zeros((128, NST, 128), np.float32)
    for st in range(NST):
        for gl in range(2):
            g = g0 + 2 * st + gl
            r0 = (2 * (st % 4) + gl) * 16
            Bre[r0:r0 + 16, st, gl * 64:(gl + 1) * 64] = b_re[g].T
            Bim[r0:r0 + 16, st, gl * 64:(gl + 1) * 64] = b_im[g].T
            Cre[gl * 64:(gl + 1) * 64, st, r0:r0 + 16] = c_re[g].T
            Cim[gl * 64:(gl + 1) * 64, st, r0:r0 + 16] = c_im[g].T
    out["Bre"], out["Bim"], out["Cre"], out["Cim"] = Bre, Bim, Cre, Cim
    out["dcol"] = np.ascontiguousarray(d[g0 * 16:(g0 + 24) * 16].reshape(3, 128).T)
    return out


ISQ = 1.0 / math.sqrt(128.0)


def gemm16(R, ps, ps_b, slab, slab_b, rhs, rhs_b, nkt=KT, width=TB, col0=0):
    kb = R.kb
    for kt in range(nkt):
        kb.op("pe", lambda h, kt=kt: h.matmul(ps[:, col0:col0 + width], slab[:, kt, :], rhs[:, kt, 0:width],
                                              start=(kt == 0), stop=(kt == nkt - 1)),
              reads=[slab_b, rhs_b[kt]], writes=[ps_b])


def next_slab(R):
    j = R.ctr % 2
    R.ctr += 1
    return j


def head_rstd(R, ps, ps_b, width):
    kb = R.kb
    kb.op("act", lambda h: h.activation(out=R.sq[0][:, 0:width], in_=ps[:, 0:width], func=AF.Square),
          reads=[ps_b], writes=[R.sq_b[0]])
    kb.op("pe", lambda h: h.matmul(R.psS[:, 0:width], R.ones[:], R.sq[0][:, 0:width], start=True, stop=True),
          reads=[R.sq_b[0], R.ones_b], writes=[R.psS_b])
    kb.op("act", lambda h: h.activation(out=R.rstd[:, 0:width], in_=R.psS[:, 0:width], func=AF.Ln, scale=1.0 / 128,
                                        bias=EPS), reads=[R.psS_b], writes=[R.rstd_b])
    kb.op("act", lambda h: h.activation(out=R.rstd[:, 0:width], in_=R.rstd[:, 0:width], func=AF.Exp, scale=-0.5),
          reads=[R.rstd_b], writes=[R.rstd_b])


def mem_prep(R, memT, w_mem_kv, slot_mem):
    kb = R.kb
    for kt in range(KT):
        kb.dma("sp", R.xblk[:, kt, 0:NMEM], memT[kt * 128:(kt + 1) * 128, :], writes=[R.xblk_b[kt]])
    rmsnorm_block(R, slot_mem, R.xblk, R.xblk_b, dst=R.memh, dst_b=R.memh_b, width=NMEM)
    for hd in range(4):
        j = next_slab(R)
        load_slab(kb, R.wA[j], R.wA_b[j], w_mem_kv, hd * 128, 128, KT)
        gemm16(R, R.psA[j], R.psA_b[j], R.wA[j], R.wA_b[j], R.memh, R.memh_b, width=NMEM)
        head_rstd(R, R.psA[j], R.psA_b[j], NMEM)
        kb.op("dve", lambda h, j=j, hd=hd: h.scalar_tensor_tensor(out=R.KmT[:, hd, :], in0=R.psA[j][:, 0:NMEM],
                                                                  scalar=R.gqk[:, 1:2], in1=R.rstd[:, 0:NMEM],
                                                                  op0=ALU.mult, op1=ALU.mult),
              reads=[R.psA_b[j], R.gqk_b, R.rstd_b], writes=[R.kv_b])
    for hd in range(4):
        j = next_slab(R)
        load_slab(kb, R.wA[j], R.wA_b[j], w_mem_kv, MEMW + hd * 128, 128, KT)
        for mt in range(2):
            for kt in range(KT):
                kb.op("pe", lambda h, kt=kt, mt=mt, j=j: h.matmul(R.psB[j][:, mt * 128:(mt + 1) * 128],
                                                                  R.memh[:, kt, mt * 128:(mt + 1) * 128],
                                                                  R.wA[j][:, kt, :], start=(kt == 0),
                                                                  stop=(kt == KT - 1)),
                      reads=[R.wA_b[j], R.memh_b[kt]], writes=[R.psB_b[j]])
        for mt in range(2):
            kb.op("act", lambda h, mt=mt, j=j, hd=hd: h.activation(out=R.Vm[:, mt, hd * 128:(hd + 1) * 128],
                                                                   in_=R.psB[j][:, mt * 128:(mt + 1) * 128],
                                                                   func=AF.Copy),
                  reads=[R.psB_b[j]], writes=[R.kv_b])


def _wb(outs):
    lst = outs.get("wb_list")
    if not lst:
        return []
    k = outs["wb_ctr"][0]
    outs["wb_ctr"][0] = k + 1
    return [lst[k % len(lst)]]


def mixin_block(R, kind, slot_mix, w_in, outs, tb):
    kb = R.kb
    tsl = slice(tb * TB, (tb + 1) * TB)
    rmsnorm_block(R, slot_mix, R.xblk, R.xblk_b)
    if kind == "sb":
        for nt in range(24):
            j = next_slab(R)
            load_slab(kb, R.wA[j], R.wA_b[j], w_in, nt * 128, 128, KT)
            gemm16(R, R.psA[j], R.psA_b[j], R.wA[j], R.wA_b[j], R.hT, R.hT_b)
            s = R.sctr % 2
            R.sctr += 1
            kb.op("act", lambda h, j=j, s=s, nt=nt: h.activation(out=R.stg[s][:], in_=R.psA[j][:], func=AF.Copy,
                                                                 scale=(ISQ if nt < 12 else 1.0)),
                  reads=[R.psA_b[j]], writes=[R.stg_b[s]])
            if "qk_dst" in outs:
                dap, dwb = outs["qk_dst"](nt, tb)
                kb.dma("sp", dap, R.stg[s][:], reads=[R.stg_b[s]], writes=dwb)
            else:
                kb.dma("sp", outs["qkT"][nt * 128:(nt + 1) * 128, tsl], R.stg[s][:], reads=[R.stg_b[s]])
        for nt in range(12):
            j = next_slab(R)
            load_slab(kb, R.wA[j], R.wA_b[j], w_in, 3072 + nt * 128, 128, KT)
            for t4 in range(4):
                for kt in range(KT):
                    kb.op("pe", lambda h, kt=kt, t4=t4, j=j: h.matmul(R.psB[j][:, t4 * 128:(t4 + 1) * 128],
                                                                      R.hT[:, kt, t4 * 128:(t4 + 1) * 128],
                                                                      R.wA[j][:, kt, :], start=(kt == 0),
                                                                      stop=(kt == KT - 1)),
                          reads=[R.wA_b[j], R.hT_b[kt]], writes=[R.psB_b[j]])
            s = R.sctr % 2
            R.sctr += 1
            kb.op("act", lambda h, j=j, s=s: h.activation(out=R.stg[s][:], in_=R.psB[j][:], func=AF.Copy),
                  reads=[R.psB_b[j]], writes=[R.stg_b[s]])
            if "v_dst" in outs:
                dap, dwb = outs["v_dst"](nt, tb)
                kb.dma("sp", dap, R.stg[s][:].rearrange("p (t c) -> p t c", t=4), reads=[R.stg_b[s]], writes=dwb)
            else:
                kb.dma("sp", outs["v"][tsl, nt * 128:(nt + 1) * 128].rearrange("(t p) c -> p t c", p=128),
                       R.stg[s][:].rearrange("p (t c) -> p t c", t=4), reads=[R.stg_b[s]])
        qm0 = 4608
    else:
        for nt in range(12):
            j = next_slab(R)
            load_slab(kb, R.wA[j], R.wA_b[j], w_in, nt * 128, 128, KT)
            gemm16(R, R.psA[j], R.psA_b[j], R.wA[j], R.wA_b[j], R.hT, R.hT_b)
            kb.op("act", lambda h, j=j: h.activation(out=R.sg[j][:], in_=R.psA[j][:], func=AF.Copy),
                  reads=[R.psA_b[j]], writes=[R.sg_b[j]])
            if "u_dst" in outs:
                dap, dwb = outs["u_dst"](nt, tb)
                kb.dma("sp", dap, R.sg[j][:], reads=[R.sg_b[j]], writes=dwb)
            else:
                kb.dma("sp", outs["uT"][nt * 128:(nt + 1) * 128, tsl], R.sg[j][:], reads=[R.sg_b[j]])
        qm0 = 1536
    for hd in range(4):
        j = next_slab(R)
        load_slab(kb, R.wA[j], R.wA_b[j], w_in, qm0 + hd * 128, 128, KT)
        gemm16(R, R.psA[j], R.psA_b[j], R.wA[j], R.wA_b[j], R.hT, R.hT_b)
        head_rstd(R, R.psA[j], R.psA_b[j], TB)
        kb.op("dve", lambda h, j=j: h.scalar_tensor_tensor(out=R.qn[:], in0=R.psA[j][:], scalar=R.gqk[:, 0:1],
                                                           in1=R.rstd[:], op0=ALU.mult, op1=ALU.mult),
              reads=[R.psA_b[j], R.gqk_b, R.rstd_b], writes=[R.qn_b])
        for mt in range(2):
            kb.op("pe", lambda h, mt=mt, hd=hd: h.matmul(R.psB[mt][:], R.KmT[:, hd, mt * 128:(mt + 1) * 128], R.qn[:],
                                                         start=True, stop=True),
                  reads=[R.kv_b, R.qn_b], writes=[R.psB_b[mt]])
            kb.op("act", lambda h, mt=mt: h.activation(out=R.pT[:, mt, :], in_=R.psB[mt][:], func=AF.Exp, scale=ISQ),
                  reads=[R.psB_b[mt]], writes=[R.pT_b[mt]])
        for mt in range(2):
            kb.op("pe", lambda h, mt=mt: h.matmul(R.psS[:], R.ones_bf[:], R.pT[:, mt, :], start=(mt == 0),
                                                  stop=(mt == 1)),
                  reads=[R.ones_b, R.pT_b[mt]], writes=[R.psS_b])
        o = hd % 2
        for mt in range(2):
            kb.op("pe", lambda h, mt=mt, hd=hd, o=o: h.matmul(R.psO[o][:], R.Vm[:, mt, hd * 128:(hd + 1) * 128],
                                                              R.pT[:, mt, :], start=(mt == 0), stop=(mt == 1)),
                  reads=[R.kv_b, R.pT_b[mt]], writes=[R.psO_b[o]])
        kb.op("dve", lambda h: h.reciprocal(out=R.rstd[:], in_=R.psS[:]), reads=[R.psS_b], writes=[R.rstd_b])
        s = R.sctr % 2
        R.sctr += 1
        kb.op("dve", lambda h, o=o, s=s: h.tensor_tensor(out=R.stg[s][:], in0=R.psO[o][:], in1=R.rstd[:],
                                                         op=ALU.mult),
              reads=[R.psO_b[o], R.rstd_b], writes=[R.stg_b[s]])
        kb.dma("sp", outs["crossT_out"][hd * 128:(hd + 1) * 128, tsl], R.stg[s][:], reads=[R.stg_b[s]],
               writes=(outs.get("cross_wb") or []))


def post_block(R, kind, tokT, crossT, w_out, w_glu, tb, tok_src=None, rd=()):
    kb = R.kb
    tsl = slice(tb * TB, (tb + 1) * TB)
    if tok_src is None:
        tok_src = lambda kt: tokT[kt * 128:(kt + 1) * 128, tsl]
    rd = list(rd)
    if kind == "sb":
        for kt in range(12):
            kb.dma("sp", R.hT[:, kt, :], tok_src(kt), reads=rd, writes=[R.hT_b[kt]])
    else:
        for kt in range(12):
            kb.dma("sp", R.actT[:, kt, :], tok_src(kt), reads=rd, writes=[R.actT_b[kt]])
        for nt in range(12):
            j = next_slab(R)
            load_slab(kb, R.wA[j], R.wA_b[j], w_glu, nt * 128, 128, 12)
            gemm16(R, R.psA[j], R.psA_b[j], R.wA[j], R.wA_b[j], R.actT, R.actT_b, nkt=12)
            kb.op("act", lambda h, j=j: h.activation(out=R.sg[j][:], in_=R.psA[j][:], func=AF.Sigmoid),
                  reads=[R.psA_b[j]], writes=[R.sg_b[j]])
            kb.op("dve", lambda h, j=j, nt=nt: h.tensor_tensor(out=R.hT[:, nt, :], in0=R.actT[:, nt, :],
                                                               in1=R.sg[j][:], op=ALU.mult),
                  reads=[R.actT_b[nt], R.sg_b[j]], writes=[R.hT_b[nt]])
    for kt in range(12, 16):
        kb.dma("sp", R.hT[:, kt, :], crossT[(kt - 12) * 128:(kt - 11) * 128, tsl], reads=rd,
               writes=[R.hT_b[kt]])
    for dt in range(KT):
        j = next_slab(R)
        load_slab(kb, R.wB[j], R.wB_b[j], w_out, dt * 128, 128, KT)
        gemm16(R, R.psO[j], R.psO_b[j], R.wB[j], R.wB_b[j], R.hT, R.hT_b)
        kb.op("dve", lambda h, dt=dt, j=j: h.tensor_tensor(out=R.xblk[:, dt, :], in0=R.psO[j][:],
                                                           in1=R.xblk[:, dt, :], op=ALU.add),
              reads=[R.psO_b[j], R.xblk_b[dt]], writes=[R.xblk_b[dt]])


def build_row(post, ffn2, ffn1, mix):
    nc = bass.Bass("TRN2", target_bir_lowering=False)

    def din(name, shape, dt=F32):
        return nc.dram_tensor(name, shape, dt, kind="ExternalInput").ap()

    def dout(name, shape, dt=F32):
        return nc.dram_tensor(name, shape, dt, kind="ExternalOutput").ap()

    xT = din("xT", [D, TPC])
    xT_out = dout("xT_out", [D, TPC])
    a = {}
    if post:
        a["tokT"] = din("tokT", [TOKW, TPC], BF16)
        a["crossT_in"] = din("crossT_in", [MEMW, TPC], BF16)
        a["w_out"] = din("w_out", [D, D])
        if post == "s5":
            a["w_glu"] = din("w_glu", [TOKW, TOKW])
    if ffn2:
        a["g_ffn2"] = din("g_ffn2", [D])
        a["w_gu2"] = din("w_gu2", [D, 2 * FF])
        a["w_down2"] = din("w_down2", [FF, D])
    if ffn1:
        a["g_ffn1"] = din("g_ffn1", [D])
        a["w_gu1"] = din("w_gu1", [D, 2 * FF])
        a["w_down1"] = din("w_down1", [FF, D])
    outs = {}
    if mix:
        a["g_mix"] = din("g_mix", [D])
        a["w_in"] = din("w_in", [D, 5120 if mix == "sb" else 2048])
        a["memT"] = din("memT", [D, NMEM])
        a["g_mem"] = din("g_mem", [D])
        a["w_mem_kv"] = din("w_mem_kv", [D, 2 * MEMW])
        a["gqk"] = din("gqk", [128, 2])
        outs["crossT_out"] = dout("crossT_out", [MEMW, TPC], BF16)
        if mix == "sb":
            outs["qkT"] = dout("qkT", [3072, TPC], BF16)
            outs["v"] = dout("v", [TPC, TOKW], BF16)
        else:
            outs["uT"] = dout("uT", [TOKW, TPC], F32)
    kb = KB(nc)
    R = RowRes(kb)
    if ffn2:
        R.load_gains(0, a["g_ffn2"])
    if ffn1:
        R.load_gains(1, a["g_ffn1"])
    if mix:
        R.load_gains(2, a["g_mix"])
        R.load_gains(3, a["g_mem"])
        kb.dma("sp", R.gqk[:], a["gqk"], writes=[R.gqk_b])
        mem_prep(R, a["memT"], a["w_mem_kv"], 3)
    for tb in range(NTB):
        load_xblk(R, xT, tb)
        if post:
            post_block(R, post, a["tokT"], a["crossT_in"], a["w_out"], a.get("w_glu"), tb)
        if ffn2:
            ffn_block(R, 0, a["w_gu2"], a["w_down2"])
        if ffn1:
            ffn_block(R, 1, a["w_gu1"], a["w_down1"])
        if mix:
            mixin_block(R, mix, 2, a["w_in"], outs, tb)
        store_xblk(R, xT_out, tb)
    kb.finish()
    return nc


_PROGS = {}


def _prog(key, fn):
    if key not in _PROGS:
        _PROGS[key] = fn()
    return _PROGS[key]


def _run(nc, in_maps):
    res = run_bass_kernel_spmd(nc, in_maps, core_ids=list(range(NCORES)))
    return res.results


def kernel_unfused(x, mem, ffn1_norm, ffn1_w_gu, ffn1_w_down, mix_norm, mem_norm, w_mem_kv, xq_norm, xk_norm, w_out,
           ffn2_norm, ffn2_w_gu, ffn2_w_down, sb_w_in, s5_w_in, s5_log_dt, s5_a_re, s5_a_im, s5_b_re, s5_b_im,
           s5_c_re, s5_c_im, s5_d, s5_w_glu, _debug=None):
    f32 = np.float32
    A = lambda t: np.ascontiguousarray(np.asarray(t, dtype=f32))
    x = A(x)
    mem = A(mem)
    xT = [np.ascontiguousarray(x[c // 4, (c % 4) * TPC:(c % 4 + 1) * TPC, :].T) for c in range(NCORES)]
    memT = [np.ascontiguousarray(mem[b].T) for b in range(NB)]
    tokT = None
    crossT = None
    for stage in range(DEPTH + 1):
        li = stage
        lp = stage - 1
        post = None if lp < 0 else ("sb" if lp % 2 == 0 else "s5")
        mix = None if li >= DEPTH else ("sb" if li % 2 == 0 else "s5")
        ffn2 = lp >= 0
        ffn1 = li < DEPTH
        nc = _prog(("row", post, ffn2, ffn1, mix), lambda: build_row(post, ffn2, ffn1, mix))
        maps = []
        for c in range(NCORES):
            m = {"xT": xT[c]}
            if post:
                m["tokT"] = tokT[c]
                m["crossT_in"] = crossT[c]
                m["w_out"] = A(w_out[lp])
                if post == "s5":
                    m["w_glu"] = A(s5_w_glu[lp // 2])
            if ffn2:
                m["g_ffn2"] = A(ffn2_norm[lp])
                m["w_gu2"] = A(ffn2_w_gu[lp])
                m["w_down2"] = A(ffn2_w_down[lp])
            if ffn1:
                m["g_ffn1"] = A(ffn1_norm[li])
                m["w_gu1"] = A(ffn1_w_gu[li])
                m["w_down1"] = A(ffn1_w_down[li])
            if mix:
                m["g_mix"] = A(mix_norm[li])
                m["w_in"] = A(sb_w_in[li // 2]) if mix == "sb" else A(s5_w_in[li // 2])
                m["memT"] = memT[c // 4]
                m["g_mem"] = A(mem_norm[li])
                m["w_mem_kv"] = A(w_mem_kv[li])
                m["gqk"] = np.ascontiguousarray(np.stack([A(xq_norm[li]), A(xk_norm[li])], axis=1))
            maps.append(m)
        res = _run(nc, maps)
        xT = [res[c]["xT_out"] for c in range(NCORES)]
        if _debug is not None:
            _debug["x_stage%d" % stage] = [a.copy() for a in xT]
        if not mix:
            break
        crossT = [res[c]["crossT_out"] for c in range(NCORES)]
        if mix == "sb":
            maps = []
            for c in range(NCORES):
                qs, ks, vs = [], [], []
                for i in range(NPAIR):
                    p = c * NPAIR + i
                    b, hd = p // 12, p % 12
                    qs.append(np.concatenate([res[b * 4 + qd]["qkT"][hd * 128:(hd + 1) * 128] for qd in range(4)], axis=1))
                    ks.append(np.concatenate([res[b * 4 + qd]["qkT"][(12 + hd) * 128:(13 + hd) * 128] for qd in range(4)], axis=1))
                    vs.append(np.concatenate([res[b * 4 + qd]["v"][:, hd * 128:(hd + 1) * 128] for qd in range(4)], axis=0))
                maps.append({"qT": np.ascontiguousarray(np.stack(qs)), "kT": np.ascontiguousarray(np.stack(ks)),
                             "v": np.ascontiguousarray(np.stack(vs))})
            nc2 = _prog(("sb",), lambda: build_sb_attn())
            r2 = _run(nc2, maps)
            tokT = []
            for c in range(NCORES):
                b, qd = c // 4, c % 4
                rows = []
                for hd in range(12):
                    p = b * 12 + hd
                    rows.append(r2[p // NPAIR]["oT"][p % NPAIR][:, qd * TPC:(qd + 1) * TPC])
                tokT.append(np.ascontiguousarray(np.concatenate(rows, axis=0)))
        else:
            jj = li // 2
            maps = []
            for c in range(NCORES):
                b, gs = c // 4, c % 4
                m = s5_host_params(A(s5_log_dt[jj]), A(s5_a_re[jj]), A(s5_a_im[jj]), A(s5_b_re[jj]), A(s5_b_im[jj]),
                                   A(s5_c_re[jj]), A(s5_c_im[jj]), A(s5_d[jj]), gs)
                m["uT"] = np.ascontiguousarray(
                    np.concatenate([res[b * 4 + qd]["uT"][gs * 384:(gs + 1) * 384] for qd in range(4)], axis=1))
                maps.append(m)
            nc2 = _prog(("s5",), lambda: build_s5())
            r2 = _run(nc2, maps)
            tokT = []
            for c in range(NCORES):
                b, qd = c // 4, c % 4
                tokT.append(np.ascontiguousarray(
                    np.concatenate([r2[b * 4 + gs]["ygT"][:, qd * TPC:(qd + 1) * TPC] for gs in range(4)], axis=0)))
        if _debug is not None:
            _debug["tok_stage%d" % stage] = [a.copy() for a in tokT]
            _debug["cross_stage%d" % stage] = [a.copy() for a in crossT]
    out = np.empty((NB, L, D), f32)
    for c in range(NCORES):
        out[c // 4, (c % 4) * TPC:(c % 4 + 1) * TPC, :] = xT[c].T
    return out


GROUPS = [[0, 1, 2, 3], [4, 5, 6, 7]]
NWB = 8


class Exchange:
    def __init__(self, kb, name, nchunks, rows, cols, dt):
        nc = kb.nc
        self.kb, self.n, self.rows, self.cols = kb, nchunks, rows, cols
        self.snd = nc.dram_tensor(name + "_snd", [nchunks * 4 * rows, cols], dt)
        self.rcv = nc.dram_tensor(name + "_rcv", [nchunks * 16 * rows, cols], dt)
        self.loc = nc.dram_tensor(name + "_loc", [nchunks * 4 * rows, cols], dt)
        self.snd_b = [Buf() for _ in range(nchunks)]
        self.rcv_b = [Buf() for _ in range(nchunks)]
        self.loc_b = Buf()

    def snd_rows(self, c, j, r0=0, n=None):
        n = self.rows if n is None else n
        base = (c * 4 + j) * self.rows + r0
        return self.snd.ap()[base:base + n, :]

    def loc_rows(self, c, r, r0=0, n=None):
        n = self.rows if n is None else n
        base = (c * 4 + r) * self.rows + r0
        return self.loc.ap()[base:base + n, :]

    def run(self, jv):
        kb = self.kb
        kb.coll_chunks(self, GROUPS)
        kb.dma("sp", self.loc.ap().rearrange("(cr x) t -> cr x t", x=self.rows),
               self.rcv.ap().rearrange("(cr j x) t -> cr j x t", j=4, x=self.rows)[:, jv],
               reads=self.rcv_b, writes=[self.loc_b])


def build_fused():
    nc = bass.Bass("TRN2", target_bir_lowering=False)

    def din(name, shape, dt=F32):
        return nc.dram_tensor(name, shape, dt, kind="ExternalInput").ap()

    xT = din("xT", [D, TPC])
    memT = din("memT", [D, NMEM])
    W = {}
    for name, shape in (("ffn1_norm", [DEPTH, D]), ("ffn1_w_gu", [DEPTH, D, 2 * FF]), ("ffn1_w_down", [DEPTH, FF, D]),
                        ("mix_norm", [DEPTH, D]), ("mem_norm", [DEPTH, D]), ("w_mem_kv", [DEPTH, D, 2 * MEMW]),
                        ("gqk", [DEPTH, 128, 2]), ("w_out", [DEPTH, D, D]), ("ffn2_norm", [DEPTH, D]),
                        ("ffn2_w_gu", [DEPTH, D, 2 * FF]), ("ffn2_w_down", [DEPTH, FF, D]),
                        ("sb_w_in", [2, D, 5120]), ("s5_w_in", [2, D, 2048]), ("s5_w_glu", [2, TOKW, TOKW]),
                        ("ldt", [2, 128, NST]), ("are", [2, 128, NST]), ("aim", [2, 128, NST]),
                        ("Bre", [2, 128, NST, 128]), ("Bim", [2, 128, NST, 128]), ("Cre", [2, 128, NST, 128]),
                        ("Cim", [2, 128, NST, 128]), ("dcol", [2, 128, 3])):
        W[name] = din(name, shape)
    yT = nc.dram_tensor("yT", [D, TPC], F32, kind="ExternalOutput").ap()

    xs = nc.dram_tensor("xs", [D, TPC], F32).ap()
    xs_b = [[Buf() for _ in range(KT)] for _ in range(NTB)]
    cross = [nc.dram_tensor("cross%d" % i, [MEMW, TPC], BF16).ap() for i in range(DEPTH)]
    cross_b = [[Buf() for _ in range(NTB)] for _ in range(DEPTH)]

    kb = KB(nc)
    pid = nc.sync.partition_id()
    jv = pid % 4
    ex_tok = None
    kb.wc = {"on": False, "first": True, "idxA": 0, "idxD": 0, "bufsA": [], "bufsD": [],
             "tA": nc.dram_tensor("wcacheA", [256, 128, KT * 128], BF16),
             "tD": nc.dram_tensor("wcacheD", [32, 128, FT * 128], BF16)}

    for li in range(DEPTH + 1):
        lp = li - 1
        post = None if lp < 0 else ("sb" if lp % 2 == 0 else "s5")
        mix = None if li >= DEPTH else ("sb" if li % 2 == 0 else "s5")
        kb.push()
        R = RowRes(kb)
        outs = {}
        if lp >= 0:
            R.load_gains(0, W["ffn2_norm"][lp])
        if li < DEPTH:
            R.load_gains(1, W["ffn1_norm"][li])
        if mix:
            R.load_gains(2, W["mix_norm"][li])
            R.load_gains(3, W["mem_norm"][li])
            kb.dma("sp", R.gqk[:], W["gqk"][li], writes=[R.gqk_b])
            mem_prep(R, memT, W["w_mem_kv"][li], 3)
            outs["crossT_out"] = cross[li]
            if mix == "sb":
                ex_qk = Exchange(kb, "qk%d" % li, 12, 128, 1024, BF16)
                ex_v = Exchange(kb, "v%d" % li, 6, 1024, 128, BF16)

                def qk_dst(nt, tb, ex=ex_qk):
                    a_, h = nt // 12, nt % 12
                    j, i = h // 3, h % 3
                    c = (a_ * 3 + i) * 2 + tb // 2
                    return ex.snd_rows(c, j)[:, (tb % 2) * TB:(tb % 2 + 1) * TB], [ex.snd_b[c]]

                def v_dst(nt, tb, ex=ex_v):
                    j, i = nt // 3, nt % 3
                    c = i * 2 + tb // 2
                    ap = ex.snd_rows(c, j, (tb % 2) * TB, TB).rearrange("(t p) c -> p t c", p=128)
                    return ap, [ex.snd_b[c]]

                outs["qk_dst"] = qk_dst
                outs["v_dst"] = v_dst
            else:
                ex_u = Exchange(kb, "u%d" % li, 12, 128, 512, F32)

                def u_dst(nt, tb, ex=ex_u):
                    j, ct = nt // 3, nt % 3
                    c = ct * 4 + tb
                    return ex.snd_rows(c, j), [ex.snd_b[c]]

                outs["u_dst"] = u_dst
        for tb in range(NTB):
            kb.wc["on"] = True
            kb.wc["first"] = (tb == 0)
            kb.wc["idxA"] = 0
            kb.wc["idxD"] = 0
            if li == 0:
                load_xblk(R, xT, tb)
            else:
                for kt in range(KT):
                    kb.dma("sp", R.xblk[:, kt, :], xs[kt * 128:(kt + 1) * 128, tb * TB:(tb + 1) * TB],
                           reads=[xs_b[tb][kt]], writes=[R.xblk_b[kt]])
            if post:
                def tok_src(kt, tb=tb, ex=ex_tok):
                    r, i = kt // 3, kt % 3
                    c = i * 2 + tb // 2
                    return ex.loc_rows(c, r)[:, (tb % 2) * TB:(tb % 2 + 1) * TB]

                post_block(R, post, None, cross[lp], W["w_out"][lp],
                           W["s5_w_glu"][lp // 2] if post == "s5" else None, tb, tok_src=tok_src,
                           rd=[ex_tok.loc_b, cross_b[lp][tb]])
                ffn_block(R, 0, W["ffn2_w_gu"][lp], W["ffn2_w_down"][lp])
            if li < DEPTH:
                ffn_block(R, 1, W["ffn1_w_gu"][li], W["ffn1_w_down"][li])
            if mix:
                outs["cross_wb"] = [cross_b[li][tb]]
                mixin_block(R, mix, 2, W["sb_w_in"][li // 2] if mix == "sb" else W["s5_w_in"][li // 2], outs, tb)
            if li == DEPTH:
                store_xblk(R, yT, tb)
            else:
                store_xblk(R, xs, tb, dbufs=xs_b[tb])
        kb.wc["on"] = False
        kb.pop()
        if not mix:
            break
        if mix == "sb":
            ex_qk.run(jv)
            ex_v.run(jv)
            ex_o = Exchange(kb, "o%d" % li, 6, 128, 1024, BF16)

            def load_fn(pi, q_s, k_s, v_s, b, ex_qk=ex_qk, ex_v=ex_v):
                for r in range(4):
                    for th in range(2):
                        c0 = r * TPC + th * 1024
                        kb.dma("sp", q_s[:, c0:c0 + 1024], ex_qk.loc_rows(pi * 2 + th, r), reads=[ex_qk.loc_b],
                               writes=[b])
                        kb.dma("sp", k_s[:, c0:c0 + 1024], ex_qk.loc_rows((3 + pi) * 2 + th, r), reads=[ex_qk.loc_b],
                               writes=[b])
                        kb.dma("sp", v_s[:, r * 16 + th * 8:r * 16 + th * 8 + 8, :],
                               ex_v.loc_rows(pi * 2 + th, r).rearrange("(kb p) d -> p kb d", p=128),
                               reads=[ex_v.loc_b], writes=[b])

            def store_fn(pi, qb, o_s, o_b, ex=ex_o):
                qd, tq = qb // 4, qb % 4
                c = pi * 2 + tq // 2
                kb.dma("sp", ex.snd_rows(c, qd)[:, (tq % 2) * QB:(tq % 2 + 1) * QB], o_s[:], reads=[o_b],
                       writes=[ex.snd_b[c]])

            kb.push()
            sb_attn_phase(kb, None, None, None, None, NPAIR, NQB, load_fn=load_fn, store_fn=store_fn)
            kb.pop()
            ex_o.run(jv)
            ex_tok = ex_o
        else:
            ex_u.run(jv)
            ex_y = Exchange(kb, "y%d" % li, 6, 128, 1024, BF16)

            def uload_fn(ch, ct, u_s, u_b, ex=ex_u):
                qd, tq = ch // 4, ch % 4
                kb.dma("sp", u_s[:], ex.loc_rows(ct * 4 + tq, qd), reads=[ex.loc_b], writes=[u_b])

            def ystore_fn(ch, ct, yo, yo_b, ex=ex_y):
                qd, tq = ch // 4, ch % 4
                c = ct * 2 + tq // 2
                kb.dma("sp", ex.snd_rows(c, qd)[:, (tq % 2) * SCH:(tq % 2 + 1) * SCH], yo[:], reads=[yo_b],
                       writes=[ex.snd_b[c]])

            jj = li // 2
            dr = {k: W[k][jj] for k in ("ldt", "are", "aim", "Bre", "Bim", "Cre", "Cim", "dcol")}
            kb.push()
            s5_phase(kb, dr, None, L // SCH, uload_fn=uload_fn, ystore_fn=ystore_fn)
            kb.pop()
            ex_y.run(jv)
            ex_tok = ex_y
    kb.finish()
    return nc


def kernel(x, mem, ffn1_norm, ffn1_w_gu, ffn1_w_down, mix_norm, mem_norm, w_mem_kv, xq_norm, xk_norm, w_out,
           ffn2_norm, ffn2_w_gu, ffn2_w_down, sb_w_in, s5_w_in, s5_log_dt, s5_a_re, s5_a_im, s5_b_re, s5_b_im,
           s5_c_re, s5_c_im, s5_d, s5_w_glu):
    f32 = np.float32
    A = lambda t: np.ascontiguousarray(np.asarray(t, dtype=f32))
    x = A(x)
    mem = A(mem)
    nc = _prog(("fused",), build_fused)
    shared = {
        "ffn1_norm": A(ffn1_norm), "ffn1_w_gu": A(ffn1_w_gu), "ffn1_w_down": A(ffn1_w_down),
        "mix_norm": A(mix_norm), "mem_norm": A(mem_norm), "w_mem_kv": A(w_mem_kv),
        "gqk": np.ascontiguousarray(np.stack([A(xq_norm), A(xk_norm)], axis=2)),
        "w_out": A(w_out), "ffn2_norm": A(ffn2_norm), "ffn2_w_gu": A(ffn2_w_gu), "ffn2_w_down": A(ffn2_w_down),
        "sb_w_in": A(sb_w_in), "s5_w_in": A(s5_w_in), "s5_w_glu": A(s5_w_glu),
    }
    s5p = []
    for gs in range(4):
        per = [s5_host_params(A(s5_log_dt[jj]), A(s5_a_re[jj]), A(s5_a_im[jj]), A(s5_b_re[jj]), A(s5_b_im[jj]),
                              A(s5_c_re[jj]), A(s5_c_im[jj]), A(s5_d[jj]), gs) for jj in range(2)]
        s5p.append({k: np.ascontiguousarray(np.stack([per[0][k], per[1][k]])) for k in per[0]})
    maps = []
    for c in range(NCORES):
        m = dict(shared)
        m.update(s5p[c % 4])
        m["xT"] = np.ascontiguousarray(x[c // 4, (c % 4) * TPC:(c % 4 + 1) * TPC, :].T)
        m["memT"] = np.ascontiguousarray(mem[c // 4].T)
        maps.append(m)
    res = _run(nc, maps)
    out = np.empty((NB, L, D), f32)
    for c in range(NCORES):
        out[c // 4, (c % 4) * TPC:(c % 4 + 1) * TPC, :] = res[c]["yT"].T
    return out
```

```python
import contextlib
import math
import numpy as np
import concourse.bass as bass
import concourse.mybir as mybir
from concourse.bass_utils import run_bass_kernel_spmd

F32 = mybir.dt.float32
BF16 = mybir.dt.bfloat16
I32 = mybir.dt.int32
AF = mybir.ActivationFunctionType
ALU = mybir.AluOpType

D = 2048
L = 8192
NB = 2
DEPTH = 4
FF = 5632
TOKW = 1536
MEMW = 512
NMEM = 256
NCORES = 8
TPC = 2048
TB = 512
NTB = TPC // TB
KT = D // 128
FT = FF // 128
EPS = 1e-6


class Buf:
    __slots__ = ("w", "r")

    def __init__(self):
        self.w = None
        self.r = {}


class Eng:
    def __init__(self, name, h, sem, sid):
        self.name = name
        self.h = h
        self.sem = sem
        self.sid = sid
        self.cnt = 0
        self.waited = {}


class KB:
    NDMA = 40

    def __init__(self, nc):
        self.nc = nc
        self.es = contextlib.ExitStack()
        self.sems = {}
        self.engs = {}
        for name, h in (("pe", nc.tensor), ("act", nc.scalar), ("dve", nc.vector),
                        ("pool", nc.gpsimd), ("sp", nc.sync)):
            sem = self.es.enter_context(nc.semaphore("s_" + name))
            self.sems[name] = sem
            self.engs[name] = Eng(name, h, sem, name)
        self.dsem = []
        self.dcnt = []
        for i in range(self.NDMA):
            sem = self.es.enter_context(nc.semaphore("d%d" % i))
            self.sems[("d", i)] = sem
            self.dsem.append(sem)
            self.dcnt.append(0)
        self.dnext = 0
        self.nalloc = 0
        self.scopes = [self.es]
        self.csem = []
        self.ccnt = []
        self.uid = 0

    def push(self):
        sc = contextlib.ExitStack()
        self.scopes.append(sc)

    def pop(self):
        self.barrier()
        self.scopes.pop().close()

    def barrier(self):
        for E in self.engs.values():
            for name, O in self.engs.items():
                if O is not E:
                    self._wait(E, name, O.cnt)
            for i in range(self.NDMA):
                self._wait(E, ("d", i), self.dcnt[i])
            for ci in range(len(self.csem)):
                self._wait(E, ("c", ci), self.ccnt[ci])

    def coll_chunks(self, ex, groups):
        E = self.engs["pool"]
        ci = len(self.csem)
        sem = self.es.enter_context(self.nc.semaphore("cc%d" % ci))
        self.csem.append(sem)
        self.ccnt.append(0)
        self.sems[("c", ci)] = sem
        R4 = 4 * ex.rows
        for c in range(ex.n):
            self._deps(E, [ex.snd_b[c]], [ex.rcv_b[c]], is_dma=True)
            ins = E.h.collective_compute("AllGather", ALU.bypass, replica_groups=groups,
                                         ins=[ex.snd.ap()[c * R4:(c + 1) * R4, :]],
                                         outs=[ex.rcv.ap()[c * 4 * R4:(c + 1) * 4 * R4, :]])
            ins.then_inc(sem)
            self.ccnt[ci] += 1
            self._commit((("c", ci), self.ccnt[ci]), [ex.snd_b[c]], [ex.rcv_b[c]])

    def coll_allgather(self, snd_t, rcv_t, groups, reads=(), writes=()):
        E = self.engs["pool"]
        ci = len(self.csem)
        sem = self.es.enter_context(self.nc.semaphore("cc%d" % ci))
        self.csem.append(sem)
        self.sems[("c", ci)] = sem
        self._deps(E, reads, writes, is_dma=True)
        ins = E.h.collective_compute("AllGather", ALU.bypass, replica_groups=groups,
                                     ins=[snd_t.ap().opt()], outs=[rcv_t.ap().opt()])
        ins.then_inc(sem)
        self.ccnt.append(1)
        self._commit((("c", ci), 1), reads, writes)

    def sb(self, name, shape, dt):
        self.uid += 1
        t = self.scopes[-1].enter_context(self.nc.sbuf_tensor("sb%d_%s" % (self.uid, name), list(shape), dt))
        return t

    def ps(self, name, shape=(128, 512), dt=F32):
        self.uid += 1
        return self.scopes[-1].enter_context(self.nc.psum_tensor("ps%d_%s" % (self.uid, name), list(shape), dt))

    def _wait(self, E, sid, val):
        if E.waited.get(sid, 0) >= val:
            return
        E.h.wait_ge(self.sems[sid], val)
        E.waited[sid] = val

    def _deps(self, E, reads, writes, is_dma=False):
        deps = {}

        def add(tok, kind):
            if tok is None:
                return
            sid, val = tok
            if sid == E.sid and not is_dma:
                if E.name == "pe":
                    return
                if kind != "raw":
                    return
            if deps.get(sid, 0) < val:
                deps[sid] = val

        for b in reads:
            add(b.w, "raw")
        for b in writes:
            add(b.w, "waw")
            for sid, val in b.r.items():
                add((sid, val), "war")
        for sid, val in deps.items():
            self._wait(E, sid, val)

    def _commit(self, tok, reads, writes):
        sid, val = tok
        for b in reads:
            if b.r.get(sid, 0) < val:
                b.r[sid] = val
        for b in writes:
            b.w = tok
            b.r = {}

    def op(self, eng, fn, reads=(), writes=()):
        E = self.engs[eng]
        self._deps(E, reads, writes)
        ins = fn(E.h)
        E.cnt += 1
        ins.then_inc(E.sem, 1)
        self._commit((E.sid, E.cnt), reads, writes)

    def dma(self, eng, out, in_, reads=(), writes=(), **kw):
        E = self.engs[eng]
        i = self.dnext
        self.dnext = (self.dnext + 1) % self.NDMA
        self._deps(E, reads, writes, is_dma=True)
        self._wait(E, ("d", i), self.dcnt[i])
        ins = E.h.dma_start(out=out, in_=in_, **kw)
        self.dcnt[i] += 16
        ins.then_inc(self.dsem[i], 16)
        self._commit((("d", i), self.dcnt[i]), reads, writes)

    def finish(self):
        E = self.engs["sp"]
        for i in range(self.NDMA):
            self._wait(E, ("d", i), self.dcnt[i])
        for name in ("pe", "act", "dve", "pool"):
            self._wait(E, name, self.engs[name].cnt)
        for ci in range(len(self.csem)):
            self._wait(E, ("c", ci), self.ccnt[ci])
        while len(self.scopes) > 1:
            self.scopes.pop().close()
        self.es.close()


class RowRes:
    def __init__(self, kb):
        self.kb = kb
        nc = kb.nc
        self.xblk = kb.sb("xblk", (128, KT, TB), F32)
        self.xblk_b = [Buf() for _ in range(KT)]
        self.hT = kb.sb("hT", (128, KT, TB), BF16)
        self.hT_b = [Buf() for _ in range(KT)]
        self.actT = kb.sb("actT", (128, FT, TB), BF16)
        self.actT_b = [Buf() for _ in range(FT)]
        self.NSLAB = 6
        self.slab = [kb.sb("slab%d" % i, (128, KT, 128), BF16) for i in range(self.NSLAB)]
        self.slab_b = [Buf() for _ in range(self.NSLAB)]
        self.wA, self.wA_b = self.slab, self.slab_b
        self.wB, self.wB_b = self.slab, self.slab_b
        self.pctr = 0
        self.sq = [kb.sb("sq%d" % i, (128, TB), F32) for i in range(2)]
        self.sq_b = [Buf() for _ in range(2)]
        self.rstd = kb.sb("rstd", (128, TB), F32)
        self.rstd_b = Buf()
        self.sg = [kb.sb("sg%d" % i, (128, TB), F32) for i in range(2)]
        self.sg_b = [Buf() for _ in range(2)]
        self.gcol = kb.sb("gcol", (128, 8, KT), F32)
        self.gcol_b = Buf()
        self.ones = kb.sb("ones", (128, 128), F32)
        self.ones_b = Buf()
        self.psA = [kb.ps("psA%d" % i) for i in range(2)]
        self.psA_b = [Buf() for _ in range(2)]
        self.psB = [kb.ps("psB%d" % i) for i in range(2)]
        self.psB_b = [Buf() for _ in range(2)]
        self.psO = [kb.ps("psO%d" % i) for i in range(2)]
        self.psO_b = [Buf() for _ in range(2)]
        self.psS = kb.ps("psS")
        self.psS_b = Buf()
        self.epsc = kb.sb("epsc", (128, 1), F32)
        self.ones_bf = kb.sb("ones_bf", (128, 128), BF16)
        kb.op("dve", lambda h: h.memset(self.ones[:], 1.0), writes=[self.ones_b])
        kb.op("dve", lambda h: h.memset(self.epsc[:], EPS), writes=[self.ones_b])
        kb.op("dve", lambda h: h.memset(self.ones_bf[:], 1.0), writes=[self.ones_b])
        self.memh = kb.sb("memh", (128, KT, NMEM), BF16)
        self.memh_b = [Buf() for _ in range(KT)]
        self.KmT = kb.sb("KmT", (128, 4, NMEM), BF16)
        self.Vm = kb.sb("Vm", (128, 2, MEMW), BF16)
        self.kv_b = Buf()
        self.qn = kb.sb("qn", (128, TB), BF16)
        self.qn_b = Buf()
        self.pT = kb.sb("pT", (128, 2, TB), BF16)
        self.pT_b = [Buf() for _ in range(2)]
        self.stg = [kb.sb("stg%d" % i, (128, TB), BF16) for i in range(2)]
        self.stg_b = [Buf() for _ in range(2)]
        self.gqk = kb.sb("gqk", (128, 2), F32)
        self.gqk_b = Buf()
        self.ctr = 0
        self.sctr = 0

    def load_gains(self, slot, g_dram):
        kb = self.kb
        with kb.nc.allow_non_contiguous_dma(reason="tiny gain vector"):
            kb.dma("sp", self.gcol[:, slot, :], g_dram.rearrange("(kt p) -> p kt", p=128),
                   writes=[self.gcol_b])


def rmsnorm_block(R, slot, src, src_b, nkt=KT, dim=D, dst=None, dst_b=None, width=TB):
    kb = R.kb
    if dst is None:
        dst, dst_b = R.hT, R.hT_b
    W = width
    for kt in range(nkt):
        j = kt % 2
        kb.op("act", lambda h, kt=kt, j=j: h.activation(out=R.sq[j][:, 0:W], in_=src[:, kt, 0:W], func=AF.Square),
              reads=[src_b[kt]], writes=[R.sq_b[j]])
        kb.op("pe", lambda h, kt=kt, j=j: h.matmul(R.psS[:, 0:W], R.ones[:], R.sq[j][:, 0:W], start=(kt == 0),
                                                    stop=(kt == nkt - 1)),
              reads=[R.sq_b[j], R.ones_b], writes=[R.psS_b])
    kb.op("act", lambda h: h.activation(out=R.rstd[:, 0:W], in_=R.psS[:, 0:W], func=AF.Ln, scale=1.0 / dim,
                                        bias=EPS),
          reads=[R.psS_b], writes=[R.rstd_b])
    kb.op("act", lambda h: h.activation(out=R.rstd[:, 0:W], in_=R.rstd[:, 0:W], func=AF.Exp, scale=-0.5),
          reads=[R.rstd_b], writes=[R.rstd_b])
    for kt in range(nkt):
        kb.op("dve", lambda h, kt=kt: h.scalar_tensor_tensor(out=dst[:, kt, 0:W], in0=src[:, kt, 0:W],
                                                             scalar=R.gcol[:, slot, kt:kt + 1], in1=R.rstd[:, 0:W],
                                                             op0=ALU.mult, op1=ALU.mult),
              reads=[src_b[kt], R.gcol_b, R.rstd_b], writes=[dst_b[kt]])


def load_slab(kb, dst, dst_b, w_dram, c0, ncols, nkt):
    wc = getattr(kb, "wc", None)
    if wc is None or not wc["on"]:
        src = w_dram[:, c0:c0 + ncols].rearrange("(kt p) n -> p kt n", p=128)
        kb.dma("pool", dst[:, 0:nkt, 0:ncols], src, writes=[dst_b])
        return
    grp = "D" if nkt > KT else "A"
    idx = wc["idx" + grp]
    wc["idx" + grp] += 1
    scr = wc["t" + grp].ap()[idx, :, 0:nkt * 128].rearrange("p (kt n) -> p kt n", n=128)
    bl = wc["bufs" + grp]
    while len(bl) <= idx:
        bl.append(Buf())
    sb_ = bl[idx]
    if wc["first"]:
        src = w_dram[:, c0:c0 + ncols].rearrange("(kt p) n -> p kt n", p=128)
        kb.dma("pool", dst[:, 0:nkt, 0:ncols], src, writes=[dst_b])
        kb.dma("sp", scr, dst[:, 0:nkt, 0:ncols], reads=[dst_b], writes=[sb_])
    else:
        kb.dma("sp", dst[:, 0:nkt, 0:ncols], scr, reads=[sb_], writes=[dst_b])


def ffn_block(R, slot, w_gu, w_down):
    kb = R.kb
    rmsnorm_block(R, slot, R.xblk, R.xblk_b)
    for ft in range(FT):
        ja = next_slab(R)
        jb = next_slab(R)
        p = next_ps(R)
        load_slab(kb, R.slab[ja], R.slab_b[ja], w_gu, ft * 128, 128, KT)
        load_slab(kb, R.slab[jb], R.slab_b[jb], w_gu, FF + ft * 128, 128, KT)
        for kt in range(KT):
            kb.op("pe", lambda h, kt=kt, ja=ja, p=p: h.matmul(R.psA[p][:], R.slab[ja][:, kt, :], R.hT[:, kt, :],
                                                              start=(kt == 0), stop=(kt == KT - 1)),
                  reads=[R.slab_b[ja], R.hT_b[kt]], writes=[R.psA_b[p]])
        for kt in range(KT):
            kb.op("pe", lambda h, kt=kt, jb=jb, p=p: h.matmul(R.psB[p][:], R.slab[jb][:, kt, :], R.hT[:, kt, :],
                                                              start=(kt == 0), stop=(kt == KT - 1)),
                  reads=[R.slab_b[jb], R.hT_b[kt]], writes=[R.psB_b[p]])
        kb.op("act", lambda h, p=p: h.activation(out=R.sg[p][:], in_=R.psA[p][:], func=AF.Silu),
              reads=[R.psA_b[p]], writes=[R.sg_b[p]])
        kb.op("dve", lambda h, p=p, ft=ft: h.tensor_tensor(out=R.actT[:, ft, :], in0=R.psB[p][:], in1=R.sg[p][:],
                                                           op=ALU.mult),
              reads=[R.psB_b[p], R.sg_b[p]], writes=[R.actT_b[ft]])
    for dt in range(KT):
        p = next_ps(R)
        f0 = 0
        while f0 < FT:
            nf = min(KT, FT - f0)
            j = next_slab(R)
            load_slab(kb, R.slab[j], R.slab_b[j], w_down[f0 * 128:(f0 + nf) * 128, :], dt * 128, 128, nf)
            for f in range(nf):
                ft = f0 + f
                kb.op("pe", lambda h, f=f, ft=ft, j=j, p=p: h.matmul(R.psO[p][:], R.slab[j][:, f, :],
                                                                     R.actT[:, ft, :], start=(ft == 0),
                                                                     stop=(ft == FT - 1)),
                      reads=[R.slab_b[j], R.actT_b[ft]], writes=[R.psO_b[p]])
            f0 += nf
        kb.op("dve", lambda h, dt=dt, p=p: h.scalar_tensor_tensor(out=R.xblk[:, dt, :], in0=R.psO[p][:], scalar=0.5,
                                                                  in1=R.xblk[:, dt, :], op0=ALU.mult, op1=ALU.add),
              reads=[R.psO_b[p], R.xblk_b[dt]], writes=[R.xblk_b[dt]])


def load_xblk(R, xT, tb, dbuf=None):
    kb = R.kb
    rd = [dbuf] if dbuf is not None else []
    for kt in range(KT):
        kb.dma("sp", R.xblk[:, kt, :], xT[kt * 128:(kt + 1) * 128, tb * TB:(tb + 1) * TB], reads=rd,
               writes=[R.xblk_b[kt]])


def store_xblk(R, xT, tb, dbufs=None):
    kb = R.kb
    for kt in range(KT):
        wr = [dbufs[kt]] if dbufs is not None else []
        kb.dma("sp", xT[kt * 128:(kt + 1) * 128, tb * TB:(tb + 1) * TB], R.xblk[:, kt, :], reads=[R.xblk_b[kt]],
               writes=wr)


def build_ffn_only():
    nc = bass.Bass("TRN2", target_bir_lowering=False)
    xT = nc.dram_tensor("xT", [D, TPC], F32, kind="ExternalInput").ap()
    g = nc.dram_tensor("g", [D], F32, kind="ExternalInput").ap()
    w_gu = nc.dram_tensor("w_gu", [D, 2 * FF], F32, kind="ExternalInput").ap()
    w_down = nc.dram_tensor("w_down", [FF, D], F32, kind="ExternalInput").ap()
    yT = nc.dram_tensor("yT", [D, TPC], F32, kind="ExternalOutput").ap()
    kb = KB(nc)
    R = RowRes(kb)
    R.load_gains(0, g)
    for tb in range(NTB):
        load_xblk(R, xT, tb)
        ffn_block(R, 0, w_gu, w_down)
        store_xblk(R, yT, tb)
    kb.finish()
    return nc


NPAIR = 3
QB = 512
NQB = L // QB
NKB = L // 128


def build_sb_attn(npair=NPAIR, nqb=NQB):
    nc = bass.Bass("TRN2", target_bir_lowering=False)
    qT = nc.dram_tensor("qT", [npair, 128, L], BF16, kind="ExternalInput").ap()
    kT = nc.dram_tensor("kT", [npair, 128, L], BF16, kind="ExternalInput").ap()
    v = nc.dram_tensor("v", [npair, L, 128], BF16, kind="ExternalInput").ap()
    oT = nc.dram_tensor("oT", [npair, 128, L], BF16, kind="ExternalOutput").ap()
    kb = KB(nc)
    sb_attn_phase(kb, qT, kT, v, oT, npair, nqb)
    kb.finish()
    return nc


def sb_attn_phase(kb, qT, kT, v, oT, npair, nqb, load_fn=None, store_fn=None):
    nc = kb.nc
    q_s = [kb.sb("q_s%d" % i, (128, L), BF16) for i in range(2)]
    k_s = [kb.sb("k_s%d" % i, (128, L), BF16) for i in range(2)]
    v_s = [kb.sb("v_s%d" % i, (128, NKB, 128), BF16) for i in range(2)]
    qkv_b = [Buf() for _ in range(2)]
    NE = 6
    NZ = 3
    e_s = [kb.sb("e_s%d" % i, (128, QB), F32) for i in range(NE)]
    e_b = [Buf() for _ in range(NE)]
    sp_s = [kb.sb("sp_s%d" % i, (128, QB), BF16) for i in range(NE)]
    sp_b = [Buf() for _ in range(NE)]
    NX = 3
    x_s = [kb.sb("x_s%d" % i, (128, QB), F32) for i in range(NX)]
    x_b = [Buf() for _ in range(NX)]
    w_s = [kb.sb("w_s%d" % i, (128, QB), BF16) for i in range(NX)]
    w_b = [Buf() for _ in range(NX)]
    o_s = [kb.sb("o_s%d" % i, (128, QB), BF16) for i in range(2)]
    o_b = [Buf() for _ in range(2)]
    ones_f = kb.sb("ones_f", (128, 128), F32)
    uinc = kb.sb("uinc", (128, 128), BF16)
    lstr = kb.sb("lstr", (128, 128), BF16)
    c_b = Buf()
    psZ = [kb.ps("psZ%d" % i) for i in range(3)]
    psZ_b = [Buf() for _ in range(3)]
    psP = [kb.ps("psP%d" % i) for i in range(2)]
    psP_b = [Buf() for _ in range(2)]
    psO = [kb.ps("psOa%d" % i) for i in range(2)]
    psO_b = [Buf() for _ in range(2)]

    kb.op("pool", lambda h: h.memset(ones_f[:], 1.0), writes=[c_b])
    kb.op("pool", lambda h: h.affine_select(out=uinc[:], in_=ones_f[:], pattern=[[-1, 128]], compare_op=ALU.is_ge,
                                            fill=0.0, base=0, channel_multiplier=1), reads=[c_b], writes=[c_b])
    kb.op("pool", lambda h: h.affine_select(out=lstr[:], in_=ones_f[:], pattern=[[1, 128]], compare_op=ALU.is_gt,
                                            fill=0.0, base=0, channel_multiplier=-1), reads=[c_b], writes=[c_b])

    def load_pair(pi):
        j = pi % 2
        if load_fn is not None:
            load_fn(pi, q_s[j], k_s[j], v_s[j], qkv_b[j])
            return
        kb.dma("sp", q_s[j][:], qT[pi], writes=[qkv_b[j]])
        kb.dma("sp", k_s[j][:], kT[pi], writes=[qkv_b[j]])
        kb.dma("sp", v_s[j][:], v[pi].rearrange("(kb p) d -> p kb d", p=128), writes=[qkv_b[j]])

    S0 = [0, 3, 4, 7, 8, 11, 12, 15]
    S1 = [1, 2, 5, 6, 9, 10, 13, 14]
    tiles = []
    for pi in range(npair):
        streams = []
        for si, qbs in enumerate((S0, S1)):
            lst = []
            for qb in qbs:
                if qb >= nqb:
                    continue
                nk = 4 * (qb + 1)
                for idx in range(nk):
                    lst.append((pi, qb, nk - 1 - idx, idx, nk, si))
            streams.append(lst)
        n_ = max(len(streams[0]), len(streams[1]))
        for k in range(n_):
            for si in range(2):
                tiles.append(streams[si][k] if k < len(streams[si]) else None)

    tctr = [0]

    def stage1(t):
        pi, qb, kbi, idx, nk, g = t
        j = pi % 2
        n = tctr[0] % NE
        z = tctr[0] % NZ
        tctr[0] += 1
        kb.op("pe", lambda h: h.matmul(psZ[z][:], k_s[j][:, kbi * 128:(kbi + 1) * 128],
                                       q_s[j][:, qb * QB:(qb + 1) * QB], start=True, stop=True),
              reads=[qkv_b[j]], writes=[psZ_b[z]])
        kb.op("act", lambda h: h.activation(out=e_s[n][:], in_=psZ[z][:], func=AF.Exp),
              reads=[psZ_b[z]], writes=[e_b[n]])
        kb.op("act", lambda h: h.activation(out=sp_s[n][:], in_=e_s[n][:], func=AF.Ln, bias=1.0),
              reads=[e_b[n]], writes=[sp_b[n]])
        dj = kbi - 4 * qb
        if dj >= 0:
            kb.op("pool", lambda h: h.affine_select(out=sp_s[n][:], in_=sp_s[n][:], pattern=[[1, QB]],
                                                    compare_op=ALU.is_gt, fill=0.0, base=-128 * dj,
                                                    channel_multiplier=-1),
                  reads=[sp_b[n]], writes=[sp_b[n]])
        return n

    xctr = [0]

    def stageB1(t, n):
        pi, qb, kbi, idx, nk, g = t
        xi = xctr[0] % NX
        xctr[0] += 1
        kb.op("pe", lambda h: h.matmul(psP[g][:], uinc[:], sp_s[n][:], start=(idx == 0), stop=(idx == nk - 1)),
              reads=[sp_b[n], c_b], writes=[psP_b[g]])
        kb.op("act", lambda h: h.activation(out=x_s[xi][:], in_=psP[g][:], func=AF.Exp, scale=-1.0),
              reads=[psP_b[g]], writes=[x_b[xi]])
        return xi

    def stageB2(t, n):
        pi, qb, kbi, idx, nk, g = t
        if idx != nk - 1:
            kb.op("pe", lambda h: h.matmul(psP[g][:], lstr[:], sp_s[n][:], start=False, stop=False),
                  reads=[sp_b[n], c_b], writes=[psP_b[g]])

    def stageC(t, n, xi):
        pi, qb, kbi, idx, nk, g = t
        j = pi % 2
        kb.op("dve", lambda h: h.tensor_tensor(out=w_s[xi][:], in0=e_s[n][:], in1=x_s[xi][:], op=ALU.mult),
              reads=[e_b[n], x_b[xi]], writes=[w_b[xi]])
        dj = kbi - 4 * qb
        if dj >= 0:
            kb.op("pool", lambda h: h.affine_select(out=w_s[xi][:], in_=w_s[xi][:], pattern=[[1, QB]],
                                                    compare_op=ALU.is_gt, fill=0.0, base=-128 * dj,
                                                    channel_multiplier=-1),
                  reads=[w_b[xi]], writes=[w_b[xi]])
        kb.op("pe", lambda h: h.matmul(psO[g][:], v_s[j][:, kbi, :], w_s[xi][:], start=(idx == 0),
                                       stop=(idx == nk - 1)),
              reads=[w_b[xi], qkv_b[j]], writes=[psO_b[g]])
        if idx == nk - 1:
            kb.op("act", lambda h: h.activation(out=o_s[g][:], in_=psO[g][:], func=AF.Copy),
                  reads=[psO_b[g]], writes=[o_b[g]])
            if store_fn is not None:
                store_fn(pi, qb, o_s[g], o_b[g])
            else:
                kb.dma("sp", oT[pi, :, qb * QB:(qb + 1) * QB], o_s[g][:], reads=[o_b[g]])

    load_pair(0)
    loaded = 1
    pipe = [None, None, None, None]
    seen_pairs = set()

    def step(new):
        a1, a2, b1, b2 = pipe
        nb1 = None
        if a2 is not None:
            t_, n_ = a2
            xi = stageB1(t_, n_)
            nb1 = (t_, n_, xi)
        nb2 = None
        if b1 is not None:
            stageB2(b1[0], b1[1])
            nb2 = b1
        if b2 is not None:
            stageC(*b2)
        pipe[0], pipe[1], pipe[2], pipe[3] = new, a1, nb1, nb2

    cur_pair = 0
    for t in tiles + [("end",)]:
        if t is None:
            step(None)
            continue
        if t[0] != cur_pair or t[0] == "end":
            for _ in range(4):
                step(None)
            if t[0] == "end":
                break
            cur_pair = t[0]
        if t[0] not in seen_pairs:
            seen_pairs.add(t[0])
            if loaded < npair and loaded <= t[0] + 1:
                load_pair(loaded)
                loaded += 1
        n = stage1(t)
        step((t, n))


NST = 12
SCH = 512
TWO_PI = 2.0 * math.pi
GELU_C = 2.0 * math.sqrt(2.0 / math.pi)


def build_s5(nchunks=L // SCH):
    nc = bass.Bass("TRN2", target_bir_lowering=False)
    dr = {}
    for name, shape in (("uT", [384, L]), ("ldt", [128, NST]), ("are", [128, NST]), ("aim", [128, NST]),
                        ("Bre", [128, NST, 128]), ("Bim", [128, NST, 128]), ("Cre", [128, NST, 128]),
                        ("Cim", [128, NST, 128]), ("dcol", [128, 3])):
        dr[name] = nc.dram_tensor(name, shape, F32, kind="ExternalInput").ap()
    ygT = nc.dram_tensor("ygT", [384, L], BF16, kind="ExternalOutput").ap()
    kb = KB(nc)
    s5_phase(kb, dr, ygT, nchunks)
    kb.finish()
    return nc


def s5_phase(kb, dr, ygT, nchunks, uload_fn=None, ystore_fn=None):
    def small(name, w=NST):
        return kb.sb(name, (128, w), F32), Buf()

    def tt(eng, out, a, b, op, reads, writes):
        kb.op(eng, lambda h: h.tensor_tensor(out=out, in0=a, in1=b, op=op), reads=reads, writes=writes)

    def ts(eng, out, a, s1, op0, reads, writes, s2=None, op1=None):
        if op1 is None:
            kb.op(eng, lambda h: h.tensor_scalar(out=out, in0=a, scalar1=s1, scalar2=None, op0=op0),
                  reads=reads, writes=writes)
        else:
            kb.op(eng, lambda h: h.tensor_scalar(out=out, in0=a, scalar1=s1, scalar2=s2, op0=op0, op1=op1),
                  reads=reads, writes=writes)

    def stt(eng, out, a, s, b, op0, op1, reads, writes):
        kb.op(eng, lambda h: h.scalar_tensor_tensor(out=out, in0=a, scalar=s, in1=b, op0=op0, op1=op1),
              reads=reads, writes=writes)

    def act(out, a, func, reads, writes, **kw):
        kb.op("act", lambda h: h.activation(out=out, in_=a, func=func, **kw), reads=reads, writes=writes)

    P = {}
    for name in ("ldt", "are", "aim"):
        P[name] = small("p_" + name)
        kb.dma("sp", P[name][0][:], dr[name], writes=[P[name][1]])
    Bre = kb.sb("Bre_s", (128, NST, 128), F32)
    Bim = kb.sb("Bim_s", (128, NST, 128), F32)
    Cre = kb.sb("Cre_s", (128, NST, 128), F32)
    Cim = kb.sb("Cim_s", (128, NST, 128), F32)
    C2re = kb.sb("C2re", (128, NST, 128), F32)
    C2im = kb.sb("C2im", (128, NST, 128), F32)
    C2ren = kb.sb("C2ren", (128, NST, 128), F32)
    dcol = kb.sb("dcol_s", (128, 3), F32)
    par_b = Buf()
    c2_b = Buf()
    for t, name in ((Bre, "Bre"), (Bim, "Bim"), (Cre, "Cre"), (Cim, "Cim"), (dcol, "dcol")):
        kb.dma("sp", t[:], dr[name], writes=[par_b])

    dt_, dt_b = small("dt_")
    act(dt_[:], P["ldt"][0][:], AF.Exp, [P["ldt"][1]], [dt_b])
    rl, rl_b = small("rl")
    th, th_b = small("th")
    tt("dve", rl[:], P["are"][0][:], dt_[:], ALU.mult, [P["are"][1], dt_b], [rl_b])
    tt("dve", th[:], P["aim"][0][:], dt_[:], ALU.mult, [P["aim"][1], dt_b], [th_b])
    r_, r_b = small("r_")
    act(r_[:], rl[:], AF.Exp, [rl_b], [r_b])
    phi, phi_b = small("phi")
    ts("dve", phi[:], th[:], 1.0 / TWO_PI, ALU.mult, [th_b], [phi_b])
    ki = kb.sb("ki", (128, NST), I32)
    ki_b = Buf()
    kb.op("dve", lambda h: h.tensor_copy(out=ki[:], in_=phi[:]), reads=[phi_b], writes=[ki_b])
    kf, kf_b = small("kf")
    kb.op("dve", lambda h: h.tensor_copy(out=kf[:], in_=ki[:]), reads=[ki_b], writes=[kf_b])
    f_, f_b = small("f_")
    tt("dve", f_[:], phi[:], kf[:], ALU.subtract, [phi_b, kf_b], [f_b])
    cos1, cos1_b = small("cos1")
    sin1, sin1_b = small("sin1")
    tmpa, tmpa_b = small("tmpa")
    tmpb, tmpb_b = small("tmpb")

    def sin_of_frac(out, out_b, frac, frac_b):
        ts("dve", tmpa[:], frac, 0.5, ALU.is_gt, [frac_b], [tmpa_b])
        tt("dve", tmpb[:], frac, tmpa[:], ALU.subtract, [frac_b, tmpa_b], [tmpb_b])
        ts("dve", tmpa[:], tmpb[:], -0.5, ALU.is_lt, [tmpb_b], [tmpa_b])
        tt("dve", tmpb[:], tmpb[:], tmpa[:], ALU.add, [tmpb_b, tmpa_b], [tmpb_b])
        act(out, tmpb[:], AF.Sin, [tmpb_b], [out_b], scale=TWO_PI)

    sin_of_frac(sin1[:], sin1_b, f_[:], f_b)
    fc, fc_b = small("fc")
    ts("dve", fc[:], f_[:], 0.25, ALU.add, [f_b], [fc_b])
    sin_of_frac(cos1[:], cos1_b, fc[:], fc_b)

    p_, p_b = small("p_")
    q_, q_b = small("q_")
    tt("dve", p_[:], r_[:], cos1[:], ALU.mult, [r_b, cos1_b], [p_b])
    ts("dve", p_[:], p_[:], -1.0, ALU.add, [p_b], [p_b])
    tt("dve", q_[:], r_[:], sin1[:], ALU.mult, [r_b, sin1_b], [q_b])
    den, den_b = small("den")
    t1, t1_b = small("t1")
    t2, t2_b = small("t2")
    are, are_b = P["are"]
    aim, aim_b = P["aim"]
    tt("dve", den[:], are[:], are[:], ALU.mult, [are_b], [den_b])
    tt("dve", t1[:], aim[:], aim[:], ALU.mult, [aim_b], [t1_b])
    tt("dve", den[:], den[:], t1[:], ALU.add, [den_b, t1_b], [den_b])
    kb.op("dve", lambda h: h.reciprocal(out=den[:], in_=den[:]), reads=[den_b], writes=[den_b])
    gre, gre_b = small("gre")
    gim, gim_b = small("gim")
    tt("dve", t1[:], p_[:], are[:], ALU.mult, [p_b, are_b], [t1_b])
    tt("dve", t2[:], q_[:], aim[:], ALU.mult, [q_b, aim_b], [t2_b])
    tt("dve", gre[:], t1[:], t2[:], ALU.add, [t1_b, t2_b], [gre_b])
    tt("dve", gre[:], gre[:], den[:], ALU.mult, [gre_b, den_b], [gre_b])
    tt("dve", t1[:], q_[:], are[:], ALU.mult, [q_b, are_b], [t1_b])
    tt("dve", t2[:], p_[:], aim[:], ALU.mult, [p_b, aim_b], [t2_b])
    tt("dve", gim[:], t1[:], t2[:], ALU.subtract, [t1_b, t2_b], [gim_b])
    tt("dve", gim[:], gim[:], den[:], ALU.mult, [gim_b, den_b], [gim_b])
    zre, zre_b = small("zre")
    zim, zim_b = small("zim")
    nzim, nzim_b = small("nzim")
    tt("dve", t1[:], gre[:], cos1[:], ALU.mult, [gre_b, cos1_b], [t1_b])
    tt("dve", t2[:], gim[:], sin1[:], ALU.mult, [gim_b, sin1_b], [t2_b])
    tt("dve", zre[:], t1[:], t2[:], ALU.add, [t1_b, t2_b], [zre_b])
    tt("dve", t1[:], gim[:], cos1[:], ALU.mult, [gim_b, cos1_b], [t1_b])
    tt("dve", t2[:], gre[:], sin1[:], ALU.mult, [gre_b, sin1_b], [t2_b])
    tt("dve", zim[:], t1[:], t2[:], ALU.subtract, [t1_b, t2_b], [zim_b])
    ts("dve", nzim[:], zim[:], -1.0, ALU.mult, [zim_b], [nzim_b])

    ctmp = kb.sb("ctmp", (128, 128), F32)
    ctmp_b = Buf()
    for st in range(NST):
        ts("dve", ctmp[:], Cim[:, st, :], zim[:, st:st + 1], ALU.mult, [par_b, zim_b], [ctmp_b])
        stt("dve", C2re[:, st, :], Cre[:, st, :], zre[:, st:st + 1], ctmp[:], ALU.mult, ALU.subtract,
            [par_b, zre_b, ctmp_b], [c2_b])
        ts("dve", ctmp[:], Cim[:, st, :], zre[:, st:st + 1], ALU.mult, [par_b, zre_b], [ctmp_b])
        stt("dve", C2im[:, st, :], Cre[:, st, :], nzim[:, st:st + 1], ctmp[:], ALU.mult, ALU.subtract,
            [par_b, nzim_b, ctmp_b], [c2_b])
        ts("dve", C2ren[:, st, :], C2re[:, st, :], -1.0, ALU.mult, [c2_b], [c2_b])

    TW = SCH + 8
    tabC = kb.sb("tabC", (128, NST, TW), F32)
    tabS = kb.sb("tabS", (128, NST, TW), F32)
    tab_b = [Buf() for _ in range(NST)]
    ttmp = kb.sb("ttmp", (128, 256), F32)
    ttmp_b = Buf()
    rt = kb.sb("rt", (128, NST, SCH), F32)
    rt_b = Buf()
    onesw = kb.sb("onesw", (128, SCH), F32)
    onesw_b = Buf()
    kb.op("pool", lambda h: h.memset(onesw[:], 1.0), writes=[onesw_b])
    for st in range(NST):
        b = tab_b[st]
        kb.op("pool", lambda h, st=st: h.memset(tabC[:, st, 0:1], 1.0), writes=[b])
        kb.op("pool", lambda h, st=st: h.memset(tabS[:, st, 0:1], 0.0), writes=[b])
        kb.op("act", lambda h, st=st: h.activation(out=tabC[:, st, 1:2], in_=cos1[:, st:st + 1], func=AF.Copy),
              reads=[cos1_b], writes=[b])
        kb.op("act", lambda h, st=st: h.activation(out=tabS[:, st, 1:2], in_=sin1[:, st:st + 1], func=AF.Copy),
              reads=[sin1_b], writes=[b])
        m = 1
        while m < SCH:
            cm = tabC[:, st, m:m + 1]
            sm = tabS[:, st, m:m + 1]
            ts("dve", ttmp[:, 0:m], tabS[:, st, 1:m + 1], sm, ALU.mult, [b], [ttmp_b])
            stt("dve", tabC[:, st, m + 1:2 * m + 1], tabC[:, st, 1:m + 1], cm, ttmp[:, 0:m], ALU.mult, ALU.subtract,
                [b, ttmp_b], [b])
            ts("dve", ttmp[:, 0:m], tabC[:, st, 1:m + 1], sm, ALU.mult, [b], [ttmp_b])
            stt("dve", tabS[:, st, m + 1:2 * m + 1], tabS[:, st, 1:m + 1], cm, ttmp[:, 0:m], ALU.mult, ALU.add,
                [b, ttmp_b], [b])
            m *= 2
        ts("pool", rt[:, st, :], onesw[:], r_[:, st:st + 1], ALU.mult, [onesw_b, r_b], [rt_b])

    u_s = [kb.sb("u_s%d" % i, (128, SCH), F32) for i in range(2)]
    u_b = [Buf() for _ in range(2)]
    NR = 2
    m_s = [[kb.sb("m%d_%d" % (k, i), (128, SCH), F32) for k in range(4)] for i in range(NR)]
    m_b = [[Buf() for k in range(4)] for i in range(NR)]
    wv_s = [[kb.sb("wv%d_%d" % (k, i), (128, SCH), F32) for k in range(2)] for i in range(NR)]
    wv_b = [[Buf() for k in range(2)] for i in range(NR)]
    n_s = [[kb.sb("n%d_%d" % (k, i), (128, SCH), F32) for k in range(4)] for i in range(NR)]
    n_b = [[Buf() for k in range(4)] for i in range(NR)]
    car_r = kb.sb("car_r", (128, NST), F32)
    car_i = kb.sb("car_i", (128, NST), F32)
    car_b = [Buf() for _ in range(NST)]
    yv = kb.sb("yv", (128, SCH), F32)
    yv_b = Buf()
    e1 = kb.sb("e1", (128, SCH), F32)
    e1_b = Buf()
    e2 = kb.sb("e2", (128, SCH), F32)
    e2_b = Buf()
    yo = [kb.sb("yo%d" % i, (128, SCH), BF16) for i in range(2)]
    yo_b = [Buf() for _ in range(2)]
    psR = [kb.ps("psR%d" % i) for i in range(2)]
    psR_b = [Buf() for _ in range(2)]
    psI = [kb.ps("psI%d" % i) for i in range(2)]
    psI_b = [Buf() for _ in range(2)]
    psY = [kb.ps("psY%d" % i) for i in range(2)]
    psY_b = [Buf() for _ in range(2)]
    psW = [kb.ps("psW%d" % i) for i in range(2)]
    psW_b = [Buf() for _ in range(2)]

    Bre16 = kb.sb("Bre16", (128, NST, 128), BF16)
    Bim16 = kb.sb("Bim16", (128, NST, 128), BF16)
    C2re16 = kb.sb("C2re16", (128, NST, 128), BF16)
    C2ren16 = kb.sb("C2ren16", (128, NST, 128), BF16)
    C2im16 = kb.sb("C2im16", (128, NST, 128), BF16)
    t16_b = Buf()
    for dst, src, sb_ in ((Bre16, Bre, par_b), (Bim16, Bim, par_b), (C2re16, C2re, c2_b), (C2ren16, C2ren, c2_b),
                          (C2im16, C2im, c2_b)):
        kb.op("pool", lambda h, dst=dst, src=src: h.tensor_copy(out=dst[:], in_=src[:]), reads=[sb_], writes=[t16_b])
    u16 = [kb.sb("u16_%d" % i, (128, SCH), BF16) for i in range(2)]
    u16_b = [Buf() for _ in range(2)]
    n16 = [[kb.sb("n16_%d_%d" % (k, i), (128, SCH), BF16) for k in range(4)] for i in range(NR)]
    n16_b = [[Buf() for k in range(4)] for i in range(NR)]
    ctmp4 = kb.sb("ctmp4", (128, 4), F32)
    ctmp4_b = Buf()

    iters = [(ch, ct, sl) for ch in range(nchunks) for ct in range(3) for sl in range(4)]

    def P(k):
        ch, ct, sl = iters[k]
        ui = (ch * 3 + ct) % 2
        if sl == 0:
            if uload_fn is not None:
                uload_fn(ch, ct, u_s[ui], u_b[ui])
            else:
                kb.dma("sp", u_s[ui][:], dr["uT"][ct * 128:(ct + 1) * 128, ch * SCH:(ch + 1) * SCH],
                       writes=[u_b[ui]])
            kb.op("act", lambda h: h.activation(out=u16[ui][:], in_=u_s[ui][:], func=AF.Copy),
                  reads=[u_b[ui]], writes=[u16_b[ui]])
        st = ct * 4 + sl
        i = k % 2
        kb.op("pe", lambda h: h.matmul(psR[i][:], Bre16[:, st, :], u16[ui][:], start=True, stop=True),
              reads=[t16_b, u16_b[ui]], writes=[psR_b[i]])
        kb.op("pe", lambda h: h.matmul(psI[i][:], Bim16[:, st, :], u16[ui][:], start=True, stop=True),
              reads=[t16_b, u16_b[ui]], writes=[psI_b[i]])

    def Dk(k):
        ch, ct, sl = iters[k]
        ui = (ch * 3 + ct) % 2
        yi = (ch * 3 + ct) % 2
        st = ct * 4 + sl
        i = k % 2
        pR, pI, pRb, pIb = psR[i], psI[i], psR_b[i], psI_b[i]
        m, mb = m_s[i], m_b[i]
        Cc = tabC[:, st, 0:SCH]
        Ss = tabS[:, st, 0:SCH]
        tb_ = tab_b[st]
        tt("dve", m[0][:], pR[:], Cc, ALU.mult, [pRb, tb_], [mb[0]])
        tt("dve", m[1][:], pI[:], Ss, ALU.mult, [pIb, tb_], [mb[1]])
        tt("dve", m[2][:], pI[:], Cc, ALU.mult, [pIb, tb_], [mb[2]])
        tt("dve", m[3][:], pR[:], Ss, ALU.mult, [pRb, tb_], [mb[3]])
        tt("pool", m[0][:], m[0][:], m[1][:], ALU.add, [mb[0], mb[1]], [mb[0]])
        tt("pool", m[2][:], m[2][:], m[3][:], ALU.subtract, [mb[2], mb[3]], [mb[2]])
        if ch == 0:
            ini_r = 0.0
            ini_i = 0.0
        else:
            ini_r = car_r[:, st:st + 1]
            ini_i = car_i[:, st:st + 1]
        kb.op("dve", lambda h: h.tensor_tensor_scan(out=psW[0][:], data0=rt[:, st, :], data1=m[0][:],
                                                    initial=ini_r, op0=ALU.mult, op1=ALU.add),
              reads=[rt_b, mb[0], car_b[st]], writes=[psW_b[0]])
        kb.op("dve", lambda h: h.tensor_tensor_scan(out=psW[1][:], data0=rt[:, st, :], data1=m[2][:],
                                                    initial=ini_i, op0=ALU.mult, op1=ALU.add),
              reads=[rt_b, mb[2], car_b[st]], writes=[psW_b[1]])
        n, nb = n16[i], n16_b[i]
        C1 = tabC[:, st, 1:SCH + 1]
        S1 = tabS[:, st, 1:SCH + 1]
        tt("dve", n[0][:], psW[0][:], C1, ALU.mult, [psW_b[0], tb_], [nb[0]])
        tt("dve", n[1][:], psW[1][:], S1, ALU.mult, [psW_b[1], tb_], [nb[1]])
        tt("dve", n[2][:], psW[0][:], S1, ALU.mult, [psW_b[0], tb_], [nb[2]])
        tt("dve", n[3][:], psW[1][:], C1, ALU.mult, [psW_b[1], tb_], [nb[3]])
        L1 = slice(SCH - 1, SCH)
        cl = tabC[:, st, SCH:SCH + 1]
        sl_ = tabS[:, st, SCH:SCH + 1]
        tt("dve", ctmp4[:, 0:1], psW[0][:, L1], cl, ALU.mult, [psW_b[0], tb_], [ctmp4_b])
        tt("dve", ctmp4[:, 1:2], psW[1][:, L1], sl_, ALU.mult, [psW_b[1], tb_], [ctmp4_b])
        tt("dve", ctmp4[:, 2:3], psW[0][:, L1], sl_, ALU.mult, [psW_b[0], tb_], [ctmp4_b])
        tt("dve", ctmp4[:, 3:4], psW[1][:, L1], cl, ALU.mult, [psW_b[1], tb_], [ctmp4_b])
        tt("dve", car_r[:, st:st + 1], ctmp4[:, 0:1], ctmp4[:, 1:2], ALU.subtract, [ctmp4_b], [car_b[st]])
        tt("dve", car_i[:, st:st + 1], ctmp4[:, 2:3], ctmp4[:, 3:4], ALU.add, [ctmp4_b], [car_b[st]])
        kb.op("pe", lambda h: h.matmul(psY[yi][:], C2re16[:, st, :], n[0][:], start=(sl == 0), stop=False),
              reads=[t16_b, nb[0]], writes=[psY_b[yi]])
        kb.op("pe", lambda h: h.matmul(psY[yi][:], C2ren16[:, st, :], n[1][:], start=False, stop=False),
              reads=[t16_b, nb[1]], writes=[psY_b[yi]])
        kb.op("pe", lambda h: h.matmul(psY[yi][:], C2im16[:, st, :], n[2][:], start=False, stop=False),
              reads=[t16_b, nb[2]], writes=[psY_b[yi]])
        kb.op("pe", lambda h: h.matmul(psY[yi][:], C2im16[:, st, :], n[3][:], start=False, stop=(sl == 3)),
              reads=[t16_b, nb[3]], writes=[psY_b[yi]])
        if sl == 3:
            stt("dve", yv[:], u_s[ui][:], dcol[:, ct:ct + 1], psY[yi][:], ALU.mult, ALU.add,
                [u_b[ui], par_b, psY_b[yi]], [yv_b])
            act(e1[:], yv[:], AF.Square, [yv_b], [e1_b])
            ts("dve", e1[:], e1[:], 0.044715, ALU.mult, [e1_b], [e1_b], s2=1.0, op1=ALU.add)
            tt("pool", e2[:], e1[:], yv[:], ALU.mult, [e1_b, yv_b], [e2_b])
            act(e2[:], e2[:], AF.Sigmoid, [e2_b], [e2_b], scale=GELU_C)
            tt("pool", yo[yi][:], e2[:], yv[:], ALU.mult, [e2_b, yv_b], [yo_b[yi]])
            if ystore_fn is not None:
                ystore_fn(ch, ct, yo[yi], yo_b[yi])
            else:
                kb.dma("sp", ygT[ct * 128:(ct + 1) * 128, ch * SCH:(ch + 1) * SCH], yo[yi][:], reads=[yo_b[yi]])

    P(0)
    for k in range(len(iters)):
        if k + 1 < len(iters):
            P(k + 1)
        Dk(k)


def s5_host_params(log_dt, a_re, a_im, b_re, b_im, c_re, c_im, d, gs):
    g0 = gs * 24
    out = {}

    def per_state(a):
        return np.ascontiguousarray(a.reshape(NST, 2, 64).transpose(1, 2, 0).reshape(128, NST))

    out["ldt"] = per_state(np.repeat(log_dt[g0:g0 + 24, None], 64, axis=1))
    out["are"] = per_state(a_re[g0:g0 + 24])
    out["aim"] = per_state(a_im[g0:g0 + 24])
    Bre = np.zeros((128, NST, 128), np.float32)
    Bim = np.zeros((128, NST, 128), np.float32)
    Cre = np.zeros((128, NST, 128), np.float32)
    Cim = np.zeros((128, NST, 128), np.float32)
    for st in range(NST):
        for gl in range(2):
            g = g0 + 2 * st + gl
            r0 = (2 * (st % 4) + gl) * 16
            Bre[r0:r0 + 16, st, gl * 64:(gl + 1) * 64] = b_re[g].T
            Bim[r0:r0 + 16, st, gl * 64:(gl + 1) * 64] = b_im[g].T
            Cre[gl * 64:(gl + 1) * 64, st, r0:r0 + 16] = c_re[g].T
            Cim[gl * 64:(gl + 1) * 64, st, r0:r0 + 16] = c_im[g].T
    out["Bre"], out["Bim"], out["Cre"], out["Cim"] = Bre, Bim, Cre, Cim
    out["dcol"] = np.ascontiguousarray(d[g0 * 16:(g0 + 24) * 16].reshape(3, 128).T)
    return out


ISQ = 1.0 / math.sqrt(128.0)


def gemm16(R, ps, ps_b, slab, slab_b, rhs, rhs_b, nkt=KT, width=TB, col0=0):
    kb = R.kb
    for kt in range(nkt):
        kb.op("pe", lambda h, kt=kt: h.matmul(ps[:, col0:col0 + width], slab[:, kt, :], rhs[:, kt, 0:width],
                                              start=(kt == 0), stop=(kt == nkt - 1)),
              reads=[slab_b, rhs_b[kt]], writes=[ps_b])


def next_slab(R):
    j = R.ctr % R.NSLAB
    R.ctr += 1
    return j


def next_ps(R):
    p = R.pctr % 2
    R.pctr += 1
    return p


def head_rstd(R, ps, ps_b, width):
    kb = R.kb
    kb.op("act", lambda h: h.activation(out=R.sq[0][:, 0:width], in_=ps[:, 0:width], func=AF.Square),
          reads=[ps_b], writes=[R.sq_b[0]])
    kb.op("pe", lambda h: h.matmul(R.psS[:, 0:width], R.ones[:], R.sq[0][:, 0:width], start=True, stop=True),
          reads=[R.sq_b[0], R.ones_b], writes=[R.psS_b])
    kb.op("act", lambda h: h.activation(out=R.rstd[:, 0:width], in_=R.psS[:, 0:width], func=AF.Ln, scale=1.0 / 128,
                                        bias=EPS), reads=[R.psS_b], writes=[R.rstd_b])
    kb.op("act", lambda h: h.activation(out=R.rstd[:, 0:width], in_=R.rstd[:, 0:width], func=AF.Exp, scale=-0.5),
          reads=[R.rstd_b], writes=[R.rstd_b])


def mem_prep(R, memT, w_mem_kv, slot_mem):
    kb = R.kb
    for kt in range(KT):
        kb.dma("sp", R.xblk[:, kt, 0:NMEM], memT[kt * 128:(kt + 1) * 128, :], writes=[R.xblk_b[kt]])
    rmsnorm_block(R, slot_mem, R.xblk, R.xblk_b, dst=R.memh, dst_b=R.memh_b, width=NMEM)
    for hd in range(4):
        j = next_slab(R)
        p = next_ps(R)
        load_slab(kb, R.wA[j], R.wA_b[j], w_mem_kv, hd * 128, 128, KT)
        gemm16(R, R.psA[p], R.psA_b[p], R.wA[j], R.wA_b[j], R.memh, R.memh_b, width=NMEM)
        head_rstd(R, R.psA[p], R.psA_b[p], NMEM)
        kb.op("dve", lambda h, j=j, p=p, hd=hd: h.scalar_tensor_tensor(out=R.KmT[:, hd, :], in0=R.psA[p][:, 0:NMEM],
                                                                  scalar=R.gqk[:, 1:2], in1=R.rstd[:, 0:NMEM],
                                                                  op0=ALU.mult, op1=ALU.mult),
              reads=[R.psA_b[p], R.gqk_b, R.rstd_b], writes=[R.kv_b])
    for hd in range(4):
        j = next_slab(R)
        p = next_ps(R)
        load_slab(kb, R.wA[j], R.wA_b[j], w_mem_kv, MEMW + hd * 128, 128, KT)
        for mt in range(2):
            for kt in range(KT):
                kb.op("pe", lambda h, kt=kt, mt=mt, j=j, p=p: h.matmul(R.psB[p][:, mt * 128:(mt + 1) * 128],
                                                                  R.memh[:, kt, mt * 128:(mt + 1) * 128],
                                                                  R.wA[j][:, kt, :], start=(kt == 0),
                                                                  stop=(kt == KT - 1)),
                      reads=[R.wA_b[j], R.memh_b[kt]], writes=[R.psB_b[p]])
        for mt in range(2):
            kb.op("act", lambda h, mt=mt, j=j, p=p, hd=hd: h.activation(out=R.Vm[:, mt, hd * 128:(hd + 1) * 128],
                                                                   in_=R.psB[p][:, mt * 128:(mt + 1) * 128],
                                                                   func=AF.Copy),
                  reads=[R.psB_b[p]], writes=[R.kv_b])


def _wb(outs):
    lst = outs.get("wb_list")
    if not lst:
        return []
    k = outs["wb_ctr"][0]
    outs["wb_ctr"][0] = k + 1
    return [lst[k % len(lst)]]


def mixin_block(R, kind, slot_mix, w_in, outs, tb):
    kb = R.kb
    tsl = slice(tb * TB, (tb + 1) * TB)
    rmsnorm_block(R, slot_mix, R.xblk, R.xblk_b)
    if kind == "sb":
        for nt in range(24):
            j = next_slab(R)
            p = next_ps(R)
            load_slab(kb, R.wA[j], R.wA_b[j], w_in, nt * 128, 128, KT)
            gemm16(R, R.psA[p], R.psA_b[p], R.wA[j], R.wA_b[j], R.hT, R.hT_b)
            s = R.sctr % 2
            R.sctr += 1
            kb.op("act", lambda h, j=j, p=p, s=s, nt=nt: h.activation(out=R.stg[s][:], in_=R.psA[p][:], func=AF.Copy,
                                                                 scale=(ISQ if nt < 12 else 1.0)),
                  reads=[R.psA_b[p]], writes=[R.stg_b[s]])
            if "qk_dst" in outs:
                dap, dwb = outs["qk_dst"](nt, tb)
                kb.dma("sp", dap, R.stg[s][:], reads=[R.stg_b[s]], writes=dwb)
            else:
                kb.dma("sp", outs["qkT"][nt * 128:(nt + 1) * 128, tsl], R.stg[s][:], reads=[R.stg_b[s]])
        for nt in range(12):
            j = next_slab(R)
            p = next_ps(R)
            load_slab(kb, R.wA[j], R.wA_b[j], w_in, 3072 + nt * 128, 128, KT)
            for t4 in range(4):
                for kt in range(KT):
                    kb.op("pe", lambda h, kt=kt, t4=t4, j=j, p=p: h.matmul(R.psB[p][:, t4 * 128:(t4 + 1) * 128],
                                                                      R.hT[:, kt, t4 * 128:(t4 + 1) * 128],
                                                                      R.wA[j][:, kt, :], start=(kt == 0),
                                                                      stop=(kt == KT - 1)),
                          reads=[R.wA_b[j], R.hT_b[kt]], writes=[R.psB_b[p]])
            s = R.sctr % 2
            R.sctr += 1
            kb.op("act", lambda h, j=j, p=p, s=s: h.activation(out=R.stg[s][:], in_=R.psB[p][:], func=AF.Copy),
                  reads=[R.psB_b[p]], writes=[R.stg_b[s]])
            if "v_dst" in outs:
                dap, dwb = outs["v_dst"](nt, tb)
                kb.dma("sp", dap, R.stg[s][:].rearrange("p (t c) -> p t c", t=4), reads=[R.stg_b[s]], writes=dwb)
            else:
                kb.dma("sp", outs["v"][tsl, nt * 128:(nt + 1) * 128].rearrange("(t p) c -> p t c", p=128),
                       R.stg[s][:].rearrange("p (t c) -> p t c", t=4), reads=[R.stg_b[s]])
        qm0 = 4608
    else:
        for nt in range(12):
            j = next_slab(R)
            p = next_ps(R)
            load_slab(kb, R.wA[j], R.wA_b[j], w_in, nt * 128, 128, KT)
            gemm16(R, R.psA[p], R.psA_b[p], R.wA[j], R.wA_b[j], R.hT, R.hT_b)
            kb.op("act", lambda h, j=j, p=p: h.activation(out=R.sg[p][:], in_=R.psA[p][:], func=AF.Copy),
                  reads=[R.psA_b[p]], writes=[R.sg_b[p]])
            if "u_dst" in outs:
                dap, dwb = outs["u_dst"](nt, tb)
                kb.dma("sp", dap, R.sg[p][:], reads=[R.sg_b[p]], writes=dwb)
            else:
                kb.dma("sp", outs["uT"][nt * 128:(nt + 1) * 128, tsl], R.sg[p][:], reads=[R.sg_b[p]])
        qm0 = 1536
    for hd in range(4):
        j = next_slab(R)
        p = next_ps(R)
        load_slab(kb, R.wA[j], R.wA_b[j], w_in, qm0 + hd * 128, 128, KT)
        gemm16(R, R.psA[p], R.psA_b[p], R.wA[j], R.wA_b[j], R.hT, R.hT_b)
        head_rstd(R, R.psA[p], R.psA_b[p], TB)
        kb.op("dve", lambda h, j=j, p=p: h.scalar_tensor_tensor(out=R.qn[:], in0=R.psA[p][:], scalar=R.gqk[:, 0:1],
                                                           in1=R.rstd[:], op0=ALU.mult, op1=ALU.mult),
              reads=[R.psA_b[p], R.gqk_b, R.rstd_b], writes=[R.qn_b])
        for mt in range(2):
            kb.op("pe", lambda h, mt=mt, hd=hd: h.matmul(R.psB[mt][:], R.KmT[:, hd, mt * 128:(mt + 1) * 128], R.qn[:],
                                                         start=True, stop=True),
                  reads=[R.kv_b, R.qn_b], writes=[R.psB_b[mt]])
            kb.op("act", lambda h, mt=mt: h.activation(out=R.pT[:, mt, :], in_=R.psB[mt][:], func=AF.Exp, scale=ISQ),
                  reads=[R.psB_b[mt]], writes=[R.pT_b[mt]])
        for mt in range(2):
            kb.op("pe", lambda h, mt=mt: h.matmul(R.psS[:], R.ones_bf[:], R.pT[:, mt, :], start=(mt == 0),
                                                  stop=(mt == 1)),
                  reads=[R.ones_b, R.pT_b[mt]], writes=[R.psS_b])
        o = hd % 2
        for mt in range(2):
            kb.op("pe", lambda h, mt=mt, hd=hd, o=o: h.matmul(R.psO[o][:], R.Vm[:, mt, hd * 128:(hd + 1) * 128],
                                                              R.pT[:, mt, :], start=(mt == 0), stop=(mt == 1)),
                  reads=[R.kv_b, R.pT_b[mt]], writes=[R.psO_b[o]])
        kb.op("dve", lambda h: h.reciprocal(out=R.rstd[:], in_=R.psS[:]), reads=[R.psS_b], writes=[R.rstd_b])
        s = R.sctr % 2
        R.sctr += 1
        kb.op("dve", lambda h, o=o, s=s: h.tensor_tensor(out=R.stg[s][:], in0=R.psO[o][:], in1=R.rstd[:],
                                                         op=ALU.mult),
              reads=[R.psO_b[o], R.rstd_b], writes=[R.stg_b[s]])
        kb.dma("sp", outs["crossT_out"][hd * 128:(hd + 1) * 128, tsl], R.stg[s][:], reads=[R.stg_b[s]],
               writes=(outs.get("cross_wb") or []))


def post_block(R, kind, tokT, crossT, w_out, w_glu, tb, tok_src=None, rd=()):
    kb = R.kb
    tsl = slice(tb * TB, (tb + 1) * TB)
    if tok_src is None:
        tok_src = lambda kt: tokT[kt * 128:(kt + 1) * 128, tsl]
    rd = list(rd)
    if kind == "sb":
        for kt in range(12):
            kb.dma("sp", R.hT[:, kt, :], tok_src(kt), reads=rd, writes=[R.hT_b[kt]])
    else:
        for kt in range(12):
            kb.dma("sp", R.actT[:, kt, :], tok_src(kt), reads=rd, writes=[R.actT_b[kt]])
        for nt in range(12):
            j = next_slab(R)
            p = next_ps(R)
            load_slab(kb, R.wA[j], R.wA_b[j], w_glu, nt * 128, 128, 12)
            gemm16(R, R.psA[p], R.psA_b[p], R.wA[j], R.wA_b[j], R.actT, R.actT_b, nkt=12)
            kb.op("act", lambda h, j=j, p=p: h.activation(out=R.sg[p][:], in_=R.psA[p][:], func=AF.Sigmoid),
                  reads=[R.psA_b[p]], writes=[R.sg_b[p]])
            kb.op("dve", lambda h, j=j, p=p, nt=nt: h.tensor_tensor(out=R.hT[:, nt, :], in0=R.actT[:, nt, :],
                                                               in1=R.sg[p][:], op=ALU.mult),
                  reads=[R.actT_b[nt], R.sg_b[p]], writes=[R.hT_b[nt]])
    for kt in range(12, 16):
        kb.dma("sp", R.hT[:, kt, :], crossT[(kt - 12) * 128:(kt - 11) * 128, tsl], reads=rd,
               writes=[R.hT_b[kt]])
    for dt in range(KT):
        j = next_slab(R)
        p = next_ps(R)
        load_slab(kb, R.wB[j], R.wB_b[j], w_out, dt * 128, 128, KT)
        gemm16(R, R.psO[p], R.psO_b[p], R.wB[j], R.wB_b[j], R.hT, R.hT_b)
        kb.op("dve", lambda h, dt=dt, j=j, p=p: h.tensor_tensor(out=R.xblk[:, dt, :], in0=R.psO[p][:],
                                                           in1=R.xblk[:, dt, :], op=ALU.add),
              reads=[R.psO_b[p], R.xblk_b[dt]], writes=[R.xblk_b[dt]])


def build_row(post, ffn2, ffn1, mix):
    nc = bass.Bass("TRN2", target_bir_lowering=False)

    def din(name, shape, dt=F32):
        return nc.dram_tensor(name, shape, dt, kind="ExternalInput").ap()

    def dout(name, shape, dt=F32):
        return nc.dram_tensor(name, shape, dt, kind="ExternalOutput").ap()

    xT = din("xT", [D, TPC])
    xT_out = dout("xT_out", [D, TPC])
    a = {}
    if post:
        a["tokT"] = din("tokT", [TOKW, TPC], BF16)
        a["crossT_in"] = din("crossT_in", [MEMW, TPC], BF16)
        a["w_out"] = din("w_out", [D, D])
        if post == "s5":
            a["w_glu"] = din("w_glu", [TOKW, TOKW])
    if ffn2:
        a["g_ffn2"] = din("g_ffn2", [D])
        a["w_gu2"] = din("w_gu2", [D, 2 * FF])
        a["w_down2"] = din("w_down2", [FF, D])
    if ffn1:
        a["g_ffn1"] = din("g_ffn1", [D])
        a["w_gu1"] = din("w_gu1", [D, 2 * FF])
        a["w_down1"] = din("w_down1", [FF, D])
    outs = {}
    if mix:
        a["g_mix"] = din("g_mix", [D])
        a["w_in"] = din("w_in", [D, 5120 if mix == "sb" else 2048])
        a["memT"] = din("memT", [D, NMEM])
        a["g_mem"] = din("g_mem", [D])
        a["w_mem_kv"] = din("w_mem_kv", [D, 2 * MEMW])
        a["gqk"] = din("gqk", [128, 2])
        outs["crossT_out"] = dout("crossT_out", [MEMW, TPC], BF16)
        if mix == "sb":
            outs["qkT"] = dout("qkT", [3072, TPC], BF16)
            outs["v"] = dout("v", [TPC, TOKW], BF16)
        else:
            outs["uT"] = dout("uT", [TOKW, TPC], F32)
    kb = KB(nc)
    R = RowRes(kb)
    if ffn2:
        R.load_gains(0, a["g_ffn2"])
    if ffn1:
        R.load_gains(1, a["g_ffn1"])
    if mix:
        R.load_gains(2, a["g_mix"])
        R.load_gains(3, a["g_mem"])
        kb.dma("sp", R.gqk[:], a["gqk"], writes=[R.gqk_b])
        mem_prep(R, a["memT"], a["w_mem_kv"], 3)
    for tb in range(NTB):
        load_xblk(R, xT, tb)
        if post:
            post_block(R, post, a["tokT"], a["crossT_in"], a["w_out"], a.get("w_glu"), tb)
        if ffn2:
            ffn_block(R, 0, a["w_gu2"], a["w_down2"])
        if ffn1:
            ffn_block(R, 1, a["w_gu1"], a["w_down1"])
        if mix:
            mixin_block(R, mix, 2, a["w_in"], outs, tb)
        store_xblk(R, xT_out, tb)
    kb.finish()
    return nc


_PROGS = {}


def _prog(key, fn):
    if key not in _PROGS:
        _PROGS[key] = fn()
    return _PROGS[key]


def _run(nc, in_maps):
    res = run_bass_kernel_spmd(nc, in_maps, core_ids=list(range(NCORES)))
    return res.results


def kernel_unfused(x, mem, ffn1_norm, ffn1_w_gu, ffn1_w_down, mix_norm, mem_norm, w_mem_kv, xq_norm, xk_norm, w_out,
           ffn2_norm, ffn2_w_gu, ffn2_w_down, sb_w_in, s5_w_in, s5_log_dt, s5_a_re, s5_a_im, s5_b_re, s5_b_im,
           s5_c_re, s5_c_im, s5_d, s5_w_glu, _debug=None):
    f32 = np.float32
    A = lambda t: np.ascontiguousarray(np.asarray(t, dtype=f32))
    x = A(x)
    mem = A(mem)
    xT = [np.ascontiguousarray(x[c // 4, (c % 4) * TPC:(c % 4 + 1) * TPC, :].T) for c in range(NCORES)]
    memT = [np.ascontiguousarray(mem[b].T) for b in range(NB)]
    tokT = None
    crossT = None
    for stage in range(DEPTH + 1):
        li = stage
        lp = stage - 1
        post = None if lp < 0 else ("sb" if lp % 2 == 0 else "s5")
        mix = None if li >= DEPTH else ("sb" if li % 2 == 0 else "s5")
        ffn2 = lp >= 0
        ffn1 = li < DEPTH
        nc = _prog(("row", post, ffn2, ffn1, mix), lambda: build_row(post, ffn2, ffn1, mix))
        maps = []
        for c in range(NCORES):
            m = {"xT": xT[c]}
            if post:
                m["tokT"] = tokT[c]
                m["crossT_in"] = crossT[c]
                m["w_out"] = A(w_out[lp])
                if post == "s5":
                    m["w_glu"] = A(s5_w_glu[lp // 2])
            if ffn2:
                m["g_ffn2"] = A(ffn2_norm[lp])
                m["w_gu2"] = A(ffn2_w_gu[lp])
                m["w_down2"] = A(ffn2_w_down[lp])
            if ffn1:
                m["g_ffn1"] = A(ffn1_norm[li])
                m["w_gu1"] = A(ffn1_w_gu[li])
                m["w_down1"] = A(ffn1_w_down[li])
            if mix:
                m["g_mix"] = A(mix_norm[li])
                m["w_in"] = A(sb_w_in[li // 2]) if mix == "sb" else A(s5_w_in[li // 2])
                m["memT"] = memT[c // 4]
                m["g_mem"] = A(mem_norm[li])
                m["w_mem_kv"] = A(w_mem_kv[li])
                m["gqk"] = np.ascontiguousarray(np.stack([A(xq_norm[li]), A(xk_norm[li])], axis=1))
            maps.append(m)
        res = _run(nc, maps)
        xT = [res[c]["xT_out"] for c in range(NCORES)]
        if _debug is not None:
            _debug["x_stage%d" % stage] = [a.copy() for a in xT]
        if not mix:
            break
        crossT = [res[c]["crossT_out"] for c in range(NCORES)]
        if mix == "sb":
            maps = []
            for c in range(NCORES):
                qs, ks, vs = [], [], []
                for i in range(NPAIR):
                    p = c * NPAIR + i
                    b, hd = p // 12, p % 12
                    qs.append(np.concatenate([res[b * 4 + qd]["qkT"][hd * 128:(hd + 1) * 128] for qd in range(4)], axis=1))
                    ks.append(np.concatenate([res[b * 4 + qd]["qkT"][(12 + hd) * 128:(13 + hd) * 128] for qd in range(4)], axis=1))
                    vs.append(np.concatenate([res[b * 4 + qd]["v"][:, hd * 128:(hd + 1) * 128] for qd in range(4)], axis=0))
                maps.append({"qT": np.ascontiguousarray(np.stack(qs)), "kT": np.ascontiguousarray(np.stack(ks)),
                             "v": np.ascontiguousarray(np.stack(vs))})
            nc2 = _prog(("sb",), lambda: build_sb_attn())
            r2 = _run(nc2, maps)
            tokT = []
            for c in range(NCORES):
                b, qd = c // 4, c % 4
                rows = []
                for hd in range(12):
                    p = b * 12 + hd
                    rows.append(r2[p // NPAIR]["oT"][p % NPAIR][:, qd * TPC:(qd + 1) * TPC])
                tokT.append(np.ascontiguousarray(np.concatenate(rows, axis=0)))
        else:
            jj = li // 2
            maps = []
            for c in range(NCORES):
                b, gs = c // 4, c % 4
                m = s5_host_params(A(s5_log_dt[jj]), A(s5_a_re[jj]), A(s5_a_im[jj]), A(s5_b_re[jj]), A(s5_b_im[jj]),
                                   A(s5_c_re[jj]), A(s5_c_im[jj]), A(s5_d[jj]), gs)
                m["uT"] = np.ascontiguousarray(
                    np.concatenate([res[b * 4 + qd]["uT"][gs * 384:(gs + 1) * 384] for qd in range(4)], axis=1))
                maps.append(m)
            nc2 = _prog(("s5",), lambda: build_s5())
            r2 = _run(nc2, maps)
            tokT = []
            for c in range(NCORES):
                b, qd = c // 4, c % 4
                tokT.append(np.ascontiguousarray(
                    np.concatenate([r2[b * 4 + gs]["ygT"][:, qd * TPC:(qd + 1) * TPC] for gs in range(4)], axis=0)))
        if _debug is not None:
            _debug["tok_stage%d" % stage] = [a.copy() for a in tokT]
            _debug["cross_stage%d" % stage] = [a.copy() for a in crossT]
    out = np.empty((NB, L, D), f32)
    for c in range(NCORES):
        out[c // 4, (c % 4) * TPC:(c % 4 + 1) * TPC, :] = xT[c].T
    return out


GROUPS = [[0, 1, 2, 3], [4, 5, 6, 7]]
NWB = 8


class Exchange:
    def __init__(self, kb, name, nchunks, rows, cols, dt):
        nc = kb.nc
        self.kb, self.n, self.rows, self.cols = kb, nchunks, rows, cols
        self.snd = nc.dram_tensor(name + "_snd", [nchunks * 4 * rows, cols], dt)
        self.rcv = nc.dram_tensor(name + "_rcv", [nchunks * 16 * rows, cols], dt)
        self.loc = nc.dram_tensor(name + "_loc", [nchunks * 4 * rows, cols], dt)
        self.snd_b = [Buf() for _ in range(nchunks)]
        self.rcv_b = [Buf() for _ in range(nchunks)]
        self.loc_b = Buf()

    def snd_rows(self, c, j, r0=0, n=None):
        n = self.rows if n is None else n
        base = (c * 4 + j) * self.rows + r0
        return self.snd.ap()[base:base + n, :]

    def loc_rows(self, c, r, r0=0, n=None):
        n = self.rows if n is None else n
        base = (c * 4 + r) * self.rows + r0
        return self.loc.ap()[base:base + n, :]

    def run(self, jv):
        kb = self.kb
        kb.coll_chunks(self, GROUPS)
        kb.dma("sp", self.loc.ap().rearrange("(cr x) t -> cr x t", x=self.rows),
               self.rcv.ap().rearrange("(cr j x) t -> cr j x t", j=4, x=self.rows)[:, jv],
               reads=self.rcv_b, writes=[self.loc_b])


def build_fused():
    nc = bass.Bass("TRN2", target_bir_lowering=False)

    def din(name, shape, dt=F32):
        return nc.dram_tensor(name, shape, dt, kind="ExternalInput").ap()

    xT = din("xT", [D, TPC])
    memT = din("memT", [D, NMEM])
    W = {}
    for name, shape in (("ffn1_norm", [DEPTH, D]), ("ffn1_w_gu", [DEPTH, D, 2 * FF]), ("ffn1_w_down", [DEPTH, FF, D]),
                        ("mix_norm", [DEPTH, D]), ("mem_norm", [DEPTH, D]), ("w_mem_kv", [DEPTH, D, 2 * MEMW]),
                        ("gqk", [DEPTH, 128, 2]), ("w_out", [DEPTH, D, D]), ("ffn2_norm", [DEPTH, D]),
                        ("ffn2_w_gu", [DEPTH, D, 2 * FF]), ("ffn2_w_down", [DEPTH, FF, D]),
                        ("sb_w_in", [2, D, 5120]), ("s5_w_in", [2, D, 2048]), ("s5_w_glu", [2, TOKW, TOKW]),
                        ("ldt", [2, 128, NST]), ("are", [2, 128, NST]), ("aim", [2, 128, NST]),
                        ("Bre", [2, 128, NST, 128]), ("Bim", [2, 128, NST, 128]), ("Cre", [2, 128, NST, 128]),
                        ("Cim", [2, 128, NST, 128]), ("dcol", [2, 128, 3])):
        W[name] = din(name, shape)
    yT = nc.dram_tensor("yT", [D, TPC], F32, kind="ExternalOutput").ap()

    xs = nc.dram_tensor("xs", [D, TPC], F32).ap()
    xs_b = [[Buf() for _ in range(KT)] for _ in range(NTB)]
    cross = [nc.dram_tensor("cross%d" % i, [MEMW, TPC], BF16).ap() for i in range(DEPTH)]
    cross_b = [[Buf() for _ in range(NTB)] for _ in range(DEPTH)]

    kb = KB(nc)
    pid = nc.sync.partition_id()
    jv = pid % 4
    ex_tok = None
    kb.wc = {"on": False, "first": True, "idxA": 0, "idxD": 0, "bufsA": [], "bufsD": [],
             "tA": nc.dram_tensor("wcacheA", [352, 128, KT * 128], BF16),
             "tD": nc.dram_tensor("wcacheD", [32, 128, FT * 128], BF16)}

    for li in range(DEPTH + 1):
        lp = li - 1
        post = None if lp < 0 else ("sb" if lp % 2 == 0 else "s5")
        mix = None if li >= DEPTH else ("sb" if li % 2 == 0 else "s5")
        kb.push()
        R = RowRes(kb)
        outs = {}
        if lp >= 0:
            R.load_gains(0, W["ffn2_norm"][lp])
        if li < DEPTH:
            R.load_gains(1, W["ffn1_norm"][li])
        if mix:
            R.load_gains(2, W["mix_norm"][li])
            R.load_gains(3, W["mem_norm"][li])
            kb.dma("sp", R.gqk[:], W["gqk"][li], writes=[R.gqk_b])
            mem_prep(R, memT, W["w_mem_kv"][li], 3)
            outs["crossT_out"] = cross[li]
            if mix == "sb":
                ex_qk = Exchange(kb, "qk%d" % li, 12, 128, 1024, BF16)
                ex_v = Exchange(kb, "v%d" % li, 6, 1024, 128, BF16)

                def qk_dst(nt, tb, ex=ex_qk):
                    a_, h = nt // 12, nt % 12
                    j, i = h // 3, h % 3
                    c = (a_ * 3 + i) * 2 + tb // 2
                    return ex.snd_rows(c, j)[:, (tb % 2) * TB:(tb % 2 + 1) * TB], [ex.snd_b[c]]

                def v_dst(nt, tb, ex=ex_v):
                    j, i = nt // 3, nt % 3
                    c = i * 2 + tb // 2
                    ap = ex.snd_rows(c, j, (tb % 2) * TB, TB).rearrange("(t p) c -> p t c", p=128)
                    return ap, [ex.snd_b[c]]

                outs["qk_dst"] = qk_dst
                outs["v_dst"] = v_dst
            else:
                ex_u = Exchange(kb, "u%d" % li, 12, 128, 512, F32)

                def u_dst(nt, tb, ex=ex_u):
                    j, ct = nt // 3, nt % 3
                    c = ct * 4 + tb
                    return ex.snd_rows(c, j), [ex.snd_b[c]]

                outs["u_dst"] = u_dst
        for tb in range(NTB):
            kb.wc["on"] = True
            kb.wc["first"] = (tb == 0)
            kb.wc["idxA"] = 0
            kb.wc["idxD"] = 0
            if li == 0:
                load_xblk(R, xT, tb)
            else:
                for kt in range(KT):
                    kb.dma("sp", R.xblk[:, kt, :], xs[kt * 128:(kt + 1) * 128, tb * TB:(tb + 1) * TB],
                           reads=[xs_b[tb][kt]], writes=[R.xblk_b[kt]])
            if post:
                def tok_src(kt, tb=tb, ex=ex_tok):
                    r, i = kt // 3, kt % 3
                    c = i * 2 + tb // 2
                    return ex.loc_rows(c, r)[:, (tb % 2) * TB:(tb % 2 + 1) * TB]

                post_block(R, post, None, cross[lp], W["w_out"][lp],
                           W["s5_w_glu"][lp // 2] if post == "s5" else None, tb, tok_src=tok_src,
                           rd=[ex_tok.loc_b, cross_b[lp][tb]])
                ffn_block(R, 0, W["ffn2_w_gu"][lp], W["ffn2_w_down"][lp])
            if li < DEPTH:
                ffn_block(R, 1, W["ffn1_w_gu"][li], W["ffn1_w_down"][li])
            if mix:
                outs["cross_wb"] = [cross_b[li][tb]]
                mixin_block(R, mix, 2, W["sb_w_in"][li // 2] if mix == "sb" else W["s5_w_in"][li // 2], outs, tb)
            if li == DEPTH:
                store_xblk(R, yT, tb)
            else:
                store_xblk(R, xs, tb, dbufs=xs_b[tb])
        kb.wc["on"] = False
        kb.pop()
        if not mix:
            break
        if mix == "sb":
            ex_qk.run(jv)
            ex_v.run(jv)
            ex_o = Exchange(kb, "o%d" % li, 6, 128, 1024, BF16)

            def load_fn(pi, q_s, k_s, v_s, b, ex_qk=ex_qk, ex_v=ex_v):
                for r in range(4):
                    for th in range(2):
                        c0 = r * TPC + th * 1024
                        kb.dma("sp", q_s[:, c0:c0 + 1024], ex_qk.loc_rows(pi * 2 + th, r), reads=[ex_qk.loc_b],
                               writes=[b])
                        kb.dma("sp", k_s[:, c0:c0 + 1024], ex_qk.loc_rows((3 + pi) * 2 + th, r), reads=[ex_qk.loc_b],
                               writes=[b])
                        kb.dma("sp", v_s[:, r * 16 + th * 8:r * 16 + th * 8 + 8, :],
                               ex_v.loc_rows(pi * 2 + th, r).rearrange("(kb p) d -> p kb d", p=128),
                               reads=[ex_v.loc_b], writes=[b])

            def store_fn(pi, qb, o_s, o_b, ex=ex_o):
                qd, tq = qb // 4, qb % 4
                c = pi * 2 + tq // 2
                kb.dma("sp", ex.snd_rows(c, qd)[:, (tq % 2) * QB:(tq % 2 + 1) * QB], o_s[:], reads=[o_b],
                       writes=[ex.snd_b[c]])

            kb.push()
            sb_attn_phase(kb, None, None, None, None, NPAIR, NQB, load_fn=load_fn, store_fn=store_fn)
            kb.pop()
            ex_o.run(jv)
            ex_tok = ex_o
        else:
            ex_u.run(jv)
            ex_y = Exchange(kb, "y%d" % li, 6, 128, 1024, BF16)

            def uload_fn(ch, ct, u_s, u_b, ex=ex_u):
                qd, tq = ch // 4, ch % 4
                kb.dma("sp", u_s[:], ex.loc_rows(ct * 4 + tq, qd), reads=[ex.loc_b], writes=[u_b])

            def ystore_fn(ch, ct, yo, yo_b, ex=ex_y):
                qd, tq = ch // 4, ch % 4
                c = ct * 2 + tq // 2
                kb.dma("sp", ex.snd_rows(c, qd)[:, (tq % 2) * SCH:(tq % 2 + 1) * SCH], yo[:], reads=[yo_b],
                       writes=[ex.snd_b[c]])

            jj = li // 2
            dr = {k: W[k][jj] for k in ("ldt", "are", "aim", "Bre", "Bim", "Cre", "Cim", "dcol")}
            kb.push()
            s5_phase(kb, dr, None, L // SCH, uload_fn=uload_fn, ystore_fn=ystore_fn)
            kb.pop()
            ex_y.run(jv)
            ex_tok = ex_y
    kb.finish()
    return nc


def kernel(x, mem, ffn1_norm, ffn1_w_gu, ffn1_w_down, mix_norm, mem_norm, w_mem_kv, xq_norm, xk_norm, w_out,
           ffn2_norm, ffn2_w_gu, ffn2_w_down, sb_w_in, s5_w_in, s5_log_dt, s5_a_re, s5_a_im, s5_b_re, s5_b_im,
           s5_c_re, s5_c_im, s5_d, s5_w_glu):
    f32 = np.float32
    A = lambda t: np.ascontiguousarray(np.asarray(t, dtype=f32))
    x = A(x)
    mem = A(mem)
    nc = _prog(("fused",), build_fused)
    shared = {
        "ffn1_norm": A(ffn1_norm), "ffn1_w_gu": A(ffn1_w_gu), "ffn1_w_down": A(ffn1_w_down),
        "mix_norm": A(mix_norm), "mem_norm": A(mem_norm), "w_mem_kv": A(w_mem_kv),
        "gqk": np.ascontiguousarray(np.stack([A(xq_norm), A(xk_norm)], axis=2)),
        "w_out": A(w_out), "ffn2_norm": A(ffn2_norm), "ffn2_w_gu": A(ffn2_w_gu), "ffn2_w_down": A(ffn2_w_down),
        "sb_w_in": A(sb_w_in), "s5_w_in": A(s5_w_in), "s5_w_glu": A(s5_w_glu),
    }
    s5p = []
    for gs in range(4):
        per = [s5_host_params(A(s5_log_dt[jj]), A(s5_a_re[jj]), A(s5_a_im[jj]), A(s5_b_re[jj]), A(s5_b_im[jj]),
                              A(s5_c_re[jj]), A(s5_c_im[jj]), A(s5_d[jj]), gs) for jj in range(2)]
        s5p.append({k: np.ascontiguousarray(np.stack([per[0][k], per[1][k]])) for k in per[0]})
    maps = []
    for c in range(NCORES):
        m = dict(shared)
        m.update(s5p[c % 4])
        m["xT"] = np.ascontiguousarray(x[c // 4, (c % 4) * TPC:(c % 4 + 1) * TPC, :].T)
        m["memT"] = np.ascontiguousarray(mem[c // 4].T)
        maps.append(m)
    res = _run(nc, maps)
    out = np.empty((NB, L, D), f32)
    for c in range(NCORES):
        out[c // 4, (c % 4) * TPC:(c % 4 + 1) * TPC, :] = res[c]["yT"].T
    return out
```

```python
import contextlib
import math
import numpy as np
import concourse.bass as bass
import concourse.mybir as mybir
from concourse.bass_utils import run_bass_kernel_spmd

F32 = mybir.dt.float32
BF16 = mybir.dt.bfloat16
I32 = mybir.dt.int32
AF = mybir.ActivationFunctionType
ALU = mybir.AluOpType

D = 2048
L = 8192
NB = 2
DEPTH = 4
FF = 5632
TOKW = 1536
MEMW = 512
NMEM = 256
NCORES = 8
TPC = 2048
TB = 512
NTB = TPC // TB
KT = D // 128
FT = FF // 128
EPS = 1e-6


class Buf:
    __slots__ = ("w", "r")

    def __init__(self):
        self.w = None
        self.r = {}


class Eng:
    def __init__(self, name, h, sem, sid):
        self.name = name
        self.h = h
        self.sem = sem
        self.sid = sid
        self.cnt = 0
        self.waited = {}


class KB:
    NDMA = 40

    def __init__(self, nc):
        self.nc = nc
        self.es = contextlib.ExitStack()
        self.sems = {}
        self.engs = {}
        for name, h in (("pe", nc.tensor), ("act", nc.scalar), ("dve", nc.vector),
                        ("pool", nc.gpsimd), ("sp", nc.sync)):
            sem = self.es.enter_context(nc.semaphore("s_" + name))
            self.sems[name] = sem
            self.engs[name] = Eng(name, h, sem, name)
        self.dsem = []
        self.dcnt = []
        for i in range(self.NDMA):
            sem = self.es.enter_context(nc.semaphore("d%d" % i))
            self.sems[("d", i)] = sem
            self.dsem.append(sem)
            self.dcnt.append(0)
        self.dnext = 0
        self.nalloc = 0
        self.scopes = [self.es]
        self.csem = []
        self.ccnt = []
        self.uid = 0

    def push(self):
        sc = contextlib.ExitStack()
        self.scopes.append(sc)

    def pop(self):
        self.barrier()
        self.scopes.pop().close()

    def barrier(self):
        for E in self.engs.values():
            for name, O in self.engs.items():
                if O is not E:
                    self._wait(E, name, O.cnt)
            for i in range(self.NDMA):
                self._wait(E, ("d", i), self.dcnt[i])
            for ci in range(len(self.csem)):
                self._wait(E, ("c", ci), self.ccnt[ci])

    def coll_chunks(self, ex, groups, chunks=None):
        E = self.engs["pool"]
        if getattr(ex, "ci", None) is None:
            ci = len(self.csem)
            sem = self.es.enter_context(self.nc.semaphore("cc%d" % ci))
            self.csem.append(sem)
            self.ccnt.append(0)
            self.sems[("c", ci)] = sem
            ex.ci = ci
            ex.done = set()
        ci = ex.ci
        sem = self.csem[ci]
        R4 = 4 * ex.rows
        for c in (range(ex.n) if chunks is None else chunks):
            if c in ex.done:
                continue
            ex.done.add(c)
            self._deps(E, [], [ex.snd_b[c], ex.rcv_b[c]], is_dma=True)
            ins = E.h.collective_compute("AllGather", ALU.bypass, replica_groups=groups,
                                         ins=[ex.snd.ap()[c * R4:(c + 1) * R4, :]],
                                         outs=[ex.rcv.ap()[c * 4 * R4:(c + 1) * 4 * R4, :]])
            ins.then_inc(sem)
            self.ccnt[ci] += 1
            self._commit((("c", ci), self.ccnt[ci]), [], [ex.snd_b[c], ex.rcv_b[c]])

    def coll_allgather(self, snd_t, rcv_t, groups, reads=(), writes=()):
        E = self.engs["pool"]
        ci = len(self.csem)
        sem = self.es.enter_context(self.nc.semaphore("cc%d" % ci))
        self.csem.append(sem)
        self.sems[("c", ci)] = sem
        self._deps(E, reads, writes, is_dma=True)
        ins = E.h.collective_compute("AllGather", ALU.bypass, replica_groups=groups,
                                     ins=[snd_t.ap().opt()], outs=[rcv_t.ap().opt()])
        ins.then_inc(sem)
        self.ccnt.append(1)
        self._commit((("c", ci), 1), reads, writes)

    def sb(self, name, shape, dt):
        self.uid += 1
        t = self.scopes[-1].enter_context(self.nc.sbuf_tensor("sb%d_%s" % (self.uid, name), list(shape), dt))
        return t

    def ps(self, name, shape=(128, 512), dt=F32):
        self.uid += 1
        return self.scopes[-1].enter_context(self.nc.psum_tensor("ps%d_%s" % (self.uid, name), list(shape), dt))

    def _wait(self, E, sid, val):
        if E.waited.get(sid, 0) >= val:
            return
        E.h.wait_ge(self.sems[sid], val)
        E.waited[sid] = val

    def _deps(self, E, reads, writes, is_dma=False):
        deps = {}

        def add(tok, kind):
            if tok is None:
                return
            sid, val = tok
            if sid == E.sid and not is_dma:
                if E.name == "pe":
                    return
                if kind != "raw":
                    return
            if deps.get(sid, 0) < val:
                deps[sid] = val

        for b in reads:
            add(b.w, "raw")
        for b in writes:
            add(b.w, "waw")
            for sid, val in b.r.items():
                add((sid, val), "war")
        for sid, val in deps.items():
            self._wait(E, sid, val)

    def _commit(self, tok, reads, writes):
        sid, val = tok
        for b in reads:
            if b.r.get(sid, 0) < val:
                b.r[sid] = val
        for b in writes:
            b.w = tok
            b.r = {}

    def op(self, eng, fn, reads=(), writes=()):
        E = self.engs[eng]
        self._deps(E, reads, writes)
        ins = fn(E.h)
        E.cnt += 1
        ins.then_inc(E.sem, 1)
        self._commit((E.sid, E.cnt), reads, writes)

    def dma(self, eng, out, in_, reads=(), writes=(), wshared=(), **kw):
        E = self.engs[eng]
        i = self.dnext
        self.dnext = (self.dnext + 1) % self.NDMA
        self._deps(E, reads, writes, is_dma=True)
        self._wait(E, ("d", i), self.dcnt[i])
        ins = E.h.dma_start(out=out, in_=in_, **kw)
        self.dcnt[i] += 16
        ins.then_inc(self.dsem[i], 16)
        tok = (("d", i), self.dcnt[i])
        self._commit(tok, reads, writes)
        for b in wshared:
            if b.r.get(tok[0], 0) < tok[1]:
                b.r[tok[0]] = tok[1]

    def finish(self):
        E = self.engs["sp"]
        for i in range(self.NDMA):
            self._wait(E, ("d", i), self.dcnt[i])
        for name in ("pe", "act", "dve", "pool"):
            self._wait(E, name, self.engs[name].cnt)
        for ci in range(len(self.csem)):
            self._wait(E, ("c", ci), self.ccnt[ci])
        while len(self.scopes) > 1:
            self.scopes.pop().close()
        self.es.close()


class RowRes:
    def __init__(self, kb):
        self.kb = kb
        nc = kb.nc
        self.xblk = kb.sb("xblk", (128, KT, TB), F32)
        self.xblk_b = [Buf() for _ in range(KT)]
        self.hT = kb.sb("hT", (128, KT, TB), BF16)
        self.hT_b = [Buf() for _ in range(KT)]
        self.actT = kb.sb("actT", (128, FT, TB), BF16)
        self.actT_b = [Buf() for _ in range(FT)]
        self.NSLAB = 6
        self.slab = [kb.sb("slab%d" % i, (128, KT, 128), BF16) for i in range(self.NSLAB)]
        self.slab_b = [Buf() for _ in range(self.NSLAB)]
        self.wA, self.wA_b = self.slab, self.slab_b
        self.wB, self.wB_b = self.slab, self.slab_b
        self.pctr = 0
        self.sq = [kb.sb("sq%d" % i, (128, TB), F32) for i in range(2)]
        self.sq_b = [Buf() for _ in range(2)]
        self.rstd = kb.sb("rstd", (128, TB), F32)
        self.rstd_b = Buf()
        self.sg = [kb.sb("sg%d" % i, (128, TB), F32) for i in range(2)]
        self.sg_b = [Buf() for _ in range(2)]
        self.gcol = kb.sb("gcol", (128, 8, KT), F32)
        self.gcol_b = Buf()
        self.ones = kb.sb("ones", (128, 128), F32)
        self.ones_b = Buf()
        self.psA = [kb.ps("psA%d" % i) for i in range(2)]
        self.psA_b = [Buf() for _ in range(2)]
        self.psB = [kb.ps("psB%d" % i) for i in range(2)]
        self.psB_b = [Buf() for _ in range(2)]
        self.psO = [kb.ps("psO%d" % i) for i in range(2)]
        self.psO_b = [Buf() for _ in range(2)]
        self.psS = kb.ps("psS")
        self.psS_b = Buf()
        self.epsc = kb.sb("epsc", (128, 1), F32)
        self.ones_bf = kb.sb("ones_bf", (128, 128), BF16)
        kb.op("dve", lambda h: h.memset(self.ones[:], 1.0), writes=[self.ones_b])
        kb.op("dve", lambda h: h.memset(self.epsc[:], EPS), writes=[self.ones_b])
        kb.op("dve", lambda h: h.memset(self.ones_bf[:], 1.0), writes=[self.ones_b])
        self.memh = kb.sb("memh", (128, KT, NMEM), BF16)
        self.memh_b = [Buf() for _ in range(KT)]
        self.KmT = kb.sb("KmT", (128, 4, NMEM), BF16)
        self.Vm = kb.sb("Vm", (128, 2, MEMW), BF16)
        self.kv_b = Buf()
        self.qn = kb.sb("qn", (128, TB), BF16)
        self.qn_b = Buf()
        self.pT = kb.sb("pT", (128, 2, TB), BF16)
        self.pT_b = [Buf() for _ in range(2)]
        self.stg = [kb.sb("stg%d" % i, (128, TB), BF16) for i in range(2)]
        self.stg_b = [Buf() for _ in range(2)]
        self.gqk = kb.sb("gqk", (128, 2), F32)
        self.gqk_b = Buf()
        self.ctr = 0
        self.sctr = 0

    def load_gains(self, slot, g_dram):
        kb = self.kb
        with kb.nc.allow_non_contiguous_dma(reason="tiny gain vector"):
            kb.dma("sp", self.gcol[:, slot, :], g_dram.rearrange("(kt p) -> p kt", p=128),
                   writes=[self.gcol_b])


def rmsnorm_block(R, slot, src, src_b, nkt=KT, dim=D, dst=None, dst_b=None, width=TB):
    kb = R.kb
    if dst is None:
        dst, dst_b = R.hT, R.hT_b
    W = width
    for kt in range(nkt):
        j = kt % 2
        kb.op("act", lambda h, kt=kt, j=j: h.activation(out=R.sq[j][:, 0:W], in_=src[:, kt, 0:W], func=AF.Square),
              reads=[src_b[kt]], writes=[R.sq_b[j]])
        kb.op("pe", lambda h, kt=kt, j=j: h.matmul(R.psS[:, 0:W], R.ones[:], R.sq[j][:, 0:W], start=(kt == 0),
                                                    stop=(kt == nkt - 1)),
              reads=[R.sq_b[j], R.ones_b], writes=[R.psS_b])
    kb.op("act", lambda h: h.activation(out=R.rstd[:, 0:W], in_=R.psS[:, 0:W], func=AF.Ln, scale=1.0 / dim,
                                        bias=EPS),
          reads=[R.psS_b], writes=[R.rstd_b])
    kb.op("act", lambda h: h.activation(out=R.rstd[:, 0:W], in_=R.rstd[:, 0:W], func=AF.Exp, scale=-0.5),
          reads=[R.rstd_b], writes=[R.rstd_b])
    for kt in range(nkt):
        kb.op("dve", lambda h, kt=kt: h.scalar_tensor_tensor(out=dst[:, kt, 0:W], in0=src[:, kt, 0:W],
                                                             scalar=R.gcol[:, slot, kt:kt + 1], in1=R.rstd[:, 0:W],
                                                             op0=ALU.mult, op1=ALU.mult),
              reads=[src_b[kt], R.gcol_b, R.rstd_b], writes=[dst_b[kt]])


def load_slab(kb, dst, dst_b, w_dram, c0, ncols, nkt):
    wc = getattr(kb, "wc", None)
    if wc is None or not wc["on"]:
        src = w_dram[:, c0:c0 + ncols].rearrange("(kt p) n -> p kt n", p=128)
        kb.dma("pool", dst[:, 0:nkt, 0:ncols], src, writes=[dst_b])
        return
    grp = "D" if nkt > KT else "A"
    idx = wc["idx" + grp]
    wc["idx" + grp] += 1
    scr = wc["t" + grp].ap()[idx, :, 0:nkt * 128].rearrange("p (kt n) -> p kt n", n=128)
    bl = wc["bufs" + grp]
    while len(bl) <= idx:
        bl.append(Buf())
    sb_ = bl[idx]
    if wc["first"]:
        src = w_dram[:, c0:c0 + ncols].rearrange("(kt p) n -> p kt n", p=128)
        kb.dma("pool", dst[:, 0:nkt, 0:ncols], src, writes=[dst_b])
        kb.dma("sp", scr, dst[:, 0:nkt, 0:ncols], reads=[dst_b], writes=[sb_])
    else:
        kb.dma("sp", dst[:, 0:nkt, 0:ncols], scr, reads=[sb_], writes=[dst_b])


def ffn_block(R, slot, w_gu, w_down):
    kb = R.kb
    rmsnorm_block(R, slot, R.xblk, R.xblk_b)
    for ft in range(FT):
        ja = next_slab(R)
        jb = next_slab(R)
        p = next_ps(R)
        load_slab(kb, R.slab[ja], R.slab_b[ja], w_gu, ft * 128, 128, KT)
        load_slab(kb, R.slab[jb], R.slab_b[jb], w_gu, FF + ft * 128, 128, KT)
        for kt in range(KT):
            kb.op("pe", lambda h, kt=kt, ja=ja, p=p: h.matmul(R.psA[p][:], R.slab[ja][:, kt, :], R.hT[:, kt, :],
                                                              start=(kt == 0), stop=(kt == KT - 1)),
                  reads=[R.slab_b[ja], R.hT_b[kt]], writes=[R.psA_b[p]])
        for kt in range(KT):
            kb.op("pe", lambda h, kt=kt, jb=jb, p=p: h.matmul(R.psB[p][:], R.slab[jb][:, kt, :], R.hT[:, kt, :],
                                                              start=(kt == 0), stop=(kt == KT - 1)),
                  reads=[R.slab_b[jb], R.hT_b[kt]], writes=[R.psB_b[p]])
        kb.op("act", lambda h, p=p: h.activation(out=R.sg[p][:], in_=R.psA[p][:], func=AF.Silu),
              reads=[R.psA_b[p]], writes=[R.sg_b[p]])
        kb.op("dve", lambda h, p=p, ft=ft: h.tensor_tensor(out=R.actT[:, ft, :], in0=R.psB[p][:], in1=R.sg[p][:],
                                                           op=ALU.mult),
              reads=[R.psB_b[p], R.sg_b[p]], writes=[R.actT_b[ft]])
    for dt in range(KT):
        p = next_ps(R)
        f0 = 0
        while f0 < FT:
            nf = min(KT, FT - f0)
            j = next_slab(R)
            load_slab(kb, R.slab[j], R.slab_b[j], w_down[f0 * 128:(f0 + nf) * 128, :], dt * 128, 128, nf)
            for f in range(nf):
                ft = f0 + f
                kb.op("pe", lambda h, f=f, ft=ft, j=j, p=p: h.matmul(R.psO[p][:], R.slab[j][:, f, :],
                                                                     R.actT[:, ft, :], start=(ft == 0),
                                                                     stop=(ft == FT - 1)),
                      reads=[R.slab_b[j], R.actT_b[ft]], writes=[R.psO_b[p]])
            f0 += nf
        kb.op("dve", lambda h, dt=dt, p=p: h.scalar_tensor_tensor(out=R.xblk[:, dt, :], in0=R.psO[p][:], scalar=0.5,
                                                                  in1=R.xblk[:, dt, :], op0=ALU.mult, op1=ALU.add),
              reads=[R.psO_b[p], R.xblk_b[dt]], writes=[R.xblk_b[dt]])


XG = 4


def load_xblk(R, xT, tb, dbuf=None, dbufs=None):
    kb = R.kb
    for k0 in range(0, KT, XG):
        rd = ([dbuf] if dbuf is not None else []) + (list(dbufs[k0:k0 + XG]) if dbufs is not None else [])
        kb.dma("sp", R.xblk[:, k0:k0 + XG, :],
               xT[k0 * 128:(k0 + XG) * 128, tb * TB:(tb + 1) * TB].rearrange("(k p) t -> p k t", p=128),
               reads=rd, writes=R.xblk_b[k0:k0 + XG])


def store_xblk(R, xT, tb, dbufs=None):
    kb = R.kb
    for k0 in range(0, KT, XG):
        wr = list(dbufs[k0:k0 + XG]) if dbufs is not None else []
        kb.dma("sp", xT[k0 * 128:(k0 + XG) * 128, tb * TB:(tb + 1) * TB].rearrange("(k p) t -> p k t", p=128),
               R.xblk[:, k0:k0 + XG, :], reads=R.xblk_b[k0:k0 + XG], writes=wr)


def build_ffn_only():
    nc = bass.Bass("TRN2", target_bir_lowering=False)
    xT = nc.dram_tensor("xT", [D, TPC], F32, kind="ExternalInput").ap()
    g = nc.dram_tensor("g", [D], F32, kind="ExternalInput").ap()
    w_gu = nc.dram_tensor("w_gu", [D, 2 * FF], F32, kind="ExternalInput").ap()
    w_down = nc.dram_tensor("w_down", [FF, D], F32, kind="ExternalInput").ap()
    yT = nc.dram_tensor("yT", [D, TPC], F32, kind="ExternalOutput").ap()
    kb = KB(nc)
    R = RowRes(kb)
    R.load_gains(0, g)
    for tb in range(NTB):
        load_xblk(R, xT, tb)
        ffn_block(R, 0, w_gu, w_down)
        store_xblk(R, yT, tb)
    kb.finish()
    return nc


NPAIR = 3
QB = 512
NQB = L // QB
NKB = L // 128


def build_sb_attn(npair=NPAIR, nqb=NQB):
    nc = bass.Bass("TRN2", target_bir_lowering=False)
    qT = nc.dram_tensor("qT", [npair, 128, L], BF16, kind="ExternalInput").ap()
    kT = nc.dram_tensor("kT", [npair, 128, L], BF16, kind="ExternalInput").ap()
    v = nc.dram_tensor("v", [npair, L, 128], BF16, kind="ExternalInput").ap()
    oT = nc.dram_tensor("oT", [npair, 128, L], BF16, kind="ExternalOutput").ap()
    kb = KB(nc)
    sb_attn_phase(kb, qT, kT, v, oT, npair, nqb)
    kb.finish()
    return nc


def sb_attn_phase(kb, qT, kT, v, oT, npair, nqb, load_fn=None, store_fn=None):
    nc = kb.nc
    q_s = [kb.sb("q_s%d" % i, (128, L), BF16) for i in range(2)]
    k_s = [kb.sb("k_s%d" % i, (128, L), BF16) for i in range(2)]
    v_s = [kb.sb("v_s%d" % i, (128, NKB, 128), BF16) for i in range(2)]
    qkv_b = [Buf() for _ in range(2)]
    NE = 6
    NZ = 3
    e_s = [kb.sb("e_s%d" % i, (128, QB), F32) for i in range(NE)]
    e_b = [Buf() for _ in range(NE)]
    sp_s = [kb.sb("sp_s%d" % i, (128, QB), BF16) for i in range(NE)]
    sp_b = [Buf() for _ in range(NE)]
    NX = 3
    x_s = [kb.sb("x_s%d" % i, (128, QB), F32) for i in range(NX)]
    x_b = [Buf() for _ in range(NX)]
    w_s = [kb.sb("w_s%d" % i, (128, QB), BF16) for i in range(NX)]
    w_b = [Buf() for _ in range(NX)]
    o_s = [kb.sb("o_s%d" % i, (128, QB), BF16) for i in range(2)]
    o_b = [Buf() for _ in range(2)]
    ones_f = kb.sb("ones_f", (128, 128), F32)
    uinc = kb.sb("uinc", (128, 128), BF16)
    lstr = kb.sb("lstr", (128, 128), BF16)
    c_b = Buf()
    psZ = [kb.ps("psZ%d" % i) for i in range(3)]
    psZ_b = [Buf() for _ in range(3)]
    psP = [kb.ps("psP%d" % i) for i in range(2)]
    psP_b = [Buf() for _ in range(2)]
    psO = [kb.ps("psOa%d" % i) for i in range(2)]
    psO_b = [Buf() for _ in range(2)]

    kb.op("pool", lambda h: h.memset(ones_f[:], 1.0), writes=[c_b])
    kb.op("pool", lambda h: h.affine_select(out=uinc[:], in_=ones_f[:], pattern=[[-1, 128]], compare_op=ALU.is_ge,
                                            fill=0.0, base=0, channel_multiplier=1), reads=[c_b], writes=[c_b])
    kb.op("pool", lambda h: h.affine_select(out=lstr[:], in_=ones_f[:], pattern=[[1, 128]], compare_op=ALU.is_gt,
                                            fill=0.0, base=0, channel_multiplier=-1), reads=[c_b], writes=[c_b])

    def load_pair(pi):
        j = pi % 2
        if load_fn is not None:
            load_fn(pi, q_s[j], k_s[j], v_s[j], qkv_b[j])
            return
        kb.dma("sp", q_s[j][:], qT[pi], writes=[qkv_b[j]])
        kb.dma("sp", k_s[j][:], kT[pi], writes=[qkv_b[j]])
        kb.dma("sp", v_s[j][:], v[pi].rearrange("(kb p) d -> p kb d", p=128), writes=[qkv_b[j]])

    S0 = [0, 3, 4, 7, 8, 11, 12, 15]
    S1 = [1, 2, 5, 6, 9, 10, 13, 14]
    tiles = []
    for pi in range(npair):
        streams = []
        for si, qbs in enumerate((S0, S1)):
            lst = []
            for qb in qbs:
                if qb >= nqb:
                    continue
                nk = 4 * (qb + 1)
                for idx in range(nk):
                    lst.append((pi, qb, nk - 1 - idx, idx, nk, si))
            streams.append(lst)
        n_ = max(len(streams[0]), len(streams[1]))
        for k in range(n_):
            for si in range(2):
                tiles.append(streams[si][k] if k < len(streams[si]) else None)

    tctr = [0]

    def stage1(t):
        pi, qb, kbi, idx, nk, g = t
        j = pi % 2
        n = tctr[0] % NE
        z = tctr[0] % NZ
        tctr[0] += 1
        kb.op("pe", lambda h: h.matmul(psZ[z][:], k_s[j][:, kbi * 128:(kbi + 1) * 128],
                                       q_s[j][:, qb * QB:(qb + 1) * QB], start=True, stop=True),
              reads=[qkv_b[j]], writes=[psZ_b[z]])
        kb.op("act", lambda h: h.activation(out=e_s[n][:], in_=psZ[z][:], func=AF.Exp),
              reads=[psZ_b[z]], writes=[e_b[n]])
        kb.op("act", lambda h: h.activation(out=sp_s[n][:], in_=e_s[n][:], func=AF.Ln, bias=1.0),
              reads=[e_b[n]], writes=[sp_b[n]])
        dj = kbi - 4 * qb
        if dj >= 0:
            kb.op("pool", lambda h: h.affine_select(out=sp_s[n][:], in_=sp_s[n][:], pattern=[[1, QB]],
                                                    compare_op=ALU.is_gt, fill=0.0, base=-128 * dj,
                                                    channel_multiplier=-1),
                  reads=[sp_b[n]], writes=[sp_b[n]])
        return n

    xctr = [0]

    def stageB1(t, n):
        pi, qb, kbi, idx, nk, g = t
        xi = xctr[0] % NX
        xctr[0] += 1
        kb.op("pe", lambda h: h.matmul(psP[g][:], uinc[:], sp_s[n][:], start=(idx == 0), stop=(idx == nk - 1)),
              reads=[sp_b[n], c_b], writes=[psP_b[g]])
        kb.op("act", lambda h: h.activation(out=x_s[xi][:], in_=psP[g][:], func=AF.Exp, scale=-1.0),
              reads=[psP_b[g]], writes=[x_b[xi]])
        return xi

    def stageB2(t, n):
        pi, qb, kbi, idx, nk, g = t
        if idx != nk - 1:
            kb.op("pe", lambda h: h.matmul(psP[g][:], lstr[:], sp_s[n][:], start=False, stop=False),
                  reads=[sp_b[n], c_b], writes=[psP_b[g]])

    def stageC(t, n, xi):
        pi, qb, kbi, idx, nk, g = t
        j = pi % 2
        kb.op("dve", lambda h: h.tensor_tensor(out=w_s[xi][:], in0=e_s[n][:], in1=x_s[xi][:], op=ALU.mult),
              reads=[e_b[n], x_b[xi]], writes=[w_b[xi]])
        dj = kbi - 4 * qb
        if dj >= 0:
            kb.op("pool", lambda h: h.affine_select(out=w_s[xi][:], in_=w_s[xi][:], pattern=[[1, QB]],
                                                    compare_op=ALU.is_gt, fill=0.0, base=-128 * dj,
                                                    channel_multiplier=-1),
                  reads=[w_b[xi]], writes=[w_b[xi]])
        kb.op("pe", lambda h: h.matmul(psO[g][:], v_s[j][:, kbi, :], w_s[xi][:], start=(idx == 0),
                                       stop=(idx == nk - 1)),
              reads=[w_b[xi], qkv_b[j]], writes=[psO_b[g]])
        if idx == nk - 1:
            kb.op("act", lambda h: h.activation(out=o_s[g][:], in_=psO[g][:], func=AF.Copy),
                  reads=[psO_b[g]], writes=[o_b[g]])
            if store_fn is not None:
                store_fn(pi, qb, o_s[g], o_b[g])
            else:
                kb.dma("sp", oT[pi, :, qb * QB:(qb + 1) * QB], o_s[g][:], reads=[o_b[g]])

    load_pair(0)
    loaded = 1
    pipe = [None, None, None, None]
    seen_pairs = set()

    def step(new):
        a1, a2, b1, b2 = pipe
        nb1 = None
        if a2 is not None:
            t_, n_ = a2
            xi = stageB1(t_, n_)
            nb1 = (t_, n_, xi)
        nb2 = None
        if b1 is not None:
            stageB2(b1[0], b1[1])
            nb2 = b1
        if b2 is not None:
            stageC(*b2)
        pipe[0], pipe[1], pipe[2], pipe[3] = new, a1, nb1, nb2

    cur_pair = 0
    for t in tiles + [("end",)]:
        if t is None:
            step(None)
            continue
        if t[0] != cur_pair or t[0] == "end":
            for _ in range(4):
                step(None)
            if t[0] == "end":
                break
            cur_pair = t[0]
        if t[0] not in seen_pairs:
            seen_pairs.add(t[0])
            if loaded < npair and loaded <= t[0] + 1:
                load_pair(loaded)
                loaded += 1
        n = stage1(t)
        step((t, n))


NST = 12
SCH = 512
TWO_PI = 2.0 * math.pi
GELU_C = 2.0 * math.sqrt(2.0 / math.pi)


def build_s5(nchunks=L // SCH):
    nc = bass.Bass("TRN2", target_bir_lowering=False)
    dr = {}
    for name, shape in (("uT", [384, L]), ("ldt", [128, NST]), ("are", [128, NST]), ("aim", [128, NST]),
                        ("Bre", [128, NST, 128]), ("Bim", [128, NST, 128]), ("Cre", [128, NST, 128]),
                        ("Cim", [128, NST, 128]), ("dcol", [128, 3])):
        dr[name] = nc.dram_tensor(name, shape, F32, kind="ExternalInput").ap()
    ygT = nc.dram_tensor("ygT", [384, L], BF16, kind="ExternalOutput").ap()
    kb = KB(nc)
    s5_phase(kb, dr, ygT, nchunks)
    kb.finish()
    return nc


def s5_phase(kb, dr, ygT, nchunks, uload_fn=None, ystore_fn=None):
    def small(name, w=NST):
        return kb.sb(name, (128, w), F32), Buf()

    def tt(eng, out, a, b, op, reads, writes):
        kb.op(eng, lambda h: h.tensor_tensor(out=out, in0=a, in1=b, op=op), reads=reads, writes=writes)

    def ts(eng, out, a, s1, op0, reads, writes, s2=None, op1=None):
        if op1 is None:
            kb.op(eng, lambda h: h.tensor_scalar(out=out, in0=a, scalar1=s1, scalar2=None, op0=op0),
                  reads=reads, writes=writes)
        else:
            kb.op(eng, lambda h: h.tensor_scalar(out=out, in0=a, scalar1=s1, scalar2=s2, op0=op0, op1=op1),
                  reads=reads, writes=writes)

    def stt(eng, out, a, s, b, op0, op1, reads, writes):
        kb.op(eng, lambda h: h.scalar_tensor_tensor(out=out, in0=a, scalar=s, in1=b, op0=op0, op1=op1),
              reads=reads, writes=writes)

    def act(out, a, func, reads, writes, **kw):
        kb.op("act", lambda h: h.activation(out=out, in_=a, func=func, **kw), reads=reads, writes=writes)

    P = {}
    for name in ("ldt", "are", "aim"):
        P[name] = small("p_" + name)
        kb.dma("sp", P[name][0][:], dr[name], writes=[P[name][1]])
    Bre = kb.sb("Bre_s", (128, NST, 128), F32)
    Bim = kb.sb("Bim_s", (128, NST, 128), F32)
    Cre = kb.sb("Cre_s", (128, NST, 128), F32)
    Cim = kb.sb("Cim_s", (128, NST, 128), F32)
    C2re = kb.sb("C2re", (128, NST, 128), F32)
    C2im = kb.sb("C2im", (128, NST, 128), F32)
    C2ren = kb.sb("C2ren", (128, NST, 128), F32)
    dcol = kb.sb("dcol_s", (128, 3), F32)
    par_b = Buf()
    c2_b = Buf()
    for t, name in ((Bre, "Bre"), (Bim, "Bim"), (Cre, "Cre"), (Cim, "Cim"), (dcol, "dcol")):
        kb.dma("sp", t[:], dr[name], writes=[par_b])

    dt_, dt_b = small("dt_")
    act(dt_[:], P["ldt"][0][:], AF.Exp, [P["ldt"][1]], [dt_b])
    rl, rl_b = small("rl")
    th, th_b = small("th")
    tt("dve", rl[:], P["are"][0][:], dt_[:], ALU.mult, [P["are"][1], dt_b], [rl_b])
    tt("dve", th[:], P["aim"][0][:], dt_[:], ALU.mult, [P["aim"][1], dt_b], [th_b])
    r_, r_b = small("r_")
    act(r_[:], rl[:], AF.Exp, [rl_b], [r_b])
    phi, phi_b = small("phi")
    ts("dve", phi[:], th[:], 1.0 / TWO_PI, ALU.mult, [th_b], [phi_b])
    ki = kb.sb("ki", (128, NST), I32)
    ki_b = Buf()
    kb.op("dve", lambda h: h.tensor_copy(out=ki[:], in_=phi[:]), reads=[phi_b], writes=[ki_b])
    kf, kf_b = small("kf")
    kb.op("dve", lambda h: h.tensor_copy(out=kf[:], in_=ki[:]), reads=[ki_b], writes=[kf_b])
    f_, f_b = small("f_")
    tt("dve", f_[:], phi[:], kf[:], ALU.subtract, [phi_b, kf_b], [f_b])
    cos1, cos1_b = small("cos1")
    sin1, sin1_b = small("sin1")
    tmpa, tmpa_b = small("tmpa")
    tmpb, tmpb_b = small("tmpb")

    def sin_of_frac(out, out_b, frac, frac_b):
        ts("dve", tmpa[:], frac, 0.5, ALU.is_gt, [frac_b], [tmpa_b])
        tt("dve", tmpb[:], frac, tmpa[:], ALU.subtract, [frac_b, tmpa_b], [tmpb_b])
        ts("dve", tmpa[:], tmpb[:], -0.5, ALU.is_lt, [tmpb_b], [tmpa_b])
        tt("dve", tmpb[:], tmpb[:], tmpa[:], ALU.add, [tmpb_b, tmpa_b], [tmpb_b])
        act(out, tmpb[:], AF.Sin, [tmpb_b], [out_b], scale=TWO_PI)

    sin_of_frac(sin1[:], sin1_b, f_[:], f_b)
    fc, fc_b = small("fc")
    ts("dve", fc[:], f_[:], 0.25, ALU.add, [f_b], [fc_b])
    sin_of_frac(cos1[:], cos1_b, fc[:], fc_b)

    p_, p_b = small("p_")
    q_, q_b = small("q_")
    tt("dve", p_[:], r_[:], cos1[:], ALU.mult, [r_b, cos1_b], [p_b])
    ts("dve", p_[:], p_[:], -1.0, ALU.add, [p_b], [p_b])
    tt("dve", q_[:], r_[:], sin1[:], ALU.mult, [r_b, sin1_b], [q_b])
    den, den_b = small("den")
    t1, t1_b = small("t1")
    t2, t2_b = small("t2")
    are, are_b = P["are"]
    aim, aim_b = P["aim"]
    tt("dve", den[:], are[:], are[:], ALU.mult, [are_b], [den_b])
    tt("dve", t1[:], aim[:], aim[:], ALU.mult, [aim_b], [t1_b])
    tt("dve", den[:], den[:], t1[:], ALU.add, [den_b, t1_b], [den_b])
    kb.op("dve", lambda h: h.reciprocal(out=den[:], in_=den[:]), reads=[den_b], writes=[den_b])
    gre, gre_b = small("gre")
    gim, gim_b = small("gim")
    tt("dve", t1[:], p_[:], are[:], ALU.mult, [p_b, are_b], [t1_b])
    tt("dve", t2[:], q_[:], aim[:], ALU.mult, [q_b, aim_b], [t2_b])
    tt("dve", gre[:], t1[:], t2[:], ALU.add, [t1_b, t2_b], [gre_b])
    tt("dve", gre[:], gre[:], den[:], ALU.mult, [gre_b, den_b], [gre_b])
    tt("dve", t1[:], q_[:], are[:], ALU.mult, [q_b, are_b], [t1_b])
    tt("dve", t2[:], p_[:], aim[:], ALU.mult, [p_b, aim_b], [t2_b])
    tt("dve", gim[:], t1[:], t2[:], ALU.subtract, [t1_b, t2_b], [gim_b])
    tt("dve", gim[:], gim[:], den[:], ALU.mult, [gim_b, den_b], [gim_b])
    zre, zre_b = small("zre")
    zim, zim_b = small("zim")
    nzim, nzim_b = small("nzim")
    tt("dve", t1[:], gre[:], cos1[:], ALU.mult, [gre_b, cos1_b], [t1_b])
    tt("dve", t2[:], gim[:], sin1[:], ALU.mult, [gim_b, sin1_b], [t2_b])
    tt("dve", zre[:], t1[:], t2[:], ALU.add, [t1_b, t2_b], [zre_b])
    tt("dve", t1[:], gim[:], cos1[:], ALU.mult, [gim_b, cos1_b], [t1_b])
    tt("dve", t2[:], gre[:], sin1[:], ALU.mult, [gre_b, sin1_b], [t2_b])
    tt("dve", zim[:], t1[:], t2[:], ALU.subtract, [t1_b, t2_b], [zim_b])
    ts("dve", nzim[:], zim[:], -1.0, ALU.mult, [zim_b], [nzim_b])

    ctmp = kb.sb("ctmp", (128, 128), F32)
    ctmp_b = Buf()
    for st in range(NST):
        ts("dve", ctmp[:], Cim[:, st, :], zim[:, st:st + 1], ALU.mult, [par_b, zim_b], [ctmp_b])
        stt("dve", C2re[:, st, :], Cre[:, st, :], zre[:, st:st + 1], ctmp[:], ALU.mult, ALU.subtract,
            [par_b, zre_b, ctmp_b], [c2_b])
        ts("dve", ctmp[:], Cim[:, st, :], zre[:, st:st + 1], ALU.mult, [par_b, zre_b], [ctmp_b])
        stt("dve", C2im[:, st, :], Cre[:, st, :], nzim[:, st:st + 1], ctmp[:], ALU.mult, ALU.subtract,
            [par_b, nzim_b, ctmp_b], [c2_b])
        ts("dve", C2ren[:, st, :], C2re[:, st, :], -1.0, ALU.mult, [c2_b], [c2_b])

    TW = SCH + 8
    tabC = kb.sb("tabC", (128, NST, TW), F32)
    tabS = kb.sb("tabS", (128, NST, TW), F32)
    tab_b = [Buf() for _ in range(NST)]
    ttmp = kb.sb("ttmp", (128, 256), F32)
    ttmp_b = Buf()
    rt = kb.sb("rt", (128, NST, SCH), F32)
    rt_b = Buf()
    onesw = kb.sb("onesw", (128, SCH), F32)
    onesw_b = Buf()
    kb.op("pool", lambda h: h.memset(onesw[:], 1.0), writes=[onesw_b])
    for st in range(NST):
        b = tab_b[st]
        kb.op("pool", lambda h, st=st: h.memset(tabC[:, st, 0:1], 1.0), writes=[b])
        kb.op("pool", lambda h, st=st: h.memset(tabS[:, st, 0:1], 0.0), writes=[b])
        kb.op("act", lambda h, st=st: h.activation(out=tabC[:, st, 1:2], in_=cos1[:, st:st + 1], func=AF.Copy),
              reads=[cos1_b], writes=[b])
        kb.op("act", lambda h, st=st: h.activation(out=tabS[:, st, 1:2], in_=sin1[:, st:st + 1], func=AF.Copy),
              reads=[sin1_b], writes=[b])
        m = 1
        while m < SCH:
            cm = tabC[:, st, m:m + 1]
            sm = tabS[:, st, m:m + 1]
            ts("dve", ttmp[:, 0:m], tabS[:, st, 1:m + 1], sm, ALU.mult, [b], [ttmp_b])
            stt("dve", tabC[:, st, m + 1:2 * m + 1], tabC[:, st, 1:m + 1], cm, ttmp[:, 0:m], ALU.mult, ALU.subtract,
                [b, ttmp_b], [b])
            ts("dve", ttmp[:, 0:m], tabC[:, st, 1:m + 1], sm, ALU.mult, [b], [ttmp_b])
            stt("dve", tabS[:, st, m + 1:2 * m + 1], tabS[:, st, 1:m + 1], cm, ttmp[:, 0:m], ALU.mult, ALU.add,
                [b, ttmp_b], [b])
            m *= 2
        ts("pool", rt[:, st, :], onesw[:], r_[:, st:st + 1], ALU.mult, [onesw_b, r_b], [rt_b])

    u_s = [kb.sb("u_s%d" % i, (128, SCH), F32) for i in range(2)]
    u_b = [Buf() for _ in range(2)]
    NR = 2
    m_s = [[kb.sb("m%d_%d" % (k, i), (128, SCH), F32) for k in range(4)] for i in range(NR)]
    m_b = [[Buf() for k in range(4)] for i in range(NR)]
    wv_s = [[kb.sb("wv%d_%d" % (k, i), (128, SCH), F32) for k in range(2)] for i in range(NR)]
    wv_b = [[Buf() for k in range(2)] for i in range(NR)]
    n_s = [[kb.sb("n%d_%d" % (k, i), (128, SCH), F32) for k in range(4)] for i in range(NR)]
    n_b = [[Buf() for k in range(4)] for i in range(NR)]
    car_r = kb.sb("car_r", (128, NST), F32)
    car_i = kb.sb("car_i", (128, NST), F32)
    car_b = [Buf() for _ in range(NST)]
    yv = kb.sb("yv", (128, SCH), F32)
    yv_b = Buf()
    e1 = kb.sb("e1", (128, SCH), F32)
    e1_b = Buf()
    e2 = kb.sb("e2", (128, SCH), F32)
    e2_b = Buf()
    yo = [kb.sb("yo%d" % i, (128, SCH), BF16) for i in range(2)]
    yo_b = [Buf() for _ in range(2)]
    psR = [kb.ps("psR%d" % i) for i in range(2)]
    psR_b = [Buf() for _ in range(2)]
    psI = [kb.ps("psI%d" % i) for i in range(2)]
    psI_b = [Buf() for _ in range(2)]
    psY = [kb.ps("psY%d" % i) for i in range(2)]
    psY_b = [Buf() for _ in range(2)]
    psW = [kb.ps("psW%d" % i) for i in range(2)]
    psW_b = [Buf() for _ in range(2)]

    Bre16 = kb.sb("Bre16", (128, NST, 128), BF16)
    Bim16 = kb.sb("Bim16", (128, NST, 128), BF16)
    C2re16 = kb.sb("C2re16", (128, NST, 128), BF16)
    C2ren16 = kb.sb("C2ren16", (128, NST, 128), BF16)
    C2im16 = kb.sb("C2im16", (128, NST, 128), BF16)
    t16_b = Buf()
    for dst, src, sb_ in ((Bre16, Bre, par_b), (Bim16, Bim, par_b), (C2re16, C2re, c2_b), (C2ren16, C2ren, c2_b),
                          (C2im16, C2im, c2_b)):
        kb.op("pool", lambda h, dst=dst, src=src: h.tensor_copy(out=dst[:], in_=src[:]), reads=[sb_], writes=[t16_b])
    u16 = [kb.sb("u16_%d" % i, (128, SCH), BF16) for i in range(2)]
    u16_b = [Buf() for _ in range(2)]
    n16 = [[kb.sb("n16_%d_%d" % (k, i), (128, SCH), BF16) for k in range(4)] for i in range(NR)]
    n16_b = [[Buf() for k in range(4)] for i in range(NR)]
    ctmp4 = kb.sb("ctmp4", (128, 4), F32)
    ctmp4_b = Buf()

    iters = [(ch, ct, sl) for ch in range(nchunks) for ct in range(3) for sl in range(4)]

    def P(k):
        ch, ct, sl = iters[k]
        ui = (ch * 3 + ct) % 2
        if sl == 0:
            if uload_fn is not None:
                uload_fn(ch, ct, u_s[ui], u_b[ui])
            else:
                kb.dma("sp", u_s[ui][:], dr["uT"][ct * 128:(ct + 1) * 128, ch * SCH:(ch + 1) * SCH],
                       writes=[u_b[ui]])
            kb.op("act", lambda h: h.activation(out=u16[ui][:], in_=u_s[ui][:], func=AF.Copy),
                  reads=[u_b[ui]], writes=[u16_b[ui]])
        st = ct * 4 + sl
        i = k % 2
        kb.op("pe", lambda h: h.matmul(psR[i][:], Bre16[:, st, :], u16[ui][:], start=True, stop=True),
              reads=[t16_b, u16_b[ui]], writes=[psR_b[i]])
        kb.op("pe", lambda h: h.matmul(psI[i][:], Bim16[:, st, :], u16[ui][:], start=True, stop=True),
              reads=[t16_b, u16_b[ui]], writes=[psI_b[i]])

    def Dk(k):
        ch, ct, sl = iters[k]
        ui = (ch * 3 + ct) % 2
        yi = (ch * 3 + ct) % 2
        st = ct * 4 + sl
        i = k % 2
        pR, pI, pRb, pIb = psR[i], psI[i], psR_b[i], psI_b[i]
        m, mb = m_s[i], m_b[i]
        Cc = tabC[:, st, 0:SCH]
        Ss = tabS[:, st, 0:SCH]
        tb_ = tab_b[st]
        tt("dve", m[0][:], pR[:], Cc, ALU.mult, [pRb, tb_], [mb[0]])
        tt("dve", m[1][:], pI[:], Ss, ALU.mult, [pIb, tb_], [mb[1]])
        tt("dve", m[2][:], pI[:], Cc, ALU.mult, [pIb, tb_], [mb[2]])
        tt("dve", m[3][:], pR[:], Ss, ALU.mult, [pRb, tb_], [mb[3]])
        tt("pool", m[0][:], m[0][:], m[1][:], ALU.add, [mb[0], mb[1]], [mb[0]])
        tt("pool", m[2][:], m[2][:], m[3][:], ALU.subtract, [mb[2], mb[3]], [mb[2]])
        if ch == 0:
            ini_r = 0.0
            ini_i = 0.0
        else:
            ini_r = car_r[:, st:st + 1]
            ini_i = car_i[:, st:st + 1]
        kb.op("dve", lambda h: h.tensor_tensor_scan(out=psW[0][:], data0=rt[:, st, :], data1=m[0][:],
                                                    initial=ini_r, op0=ALU.mult, op1=ALU.add),
              reads=[rt_b, mb[0], car_b[st]], writes=[psW_b[0]])
        kb.op("dve", lambda h: h.tensor_tensor_scan(out=psW[1][:], data0=rt[:, st, :], data1=m[2][:],
                                                    initial=ini_i, op0=ALU.mult, op1=ALU.add),
              reads=[rt_b, mb[2], car_b[st]], writes=[psW_b[1]])
        n, nb = n16[i], n16_b[i]
        C1 = tabC[:, st, 1:SCH + 1]
        S1 = tabS[:, st, 1:SCH + 1]
        tt("dve", n[0][:], psW[0][:], C1, ALU.mult, [psW_b[0], tb_], [nb[0]])
        tt("dve", n[1][:], psW[1][:], S1, ALU.mult, [psW_b[1], tb_], [nb[1]])
        tt("dve", n[2][:], psW[0][:], S1, ALU.mult, [psW_b[0], tb_], [nb[2]])
        tt("dve", n[3][:], psW[1][:], C1, ALU.mult, [psW_b[1], tb_], [nb[3]])
        L1 = slice(SCH - 1, SCH)
        cl = tabC[:, st, SCH:SCH + 1]
        sl_ = tabS[:, st, SCH:SCH + 1]
        tt("dve", ctmp4[:, 0:1], psW[0][:, L1], cl, ALU.mult, [psW_b[0], tb_], [ctmp4_b])
        tt("dve", ctmp4[:, 1:2], psW[1][:, L1], sl_, ALU.mult, [psW_b[1], tb_], [ctmp4_b])
        tt("dve", ctmp4[:, 2:3], psW[0][:, L1], sl_, ALU.mult, [psW_b[0], tb_], [ctmp4_b])
        tt("dve", ctmp4[:, 3:4], psW[1][:, L1], cl, ALU.mult, [psW_b[1], tb_], [ctmp4_b])
        tt("dve", car_r[:, st:st + 1], ctmp4[:, 0:1], ctmp4[:, 1:2], ALU.subtract, [ctmp4_b], [car_b[st]])
        tt("dve", car_i[:, st:st + 1], ctmp4[:, 2:3], ctmp4[:, 3:4], ALU.add, [ctmp4_b], [car_b[st]])
        kb.op("pe", lambda h: h.matmul(psY[yi][:], C2re16[:, st, :], n[0][:], start=(sl == 0), stop=False),
              reads=[t16_b, nb[0]], writes=[psY_b[yi]])
        kb.op("pe", lambda h: h.matmul(psY[yi][:], C2ren16[:, st, :], n[1][:], start=False, stop=False),
              reads=[t16_b, nb[1]], writes=[psY_b[yi]])
        kb.op("pe", lambda h: h.matmul(psY[yi][:], C2im16[:, st, :], n[2][:], start=False, stop=False),
              reads=[t16_b, nb[2]], writes=[psY_b[yi]])
        kb.op("pe", lambda h: h.matmul(psY[yi][:], C2im16[:, st, :], n[3][:], start=False, stop=(sl == 3)),
              reads=[t16_b, nb[3]], writes=[psY_b[yi]])
        if sl == 3:
            stt("dve", yv[:], u_s[ui][:], dcol[:, ct:ct + 1], psY[yi][:], ALU.mult, ALU.add,
                [u_b[ui], par_b, psY_b[yi]], [yv_b])
            act(e1[:], yv[:], AF.Square, [yv_b], [e1_b])
            ts("dve", e1[:], e1[:], 0.044715, ALU.mult, [e1_b], [e1_b], s2=1.0, op1=ALU.add)
            tt("pool", e2[:], e1[:], yv[:], ALU.mult, [e1_b, yv_b], [e2_b])
            act(e2[:], e2[:], AF.Sigmoid, [e2_b], [e2_b], scale=GELU_C)
            tt("pool", yo[yi][:], e2[:], yv[:], ALU.mult, [e2_b, yv_b], [yo_b[yi]])
            if ystore_fn is not None:
                ystore_fn(ch, ct, yo[yi], yo_b[yi])
            else:
                kb.dma("sp", ygT[ct * 128:(ct + 1) * 128, ch * SCH:(ch + 1) * SCH], yo[yi][:], reads=[yo_b[yi]])

    P(0)
    for k in range(len(iters)):
        if k + 1 < len(iters):
            P(k + 1)
        Dk(k)


def s5_host_params(log_dt, a_re, a_im, b_re, b_im, c_re, c_im, d, gs):
    g0 = gs * 24
    out = {}

    def per_state(a):
        return np.ascontiguousarray(a.reshape(NST, 2, 64).transpose(1, 2, 0).reshape(128, NST))

    out["ldt"] = per_state(np.repeat(log_dt[g0:g0 + 24, None], 64, axis=1))
    out["are"] = per_state(a_re[g0:g0 + 24])
    out["aim"] = per_state(a_im[g0:g0 + 24])
    Bre = np.zeros((128, NST, 128), np.float32)
    Bim = np.zeros((128, NST, 128), np.float32)
    Cre = np.zeros((128, NST, 128), np.float32)
    Cim = np.zeros((128, NST, 128), np.float32)
    for st in range(NST):
        for gl in range(2):
            g = g0 + 2 * st + gl
            r0 = (2 * (st % 4) + gl) * 16
            Bre[r0:r0 + 16, st, gl * 64:(gl + 1) * 64] = b_re[g].T
            Bim[r0:r0 + 16, st, gl * 64:(gl + 1) * 64] = b_im[g].T
            Cre[gl * 64:(gl + 1) * 64, st, r0:r0 + 16] = c_re[g].T
            Cim[gl * 64:(gl + 1) * 64, st, r0:r0 + 16] = c_im[g].T
    out["Bre"], out["Bim"], out["Cre"], out["Cim"] = Bre, Bim, Cre, Cim
    out["dcol"] = np.ascontiguousarray(d[g0 * 16:(g0 + 24) * 16].reshape(3, 128).T)
    return out


ISQ = 1.0 / math.sqrt(128.0)


def gemm16(R, ps, ps_b, slab, slab_b, rhs, rhs_b, nkt=KT, width=TB, col0=0):
    kb = R.kb
    for kt in range(nkt):
        kb.op("pe", lambda h, kt=kt: h.matmul(ps[:, col0:col0 + width], slab[:, kt, :], rhs[:, kt, 0:width],
                                              start=(kt == 0), stop=(kt == nkt - 1)),
              reads=[slab_b, rhs_b[kt]], writes=[ps_b])


def next_slab(R):
    j = R.ctr % R.NSLAB
    R.ctr += 1
    return j


def next_ps(R):
    p = R.pctr % 2
    R.pctr += 1
    return p


def head_rstd(R, ps, ps_b, width):
    kb = R.kb
    kb.op("act", lambda h: h.activation(out=R.sq[0][:, 0:width], in_=ps[:, 0:width], func=AF.Square),
          reads=[ps_b], writes=[R.sq_b[0]])
    kb.op("pe", lambda h: h.matmul(R.psS[:, 0:width], R.ones[:], R.sq[0][:, 0:width], start=True, stop=True),
          reads=[R.sq_b[0], R.ones_b], writes=[R.psS_b])
    kb.op("act", lambda h: h.activation(out=R.rstd[:, 0:width], in_=R.psS[:, 0:width], func=AF.Ln, scale=1.0 / 128,
                                        bias=EPS), reads=[R.psS_b], writes=[R.rstd_b])
    kb.op("act", lambda h: h.activation(out=R.rstd[:, 0:width], in_=R.rstd[:, 0:width], func=AF.Exp, scale=-0.5),
          reads=[R.rstd_b], writes=[R.rstd_b])


def mem_prep(R, memT, w_mem_kv, slot_mem):
    kb = R.kb
    for kt in range(KT):
        kb.dma("sp", R.xblk[:, kt, 0:NMEM], memT[kt * 128:(kt + 1) * 128, :], writes=[R.xblk_b[kt]])
    rmsnorm_block(R, slot_mem, R.xblk, R.xblk_b, dst=R.memh, dst_b=R.memh_b, width=NMEM)
    for hd in range(4):
        j = next_slab(R)
        p = next_ps(R)
        load_slab(kb, R.wA[j], R.wA_b[j], w_mem_kv, hd * 128, 128, KT)
        gemm16(R, R.psA[p], R.psA_b[p], R.wA[j], R.wA_b[j], R.memh, R.memh_b, width=NMEM)
        head_rstd(R, R.psA[p], R.psA_b[p], NMEM)
        kb.op("dve", lambda h, j=j, p=p, hd=hd: h.scalar_tensor_tensor(out=R.KmT[:, hd, :], in0=R.psA[p][:, 0:NMEM],
                                                                  scalar=R.gqk[:, 1:2], in1=R.rstd[:, 0:NMEM],
                                                                  op0=ALU.mult, op1=ALU.mult),
              reads=[R.psA_b[p], R.gqk_b, R.rstd_b], writes=[R.kv_b])
    for hd in range(4):
        j = next_slab(R)
        p = next_ps(R)
        load_slab(kb, R.wA[j], R.wA_b[j], w_mem_kv, MEMW + hd * 128, 128, KT)
        for mt in range(2):
            for kt in range(KT):
                kb.op("pe", lambda h, kt=kt, mt=mt, j=j, p=p: h.matmul(R.psB[p][:, mt * 128:(mt + 1) * 128],
                                                                  R.memh[:, kt, mt * 128:(mt + 1) * 128],
                                                                  R.wA[j][:, kt, :], start=(kt == 0),
                                                                  stop=(kt == KT - 1)),
                      reads=[R.wA_b[j], R.memh_b[kt]], writes=[R.psB_b[p]])
        for mt in range(2):
            kb.op("act", lambda h, mt=mt, j=j, p=p, hd=hd: h.activation(out=R.Vm[:, mt, hd * 128:(hd + 1) * 128],
                                                                   in_=R.psB[p][:, mt * 128:(mt + 1) * 128],
                                                                   func=AF.Copy),
                  reads=[R.psB_b[p]], writes=[R.kv_b])


def _wb(outs):
    lst = outs.get("wb_list")
    if not lst:
        return []
    k = outs["wb_ctr"][0]
    outs["wb_ctr"][0] = k + 1
    return [lst[k % len(lst)]]


def mixin_block(R, kind, slot_mix, w_in, outs, tb):
    kb = R.kb
    tsl = slice(tb * TB, (tb + 1) * TB)
    rmsnorm_block(R, slot_mix, R.xblk, R.xblk_b)
    if kind == "sb":
        for nt in range(24):
            j = next_slab(R)
            p = next_ps(R)
            load_slab(kb, R.wA[j], R.wA_b[j], w_in, nt * 128, 128, KT)
            gemm16(R, R.psA[p], R.psA_b[p], R.wA[j], R.wA_b[j], R.hT, R.hT_b)
            s = R.sctr % 2
            R.sctr += 1
            kb.op("act", lambda h, j=j, p=p, s=s, nt=nt: h.activation(out=R.stg[s][:], in_=R.psA[p][:], func=AF.Copy,
                                                                 scale=(ISQ if nt < 12 else 1.0)),
                  reads=[R.psA_b[p]], writes=[R.stg_b[s]])
            if "qk_dst" in outs:
                dap, dwb = outs["qk_dst"](nt, tb)
                kb.dma("sp", dap, R.stg[s][:], reads=[R.stg_b[s]], wshared=dwb)
            else:
                kb.dma("sp", outs["qkT"][nt * 128:(nt + 1) * 128, tsl], R.stg[s][:], reads=[R.stg_b[s]])
        for nt in range(12):
            j = next_slab(R)
            p = next_ps(R)
            load_slab(kb, R.wA[j], R.wA_b[j], w_in, 3072 + nt * 128, 128, KT)
            for t4 in range(4):
                for kt in range(KT):
                    kb.op("pe", lambda h, kt=kt, t4=t4, j=j, p=p: h.matmul(R.psB[p][:, t4 * 128:(t4 + 1) * 128],
                                                                      R.hT[:, kt, t4 * 128:(t4 + 1) * 128],
                                                                      R.wA[j][:, kt, :], start=(kt == 0),
                                                                      stop=(kt == KT - 1)),
                          reads=[R.wA_b[j], R.hT_b[kt]], writes=[R.psB_b[p]])
            s = R.sctr % 2
            R.sctr += 1
            kb.op("act", lambda h, j=j, p=p, s=s: h.activation(out=R.stg[s][:], in_=R.psB[p][:], func=AF.Copy),
                  reads=[R.psB_b[p]], writes=[R.stg_b[s]])
            if "v_dst" in outs:
                dap, dwb = outs["v_dst"](nt, tb)
                kb.dma("sp", dap, R.stg[s][:].rearrange("p (t c) -> p t c", t=4), reads=[R.stg_b[s]], wshared=dwb)
            else:
                kb.dma("sp", outs["v"][tsl, nt * 128:(nt + 1) * 128].rearrange("(t p) c -> p t c", p=128),
                       R.stg[s][:].rearrange("p (t c) -> p t c", t=4), reads=[R.stg_b[s]])
        qm0 = 4608
    else:
        for nt in range(12):
            j = next_slab(R)
            p = next_ps(R)
            load_slab(kb, R.wA[j], R.wA_b[j], w_in, nt * 128, 128, KT)
            gemm16(R, R.psA[p], R.psA_b[p], R.wA[j], R.wA_b[j], R.hT, R.hT_b)
            kb.op("act", lambda h, j=j, p=p: h.activation(out=R.sg[p][:], in_=R.psA[p][:], func=AF.Copy),
                  reads=[R.psA_b[p]], writes=[R.sg_b[p]])
            if "u_dst" in outs:
                dap, dwb = outs["u_dst"](nt, tb)
                kb.dma("sp", dap, R.sg[p][:], reads=[R.sg_b[p]], wshared=dwb)
            else:
                kb.dma("sp", outs["uT"][nt * 128:(nt + 1) * 128, tsl], R.sg[p][:], reads=[R.sg_b[p]])
        qm0 = 1536
    for hd in range(4):
        j = next_slab(R)
        p = next_ps(R)
        load_slab(kb, R.wA[j], R.wA_b[j], w_in, qm0 + hd * 128, 128, KT)
        gemm16(R, R.psA[p], R.psA_b[p], R.wA[j], R.wA_b[j], R.hT, R.hT_b)
        head_rstd(R, R.psA[p], R.psA_b[p], TB)
        kb.op("dve", lambda h, j=j, p=p: h.scalar_tensor_tensor(out=R.qn[:], in0=R.psA[p][:], scalar=R.gqk[:, 0:1],
                                                           in1=R.rstd[:], op0=ALU.mult, op1=ALU.mult),
              reads=[R.psA_b[p], R.gqk_b, R.rstd_b], writes=[R.qn_b])
        for mt in range(2):
            kb.op("pe", lambda h, mt=mt, hd=hd: h.matmul(R.psB[mt][:], R.KmT[:, hd, mt * 128:(mt + 1) * 128], R.qn[:],
                                                         start=True, stop=True),
                  reads=[R.kv_b, R.qn_b], writes=[R.psB_b[mt]])
            kb.op("act", lambda h, mt=mt: h.activation(out=R.pT[:, mt, :], in_=R.psB[mt][:], func=AF.Exp, scale=ISQ),
                  reads=[R.psB_b[mt]], writes=[R.pT_b[mt]])
        for mt in range(2):
            kb.op("pe", lambda h, mt=mt: h.matmul(R.psS[:], R.ones_bf[:], R.pT[:, mt, :], start=(mt == 0),
                                                  stop=(mt == 1)),
                  reads=[R.ones_b, R.pT_b[mt]], writes=[R.psS_b])
        o = hd % 2
        for mt in range(2):
            kb.op("pe", lambda h, mt=mt, hd=hd, o=o: h.matmul(R.psO[o][:], R.Vm[:, mt, hd * 128:(hd + 1) * 128],
                                                              R.pT[:, mt, :], start=(mt == 0), stop=(mt == 1)),
                  reads=[R.kv_b, R.pT_b[mt]], writes=[R.psO_b[o]])
        kb.op("dve", lambda h: h.reciprocal(out=R.rstd[:], in_=R.psS[:]), reads=[R.psS_b], writes=[R.rstd_b])
        s = R.sctr % 2
        R.sctr += 1
        kb.op("dve", lambda h, o=o, s=s: h.tensor_tensor(out=R.stg[s][:], in0=R.psO[o][:], in1=R.rstd[:],
                                                         op=ALU.mult),
              reads=[R.psO_b[o], R.rstd_b], writes=[R.stg_b[s]])
        kb.dma("sp", outs["crossT_out"][hd * 128:(hd + 1) * 128, tsl], R.stg[s][:], reads=[R.stg_b[s]],
               writes=(outs.get("cross_wb") or []))


def post_block(R, kind, tokT, crossT, w_out, w_glu, tb, tok_src=None, rd=()):
    kb = R.kb
    tsl = slice(tb * TB, (tb + 1) * TB)
    if tok_src is None:
        tok_src = lambda kt: tokT[kt * 128:(kt + 1) * 128, tsl]
    rd = list(rd)
    if kind == "sb":
        for kt in range(12):
            kb.dma("sp", R.hT[:, kt, :], tok_src(kt), reads=rd, writes=[R.hT_b[kt]])
    else:
        for kt in range(12):
            kb.dma("sp", R.actT[:, kt, :], tok_src(kt), reads=rd, writes=[R.actT_b[kt]])
        for nt in range(12):
            j = next_slab(R)
            p = next_ps(R)
            load_slab(kb, R.wA[j], R.wA_b[j], w_glu, nt * 128, 128, 12)
            gemm16(R, R.psA[p], R.psA_b[p], R.wA[j], R.wA_b[j], R.actT, R.actT_b, nkt=12)
            kb.op("act", lambda h, j=j, p=p: h.activation(out=R.sg[p][:], in_=R.psA[p][:], func=AF.Sigmoid),
                  reads=[R.psA_b[p]], writes=[R.sg_b[p]])
            kb.op("dve", lambda h, j=j, p=p, nt=nt: h.tensor_tensor(out=R.hT[:, nt, :], in0=R.actT[:, nt, :],
                                                               in1=R.sg[p][:], op=ALU.mult),
                  reads=[R.actT_b[nt], R.sg_b[p]], writes=[R.hT_b[nt]])
    for kt in range(12, 16):
        kb.dma("sp", R.hT[:, kt, :], crossT[(kt - 12) * 128:(kt - 11) * 128, tsl], reads=rd,
               writes=[R.hT_b[kt]])
    for dt in range(KT):
        j = next_slab(R)
        p = next_ps(R)
        load_slab(kb, R.wB[j], R.wB_b[j], w_out, dt * 128, 128, KT)
        gemm16(R, R.psO[p], R.psO_b[p], R.wB[j], R.wB_b[j], R.hT, R.hT_b)
        kb.op("dve", lambda h, dt=dt, j=j, p=p: h.tensor_tensor(out=R.xblk[:, dt, :], in0=R.psO[p][:],
                                                           in1=R.xblk[:, dt, :], op=ALU.add),
              reads=[R.psO_b[p], R.xblk_b[dt]], writes=[R.xblk_b[dt]])


def build_row(post, ffn2, ffn1, mix):
    nc = bass.Bass("TRN2", target_bir_lowering=False)

    def din(name, shape, dt=F32):
        return nc.dram_tensor(name, shape, dt, kind="ExternalInput").ap()

    def dout(name, shape, dt=F32):
        return nc.dram_tensor(name, shape, dt, kind="ExternalOutput").ap()

    xT = din("xT", [D, TPC])
    xT_out = dout("xT_out", [D, TPC])
    a = {}
    if post:
        a["tokT"] = din("tokT", [TOKW, TPC], BF16)
        a["crossT_in"] = din("crossT_in", [MEMW, TPC], BF16)
        a["w_out"] = din("w_out", [D, D])
        if post == "s5":
            a["w_glu"] = din("w_glu", [TOKW, TOKW])
    if ffn2:
        a["g_ffn2"] = din("g_ffn2", [D])
        a["w_gu2"] = din("w_gu2", [D, 2 * FF])
        a["w_down2"] = din("w_down2", [FF, D])
    if ffn1:
        a["g_ffn1"] = din("g_ffn1", [D])
        a["w_gu1"] = din("w_gu1", [D, 2 * FF])
        a["w_down1"] = din("w_down1", [FF, D])
    outs = {}
    if mix:
        a["g_mix"] = din("g_mix", [D])
        a["w_in"] = din("w_in", [D, 5120 if mix == "sb" else 2048])
        a["memT"] = din("memT", [D, NMEM])
        a["g_mem"] = din("g_mem", [D])
        a["w_mem_kv"] = din("w_mem_kv", [D, 2 * MEMW])
        a["gqk"] = din("gqk", [128, 2])
        outs["crossT_out"] = dout("crossT_out", [MEMW, TPC], BF16)
        if mix == "sb":
            outs["qkT"] = dout("qkT", [3072, TPC], BF16)
            outs["v"] = dout("v", [TPC, TOKW], BF16)
        else:
            outs["uT"] = dout("uT", [TOKW, TPC], F32)
    kb = KB(nc)
    R = RowRes(kb)
    if ffn2:
        R.load_gains(0, a["g_ffn2"])
    if ffn1:
        R.load_gains(1, a["g_ffn1"])
    if mix:
        R.load_gains(2, a["g_mix"])
        R.load_gains(3, a["g_mem"])
        kb.dma("sp", R.gqk[:], a["gqk"], writes=[R.gqk_b])
        mem_prep(R, a["memT"], a["w_mem_kv"], 3)
    for tb in range(NTB):
        load_xblk(R, xT, tb)
        if post:
            post_block(R, post, a["tokT"], a["crossT_in"], a["w_out"], a.get("w_glu"), tb)
        if ffn2:
            ffn_block(R, 0, a["w_gu2"], a["w_down2"])
        if ffn1:
            ffn_block(R, 1, a["w_gu1"], a["w_down1"])
        if mix:
            mixin_block(R, mix, 2, a["w_in"], outs, tb)
        store_xblk(R, xT_out, tb)
    kb.finish()
    return nc


_PROGS = {}


def _prog(key, fn):
    if key not in _PROGS:
        _PROGS[key] = fn()
    return _PROGS[key]


def _run(nc, in_maps):
    res = run_bass_kernel_spmd(nc, in_maps, core_ids=list(range(NCORES)))
    return res.results


def kernel_unfused(x, mem, ffn1_norm, ffn1_w_gu, ffn1_w_down, mix_norm, mem_norm, w_mem_kv, xq_norm, xk_norm, w_out,
           ffn2_norm, ffn2_w_gu, ffn2_w_down, sb_w_in, s5_w_in, s5_log_dt, s5_a_re, s5_a_im, s5_b_re, s5_b_im,
           s5_c_re, s5_c_im, s5_d, s5_w_glu, _debug=None):
    f32 = np.float32
    A = lambda t: np.ascontiguousarray(np.asarray(t, dtype=f32))
    x = A(x)
    mem = A(mem)
    xT = [np.ascontiguousarray(x[c // 4, (c % 4) * TPC:(c % 4 + 1) * TPC, :].T) for c in range(NCORES)]
    memT = [np.ascontiguousarray(mem[b].T) for b in range(NB)]
    tokT = None
    crossT = None
    for stage in range(DEPTH + 1):
        li = stage
        lp = stage - 1
        post = None if lp < 0 else ("sb" if lp % 2 == 0 else "s5")
        mix = None if li >= DEPTH else ("sb" if li % 2 == 0 else "s5")
        ffn2 = lp >= 0
        ffn1 = li < DEPTH
        nc = _prog(("row", post, ffn2, ffn1, mix), lambda: build_row(post, ffn2, ffn1, mix))
        maps = []
        for c in range(NCORES):
            m = {"xT": xT[c]}
            if post:
                m["tokT"] = tokT[c]
                m["crossT_in"] = crossT[c]
                m["w_out"] = A(w_out[lp])
                if post == "s5":
                    m["w_glu"] = A(s5_w_glu[lp // 2])
            if ffn2:
                m["g_ffn2"] = A(ffn2_norm[lp])
                m["w_gu2"] = A(ffn2_w_gu[lp])
                m["w_down2"] = A(ffn2_w_down[lp])
            if ffn1:
                m["g_ffn1"] = A(ffn1_norm[li])
                m["w_gu1"] = A(ffn1_w_gu[li])
                m["w_down1"] = A(ffn1_w_down[li])
            if mix:
                m["g_mix"] = A(mix_norm[li])
                m["w_in"] = A(sb_w_in[li // 2]) if mix == "sb" else A(s5_w_in[li // 2])
                m["memT"] = memT[c // 4]
                m["g_mem"] = A(mem_norm[li])
                m["w_mem_kv"] = A(w_mem_kv[li])
                m["gqk"] = np.ascontiguousarray(np.stack([A(xq_norm[li]), A(xk_norm[li])], axis=1))
            maps.append(m)
        res = _run(nc, maps)
        xT = [res[c]["xT_out"] for c in range(NCORES)]
        if _debug is not None:
            _debug["x_stage%d" % stage] = [a.copy() for a in xT]
        if not mix:
            break
        crossT = [res[c]["crossT_out"] for c in range(NCORES)]
        if mix == "sb":
            maps = []
            for c in range(NCORES):
                qs, ks, vs = [], [], []
                for i in range(NPAIR):
                    p = c * NPAIR + i
                    b, hd = p // 12, p % 12
                    qs.append(np.concatenate([res[b * 4 + qd]["qkT"][hd * 128:(hd + 1) * 128] for qd in range(4)], axis=1))
                    ks.append(np.concatenate([res[b * 4 + qd]["qkT"][(12 + hd) * 128:(13 + hd) * 128] for qd in range(4)], axis=1))
                    vs.append(np.concatenate([res[b * 4 + qd]["v"][:, hd * 128:(hd + 1) * 128] for qd in range(4)], axis=0))
                maps.append({"qT": np.ascontiguousarray(np.stack(qs)), "kT": np.ascontiguousarray(np.stack(ks)),
                             "v": np.ascontiguousarray(np.stack(vs))})
            nc2 = _prog(("sb",), lambda: build_sb_attn())
            r2 = _run(nc2, maps)
            tokT = []
            for c in range(NCORES):
                b, qd = c // 4, c % 4
                rows = []
                for hd in range(12):
                    p = b * 12 + hd
                    rows.append(r2[p // NPAIR]["oT"][p % NPAIR][:, qd * TPC:(qd + 1) * TPC])
                tokT.append(np.ascontiguousarray(np.concatenate(rows, axis=0)))
        else:
            jj = li // 2
            maps = []
            for c in range(NCORES):
                b, gs = c // 4, c % 4
                m = s5_host_params(A(s5_log_dt[jj]), A(s5_a_re[jj]), A(s5_a_im[jj]), A(s5_b_re[jj]), A(s5_b_im[jj]),
                                   A(s5_c_re[jj]), A(s5_c_im[jj]), A(s5_d[jj]), gs)
                m["uT"] = np.ascontiguousarray(
                    np.concatenate([res[b * 4 + qd]["uT"][gs * 384:(gs + 1) * 384] for qd in range(4)], axis=1))
                maps.append(m)
            nc2 = _prog(("s5",), lambda: build_s5())
            r2 = _run(nc2, maps)
            tokT = []
            for c in range(NCORES):
                b, qd = c // 4, c % 4
                tokT.append(np.ascontiguousarray(
                    np.concatenate([r2[b * 4 + gs]["ygT"][:, qd * TPC:(qd + 1) * TPC] for gs in range(4)], axis=0)))
        if _debug is not None:
            _debug["tok_stage%d" % stage] = [a.copy() for a in tokT]
            _debug["cross_stage%d" % stage] = [a.copy() for a in crossT]
    out = np.empty((NB, L, D), f32)
    for c in range(NCORES):
        out[c // 4, (c % 4) * TPC:(c % 4 + 1) * TPC, :] = xT[c].T
    return out


GROUPS = [[0, 1, 2, 3], [4, 5, 6, 7]]
NWB = 8


class Exchange:
    def __init__(self, kb, name, nchunks, rows, cols, dt):
        nc = kb.nc
        self.kb, self.n, self.rows, self.cols = kb, nchunks, rows, cols
        self.snd = nc.dram_tensor(name + "_snd", [nchunks * 4 * rows, cols], dt)
        self.rcv = nc.dram_tensor(name + "_rcv", [nchunks * 16 * rows, cols], dt)
        self.loc = nc.dram_tensor(name + "_loc", [nchunks * 4 * rows, cols], dt)
        self.snd_b = [Buf() for _ in range(nchunks)]
        self.rcv_b = [Buf() for _ in range(nchunks)]
        self.loc_b = Buf()
        self.ci = None

    def early(self, chunks):
        self.kb.coll_chunks(self, GROUPS, chunks=chunks)

    def snd_rows(self, c, j, r0=0, n=None):
        n = self.rows if n is None else n
        base = (c * 4 + j) * self.rows + r0
        return self.snd.ap()[base:base + n, :]

    def loc_rows(self, c, r, r0=0, n=None):
        n = self.rows if n is None else n
        base = (c * 4 + r) * self.rows + r0
        return self.loc.ap()[base:base + n, :]

    def run(self, jv):
        kb = self.kb
        kb.coll_chunks(self, GROUPS)
        kb.dma("sp", self.loc.ap().rearrange("(cr x) t -> cr x t", x=self.rows),
               self.rcv.ap().rearrange("(cr j x) t -> cr j x t", j=4, x=self.rows)[:, jv],
               reads=self.rcv_b, writes=[self.loc_b])


def build_fused():
    nc = bass.Bass("TRN2", target_bir_lowering=False)

    def din(name, shape, dt=F32):
        return nc.dram_tensor(name, shape, dt, kind="ExternalInput").ap()

    xT = din("xT", [D, TPC])
    memT = din("memT", [D, NMEM])
    W = {}
    for name, shape in (("ffn1_norm", [DEPTH, D]), ("ffn1_w_gu", [DEPTH, D, 2 * FF]), ("ffn1_w_down", [DEPTH, FF, D]),
                        ("mix_norm", [DEPTH, D]), ("mem_norm", [DEPTH, D]), ("w_mem_kv", [DEPTH, D, 2 * MEMW]),
                        ("gqk", [DEPTH, 128, 2]), ("w_out", [DEPTH, D, D]), ("ffn2_norm", [DEPTH, D]),
                        ("ffn2_w_gu", [DEPTH, D, 2 * FF]), ("ffn2_w_down", [DEPTH, FF, D]),
                        ("sb_w_in", [2, D, 5120]), ("s5_w_in", [2, D, 2048]), ("s5_w_glu", [2, TOKW, TOKW]),
                        ("ldt", [2, 128, NST]), ("are", [2, 128, NST]), ("aim", [2, 128, NST]),
                        ("Bre", [2, 128, NST, 128]), ("Bim", [2, 128, NST, 128]), ("Cre", [2, 128, NST, 128]),
                        ("Cim", [2, 128, NST, 128]), ("dcol", [2, 128, 3])):
        W[name] = din(name, shape)
    yT = nc.dram_tensor("yT", [D, TPC], F32, kind="ExternalOutput").ap()

    xs = nc.dram_tensor("xs", [D, TPC], F32).ap()
    xs_b = [[Buf() for _ in range(KT)] for _ in range(NTB)]
    cross = [nc.dram_tensor("cross%d" % i, [MEMW, TPC], BF16).ap() for i in range(DEPTH)]
    cross_b = [[Buf() for _ in range(NTB)] for _ in range(DEPTH)]

    kb = KB(nc)
    pid = nc.sync.partition_id()
    jv = pid % 4
    ex_tok = None
    kb.wc = {"on": False, "first": True, "idxA": 0, "idxD": 0, "bufsA": [], "bufsD": [],
             "tA": nc.dram_tensor("wcacheA", [352, 128, KT * 128], BF16),
             "tD": nc.dram_tensor("wcacheD", [32, 128, FT * 128], BF16)}

    for li in range(DEPTH + 1):
        lp = li - 1
        post = None if lp < 0 else ("sb" if lp % 2 == 0 else "s5")
        mix = None if li >= DEPTH else ("sb" if li % 2 == 0 else "s5")
        kb.push()
        R = RowRes(kb)
        outs = {}
        if lp >= 0:
            R.load_gains(0, W["ffn2_norm"][lp])
        if li < DEPTH:
            R.load_gains(1, W["ffn1_norm"][li])
        if mix:
            R.load_gains(2, W["mix_norm"][li])
            R.load_gains(3, W["mem_norm"][li])
            kb.dma("sp", R.gqk[:], W["gqk"][li], writes=[R.gqk_b])
            mem_prep(R, memT, W["w_mem_kv"][li], 3)
            outs["crossT_out"] = cross[li]
            if mix == "sb":
                ex_qk = Exchange(kb, "qk%d" % li, 12, 128, 1024, BF16)
                ex_v = Exchange(kb, "v%d" % li, 6, 1024, 128, BF16)

                def qk_dst(nt, tb, ex=ex_qk):
                    a_, h = nt // 12, nt % 12
                    j, i = h // 3, h % 3
                    c = (a_ * 3 + i) * 2 + tb // 2
                    return ex.snd_rows(c, j)[:, (tb % 2) * TB:(tb % 2 + 1) * TB], [ex.snd_b[c]]

                def v_dst(nt, tb, ex=ex_v):
                    j, i = nt // 3, nt % 3
                    c = i * 2 + tb // 2
                    ap = ex.snd_rows(c, j, (tb % 2) * TB, TB).rearrange("(t p) c -> p t c", p=128)
                    return ap, [ex.snd_b[c]]

                outs["qk_dst"] = qk_dst
                outs["v_dst"] = v_dst
            else:
                ex_u = Exchange(kb, "u%d" % li, 12, 128, 512, F32)

                def u_dst(nt, tb, ex=ex_u):
                    j, ct = nt // 3, nt % 3
                    c = ct * 4 + tb
                    return ex.snd_rows(c, j), [ex.snd_b[c]]

                outs["u_dst"] = u_dst
        for tb in range(NTB):
            kb.wc["on"] = True
            kb.wc["first"] = (tb == 0)
            kb.wc["idxA"] = 0
            kb.wc["idxD"] = 0
            if li == 0:
                load_xblk(R, xT, tb)
            else:
                load_xblk(R, xs, tb, dbufs=xs_b[tb])
            if post:
                def tok_src(kt, tb=tb, ex=ex_tok):
                    r, i = kt // 3, kt % 3
                    c = i * 2 + tb // 2
                    return ex.loc_rows(c, r)[:, (tb % 2) * TB:(tb % 2 + 1) * TB]

                post_block(R, post, None, cross[lp], W["w_out"][lp],
                           W["s5_w_glu"][lp // 2] if post == "s5" else None, tb, tok_src=tok_src,
                           rd=[ex_tok.loc_b, cross_b[lp][tb]])
                ffn_block(R, 0, W["ffn2_w_gu"][lp], W["ffn2_w_down"][lp])
            if li < DEPTH:
                ffn_block(R, 1, W["ffn1_w_gu"][li], W["ffn1_w_down"][li])
            if mix:
                outs["cross_wb"] = [cross_b[li][tb]]
                mixin_block(R, mix, 2, W["sb_w_in"][li // 2] if mix == "sb" else W["s5_w_in"][li // 2], outs, tb)
                if mix == "sb" and tb == 1:
                    ex_qk.early([c for c in range(12) if c % 2 == 0])
                    ex_v.early([c for c in range(6) if c % 2 == 0])
                elif mix == "s5" and tb < NTB - 1:
                    ex_u.early([ct * 4 + tb for ct in range(3)])
            if li == DEPTH:
                store_xblk(R, yT, tb)
            else:
                store_xblk(R, xs, tb, dbufs=xs_b[tb])
        kb.wc["on"] = False
        kb.pop()
        if not mix:
            break
        if mix == "sb":
            ex_qk.run(jv)
            ex_v.run(jv)
            ex_o = Exchange(kb, "o%d" % li, 6, 128, 1024, BF16)

            def load_fn(pi, q_s, k_s, v_s, b, ex_qk=ex_qk, ex_v=ex_v):
                for r in range(4):
                    for th in range(2):
                        c0 = r * TPC + th * 1024
                        kb.dma("sp", q_s[:, c0:c0 + 1024], ex_qk.loc_rows(pi * 2 + th, r), reads=[ex_qk.loc_b],
                               writes=[b])
                        kb.dma("sp", k_s[:, c0:c0 + 1024], ex_qk.loc_rows((3 + pi) * 2 + th, r), reads=[ex_qk.loc_b],
                               writes=[b])
                        kb.dma("sp", v_s[:, r * 16 + th * 8:r * 16 + th * 8 + 8, :],
                               ex_v.loc_rows(pi * 2 + th, r).rearrange("(kb p) d -> p kb d", p=128),
                               reads=[ex_v.loc_b], writes=[b])

            def store_fn(pi, qb, o_s, o_b, ex=ex_o):
                qd, tq = qb // 4, qb % 4
                c = pi * 2 + tq // 2
                kb.dma("sp", ex.snd_rows(c, qd)[:, (tq % 2) * QB:(tq % 2 + 1) * QB], o_s[:], reads=[o_b],
                       wshared=[ex.snd_b[c]])

            kb.push()
            sb_attn_phase(kb, None, None, None, None, NPAIR, NQB, load_fn=load_fn, store_fn=store_fn)
            kb.pop()
            ex_o.run(jv)
            ex_tok = ex_o
        else:
            ex_u.run(jv)
            ex_y = Exchange(kb, "y%d" % li, 6, 128, 1024, BF16)

            def uload_fn(ch, ct, u_s, u_b, ex=ex_u):
                qd, tq = ch // 4, ch % 4
                kb.dma("sp", u_s[:], ex.loc_rows(ct * 4 + tq, qd), reads=[ex.loc_b], writes=[u_b])

            def ystore_fn(ch, ct, yo, yo_b, ex=ex_y):
                qd, tq = ch // 4, ch % 4
                c = ct * 2 + tq // 2
                kb.dma("sp", ex.snd_rows(c, qd)[:, (tq % 2) * SCH:(tq % 2 + 1) * SCH], yo[:], reads=[yo_b],
                       wshared=[ex.snd_b[c]])

            jj = li // 2
            dr = {k: W[k][jj] for k in ("ldt", "are", "aim", "Bre", "Bim", "Cre", "Cim", "dcol")}
            kb.push()
            s5_phase(kb, dr, None, L // SCH, uload_fn=uload_fn, ystore_fn=ystore_fn)
            kb.pop()
            ex_y.run(jv)
            ex_tok = ex_y
    kb.finish()
    return nc


def kernel(x, mem, ffn1_norm, ffn1_w_gu, ffn1_w_down, mix_norm, mem_norm, w_mem_kv, xq_norm, xk_norm, w_out,
           ffn2_norm, ffn2_w_gu, ffn2_w_down, sb_w_in, s5_w_in, s5_log_dt, s5_a_re, s5_a_im, s5_b_re, s5_b_im,
           s5_c_re, s5_c_im, s5_d, s5_w_glu):
    f32 = np.float32
    A = lambda t: np.ascontiguousarray(np.asarray(t, dtype=f32))
    x = A(x)
    mem = A(mem)
    nc = _prog(("fused",), build_fused)
    shared = {
        "ffn1_norm": A(ffn1_norm), "ffn1_w_gu": A(ffn1_w_gu), "ffn1_w_down": A(ffn1_w_down),
        "mix_norm": A(mix_norm), "mem_norm": A(mem_norm), "w_mem_kv": A(w_mem_kv),
        "gqk": np.ascontiguousarray(np.stack([A(xq_norm), A(xk_norm)], axis=2)),
        "w_out": A(w_out), "ffn2_norm": A(ffn2_norm), "ffn2_w_gu": A(ffn2_w_gu), "ffn2_w_down": A(ffn2_w_down),
        "sb_w_in": A(sb_w_in), "s5_w_in": A(s5_w_in), "s5_w_glu": A(s5_w_glu),
    }
    s5p = []
    for gs in range(4):
        per = [s5_host_params(A(s5_log_dt[jj]), A(s5_a_re[jj]), A(s5_a_im[jj]), A(s5_b_re[jj]), A(s5_b_im[jj]),
                              A(s5_c_re[jj]), A(s5_c_im[jj]), A(s5_d[jj]), gs) for jj in range(2)]
        s5p.append({k: np.ascontiguousarray(np.stack([per[0][k], per[1][k]])) for k in per[0]})
    maps = []
    for c in range(NCORES):
        m = dict(shared)
        m.update(s5p[c % 4])
        m["xT"] = np.ascontiguousarray(x[c // 4, (c % 4) * TPC:(c % 4 + 1) * TPC, :].T)
        m["memT"] = np.ascontiguousarray(mem[c // 4].T)
        maps.append(m)
    res = _run(nc, maps)
    out = np.empty((NB, L, D), f32)
    for c in range(NCORES):
        out[c // 4, (c % 4) * TPC:(c % 4 + 1) * TPC, :] = res[c]["yT"].T
    return out
```

```python
import contextlib
import math
import numpy as np
import concourse.bass as bass
import concourse.mybir as mybir
from concourse.bass_utils import run_bass_kernel_spmd

F32 = mybir.dt.float32
BF16 = mybir.dt.bfloat16
I32 = mybir.dt.int32
AF = mybir.ActivationFunctionType
ALU = mybir.AluOpType

D = 2048
L = 8192
NB = 2
DEPTH = 4
FF = 5632
TOKW = 1536
MEMW = 512
NMEM = 256
NCORES = 8
TPC = 2048
TB = 512
NTB = TPC // TB
KT = D // 128
FT = FF // 128
EPS = 1e-6


class Buf:
    __slots__ = ("w", "r")

    def __init__(self):
        self.w = None
        self.r = {}


class Eng:
    def __init__(self, name, h, sem, sid):
        self.name = name
        self.h = h
        self.sem = sem
        self.sid = sid
        self.cnt = 0
        self.waited = {}


class KB:
    NDMA = 40

    def __init__(self, nc):
        self.nc = nc
        self.es = contextlib.ExitStack()
        self.sems = {}
        self.engs = {}
        for name, h in (("pe", nc.tensor), ("act", nc.scalar), ("dve", nc.vector),
                        ("pool", nc.gpsimd), ("sp", nc.sync)):
            sem = self.es.enter_context(nc.semaphore("s_" + name))
            self.sems[name] = sem
            self.engs[name] = Eng(name, h, sem, name)
        self.dsem = []
        self.dcnt = []
        for i in range(self.NDMA):
            sem = self.es.enter_context(nc.semaphore("d%d" % i))
            self.sems[("d", i)] = sem
            self.dsem.append(sem)
            self.dcnt.append(0)
        self.dnext = 0
        self.nalloc = 0
        self.scopes = [self.es]
        self.csem = []
        self.ccnt = []
        self.uid = 0

    def push(self):
        sc = contextlib.ExitStack()
        self.scopes.append(sc)

    def pop(self):
        self.barrier()
        self.scopes.pop().close()

    def barrier(self):
        for E in self.engs.values():
            for name, O in self.engs.items():
                if O is not E:
                    self._wait(E, name, O.cnt)
            for i in range(self.NDMA):
                self._wait(E, ("d", i), self.dcnt[i])
            for ci in range(len(self.csem)):
                self._wait(E, ("c", ci), self.ccnt[ci])

    def coll_chunks(self, ex, groups, chunks=None):
        E = self.engs["pool"]
        if getattr(ex, "ci", None) is None:
            ci = len(self.csem)
            sem = self.es.enter_context(self.nc.semaphore("cc%d" % ci))
            self.csem.append(sem)
            self.ccnt.append(0)
            self.sems[("c", ci)] = sem
            ex.ci = ci
            ex.done = set()
        ci = ex.ci
        sem = self.csem[ci]
        R4 = 4 * ex.rows
        for c in (range(ex.n) if chunks is None else chunks):
            if c in ex.done:
                continue
            ex.done.add(c)
            self._deps(E, [], [ex.snd_b[c], ex.rcv_b[c]], is_dma=True)
            ins = E.h.collective_compute("AllGather", ALU.bypass, replica_groups=groups,
                                         ins=[ex.snd.ap()[c * R4:(c + 1) * R4, :]],
                                         outs=[ex.rcv.ap()[c * 4 * R4:(c + 1) * 4 * R4, :]])
            ins.then_inc(sem)
            self.ccnt[ci] += 1
            self._commit((("c", ci), self.ccnt[ci]), [], [ex.snd_b[c], ex.rcv_b[c]])

    def coll_allgather(self, snd_t, rcv_t, groups, reads=(), writes=()):
        E = self.engs["pool"]
        ci = len(self.csem)
        sem = self.es.enter_context(self.nc.semaphore("cc%d" % ci))
        self.csem.append(sem)
        self.sems[("c", ci)] = sem
        self._deps(E, reads, writes, is_dma=True)
        ins = E.h.collective_compute("AllGather", ALU.bypass, replica_groups=groups,
                                     ins=[snd_t.ap().opt()], outs=[rcv_t.ap().opt()])
        ins.then_inc(sem)
        self.ccnt.append(1)
        self._commit((("c", ci), 1), reads, writes)

    def sb(self, name, shape, dt):
        self.uid += 1
        t = self.scopes[-1].enter_context(self.nc.sbuf_tensor("sb%d_%s" % (self.uid, name), list(shape), dt))
        return t

    def ps(self, name, shape=(128, 512), dt=F32):
        self.uid += 1
        return self.scopes[-1].enter_context(self.nc.psum_tensor("ps%d_%s" % (self.uid, name), list(shape), dt))

    def _wait(self, E, sid, val):
        if E.waited.get(sid, 0) >= val:
            return
        E.h.wait_ge(self.sems[sid], val)
        E.waited[sid] = val

    def _deps(self, E, reads, writes, is_dma=False):
        deps = {}

        def add(tok, kind):
            if tok is None:
                return
            sid, val = tok
            if sid == E.sid and not is_dma:
                if E.name == "pe":
                    return
                if kind != "raw":
                    return
            if deps.get(sid, 0) < val:
                deps[sid] = val

        for b in reads:
            add(b.w, "raw")
        for b in writes:
            add(b.w, "waw")
            for sid, val in b.r.items():
                add((sid, val), "war")
        for sid, val in deps.items():
            self._wait(E, sid, val)

    def _commit(self, tok, reads, writes):
        sid, val = tok
        for b in reads:
            if b.r.get(sid, 0) < val:
                b.r[sid] = val
        for b in writes:
            b.w = tok
            b.r = {}

    def op(self, eng, fn, reads=(), writes=()):
        E = self.engs[eng]
        self._deps(E, reads, writes)
        ins = fn(E.h)
        E.cnt += 1
        ins.then_inc(E.sem, 1)
        self._commit((E.sid, E.cnt), reads, writes)

    def dma(self, eng, out, in_, reads=(), writes=(), wshared=(), **kw):
        E = self.engs[eng]
        i = self.dnext
        self.dnext = (self.dnext + 1) % self.NDMA
        self._deps(E, reads, writes, is_dma=True)
        self._wait(E, ("d", i), self.dcnt[i])
        ins = E.h.dma_start(out=out, in_=in_, **kw)
        self.dcnt[i] += 16
        ins.then_inc(self.dsem[i], 16)
        tok = (("d", i), self.dcnt[i])
        self._commit(tok, reads, writes)
        for b in wshared:
            if b.r.get(tok[0], 0) < tok[1]:
                b.r[tok[0]] = tok[1]

    def finish(self):
        E = self.engs["sp"]
        for i in range(self.NDMA):
            self._wait(E, ("d", i), self.dcnt[i])
        for name in ("pe", "act", "dve", "pool"):
            self._wait(E, name, self.engs[name].cnt)
        for ci in range(len(self.csem)):
            self._wait(E, ("c", ci), self.ccnt[ci])
        while len(self.scopes) > 1:
            self.scopes.pop().close()
        self.es.close()


class RowRes:
    def __init__(self, kb):
        self.kb = kb
        nc = kb.nc
        self.xblk = kb.sb("xblk", (128, KT, TB), F32)
        self.xblk_b = [Buf() for _ in range(KT)]
        self.hT = kb.sb("hT", (128, KT, TB), BF16)
        self.hT_b = [Buf() for _ in range(KT)]
        self.actT = kb.sb("actT", (128, FT, TB), BF16)
        self.actT_b = [Buf() for _ in range(FT)]
        self.NSLAB = 6
        self.slab = [kb.sb("slab%d" % i, (128, KT, 128), BF16) for i in range(self.NSLAB)]
        self.slab_b = [Buf() for _ in range(self.NSLAB)]
        self.wA, self.wA_b = self.slab, self.slab_b
        self.wB, self.wB_b = self.slab, self.slab_b
        self.pctr = 0
        self.sq = [kb.sb("sq%d" % i, (128, TB), F32) for i in range(2)]
        self.sq_b = [Buf() for _ in range(2)]
        self.rstd = kb.sb("rstd", (128, TB), F32)
        self.rstd_b = Buf()
        self.sg = [kb.sb("sg%d" % i, (128, TB), F32) for i in range(2)]
        self.sg_b = [Buf() for _ in range(2)]
        self.gcol = kb.sb("gcol", (128, 8, KT), F32)
        self.gcol_b = Buf()
        self.ones = kb.sb("ones", (128, 128), F32)
        self.ones_b = Buf()
        self.psA = [kb.ps("psA%d" % i) for i in range(2)]
        self.psA_b = [Buf() for _ in range(2)]
        self.psB = [kb.ps("psB%d" % i) for i in range(2)]
        self.psB_b = [Buf() for _ in range(2)]
        self.psO = [kb.ps("psO%d" % i) for i in range(2)]
        self.psO_b = [Buf() for _ in range(2)]
        self.psS = kb.ps("psS")
        self.psS_b = Buf()
        self.epsc = kb.sb("epsc", (128, 1), F32)
        self.ones_bf = kb.sb("ones_bf", (128, 128), BF16)
        kb.op("dve", lambda h: h.memset(self.ones[:], 1.0), writes=[self.ones_b])
        kb.op("dve", lambda h: h.memset(self.epsc[:], EPS), writes=[self.ones_b])
        kb.op("dve", lambda h: h.memset(self.ones_bf[:], 1.0), writes=[self.ones_b])
        self.memh = kb.sb("memh", (128, KT, NMEM), BF16)
        self.memh_b = [Buf() for _ in range(KT)]
        self.KmT = kb.sb("KmT", (128, 4, NMEM), BF16)
        self.Vm = kb.sb("Vm", (128, 2, MEMW), BF16)
        self.kv_b = Buf()
        self.qn = kb.sb("qn", (128, TB), BF16)
        self.qn_b = Buf()
        self.pT = kb.sb("pT", (128, 2, TB), BF16)
        self.pT_b = [Buf() for _ in range(2)]
        self.stg = [kb.sb("stg%d" % i, (128, TB), BF16) for i in range(2)]
        self.stg_b = [Buf() for _ in range(2)]
        self.gqk = kb.sb("gqk", (128, 2), F32)
        self.gqk_b = Buf()
        self.ctr = 0
        self.sctr = 0

    def load_gains(self, slot, g_dram):
        kb = self.kb
        with kb.nc.allow_non_contiguous_dma(reason="tiny gain vector"):
            kb.dma("sp", self.gcol[:, slot, :], g_dram.rearrange("(kt p) -> p kt", p=128),
                   writes=[self.gcol_b])


def rmsnorm_block(R, slot, src, src_b, nkt=KT, dim=D, dst=None, dst_b=None, width=TB):
    kb = R.kb
    if dst is None:
        dst, dst_b = R.hT, R.hT_b
    W = width
    for kt in range(nkt):
        j = kt % 2
        kb.op("act", lambda h, kt=kt, j=j: h.activation(out=R.sq[j][:, 0:W], in_=src[:, kt, 0:W], func=AF.Square),
              reads=[src_b[kt]], writes=[R.sq_b[j]])
        kb.op("pe", lambda h, kt=kt, j=j: h.matmul(R.psS[:, 0:W], R.ones[:], R.sq[j][:, 0:W], start=(kt == 0),
                                                    stop=(kt == nkt - 1)),
              reads=[R.sq_b[j], R.ones_b], writes=[R.psS_b])
    kb.op("act", lambda h: h.activation(out=R.rstd[:, 0:W], in_=R.psS[:, 0:W], func=AF.Ln, scale=1.0 / dim,
                                        bias=EPS),
          reads=[R.psS_b], writes=[R.rstd_b])
    kb.op("act", lambda h: h.activation(out=R.rstd[:, 0:W], in_=R.rstd[:, 0:W], func=AF.Exp, scale=-0.5),
          reads=[R.rstd_b], writes=[R.rstd_b])
    for kt in range(nkt):
        kb.op("dve", lambda h, kt=kt: h.scalar_tensor_tensor(out=dst[:, kt, 0:W], in0=src[:, kt, 0:W],
                                                             scalar=R.gcol[:, slot, kt:kt + 1], in1=R.rstd[:, 0:W],
                                                             op0=ALU.mult, op1=ALU.mult),
              reads=[src_b[kt], R.gcol_b, R.rstd_b], writes=[dst_b[kt]])


def load_slab(kb, dst, dst_b, w_dram, c0, ncols, nkt):
    wc = getattr(kb, "wc", None)
    if wc is None or not wc["on"]:
        src = w_dram[:, c0:c0 + ncols].rearrange("(kt p) n -> p kt n", p=128)
        kb.dma("pool", dst[:, 0:nkt, 0:ncols], src, writes=[dst_b])
        return
    grp = "D" if nkt > KT else "A"
    idx = wc["idx" + grp]
    wc["idx" + grp] += 1
    scr = wc["t" + grp].ap()[idx, :, 0:nkt * 128].rearrange("p (kt n) -> p kt n", n=128)
    bl = wc["bufs" + grp]
    while len(bl) <= idx:
        bl.append(Buf())
    sb_ = bl[idx]
    if wc["first"]:
        src = w_dram[:, c0:c0 + ncols].rearrange("(kt p) n -> p kt n", p=128)
        kb.dma("pool", dst[:, 0:nkt, 0:ncols], src, writes=[dst_b])
        kb.dma("sp", scr, dst[:, 0:nkt, 0:ncols], reads=[dst_b], writes=[sb_])
    else:
        kb.dma("sp", dst[:, 0:nkt, 0:ncols], scr, reads=[sb_], writes=[dst_b])


def ffn_block(R, slot, w_gu, w_down):
    kb = R.kb
    rmsnorm_block(R, slot, R.xblk, R.xblk_b)
    for ft in range(FT):
        ja = next_slab(R)
        jb = next_slab(R)
        p = next_ps(R)
        load_slab(kb, R.slab[ja], R.slab_b[ja], w_gu, ft * 128, 128, KT)
        load_slab(kb, R.slab[jb], R.slab_b[jb], w_gu, FF + ft * 128, 128, KT)
        for kt in range(KT):
            kb.op("pe", lambda h, kt=kt, ja=ja, p=p: h.matmul(R.psA[p][:], R.slab[ja][:, kt, :], R.hT[:, kt, :],
                                                              start=(kt == 0), stop=(kt == KT - 1)),
                  reads=[R.slab_b[ja], R.hT_b[kt]], writes=[R.psA_b[p]])
        for kt in range(KT):
            kb.op("pe", lambda h, kt=kt, jb=jb, p=p: h.matmul(R.psB[p][:], R.slab[jb][:, kt, :], R.hT[:, kt, :],
                                                              start=(kt == 0), stop=(kt == KT - 1)),
                  reads=[R.slab_b[jb], R.hT_b[kt]], writes=[R.psB_b[p]])
        kb.op("act", lambda h, p=p: h.activation(out=R.sg[p][:], in_=R.psA[p][:], func=AF.Silu),
              reads=[R.psA_b[p]], writes=[R.sg_b[p]])
        kb.op("dve", lambda h, p=p, ft=ft: h.tensor_tensor(out=R.actT[:, ft, :], in0=R.psB[p][:], in1=R.sg[p][:],
                                                           op=ALU.mult),
              reads=[R.psB_b[p], R.sg_b[p]], writes=[R.actT_b[ft]])
    for dt in range(KT):
        p = next_ps(R)
        f0 = 0
        while f0 < FT:
            nf = min(KT, FT - f0)
            j = next_slab(R)
            load_slab(kb, R.slab[j], R.slab_b[j], w_down[f0 * 128:(f0 + nf) * 128, :], dt * 128, 128, nf)
            for f in range(nf):
                ft = f0 + f
                kb.op("pe", lambda h, f=f, ft=ft, j=j, p=p: h.matmul(R.psO[p][:], R.slab[j][:, f, :],
                                                                     R.actT[:, ft, :], start=(ft == 0),
                                                                     stop=(ft == FT - 1)),
                      reads=[R.slab_b[j], R.actT_b[ft]], writes=[R.psO_b[p]])
            f0 += nf
        kb.op("dve", lambda h, dt=dt, p=p: h.scalar_tensor_tensor(out=R.xblk[:, dt, :], in0=R.psO[p][:], scalar=0.5,
                                                                  in1=R.xblk[:, dt, :], op0=ALU.mult, op1=ALU.add),
              reads=[R.psO_b[p], R.xblk_b[dt]], writes=[R.xblk_b[dt]])


XG = 4


def load_xblk(R, xT, tb, dbuf=None, dbufs=None):
    kb = R.kb
    for k0 in range(0, KT, XG):
        rd = ([dbuf] if dbuf is not None else []) + (list(dbufs[k0:k0 + XG]) if dbufs is not None else [])
        kb.dma("sp", R.xblk[:, k0:k0 + XG, :],
               xT[k0 * 128:(k0 + XG) * 128, tb * TB:(tb + 1) * TB].rearrange("(k p) t -> p k t", p=128),
               reads=rd, writes=R.xblk_b[k0:k0 + XG])


def store_xblk(R, xT, tb, dbufs=None):
    kb = R.kb
    for k0 in range(0, KT, XG):
        wr = list(dbufs[k0:k0 + XG]) if dbufs is not None else []
        kb.dma("sp", xT[k0 * 128:(k0 + XG) * 128, tb * TB:(tb + 1) * TB].rearrange("(k p) t -> p k t", p=128),
               R.xblk[:, k0:k0 + XG, :], reads=R.xblk_b[k0:k0 + XG], writes=wr)


def build_ffn_only():
    nc = bass.Bass("TRN2", target_bir_lowering=False)
    xT = nc.dram_tensor("xT", [D, TPC], F32, kind="ExternalInput").ap()
    g = nc.dram_tensor("g", [D], F32, kind="ExternalInput").ap()
    w_gu = nc.dram_tensor("w_gu", [D, 2 * FF], F32, kind="ExternalInput").ap()
    w_down = nc.dram_tensor("w_down", [FF, D], F32, kind="ExternalInput").ap()
    yT = nc.dram_tensor("yT", [D, TPC], F32, kind="ExternalOutput").ap()
    kb = KB(nc)
    R = RowRes(kb)
    R.load_gains(0, g)
    for tb in range(NTB):
        load_xblk(R, xT, tb)
        ffn_block(R, 0, w_gu, w_down)
        store_xblk(R, yT, tb)
    kb.finish()
    return nc


NPAIR = 3
QB = 512
NQB = L // QB
NKB = L // 128


def build_sb_attn(npair=NPAIR, nqb=NQB):
    nc = bass.Bass("TRN2", target_bir_lowering=False)
    qT = nc.dram_tensor("qT", [npair, 128, L], BF16, kind="ExternalInput").ap()
    kT = nc.dram_tensor("kT", [npair, 128, L], BF16, kind="ExternalInput").ap()
    v = nc.dram_tensor("v", [npair, L, 128], BF16, kind="ExternalInput").ap()
    oT = nc.dram_tensor("oT", [npair, 128, L], BF16, kind="ExternalOutput").ap()
    kb = KB(nc)
    sb_attn_phase(kb, qT, kT, v, oT, npair, nqb)
    kb.finish()
    return nc


def sb_attn_phase(kb, qT, kT, v, oT, npair, nqb, load_fn=None, store_fn=None):
    nc = kb.nc
    q_s = [kb.sb("q_s%d" % i, (128, L), BF16) for i in range(2)]
    k_s = [kb.sb("k_s%d" % i, (128, L), BF16) for i in range(2)]
    v_s = [kb.sb("v_s%d" % i, (128, NKB, 128), BF16) for i in range(2)]
    qkv_b = [Buf() for _ in range(2)]
    NE = 6
    NZ = 3
    e_s = [kb.sb("e_s%d" % i, (128, QB), F32) for i in range(NE)]
    e_b = [Buf() for _ in range(NE)]
    sp_s = [kb.sb("sp_s%d" % i, (128, QB), BF16) for i in range(NE)]
    sp_b = [Buf() for _ in range(NE)]
    NX = 3
    x_s = [kb.sb("x_s%d" % i, (128, QB), F32) for i in range(NX)]
    x_b = [Buf() for _ in range(NX)]
    w_s = [kb.sb("w_s%d" % i, (128, QB), BF16) for i in range(NX)]
    w_b = [Buf() for _ in range(NX)]
    o_s = [kb.sb("o_s%d" % i, (128, QB), BF16) for i in range(2)]
    o_b = [Buf() for _ in range(2)]
    ones_f = kb.sb("ones_f", (128, 128), F32)
    uinc = kb.sb("uinc", (128, 128), BF16)
    lstr = kb.sb("lstr", (128, 128), BF16)
    c_b = Buf()
    psZ = [kb.ps("psZ%d" % i) for i in range(3)]
    psZ_b = [Buf() for _ in range(3)]
    psP = [kb.ps("psP%d" % i) for i in range(2)]
    psP_b = [Buf() for _ in range(2)]
    psO = [kb.ps("psOa%d" % i) for i in range(2)]
    psO_b = [Buf() for _ in range(2)]

    kb.op("pool", lambda h: h.memset(ones_f[:], 1.0), writes=[c_b])
    kb.op("pool", lambda h: h.affine_select(out=uinc[:], in_=ones_f[:], pattern=[[-1, 128]], compare_op=ALU.is_ge,
                                            fill=0.0, base=0, channel_multiplier=1), reads=[c_b], writes=[c_b])
    kb.op("pool", lambda h: h.affine_select(out=lstr[:], in_=ones_f[:], pattern=[[1, 128]], compare_op=ALU.is_gt,
                                            fill=0.0, base=0, channel_multiplier=-1), reads=[c_b], writes=[c_b])

    def load_pair(pi):
        j = pi % 2
        if load_fn is not None:
            load_fn(pi, q_s[j], k_s[j], v_s[j], qkv_b[j])
            return
        kb.dma("sp", q_s[j][:], qT[pi], writes=[qkv_b[j]])
        kb.dma("sp", k_s[j][:], kT[pi], writes=[qkv_b[j]])
        kb.dma("sp", v_s[j][:], v[pi].rearrange("(kb p) d -> p kb d", p=128), writes=[qkv_b[j]])

    S0 = [0, 3, 4, 7, 8, 11, 12, 15]
    S1 = [1, 2, 5, 6, 9, 10, 13, 14]
    tiles = []
    for pi in range(npair):
        streams = []
        for si, qbs in enumerate((S0, S1)):
            lst = []
            for qb in qbs:
                if qb >= nqb:
                    continue
                nk = 4 * (qb + 1)
                for idx in range(nk):
                    lst.append((pi, qb, nk - 1 - idx, idx, nk, si))
            streams.append(lst)
        n_ = max(len(streams[0]), len(streams[1]))
        for k in range(n_):
            for si in range(2):
                tiles.append(streams[si][k] if k < len(streams[si]) else None)

    tctr = [0]

    def stage1(t):
        pi, qb, kbi, idx, nk, g = t
        j = pi % 2
        n = tctr[0] % NE
        z = tctr[0] % NZ
        tctr[0] += 1
        kb.op("pe", lambda h: h.matmul(psZ[z][:], k_s[j][:, kbi * 128:(kbi + 1) * 128],
                                       q_s[j][:, qb * QB:(qb + 1) * QB], start=True, stop=True),
              reads=[qkv_b[j]], writes=[psZ_b[z]])
        kb.op("act", lambda h: h.activation(out=e_s[n][:], in_=psZ[z][:], func=AF.Exp),
              reads=[psZ_b[z]], writes=[e_b[n]])
        kb.op("act", lambda h: h.activation(out=sp_s[n][:], in_=e_s[n][:], func=AF.Ln, bias=1.0),
              reads=[e_b[n]], writes=[sp_b[n]])
        dj = kbi - 4 * qb
        if dj >= 0:
            kb.op("pool", lambda h: h.affine_select(out=sp_s[n][:], in_=sp_s[n][:], pattern=[[1, QB]],
                                                    compare_op=ALU.is_gt, fill=0.0, base=-128 * dj,
                                                    channel_multiplier=-1),
                  reads=[sp_b[n]], writes=[sp_b[n]])
        return n

    xctr = [0]

    def stageB1(t, n):
        pi, qb, kbi, idx, nk, g = t
        xi = xctr[0] % NX
        xctr[0] += 1
        kb.op("pe", lambda h: h.matmul(psP[g][:], uinc[:], sp_s[n][:], start=(idx == 0), stop=(idx == nk - 1)),
              reads=[sp_b[n], c_b], writes=[psP_b[g]])
        kb.op("act", lambda h: h.activation(out=x_s[xi][:], in_=psP[g][:], func=AF.Exp, scale=-1.0),
              reads=[psP_b[g]], writes=[x_b[xi]])
        return xi

    def stageB2(t, n):
        pi, qb, kbi, idx, nk, g = t
        if idx != nk - 1:
            kb.op("pe", lambda h: h.matmul(psP[g][:], lstr[:], sp_s[n][:], start=False, stop=False),
                  reads=[sp_b[n], c_b], writes=[psP_b[g]])

    def stageC(t, n, xi):
        pi, qb, kbi, idx, nk, g = t
        j = pi % 2
        kb.op("dve", lambda h: h.tensor_tensor(out=w_s[xi][:], in0=e_s[n][:], in1=x_s[xi][:], op=ALU.mult),
              reads=[e_b[n], x_b[xi]], writes=[w_b[xi]])
        dj = kbi - 4 * qb
        if dj >= 0:
            kb.op("pool", lambda h: h.affine_select(out=w_s[xi][:], in_=w_s[xi][:], pattern=[[1, QB]],
                                                    compare_op=ALU.is_gt, fill=0.0, base=-128 * dj,
                                                    channel_multiplier=-1),
                  reads=[w_b[xi]], writes=[w_b[xi]])
        kb.op("pe", lambda h: h.matmul(psO[g][:], v_s[j][:, kbi, :], w_s[xi][:], start=(idx == 0),
                                       stop=(idx == nk - 1)),
              reads=[w_b[xi], qkv_b[j]], writes=[psO_b[g]])
        if idx == nk - 1:
            kb.op("act", lambda h: h.activation(out=o_s[g][:], in_=psO[g][:], func=AF.Copy),
                  reads=[psO_b[g]], writes=[o_b[g]])
            if store_fn is not None:
                store_fn(pi, qb, o_s[g], o_b[g])
            else:
                kb.dma("sp", oT[pi, :, qb * QB:(qb + 1) * QB], o_s[g][:], reads=[o_b[g]])

    load_pair(0)
    loaded = 1
    pipe = [None, None, None, None]
    seen_pairs = set()

    def step(new):
        a1, a2, b1, b2 = pipe
        nb1 = None
        if a2 is not None:
            t_, n_ = a2
            xi = stageB1(t_, n_)
            nb1 = (t_, n_, xi)
        nb2 = None
        if b1 is not None:
            stageB2(b1[0], b1[1])
            nb2 = b1
        if b2 is not None:
            stageC(*b2)
        pipe[0], pipe[1], pipe[2], pipe[3] = new, a1, nb1, nb2

    cur_pair = 0
    for t in tiles + [("end",)]:
        if t is None:
            step(None)
            continue
        if t[0] != cur_pair or t[0] == "end":
            for _ in range(4):
                step(None)
            if t[0] == "end":
                break
            cur_pair = t[0]
        if t[0] not in seen_pairs:
            seen_pairs.add(t[0])
            if loaded < npair and loaded <= t[0] + 1:
                load_pair(loaded)
                loaded += 1
        n = stage1(t)
        step((t, n))


NST = 12
SCH = 512
TWO_PI = 2.0 * math.pi
GELU_C = 2.0 * math.sqrt(2.0 / math.pi)


def build_s5(nchunks=L // SCH):
    nc = bass.Bass("TRN2", target_bir_lowering=False)
    dr = {}
    for name, shape in (("uT", [384, L]), ("ldt", [128, NST]), ("are", [128, NST]), ("aim", [128, NST]),
                        ("Bre", [128, NST, 128]), ("Bim", [128, NST, 128]), ("Cre", [128, NST, 128]),
                        ("Cim", [128, NST, 128]), ("dcol", [128, 3])):
        dr[name] = nc.dram_tensor(name, shape, F32, kind="ExternalInput").ap()
    ygT = nc.dram_tensor("ygT", [384, L], BF16, kind="ExternalOutput").ap()
    kb = KB(nc)
    s5_phase(kb, dr, ygT, nchunks)
    kb.finish()
    return nc


def s5_phase(kb, dr, ygT, nchunks, uload_fn=None, ystore_fn=None):
    def small(name, w=NST):
        return kb.sb(name, (128, w), F32), Buf()

    def tt(eng, out, a, b, op, reads, writes):
        kb.op(eng, lambda h: h.tensor_tensor(out=out, in0=a, in1=b, op=op), reads=reads, writes=writes)

    def ts(eng, out, a, s1, op0, reads, writes, s2=None, op1=None):
        if op1 is None:
            kb.op(eng, lambda h: h.tensor_scalar(out=out, in0=a, scalar1=s1, scalar2=None, op0=op0),
                  reads=reads, writes=writes)
        else:
            kb.op(eng, lambda h: h.tensor_scalar(out=out, in0=a, scalar1=s1, scalar2=s2, op0=op0, op1=op1),
                  reads=reads, writes=writes)

    def stt(eng, out, a, s, b, op0, op1, reads, writes):
        kb.op(eng, lambda h: h.scalar_tensor_tensor(out=out, in0=a, scalar=s, in1=b, op0=op0, op1=op1),
              reads=reads, writes=writes)

    def act(out, a, func, reads, writes, **kw):
        kb.op("act", lambda h: h.activation(out=out, in_=a, func=func, **kw), reads=reads, writes=writes)

    P = {}
    for name in ("ldt", "are", "aim"):
        P[name] = small("p_" + name)
        kb.dma("sp", P[name][0][:], dr[name], writes=[P[name][1]])
    Bre = kb.sb("Bre_s", (128, NST, 128), F32)
    Bim = kb.sb("Bim_s", (128, NST, 128), F32)
    Cre = kb.sb("Cre_s", (128, NST, 128), F32)
    Cim = kb.sb("Cim_s", (128, NST, 128), F32)
    C2re = kb.sb("C2re", (128, NST, 128), F32)
    C2im = kb.sb("C2im", (128, NST, 128), F32)
    C2ren = kb.sb("C2ren", (128, NST, 128), F32)
    dcol = kb.sb("dcol_s", (128, 3), F32)
    par_b = Buf()
    c2_b = Buf()
    for t, name in ((Bre, "Bre"), (Bim, "Bim"), (Cre, "Cre"), (Cim, "Cim"), (dcol, "dcol")):
        kb.dma("sp", t[:], dr[name], writes=[par_b])

    dt_, dt_b = small("dt_")
    act(dt_[:], P["ldt"][0][:], AF.Exp, [P["ldt"][1]], [dt_b])
    rl, rl_b = small("rl")
    th, th_b = small("th")
    tt("dve", rl[:], P["are"][0][:], dt_[:], ALU.mult, [P["are"][1], dt_b], [rl_b])
    tt("dve", th[:], P["aim"][0][:], dt_[:], ALU.mult, [P["aim"][1], dt_b], [th_b])
    r_, r_b = small("r_")
    act(r_[:], rl[:], AF.Exp, [rl_b], [r_b])
    phi, phi_b = small("phi")
    ts("dve", phi[:], th[:], 1.0 / TWO_PI, ALU.mult, [th_b], [phi_b])
    ki = kb.sb("ki", (128, NST), I32)
    ki_b = Buf()
    kb.op("dve", lambda h: h.tensor_copy(out=ki[:], in_=phi[:]), reads=[phi_b], writes=[ki_b])
    kf, kf_b = small("kf")
    kb.op("dve", lambda h: h.tensor_copy(out=kf[:], in_=ki[:]), reads=[ki_b], writes=[kf_b])
    f_, f_b = small("f_")
    tt("dve", f_[:], phi[:], kf[:], ALU.subtract, [phi_b, kf_b], [f_b])
    cos1, cos1_b = small("cos1")
    sin1, sin1_b = small("sin1")
    tmpa, tmpa_b = small("tmpa")
    tmpb, tmpb_b = small("tmpb")

    def sin_of_frac(out, out_b, frac, frac_b):
        ts("dve", tmpa[:], frac, 0.5, ALU.is_gt, [frac_b], [tmpa_b])
        tt("dve", tmpb[:], frac, tmpa[:], ALU.subtract, [frac_b, tmpa_b], [tmpb_b])
        ts("dve", tmpa[:], tmpb[:], -0.5, ALU.is_lt, [tmpb_b], [tmpa_b])
        tt("dve", tmpb[:], tmpb[:], tmpa[:], ALU.add, [tmpb_b, tmpa_b], [tmpb_b])
        act(out, tmpb[:], AF.Sin, [tmpb_b], [out_b], scale=TWO_PI)

    sin_of_frac(sin1[:], sin1_b, f_[:], f_b)
    fc, fc_b = small("fc")
    ts("dve", fc[:], f_[:], 0.25, ALU.add, [f_b], [fc_b])
    sin_of_frac(cos1[:], cos1_b, fc[:], fc_b)

    p_, p_b = small("p_")
    q_, q_b = small("q_")
    tt("dve", p_[:], r_[:], cos1[:], ALU.mult, [r_b, cos1_b], [p_b])
    ts("dve", p_[:], p_[:], -1.0, ALU.add, [p_b], [p_b])
    tt("dve", q_[:], r_[:], sin1[:], ALU.mult, [r_b, sin1_b], [q_b])
    den, den_b = small("den")
    t1, t1_b = small("t1")
    t2, t2_b = small("t2")
    are, are_b = P["are"]
    aim, aim_b = P["aim"]
    tt("dve", den[:], are[:], are[:], ALU.mult, [are_b], [den_b])
    tt("dve", t1[:], aim[:], aim[:], ALU.mult, [aim_b], [t1_b])
    tt("dve", den[:], den[:], t1[:], ALU.add, [den_b, t1_b], [den_b])
    kb.op("dve", lambda h: h.reciprocal(out=den[:], in_=den[:]), reads=[den_b], writes=[den_b])
    gre, gre_b = small("gre")
    gim, gim_b = small("gim")
    tt("dve", t1[:], p_[:], are[:], ALU.mult, [p_b, are_b], [t1_b])
    tt("dve", t2[:], q_[:], aim[:], ALU.mult, [q_b, aim_b], [t2_b])
    tt("dve", gre[:], t1[:], t2[:], ALU.add, [t1_b, t2_b], [gre_b])
    tt("dve", gre[:], gre[:], den[:], ALU.mult, [gre_b, den_b], [gre_b])
    tt("dve", t1[:], q_[:], are[:], ALU.mult, [q_b, are_b], [t1_b])
    tt("dve", t2[:], p_[:], aim[:], ALU.mult, [p_b, aim_b], [t2_b])
    tt("dve", gim[:], t1[:], t2[:], ALU.subtract, [t1_b, t2_b], [gim_b])
    tt("dve", gim[:], gim[:], den[:], ALU.mult, [gim_b, den_b], [gim_b])
    zre, zre_b = small("zre")
    zim, zim_b = small("zim")
    nzim, nzim_b = small("nzim")
    tt("dve", t1[:], gre[:], cos1[:], ALU.mult, [gre_b, cos1_b], [t1_b])
    tt("dve", t2[:], gim[:], sin1[:], ALU.mult, [gim_b, sin1_b], [t2_b])
    tt("dve", zre[:], t1[:], t2[:], ALU.add, [t1_b, t2_b], [zre_b])
    tt("dve", t1[:], gim[:], cos1[:], ALU.mult, [gim_b, cos1_b], [t1_b])
    tt("dve", t2[:], gre[:], sin1[:], ALU.mult, [gre_b, sin1_b], [t2_b])
    tt("dve", zim[:], t1[:], t2[:], ALU.subtract, [t1_b, t2_b], [zim_b])
    ts("dve", nzim[:], zim[:], -1.0, ALU.mult, [zim_b], [nzim_b])

    ctmp = kb.sb("ctmp", (128, 128), F32)
    ctmp_b = Buf()
    for st in range(NST):
        ts("dve", ctmp[:], Cim[:, st, :], zim[:, st:st + 1], ALU.mult, [par_b, zim_b], [ctmp_b])
        stt("dve", C2re[:, st, :], Cre[:, st, :], zre[:, st:st + 1], ctmp[:], ALU.mult, ALU.subtract,
            [par_b, zre_b, ctmp_b], [c2_b])
        ts("dve", ctmp[:], Cim[:, st, :], zre[:, st:st + 1], ALU.mult, [par_b, zre_b], [ctmp_b])
        stt("dve", C2im[:, st, :], Cre[:, st, :], nzim[:, st:st + 1], ctmp[:], ALU.mult, ALU.subtract,
            [par_b, nzim_b, ctmp_b], [c2_b])
        ts("dve", C2ren[:, st, :], C2re[:, st, :], -1.0, ALU.mult, [c2_b], [c2_b])

    TW = SCH + 8
    tabC = kb.sb("tabC", (128, NST, TW), F32)
    tabS = kb.sb("tabS", (128, NST, TW), F32)
    tab_b = [Buf() for _ in range(NST)]
    ttmp = kb.sb("ttmp", (128, 256), F32)
    ttmp_b = Buf()
    rt = kb.sb("rt", (128, NST, SCH), F32)
    rt_b = Buf()
    onesw = kb.sb("onesw", (128, SCH), F32)
    onesw_b = Buf()
    kb.op("pool", lambda h: h.memset(onesw[:], 1.0), writes=[onesw_b])
    for st in range(NST):
        b = tab_b[st]
        kb.op("pool", lambda h, st=st: h.memset(tabC[:, st, 0:1], 1.0), writes=[b])
        kb.op("pool", lambda h, st=st: h.memset(tabS[:, st, 0:1], 0.0), writes=[b])
        kb.op("act", lambda h, st=st: h.activation(out=tabC[:, st, 1:2], in_=cos1[:, st:st + 1], func=AF.Copy),
              reads=[cos1_b], writes=[b])
        kb.op("act", lambda h, st=st: h.activation(out=tabS[:, st, 1:2], in_=sin1[:, st:st + 1], func=AF.Copy),
              reads=[sin1_b], writes=[b])
        m = 1
        while m < SCH:
            cm = tabC[:, st, m:m + 1]
            sm = tabS[:, st, m:m + 1]
            ts("dve", ttmp[:, 0:m], tabS[:, st, 1:m + 1], sm, ALU.mult, [b], [ttmp_b])
            stt("dve", tabC[:, st, m + 1:2 * m + 1], tabC[:, st, 1:m + 1], cm, ttmp[:, 0:m], ALU.mult, ALU.subtract,
                [b, ttmp_b], [b])
            ts("dve", ttmp[:, 0:m], tabC[:, st, 1:m + 1], sm, ALU.mult, [b], [ttmp_b])
            stt("dve", tabS[:, st, m + 1:2 * m + 1], tabS[:, st, 1:m + 1], cm, ttmp[:, 0:m], ALU.mult, ALU.add,
                [b, ttmp_b], [b])
            m *= 2
        ts("pool", rt[:, st, :], onesw[:], r_[:, st:st + 1], ALU.mult, [onesw_b, r_b], [rt_b])

    u_s = [kb.sb("u_s%d" % i, (128, SCH), F32) for i in range(2)]
    u_b = [Buf() for _ in range(2)]
    NR = 2
    m_s = [[kb.sb("m%d_%d" % (k, i), (128, SCH), F32) for k in range(4)] for i in range(NR)]
    m_b = [[Buf() for k in range(4)] for i in range(NR)]
    wv_s = [[kb.sb("wv%d_%d" % (k, i), (128, SCH), F32) for k in range(2)] for i in range(NR)]
    wv_b = [[Buf() for k in range(2)] for i in range(NR)]
    n_s = [[kb.sb("n%d_%d" % (k, i), (128, SCH), F32) for k in range(4)] for i in range(NR)]
    n_b = [[Buf() for k in range(4)] for i in range(NR)]
    car_r = kb.sb("car_r", (128, NST), F32)
    car_i = kb.sb("car_i", (128, NST), F32)
    car_b = [Buf() for _ in range(NST)]
    yv = kb.sb("yv", (128, SCH), F32)
    yv_b = Buf()
    e1 = kb.sb("e1", (128, SCH), F32)
    e1_b = Buf()
    e2 = kb.sb("e2", (128, SCH), F32)
    e2_b = Buf()
    yo = [kb.sb("yo%d" % i, (128, SCH), BF16) for i in range(2)]
    yo_b = [Buf() for _ in range(2)]
    psR = [kb.ps("psR%d" % i) for i in range(2)]
    psR_b = [Buf() for _ in range(2)]
    psI = [kb.ps("psI%d" % i) for i in range(2)]
    psI_b = [Buf() for _ in range(2)]
    psY = [kb.ps("psY%d" % i) for i in range(2)]
    psY_b = [Buf() for _ in range(2)]
    psW = [kb.ps("psW%d" % i) for i in range(2)]
    psW_b = [Buf() for _ in range(2)]

    Bre16 = kb.sb("Bre16", (128, NST, 128), BF16)
    Bim16 = kb.sb("Bim16", (128, NST, 128), BF16)
    C2re16 = kb.sb("C2re16", (128, NST, 128), BF16)
    C2ren16 = kb.sb("C2ren16", (128, NST, 128), BF16)
    C2im16 = kb.sb("C2im16", (128, NST, 128), BF16)
    t16_b = Buf()
    for dst, src, sb_ in ((Bre16, Bre, par_b), (Bim16, Bim, par_b), (C2re16, C2re, c2_b), (C2ren16, C2ren, c2_b),
                          (C2im16, C2im, c2_b)):
        kb.op("pool", lambda h, dst=dst, src=src: h.tensor_copy(out=dst[:], in_=src[:]), reads=[sb_], writes=[t16_b])
    u16 = [kb.sb("u16_%d" % i, (128, SCH), BF16) for i in range(2)]
    u16_b = [Buf() for _ in range(2)]
    n16 = [[kb.sb("n16_%d_%d" % (k, i), (128, SCH), BF16) for k in range(4)] for i in range(NR)]
    n16_b = [[Buf() for k in range(4)] for i in range(NR)]
    ctmp4 = kb.sb("ctmp4", (128, 4), F32)
    ctmp4_b = Buf()

    iters = [(ch, ct, sl) for ch in range(nchunks) for ct in range(3) for sl in range(4)]

    def P(k):
        ch, ct, sl = iters[k]
        ui = (ch * 3 + ct) % 2
        if sl == 0:
            if uload_fn is not None:
                uload_fn(ch, ct, u_s[ui], u_b[ui])
            else:
                kb.dma("sp", u_s[ui][:], dr["uT"][ct * 128:(ct + 1) * 128, ch * SCH:(ch + 1) * SCH],
                       writes=[u_b[ui]])
            kb.op("act", lambda h: h.activation(out=u16[ui][:], in_=u_s[ui][:], func=AF.Copy),
                  reads=[u_b[ui]], writes=[u16_b[ui]])
        st = ct * 4 + sl
        i = k % 2
        kb.op("pe", lambda h: h.matmul(psR[i][:], Bre16[:, st, :], u16[ui][:], start=True, stop=True),
              reads=[t16_b, u16_b[ui]], writes=[psR_b[i]])
        kb.op("pe", lambda h: h.matmul(psI[i][:], Bim16[:, st, :], u16[ui][:], start=True, stop=True),
              reads=[t16_b, u16_b[ui]], writes=[psI_b[i]])

    def Dk(k):
        ch, ct, sl = iters[k]
        ui = (ch * 3 + ct) % 2
        yi = (ch * 3 + ct) % 2
        st = ct * 4 + sl
        i = k % 2
        pR, pI, pRb, pIb = psR[i], psI[i], psR_b[i], psI_b[i]
        m, mb = m_s[i], m_b[i]
        Cc = tabC[:, st, 0:SCH]
        Ss = tabS[:, st, 0:SCH]
        tb_ = tab_b[st]
        tt("dve", m[0][:], pR[:], Cc, ALU.mult, [pRb, tb_], [mb[0]])
        tt("dve", m[1][:], pI[:], Ss, ALU.mult, [pIb, tb_], [mb[1]])
        tt("dve", m[2][:], pI[:], Cc, ALU.mult, [pIb, tb_], [mb[2]])
        tt("dve", m[3][:], pR[:], Ss, ALU.mult, [pRb, tb_], [mb[3]])
        tt("pool", m[0][:], m[0][:], m[1][:], ALU.add, [mb[0], mb[1]], [mb[0]])
        tt("pool", m[2][:], m[2][:], m[3][:], ALU.subtract, [mb[2], mb[3]], [mb[2]])
        if ch == 0:
            ini_r = 0.0
            ini_i = 0.0
        else:
            ini_r = car_r[:, st:st + 1]
            ini_i = car_i[:, st:st + 1]
        kb.op("dve", lambda h: h.tensor_tensor_scan(out=psW[0][:], data0=rt[:, st, :], data1=m[0][:],
                                                    initial=ini_r, op0=ALU.mult, op1=ALU.add),
              reads=[rt_b, mb[0], car_b[st]], writes=[psW_b[0]])
        kb.op("dve", lambda h: h.tensor_tensor_scan(out=psW[1][:], data0=rt[:, st, :], data1=m[2][:],
                                                    initial=ini_i, op0=ALU.mult, op1=ALU.add),
              reads=[rt_b, mb[2], car_b[st]], writes=[psW_b[1]])
        n, nb = n16[i], n16_b[i]
        C1 = tabC[:, st, 1:SCH + 1]
        S1 = tabS[:, st, 1:SCH + 1]
        tt("dve", n[0][:], psW[0][:], C1, ALU.mult, [psW_b[0], tb_], [nb[0]])
        tt("dve", n[1][:], psW[1][:], S1, ALU.mult, [psW_b[1], tb_], [nb[1]])
        tt("dve", n[2][:], psW[0][:], S1, ALU.mult, [psW_b[0], tb_], [nb[2]])
        tt("dve", n[3][:], psW[1][:], C1, ALU.mult, [psW_b[1], tb_], [nb[3]])
        L1 = slice(SCH - 1, SCH)
        cl = tabC[:, st, SCH:SCH + 1]
        sl_ = tabS[:, st, SCH:SCH + 1]
        tt("dve", ctmp4[:, 0:1], psW[0][:, L1], cl, ALU.mult, [psW_b[0], tb_], [ctmp4_b])
        tt("dve", ctmp4[:, 1:2], psW[1][:, L1], sl_, ALU.mult, [psW_b[1], tb_], [ctmp4_b])
        tt("dve", ctmp4[:, 2:3], psW[0][:, L1], sl_, ALU.mult, [psW_b[0], tb_], [ctmp4_b])
        tt("dve", ctmp4[:, 3:4], psW[1][:, L1], cl, ALU.mult, [psW_b[1], tb_], [ctmp4_b])
        tt("dve", car_r[:, st:st + 1], ctmp4[:, 0:1], ctmp4[:, 1:2], ALU.subtract, [ctmp4_b], [car_b[st]])
        tt("dve", car_i[:, st:st + 1], ctmp4[:, 2:3], ctmp4[:, 3:4], ALU.add, [ctmp4_b], [car_b[st]])
        kb.op("pe", lambda h: h.matmul(psY[yi][:], C2re16[:, st, :], n[0][:], start=(sl == 0), stop=False),
              reads=[t16_b, nb[0]], writes=[psY_b[yi]])
        kb.op("pe", lambda h: h.matmul(psY[yi][:], C2ren16[:, st, :], n[1][:], start=False, stop=False),
              reads=[t16_b, nb[1]], writes=[psY_b[yi]])
        kb.op("pe", lambda h: h.matmul(psY[yi][:], C2im16[:, st, :], n[2][:], start=False, stop=False),
              reads=[t16_b, nb[2]], writes=[psY_b[yi]])
        kb.op("pe", lambda h: h.matmul(psY[yi][:], C2im16[:, st, :], n[3][:], start=False, stop=(sl == 3)),
              reads=[t16_b, nb[3]], writes=[psY_b[yi]])
        if sl == 3:
            stt("dve", yv[:], u_s[ui][:], dcol[:, ct:ct + 1], psY[yi][:], ALU.mult, ALU.add,
                [u_b[ui], par_b, psY_b[yi]], [yv_b])
            act(e1[:], yv[:], AF.Square, [yv_b], [e1_b])
            ts("dve", e1[:], e1[:], 0.044715, ALU.mult, [e1_b], [e1_b], s2=1.0, op1=ALU.add)
            tt("pool", e2[:], e1[:], yv[:], ALU.mult, [e1_b, yv_b], [e2_b])
            act(e2[:], e2[:], AF.Sigmoid, [e2_b], [e2_b], scale=GELU_C)
            tt("pool", yo[yi][:], e2[:], yv[:], ALU.mult, [e2_b, yv_b], [yo_b[yi]])
            if ystore_fn is not None:
                ystore_fn(ch, ct, yo[yi], yo_b[yi])
            else:
                kb.dma("sp", ygT[ct * 128:(ct + 1) * 128, ch * SCH:(ch + 1) * SCH], yo[yi][:], reads=[yo_b[yi]])

    P(0)
    for k in range(len(iters)):
        if k + 1 < len(iters):
            P(k + 1)
        Dk(k)


def s5_host_params(log_dt, a_re, a_im, b_re, b_im, c_re, c_im, d, gs):
    g0 = gs * 24
    out = {}

    def per_state(a):
        return np.ascontiguousarray(a.reshape(NST, 2, 64).transpose(1, 2, 0).reshape(128, NST))

    out["ldt"] = per_state(np.repeat(log_dt[g0:g0 + 24, None], 64, axis=1))
    out["are"] = per_state(a_re[g0:g0 + 24])
    out["aim"] = per_state(a_im[g0:g0 + 24])
    Bre = np.zeros((128, NST, 128), np.float32)
    Bim = np.zeros((128, NST, 128), np.float32)
    Cre = np.zeros((128, NST, 128), np.float32)
    Cim = np.zeros((128, NST, 128), np.float32)
    for st in range(NST):
        for gl in range(2):
            g = g0 + 2 * st + gl
            r0 = (2 * (st % 4) + gl) * 16
            Bre[r0:r0 + 16, st, gl * 64:(gl + 1) * 64] = b_re[g].T
            Bim[r0:r0 + 16, st, gl * 64:(gl + 1) * 64] = b_im[g].T
            Cre[gl * 64:(gl + 1) * 64, st, r0:r0 + 16] = c_re[g].T
            Cim[gl * 64:(gl + 1) * 64, st, r0:r0 + 16] = c_im[g].T
    out["Bre"], out["Bim"], out["Cre"], out["Cim"] = Bre, Bim, Cre, Cim
    out["dcol"] = np.ascontiguousarray(d[g0 * 16:(g0 + 24) * 16].reshape(3, 128).T)
    return out


ISQ = 1.0 / math.sqrt(128.0)


def gemm16(R, ps, ps_b, slab, slab_b, rhs, rhs_b, nkt=KT, width=TB, col0=0):
    kb = R.kb
    for kt in range(nkt):
        kb.op("pe", lambda h, kt=kt: h.matmul(ps[:, col0:col0 + width], slab[:, kt, :], rhs[:, kt, 0:width],
                                              start=(kt == 0), stop=(kt == nkt - 1)),
              reads=[slab_b, rhs_b[kt]], writes=[ps_b])


def next_slab(R):
    j = R.ctr % R.NSLAB
    R.ctr += 1
    return j


def next_ps(R):
    p = R.pctr % 2
    R.pctr += 1
    return p


def head_rstd(R, ps, ps_b, width):
    kb = R.kb
    kb.op("act", lambda h: h.activation(out=R.sq[0][:, 0:width], in_=ps[:, 0:width], func=AF.Square),
          reads=[ps_b], writes=[R.sq_b[0]])
    kb.op("pe", lambda h: h.matmul(R.psS[:, 0:width], R.ones[:], R.sq[0][:, 0:width], start=True, stop=True),
          reads=[R.sq_b[0], R.ones_b], writes=[R.psS_b])
    kb.op("act", lambda h: h.activation(out=R.rstd[:, 0:width], in_=R.psS[:, 0:width], func=AF.Ln, scale=1.0 / 128,
                                        bias=EPS), reads=[R.psS_b], writes=[R.rstd_b])
    kb.op("act", lambda h: h.activation(out=R.rstd[:, 0:width], in_=R.rstd[:, 0:width], func=AF.Exp, scale=-0.5),
          reads=[R.rstd_b], writes=[R.rstd_b])


def mem_prep(R, memT, w_mem_kv, slot_mem):
    kb = R.kb
    for kt in range(KT):
        kb.dma("sp", R.xblk[:, kt, 0:NMEM], memT[kt * 128:(kt + 1) * 128, :], writes=[R.xblk_b[kt]])
    rmsnorm_block(R, slot_mem, R.xblk, R.xblk_b, dst=R.memh, dst_b=R.memh_b, width=NMEM)
    for hd in range(4):
        j = next_slab(R)
        p = next_ps(R)
        load_slab(kb, R.wA[j], R.wA_b[j], w_mem_kv, hd * 128, 128, KT)
        gemm16(R, R.psA[p], R.psA_b[p], R.wA[j], R.wA_b[j], R.memh, R.memh_b, width=NMEM)
        head_rstd(R, R.psA[p], R.psA_b[p], NMEM)
        kb.op("dve", lambda h, j=j, p=p, hd=hd: h.scalar_tensor_tensor(out=R.KmT[:, hd, :], in0=R.psA[p][:, 0:NMEM],
                                                                  scalar=R.gqk[:, 1:2], in1=R.rstd[:, 0:NMEM],
                                                                  op0=ALU.mult, op1=ALU.mult),
              reads=[R.psA_b[p], R.gqk_b, R.rstd_b], writes=[R.kv_b])
    for hd in range(4):
        j = next_slab(R)
        p = next_ps(R)
        load_slab(kb, R.wA[j], R.wA_b[j], w_mem_kv, MEMW + hd * 128, 128, KT)
        for mt in range(2):
            for kt in range(KT):
                kb.op("pe", lambda h, kt=kt, mt=mt, j=j, p=p: h.matmul(R.psB[p][:, mt * 128:(mt + 1) * 128],
                                                                  R.memh[:, kt, mt * 128:(mt + 1) * 128],
                                                                  R.wA[j][:, kt, :], start=(kt == 0),
                                                                  stop=(kt == KT - 1)),
                      reads=[R.wA_b[j], R.memh_b[kt]], writes=[R.psB_b[p]])
        for mt in range(2):
            kb.op("act", lambda h, mt=mt, j=j, p=p, hd=hd: h.activation(out=R.Vm[:, mt, hd * 128:(hd + 1) * 128],
                                                                   in_=R.psB[p][:, mt * 128:(mt + 1) * 128],
                                                                   func=AF.Copy),
                  reads=[R.psB_b[p]], writes=[R.kv_b])


def _wb(outs):
    lst = outs.get("wb_list")
    if not lst:
        return []
    k = outs["wb_ctr"][0]
    outs["wb_ctr"][0] = k + 1
    return [lst[k % len(lst)]]


def mixin_block(R, kind, slot_mix, w_in, outs, tb):
    kb = R.kb
    tsl = slice(tb * TB, (tb + 1) * TB)
    rmsnorm_block(R, slot_mix, R.xblk, R.xblk_b)
    if kind == "sb":
        for nt in range(24):
            j = next_slab(R)
            p = next_ps(R)
            load_slab(kb, R.wA[j], R.wA_b[j], w_in, nt * 128, 128, KT)
            gemm16(R, R.psA[p], R.psA_b[p], R.wA[j], R.wA_b[j], R.hT, R.hT_b)
            s = R.sctr % 2
            R.sctr += 1
            kb.op("act", lambda h, j=j, p=p, s=s, nt=nt: h.activation(out=R.stg[s][:], in_=R.psA[p][:], func=AF.Copy,
                                                                 scale=(ISQ if nt < 12 else 1.0)),
                  reads=[R.psA_b[p]], writes=[R.stg_b[s]])
            if "qk_dst" in outs:
                dap, dwb = outs["qk_dst"](nt, tb)
                kb.dma("sp", dap, R.stg[s][:], reads=[R.stg_b[s]], wshared=dwb)
            else:
                kb.dma("sp", outs["qkT"][nt * 128:(nt + 1) * 128, tsl], R.stg[s][:], reads=[R.stg_b[s]])
        for nt in range(12):
            j = next_slab(R)
            p = next_ps(R)
            load_slab(kb, R.wA[j], R.wA_b[j], w_in, 3072 + nt * 128, 128, KT)
            for t4 in range(4):
                for kt in range(KT):
                    kb.op("pe", lambda h, kt=kt, t4=t4, j=j, p=p: h.matmul(R.psB[p][:, t4 * 128:(t4 + 1) * 128],
                                                                      R.hT[:, kt, t4 * 128:(t4 + 1) * 128],
                                                                      R.wA[j][:, kt, :], start=(kt == 0),
                                                                      stop=(kt == KT - 1)),
                          reads=[R.wA_b[j], R.hT_b[kt]], writes=[R.psB_b[p]])
            s = R.sctr % 2
            R.sctr += 1
            kb.op("act", lambda h, j=j, p=p, s=s: h.activation(out=R.stg[s][:], in_=R.psB[p][:], func=AF.Copy),
                  reads=[R.psB_b[p]], writes=[R.stg_b[s]])
            if "v_dst" in outs:
                dap, dwb = outs["v_dst"](nt, tb)
                kb.dma("sp", dap, R.stg[s][:].rearrange("p (t c) -> p t c", t=4), reads=[R.stg_b[s]], wshared=dwb)
            else:
                kb.dma("sp", outs["v"][tsl, nt * 128:(nt + 1) * 128].rearrange("(t p) c -> p t c", p=128),
                       R.stg[s][:].rearrange("p (t c) -> p t c", t=4), reads=[R.stg_b[s]])
        qm0 = 4608
    else:
        for nt in range(12):
            j = next_slab(R)
            p = next_ps(R)
            load_slab(kb, R.wA[j], R.wA_b[j], w_in, nt * 128, 128, KT)
            gemm16(R, R.psA[p], R.psA_b[p], R.wA[j], R.wA_b[j], R.hT, R.hT_b)
            kb.op("act", lambda h, j=j, p=p: h.activation(out=R.sg[p][:], in_=R.psA[p][:], func=AF.Copy),
                  reads=[R.psA_b[p]], writes=[R.sg_b[p]])
            if "u_dst" in outs:
                dap, dwb = outs["u_dst"](nt, tb)
                kb.dma("sp", dap, R.sg[p][:], reads=[R.sg_b[p]], wshared=dwb)
            else:
                kb.dma("sp", outs["uT"][nt * 128:(nt + 1) * 128, tsl], R.sg[p][:], reads=[R.sg_b[p]])
        qm0 = 1536
    for hd in range(4):
        j = next_slab(R)
        p = next_ps(R)
        load_slab(kb, R.wA[j], R.wA_b[j], w_in, qm0 + hd * 128, 128, KT)
        gemm16(R, R.psA[p], R.psA_b[p], R.wA[j], R.wA_b[j], R.hT, R.hT_b)
        head_rstd(R, R.psA[p], R.psA_b[p], TB)
        kb.op("dve", lambda h, j=j, p=p: h.scalar_tensor_tensor(out=R.qn[:], in0=R.psA[p][:], scalar=R.gqk[:, 0:1],
                                                           in1=R.rstd[:], op0=ALU.mult, op1=ALU.mult),
              reads=[R.psA_b[p], R.gqk_b, R.rstd_b], writes=[R.qn_b])
        for mt in range(2):
            kb.op("pe", lambda h, mt=mt, hd=hd: h.matmul(R.psB[mt][:], R.KmT[:, hd, mt * 128:(mt + 1) * 128], R.qn[:],
                                                         start=True, stop=True),
                  reads=[R.kv_b, R.qn_b], writes=[R.psB_b[mt]])
            kb.op("act", lambda h, mt=mt: h.activation(out=R.pT[:, mt, :], in_=R.psB[mt][:], func=AF.Exp, scale=ISQ),
                  reads=[R.psB_b[mt]], writes=[R.pT_b[mt]])
        for mt in range(2):
            kb.op("pe", lambda h, mt=mt: h.matmul(R.psS[:], R.ones_bf[:], R.pT[:, mt, :], start=(mt == 0),
                                                  stop=(mt == 1)),
                  reads=[R.ones_b, R.pT_b[mt]], writes=[R.psS_b])
        o = hd % 2
        for mt in range(2):
            kb.op("pe", lambda h, mt=mt, hd=hd, o=o: h.matmul(R.psO[o][:], R.Vm[:, mt, hd * 128:(hd + 1) * 128],
                                                              R.pT[:, mt, :], start=(mt == 0), stop=(mt == 1)),
                  reads=[R.kv_b, R.pT_b[mt]], writes=[R.psO_b[o]])
        kb.op("dve", lambda h: h.reciprocal(out=R.rstd[:], in_=R.psS[:]), reads=[R.psS_b], writes=[R.rstd_b])
        s = R.sctr % 2
        R.sctr += 1
        kb.op("dve", lambda h, o=o, s=s: h.tensor_tensor(out=R.stg[s][:], in0=R.psO[o][:], in1=R.rstd[:],
                                                         op=ALU.mult),
              reads=[R.psO_b[o], R.rstd_b], writes=[R.stg_b[s]])
        kb.dma("sp", outs["crossT_out"][hd * 128:(hd + 1) * 128, tsl], R.stg[s][:], reads=[R.stg_b[s]],
               writes=(outs.get("cross_wb") or []))


def post_block(R, kind, tokT, crossT, w_out, w_glu, tb, tok_src=None, rd=()):
    kb = R.kb
    tsl = slice(tb * TB, (tb + 1) * TB)
    if tok_src is None:
        tok_src = lambda kt: tokT[kt * 128:(kt + 1) * 128, tsl]
    rd = list(rd)
    if kind == "sb":
        for kt in range(12):
            kb.dma("sp", R.hT[:, kt, :], tok_src(kt), reads=rd, writes=[R.hT_b[kt]])
    else:
        for kt in range(12):
            kb.dma("sp", R.actT[:, kt, :], tok_src(kt), reads=rd, writes=[R.actT_b[kt]])
        for nt in range(12):
            j = next_slab(R)
            p = next_ps(R)
            load_slab(kb, R.wA[j], R.wA_b[j], w_glu, nt * 128, 128, 12)
            gemm16(R, R.psA[p], R.psA_b[p], R.wA[j], R.wA_b[j], R.actT, R.actT_b, nkt=12)
            kb.op("act", lambda h, j=j, p=p: h.activation(out=R.sg[p][:], in_=R.psA[p][:], func=AF.Sigmoid),
                  reads=[R.psA_b[p]], writes=[R.sg_b[p]])
            kb.op("dve", lambda h, j=j, p=p, nt=nt: h.tensor_tensor(out=R.hT[:, nt, :], in0=R.actT[:, nt, :],
                                                               in1=R.sg[p][:], op=ALU.mult),
                  reads=[R.actT_b[nt], R.sg_b[p]], writes=[R.hT_b[nt]])
    for kt in range(12, 16):
        kb.dma("sp", R.hT[:, kt, :], crossT[(kt - 12) * 128:(kt - 11) * 128, tsl], reads=rd,
               writes=[R.hT_b[kt]])
    for dt in range(KT):
        j = next_slab(R)
        p = next_ps(R)
        load_slab(kb, R.wB[j], R.wB_b[j], w_out, dt * 128, 128, KT)
        gemm16(R, R.psO[p], R.psO_b[p], R.wB[j], R.wB_b[j], R.hT, R.hT_b)
        kb.op("dve", lambda h, dt=dt, j=j, p=p: h.tensor_tensor(out=R.xblk[:, dt, :], in0=R.psO[p][:],
                                                           in1=R.xblk[:, dt, :], op=ALU.add),
              reads=[R.psO_b[p], R.xblk_b[dt]], writes=[R.xblk_b[dt]])


def build_row(post, ffn2, ffn1, mix):
    nc = bass.Bass("TRN2", target_bir_lowering=False)

    def din(name, shape, dt=F32):
        return nc.dram_tensor(name, shape, dt, kind="ExternalInput").ap()

    def dout(name, shape, dt=F32):
        return nc.dram_tensor(name, shape, dt, kind="ExternalOutput").ap()

    xT = din("xT", [D, TPC])
    xT_out = dout("xT_out", [D, TPC])
    a = {}
    if post:
        a["tokT"] = din("tokT", [TOKW, TPC], BF16)
        a["crossT_in"] = din("crossT_in", [MEMW, TPC], BF16)
        a["w_out"] = din("w_out", [D, D])
        if post == "s5":
            a["w_glu"] = din("w_glu", [TOKW, TOKW])
    if ffn2:
        a["g_ffn2"] = din("g_ffn2", [D])
        a["w_gu2"] = din("w_gu2", [D, 2 * FF])
        a["w_down2"] = din("w_down2", [FF, D])
    if ffn1:
        a["g_ffn1"] = din("g_ffn1", [D])
        a["w_gu1"] = din("w_gu1", [D, 2 * FF])
        a["w_down1"] = din("w_down1", [FF, D])
    outs = {}
    if mix:
        a["g_mix"] = din("g_mix", [D])
        a["w_in"] = din("w_in", [D, 5120 if mix == "sb" else 2048])
        a["memT"] = din("memT", [D, NMEM])
        a["g_mem"] = din("g_mem", [D])
        a["w_mem_kv"] = din("w_mem_kv", [D, 2 * MEMW])
        a["gqk"] = din("gqk", [128, 2])
        outs["crossT_out"] = dout("crossT_out", [MEMW, TPC], BF16)
        if mix == "sb":
            outs["qkT"] = dout("qkT", [3072, TPC], BF16)
            outs["v"] = dout("v", [TPC, TOKW], BF16)
        else:
            outs["uT"] = dout("uT", [TOKW, TPC], F32)
    kb = KB(nc)
    R = RowRes(kb)
    if ffn2:
        R.load_gains(0, a["g_ffn2"])
    if ffn1:
        R.load_gains(1, a["g_ffn1"])
    if mix:
        R.load_gains(2, a["g_mix"])
        R.load_gains(3, a["g_mem"])
        kb.dma("sp", R.gqk[:], a["gqk"], writes=[R.gqk_b])
        mem_prep(R, a["memT"], a["w_mem_kv"], 3)
    for tb in range(NTB):
        load_xblk(R, xT, tb)
        if post:
            post_block(R, post, a["tokT"], a["crossT_in"], a["w_out"], a.get("w_glu"), tb)
        if ffn2:
            ffn_block(R, 0, a["w_gu2"], a["w_down2"])
        if ffn1:
            ffn_block(R, 1, a["w_gu1"], a["w_down1"])
        if mix:
            mixin_block(R, mix, 2, a["w_in"], outs, tb)
        store_xblk(R, xT_out, tb)
    kb.finish()
    return nc


_PROGS = {}


def _prog(key, fn):
    if key not in _PROGS:
        _PROGS[key] = fn()
    return _PROGS[key]


def _run(nc, in_maps):
    res = run_bass_kernel_spmd(nc, in_maps, core_ids=list(range(NCORES)))
    return res.results


def kernel_unfused(x, mem, ffn1_norm, ffn1_w_gu, ffn1_w_down, mix_norm, mem_norm, w_mem_kv, xq_norm, xk_norm, w_out,
           ffn2_norm, ffn2_w_gu, ffn2_w_down, sb_w_in, s5_w_in, s5_log_dt, s5_a_re, s5_a_im, s5_b_re, s5_b_im,
           s5_c_re, s5_c_im, s5_d, s5_w_glu, _debug=None):
    f32 = np.float32
    A = lambda t: np.ascontiguousarray(np.asarray(t, dtype=f32))
    x = A(x)
    mem = A(mem)
    xT = [np.ascontiguousarray(x[c // 4, (c % 4) * TPC:(c % 4 + 1) * TPC, :].T) for c in range(NCORES)]
    memT = [np.ascontiguousarray(mem[b].T) for b in range(NB)]
    tokT = None
    crossT = None
    for stage in range(DEPTH + 1):
        li = stage
        lp = stage - 1
        post = None if lp < 0 else ("sb" if lp % 2 == 0 else "s5")
        mix = None if li >= DEPTH else ("sb" if li % 2 == 0 else "s5")
        ffn2 = lp >= 0
        ffn1 = li < DEPTH
        nc = _prog(("row", post, ffn2, ffn1, mix), lambda: build_row(post, ffn2, ffn1, mix))
        maps = []
        for c in range(NCORES):
            m = {"xT": xT[c]}
            if post:
                m["tokT"] = tokT[c]
                m["crossT_in"] = crossT[c]
                m["w_out"] = A(w_out[lp])
                if post == "s5":
                    m["w_glu"] = A(s5_w_glu[lp // 2])
            if ffn2:
                m["g_ffn2"] = A(ffn2_norm[lp])
                m["w_gu2"] = A(ffn2_w_gu[lp])
                m["w_down2"] = A(ffn2_w_down[lp])
            if ffn1:
                m["g_ffn1"] = A(ffn1_norm[li])
                m["w_gu1"] = A(ffn1_w_gu[li])
                m["w_down1"] = A(ffn1_w_down[li])
            if mix:
                m["g_mix"] = A(mix_norm[li])
                m["w_in"] = A(sb_w_in[li // 2]) if mix == "sb" else A(s5_w_in[li // 2])
                m["memT"] = memT[c // 4]
                m["g_mem"] = A(mem_norm[li])
                m["w_mem_kv"] = A(w_mem_kv[li])
                m["gqk"] = np.ascontiguousarray(np.stack([A(xq_norm[li]), A(xk_norm[li])], axis=1))
            maps.append(m)
        res = _run(nc, maps)
        xT = [res[c]["xT_out"] for c in range(NCORES)]
        if _debug is not None:
            _debug["x_stage%d" % stage] = [a.copy() for a in xT]
        if not mix:
            break
        crossT = [res[c]["crossT_out"] for c in range(NCORES)]
        if mix == "sb":
            maps = []
            for c in range(NCORES):
                qs, ks, vs = [], [], []
                for i in range(NPAIR):
                    p = c * NPAIR + i
                    b, hd = p // 12, p % 12
                    qs.append(np.concatenate([res[b * 4 + qd]["qkT"][hd * 128:(hd + 1) * 128] for qd in range(4)], axis=1))
                    ks.append(np.concatenate([res[b * 4 + qd]["qkT"][(12 + hd) * 128:(13 + hd) * 128] for qd in range(4)], axis=1))
                    vs.append(np.concatenate([res[b * 4 + qd]["v"][:, hd * 128:(hd + 1) * 128] for qd in range(4)], axis=0))
                maps.append({"qT": np.ascontiguousarray(np.stack(qs)), "kT": np.ascontiguousarray(np.stack(ks)),
                             "v": np.ascontiguousarray(np.stack(vs))})
            nc2 = _prog(("sb",), lambda: build_sb_attn())
            r2 = _run(nc2, maps)
            tokT = []
            for c in range(NCORES):
                b, qd = c // 4, c % 4
                rows = []
                for hd in range(12):
                    p = b * 12 + hd
                    rows.append(r2[p // NPAIR]["oT"][p % NPAIR][:, qd * TPC:(qd + 1) * TPC])
                tokT.append(np.ascontiguousarray(np.concatenate(rows, axis=0)))
        else:
            jj = li // 2
            maps = []
            for c in range(NCORES):
                b, gs = c // 4, c % 4
                m = s5_host_params(A(s5_log_dt[jj]), A(s5_a_re[jj]), A(s5_a_im[jj]), A(s5_b_re[jj]), A(s5_b_im[jj]),
                                   A(s5_c_re[jj]), A(s5_c_im[jj]), A(s5_d[jj]), gs)
                m["uT"] = np.ascontiguousarray(
                    np.concatenate([res[b * 4 + qd]["uT"][gs * 384:(gs + 1) * 384] for qd in range(4)], axis=1))
                maps.append(m)
            nc2 = _prog(("s5",), lambda: build_s5())
            r2 = _run(nc2, maps)
            tokT = []
            for c in range(NCORES):
                b, qd = c // 4, c % 4
                tokT.append(np.ascontiguousarray(
                    np.concatenate([r2[b * 4 + gs]["ygT"][:, qd * TPC:(qd + 1) * TPC] for gs in range(4)], axis=0)))
        if _debug is not None:
            _debug["tok_stage%d" % stage] = [a.copy() for a in tokT]
            _debug["cross_stage%d" % stage] = [a.copy() for a in crossT]
    out = np.empty((NB, L, D), f32)
    for c in range(NCORES):
        out[c // 4, (c % 4) * TPC:(c % 4 + 1) * TPC, :] = xT[c].T
    return out


GROUPS = [[0, 1, 2, 3], [4, 5, 6, 7]]
NWB = 8


class Exchange:
    def __init__(self, kb, name, nchunks, rows, cols, dt):
        nc = kb.nc
        self.kb, self.n, self.rows, self.cols = kb, nchunks, rows, cols
        self.snd = nc.dram_tensor(name + "_snd", [nchunks * 4 * rows, cols], dt)
        self.rcv = nc.dram_tensor(name + "_rcv", [nchunks * 16 * rows, cols], dt)
        self.loc = nc.dram_tensor(name + "_loc", [nchunks * 4 * rows, cols], dt)
        self.snd_b = [Buf() for _ in range(nchunks)]
        self.rcv_b = [Buf() for _ in range(nchunks)]
        self.loc_b = Buf()
        self.ci = None

    def early(self, chunks):
        self.kb.coll_chunks(self, GROUPS, chunks=chunks)

    def snd_rows(self, c, j, r0=0, n=None):
        n = self.rows if n is None else n
        base = (c * 4 + j) * self.rows + r0
        return self.snd.ap()[base:base + n, :]

    def loc_rows(self, c, r, r0=0, n=None):
        n = self.rows if n is None else n
        base = (c * 4 + r) * self.rows + r0
        return self.loc.ap()[base:base + n, :]

    def run(self, jv):
        kb = self.kb
        kb.coll_chunks(self, GROUPS)
        kb.dma("sp", self.loc.ap().rearrange("(cr x) t -> cr x t", x=self.rows),
               self.rcv.ap().rearrange("(cr j x) t -> cr j x t", j=4, x=self.rows)[:, jv],
               reads=self.rcv_b, writes=[self.loc_b])


def build_fused():
    nc = bass.Bass("TRN2", target_bir_lowering=False)

    def din(name, shape, dt=F32):
        return nc.dram_tensor(name, shape, dt, kind="ExternalInput").ap()

    xT = din("xT", [D, TPC])
    memT = din("memT", [D, NMEM])
    W = {}
    for name, shape in (("ffn1_norm", [DEPTH, D]), ("ffn1_w_gu", [DEPTH, D, 2 * FF]), ("ffn1_w_down", [DEPTH, FF, D]),
                        ("mix_norm", [DEPTH, D]), ("mem_norm", [DEPTH, D]), ("w_mem_kv", [DEPTH, D, 2 * MEMW]),
                        ("gqk", [DEPTH, 128, 2]), ("w_out", [DEPTH, D, D]), ("ffn2_norm", [DEPTH, D]),
                        ("ffn2_w_gu", [DEPTH, D, 2 * FF]), ("ffn2_w_down", [DEPTH, FF, D]),
                        ("sb_w_in", [2, D, 5120]), ("s5_w_in", [2, D, 2048]), ("s5_w_glu", [2, TOKW, TOKW]),
                        ("ldt", [2, 128, NST]), ("are", [2, 128, NST]), ("aim", [2, 128, NST]),
                        ("Bre", [2, 128, NST, 128]), ("Bim", [2, 128, NST, 128]), ("Cre", [2, 128, NST, 128]),
                        ("Cim", [2, 128, NST, 128]), ("dcol", [2, 128, 3])):
        W[name] = din(name, shape)
    yT = nc.dram_tensor("yT", [D, TPC], F32, kind="ExternalOutput").ap()

    xs = nc.dram_tensor("xs", [D, TPC], F32).ap()
    xs_b = [[Buf() for _ in range(KT)] for _ in range(NTB)]
    cross = [nc.dram_tensor("cross%d" % i, [MEMW, TPC], BF16).ap() for i in range(DEPTH)]
    cross_b = [[Buf() for _ in range(NTB)] for _ in range(DEPTH)]

    kb = KB(nc)
    pid = nc.sync.partition_id()
    jv = pid % 4
    ex_tok = None
    kb.wc = {"on": False, "first": True, "idxA": 0, "idxD": 0, "bufsA": [], "bufsD": [],
             "tA": nc.dram_tensor("wcacheA", [352, 128, KT * 128], BF16),
             "tD": nc.dram_tensor("wcacheD", [32, 128, FT * 128], BF16)}

    for li in range(DEPTH + 1):
        lp = li - 1
        post = None if lp < 0 else ("sb" if lp % 2 == 0 else "s5")
        mix = None if li >= DEPTH else ("sb" if li % 2 == 0 else "s5")
        kb.push()
        R = RowRes(kb)
        outs = {}
        if lp >= 0:
            R.load_gains(0, W["ffn2_norm"][lp])
        if li < DEPTH:
            R.load_gains(1, W["ffn1_norm"][li])
        if mix:
            R.load_gains(2, W["mix_norm"][li])
            R.load_gains(3, W["mem_norm"][li])
            kb.dma("sp", R.gqk[:], W["gqk"][li], writes=[R.gqk_b])
            mem_prep(R, memT, W["w_mem_kv"][li], 3)
            outs["crossT_out"] = cross[li]
            if mix == "sb":
                ex_qk = Exchange(kb, "qk%d" % li, 12, 128, 1024, BF16)
                ex_v = Exchange(kb, "v%d" % li, 6, 1024, 128, BF16)

                def qk_dst(nt, tb, ex=ex_qk):
                    a_, h = nt // 12, nt % 12
                    j, i = h // 3, h % 3
                    c = (a_ * 3 + i) * 2 + tb // 2
                    return ex.snd_rows(c, j)[:, (tb % 2) * TB:(tb % 2 + 1) * TB], [ex.snd_b[c]]

                def v_dst(nt, tb, ex=ex_v):
                    j, i = nt // 3, nt % 3
                    c = i * 2 + tb // 2
                    ap = ex.snd_rows(c, j, (tb % 2) * TB, TB).rearrange("(t p) c -> p t c", p=128)
                    return ap, [ex.snd_b[c]]

                outs["qk_dst"] = qk_dst
                outs["v_dst"] = v_dst
            else:
                ex_u = Exchange(kb, "u%d" % li, 12, 128, 512, F32)

                def u_dst(nt, tb, ex=ex_u):
                    j, ct = nt // 3, nt % 3
                    c = ct * 4 + tb
                    return ex.snd_rows(c, j), [ex.snd_b[c]]

                outs["u_dst"] = u_dst
        for tb in range(NTB):
            kb.wc["on"] = False
            kb.wc["first"] = (tb == 0)
            kb.wc["idxA"] = 0
            kb.wc["idxD"] = 0
            if li == 0:
                load_xblk(R, xT, tb)
            else:
                load_xblk(R, xs, tb, dbufs=xs_b[tb])
            if post:
                def tok_src(kt, tb=tb, ex=ex_tok):
                    r, i = kt // 3, kt % 3
                    c = i * 2 + tb // 2
                    return ex.loc_rows(c, r)[:, (tb % 2) * TB:(tb % 2 + 1) * TB]

                post_block(R, post, None, cross[lp], W["w_out"][lp],
                           W["s5_w_glu"][lp // 2] if post == "s5" else None, tb, tok_src=tok_src,
                           rd=[ex_tok.loc_b, cross_b[lp][tb]])
                ffn_block(R, 0, W["ffn2_w_gu"][lp], W["ffn2_w_down"][lp])
            if li < DEPTH:
                ffn_block(R, 1, W["ffn1_w_gu"][li], W["ffn1_w_down"][li])
            if mix:
                outs["cross_wb"] = [cross_b[li][tb]]
                mixin_block(R, mix, 2, W["sb_w_in"][li // 2] if mix == "sb" else W["s5_w_in"][li // 2], outs, tb)
                if mix == "sb" and tb == 1:
                    ex_qk.early([c for c in range(12) if c % 2 == 0])
                    ex_v.early([c for c in range(6) if c % 2 == 0])
                elif mix == "s5" and tb < NTB - 1:
                    ex_u.early([ct * 4 + tb for ct in range(3)])
            if li == DEPTH:
                store_xblk(R, yT, tb)
            else:
                store_xblk(R, xs, tb, dbufs=xs_b[tb])
        kb.wc["on"] = False
        kb.pop()
        if not mix:
            break
        if mix == "sb":
            ex_qk.run(jv)
            ex_v.run(jv)
            ex_o = Exchange(kb, "o%d" % li, 6, 128, 1024, BF16)

            def load_fn(pi, q_s, k_s, v_s, b, ex_qk=ex_qk, ex_v=ex_v):
                for r in range(4):
                    for th in range(2):
                        c0 = r * TPC + th * 1024
                        kb.dma("sp", q_s[:, c0:c0 + 1024], ex_qk.loc_rows(pi * 2 + th, r), reads=[ex_qk.loc_b],
                               writes=[b])
                        kb.dma("sp", k_s[:, c0:c0 + 1024], ex_qk.loc_rows((3 + pi) * 2 + th, r), reads=[ex_qk.loc_b],
                               writes=[b])
                        kb.dma("sp", v_s[:, r * 16 + th * 8:r * 16 + th * 8 + 8, :],
                               ex_v.loc_rows(pi * 2 + th, r).rearrange("(kb p) d -> p kb d", p=128),
                               reads=[ex_v.loc_b], writes=[b])

            def store_fn(pi, qb, o_s, o_b, ex=ex_o):
                qd, tq = qb // 4, qb % 4
                c = pi * 2 + tq // 2
                kb.dma("sp", ex.snd_rows(c, qd)[:, (tq % 2) * QB:(tq % 2 + 1) * QB], o_s[:], reads=[o_b],
                       wshared=[ex.snd_b[c]])

            kb.push()
            sb_attn_phase(kb, None, None, None, None, NPAIR, NQB, load_fn=load_fn, store_fn=store_fn)
            kb.pop()
            ex_o.run(jv)
            ex_tok = ex_o
        else:
            ex_u.run(jv)
            ex_y = Exchange(kb, "y%d" % li, 6, 128, 1024, BF16)

            def uload_fn(ch, ct, u_s, u_b, ex=ex_u):
                qd, tq = ch // 4, ch % 4
                kb.dma("sp", u_s[:], ex.loc_rows(ct * 4 + tq, qd), reads=[ex.loc_b], writes=[u_b])

            def ystore_fn(ch, ct, yo, yo_b, ex=ex_y):
                qd, tq = ch // 4, ch % 4
                c = ct * 2 + tq // 2
                kb.dma("sp", ex.snd_rows(c, qd)[:, (tq % 2) * SCH:(tq % 2 + 1) * SCH], yo[:], reads=[yo_b],
                       wshared=[ex.snd_b[c]])

            jj = li // 2
            dr = {k: W[k][jj] for k in ("ldt", "are", "aim", "Bre", "Bim", "Cre", "Cim", "dcol")}
            kb.push()
            s5_phase(kb, dr, None, L // SCH, uload_fn=uload_fn, ystore_fn=ystore_fn)
            kb.pop()
            ex_y.run(jv)
            ex_tok = ex_y
    kb.finish()
    return nc


def kernel(x, mem, ffn1_norm, ffn1_w_gu, ffn1_w_down, mix_norm, mem_norm, w_mem_kv, xq_norm, xk_norm, w_out,
           ffn2_norm, ffn2_w_gu, ffn2_w_down, sb_w_in, s5_w_in, s5_log_dt, s5_a_re, s5_a_im, s5_b_re, s5_b_im,
           s5_c_re, s5_c_im, s5_d, s5_w_glu):
    f32 = np.float32
    A = lambda t: np.ascontiguousarray(np.asarray(t, dtype=f32))
    x = A(x)
    mem = A(mem)
    nc = _prog(("fused",), build_fused)
    shared = {
        "ffn1_norm": A(ffn1_norm), "ffn1_w_gu": A(ffn1_w_gu), "ffn1_w_down": A(ffn1_w_down),
        "mix_norm": A(mix_norm), "mem_norm": A(mem_norm), "w_mem_kv": A(w_mem_kv),
        "gqk": np.ascontiguousarray(np.stack([A(xq_norm), A(xk_norm)], axis=2)),
        "w_out": A(w_out), "ffn2_norm": A(ffn2_norm), "ffn2_w_gu": A(ffn2_w_gu), "ffn2_w_down": A(ffn2_w_down),
        "sb_w_in": A(sb_w_in), "s5_w_in": A(s5_w_in), "s5_w_glu": A(s5_w_glu),
    }
    s5p = []
    for gs in range(4):
        per = [s5_host_params(A(s5_log_dt[jj]), A(s5_a_re[jj]), A(s5_a_im[jj]), A(s5_b_re[jj]), A(s5_b_im[jj]),
                              A(s5_c_re[jj]), A(s5_c_im[jj]), A(s5_d[jj]), gs) for jj in range(2)]
        s5p.append({k: np.ascontiguousarray(np.stack([per[0][k], per[1][k]])) for k in per[0]})
    maps = []
    for c in range(NCORES):
        m = dict(shared)
        m.update(s5p[c % 4])
        m["xT"] = np.ascontiguousarray(x[c // 4, (c % 4) * TPC:(c % 4 + 1) * TPC, :].T)
        m["memT"] = np.ascontiguousarray(mem[c // 4].T)
        maps.append(m)
    res = _run(nc, maps)
    out = np.empty((NB, L, D), f32)
    for c in range(NCORES):
        out[c // 4, (c % 4) * TPC:(c % 4 + 1) * TPC, :] = res[c]["yT"].T
    return out
```
